# Optimizing a Trainium2 kernel written in Bass

```python
import math
import jax
import jax.numpy as jnp
from jax import lax
import numpy as np

D_MODEL = 1024
BATCH = 16
SEQ = 256
DEPTH = 4
DEC_BATCH = 4
DEC_SEQ = 2048
PAST_LEN = 256

GRID_W = 64
F32 = jnp.float32

MLA_HEADS = 8
QK_NOPE = 64
QK_ROPE = 32
QK_HEAD = QK_NOPE + QK_ROPE
V_HEAD = 64
Q_RANK = 256
KV_RANK = 128
ROPE_THETA = 10000.0
Q_BLOCK = 128

SSD_HEADS = 4
SSD_HEAD_DIM = 64
SSD_INNER = SSD_HEADS * SSD_HEAD_DIM
SSD_GROUPS = 2
SSD_STATE = 64
SSD_CONV = 5
SSD_CHUNK = 128

CM_CH = 256
CM_WIDTH = 31

D_FF = -(-8 * D_MODEL // (3 * 256)) * 256
N_MOD = 6

MLA_IN = Q_RANK + KV_RANK + QK_ROPE
SSD_XBC = SSD_INNER + 2 * SSD_GROUPS * SSD_STATE
SSD_IN = SSD_INNER + SSD_XBC + 2 * SSD_HEADS
CM_IN = 2 * CM_CH
OFF_SSD = MLA_IN
OFF_CM = MLA_IN + SSD_IN
IN_WIDTH = OFF_CM + CM_IN
MIX_WIDTH = MLA_HEADS * V_HEAD + SSD_INNER + CM_CH

kernel_name = 'hybrid_mla_ssd_conformer_dit_step'


def rmsnorm(x, g, eps=1e-6):
    xf = x.astype(F32)
    y = xf * lax.rsqrt(jnp.mean(xf * xf, axis=-1, keepdims=True) + eps)
    return (y * g.astype(F32)).astype(x.dtype)


def layernorm(x, g, b, eps=1e-5):
    xf = x.astype(F32)
    mu = jnp.mean(xf, axis=-1, keepdims=True)
    var = jnp.mean(jnp.square(xf - mu), axis=-1, keepdims=True)
    y = (xf - mu) * lax.rsqrt(var + eps)
    return (y * g.astype(F32) + b.astype(F32)).astype(x.dtype)


def dwconv(x, w, b):
    k = w.shape[0]
    y = lax.conv_general_dilated(x, w[:, None, :].astype(x.dtype), (1,), [(k // 2, k // 2)],
                                 dimension_numbers=('NWC', 'WIO', 'NWC'),
                                 feature_group_count=x.shape[-1])
    return y + b


def rope_tables(t):
    rows = t // GRID_W
    row = jnp.repeat(jnp.arange(rows), GRID_W).astype(F32)
    col = (jnp.arange(rows * GRID_W) % GRID_W).astype(F32)
    nf = QK_ROPE // 4
    inv = ROPE_THETA ** (-jnp.arange(nf, dtype=F32) / nf)
    ang = jnp.stack([row[:, None] * inv, col[:, None] * inv], axis=1)
    return jnp.cos(ang), jnp.sin(ang)


def apply_rope(x, cos, sin):
    xf = x.astype(F32).reshape(x.shape[:-1] + (2, 2, QK_ROPE // 4))
    x1, x2 = xf[..., 0, :], xf[..., 1, :]
    c, s = cos[:, None], sin[:, None]
    out = jnp.stack([x1 * c - x2 * s, x2 * c + x1 * s], axis=-2)
    return out.reshape(x.shape).astype(x.dtype)


def block_attention(q, k, v):
    b, tq, h, dk = q.shape
    nb = tq // Q_BLOCK
    scale = dk ** -0.5
    qb = q.reshape(b, nb, Q_BLOCK, h, dk).transpose(1, 0, 2, 3, 4)

    def one_block(qi):
        s = jnp.einsum('bqhd,bkhd->bhqk', qi, k).astype(F32) * scale
        p = jax.nn.softmax(s, axis=-1).astype(v.dtype)
        return jnp.einsum('bhqk,bkhd->bqhd', p, v)

    o = lax.map(one_block, qb)
    return o.transpose(1, 0, 2, 3, 4).reshape(b, tq, h, v.shape[-1])


def mla_kv(ckv, kr_h, w_ukv):
    b, t, _ = ckv.shape
    kv = (ckv @ w_ukv).reshape(b, t, MLA_HEADS, QK_NOPE + V_HEAD)
    k = jnp.concatenate([kv[..., :QK_NOPE],
                         jnp.broadcast_to(kr_h, (b, t, MLA_HEADS, QK_ROPE))], axis=-1)
    return k, kv[..., QK_NOPE:]


def mla_mixer(comb, lp, rope, ctx):
    b, t, _ = comb.shape
    q = rmsnorm(comb[..., :Q_RANK], lp['g_q']) @ lp['w_uq']
    q = q.reshape(b, t, MLA_HEADS, QK_HEAD)
    ckv = rmsnorm(comb[..., Q_RANK:Q_RANK + KV_RANK], lp['g_kv'])
    kr = comb[..., Q_RANK + KV_RANK:MLA_IN]
    q_nope, q_rope = q[..., :QK_NOPE], q[..., QK_NOPE:]
    kr_h = kr[:, :, None, :]
    if rope is not None:
        q_rope = apply_rope(q_rope, *rope)
        kr_h = apply_rope(kr_h, *rope)
    q = jnp.concatenate([q_nope, q_rope], axis=-1)
    k, v = mla_kv(ckv, kr_h, lp['w_ukv'])
    if ctx is not None:
        k_c, v_c = mla_kv(ctx[0], ctx[1][:, :, None, :], lp['w_ukv'])
        k = jnp.concatenate([k_c, k], axis=1)
        v = jnp.concatenate([v_c, v], axis=1)
    o = block_attention(q, k, v).reshape(b, t, MLA_HEADS * V_HEAD)
    return o, ckv, kr


def ssd_scan(x, dt, bm, cm, a, h0):
    bsz, t, h, p = x.shape
    n = bm.shape[-1]
    nc = t // SSD_CHUNK
    L = SSD_CHUNK
    xf = (x.astype(F32) * dt[..., None]).reshape(bsz, nc, L, h, p)
    bf = bm.astype(F32).reshape(bsz, nc, L, h, n)
    cf = cm.astype(F32).reshape(bsz, nc, L, h, n)
    acs = jnp.cumsum((dt * a).reshape(bsz, nc, L, h).transpose(0, 3, 1, 2), axis=-1)
    lower = jnp.tril(jnp.ones((L, L), bool))
    seg = jnp.exp(jnp.where(lower, acs[..., :, None] - acs[..., None, :], -jnp.inf))
    y_diag = jnp.einsum('bclhn,bcshn,bhcls,bcshp->bclhp', cf, bf, seg, xf)
    decay_to_end = jnp.exp(acs[..., -1:] - acs)
    chunk_states = jnp.einsum('bclhn,bhcl,bclhp->bchpn', bf, decay_to_end, xf)
    chunk_decay = jnp.exp(acs[..., -1])

    def step(hc, inp):
        s_c, d_c = inp
        return d_c[:, :, None, None] * hc + s_c, hc

    h_fin, h_in = lax.scan(step, h0.astype(F32),
                           (chunk_states.transpose(1, 0, 2, 3, 4), chunk_decay.transpose(2, 0, 1)))
    y_off = jnp.einsum('bclhn,cbhpn,bhcl->bclhp', cf, h_in, jnp.exp(acs))
    y = (y_diag + y_off).reshape(bsz, t, h, p)
    return y.astype(x.dtype), h_fin.astype(h0.dtype)


def ssd_mixer(comb, lp, h0):
    b, t, _ = comb.shape
    z = comb[..., :SSD_INNER]
    xbc = jax.nn.silu(dwconv(comb[..., SSD_INNER:SSD_INNER + SSD_XBC], lp['ssd_conv_w'], lp['ssd_conv_b']))
    xs = xbc[..., :SSD_INNER].reshape(b, t, SSD_HEADS, SSD_HEAD_DIM)
    gn = SSD_GROUPS * SSD_STATE
    rep = SSD_HEADS // SSD_GROUPS
    bm = jnp.repeat(xbc[..., SSD_INNER:SSD_INNER + gn].reshape(b, t, SSD_GROUPS, SSD_STATE), rep, axis=2)
    cm = jnp.repeat(xbc[..., SSD_INNER + gn:].reshape(b, t, SSD_GROUPS, SSD_STATE), rep, axis=2)
    dt_raw = comb[..., SSD_INNER + SSD_XBC:].reshape(b, t, 2, SSD_HEADS)
    y = jnp.zeros_like(xs)
    finals = []
    for d in range(2):
        dt = jax.nn.softplus(dt_raw[:, :, d].astype(F32) + lp['ssd_dt_bias'][d].astype(F32))
        a = -jnp.exp(lp['ssd_a_log'][d].astype(F32))
        args = (xs, dt, bm, cm)
        if d == 1:
            args = tuple(jnp.flip(u, axis=1) for u in args)
        yd, hd = ssd_scan(*args, a, h0[:, d])
        if d == 1:
            yd = jnp.flip(yd, axis=1)
        y = y + yd + lp['ssd_d'][d][:, None] * xs
        finals.append(hd)
    y = y.reshape(b, t, SSD_INNER)
    y = rmsnorm(y * jax.nn.silu(z), lp['ssd_norm_g'])
    return y, jnp.stack(finals, axis=1)


def conv_module(comb, lp):
    g = comb[..., :CM_CH] * jax.nn.sigmoid(comb[..., CM_CH:])
    g = dwconv(g, lp['cm_conv_w'], lp['cm_conv_b'])
    return jax.nn.silu(layernorm(g, lp['cm_ln_g'], lp['cm_ln_b']))


def layer(x, cond, lp, rope=None, ctx=None):
    mod = (jax.nn.silu(cond) @ lp['w_ada'] + lp['b_ada'])[:, None, :]
    sh1, sc1, g1, sh2, sc2, g2 = jnp.split(mod, N_MOD, axis=-1)
    h = rmsnorm(x, lp['g_mix']) * (1 + sc1) + sh1
    comb = h @ lp['w_in']
    attn, ckv, kr = mla_mixer(comb[..., :MLA_IN], lp, rope, None if ctx is None else ctx[:2])
    if ctx is None:
        h0 = jnp.zeros((x.shape[0], 2, SSD_HEADS, SSD_HEAD_DIM, SSD_STATE), x.dtype)
    else:
        h0 = ctx[2]
    ssm, h_fin = ssd_mixer(comb[..., OFF_SSD:OFF_CM], lp, h0)
    conv = conv_module(comb[..., OFF_CM:], lp)
    mixed = jnp.concatenate([attn, ssm, conv], axis=-1)
    x = x + g1 * (mixed @ lp['w_out'])
    h = rmsnorm(x, lp['g_ffn']) * (1 + sc2) + sh2
    ff = (jax.nn.silu(h @ lp['w_gate']) * (h @ lp['w_up'])) @ lp['w_down']
    x = x + g2 * ff
    return x, ckv, kr, h_fin


def setup_inputs(seed: int = 0) -> dict:
    key = jax.random.key(seed)
    ks = jax.random.split(key, 32)
    L = DEPTH

    def nrm(k, shape, s):
        return jax.random.normal(k, shape, F32) * s

    def gain(k, shape):
        return 1.0 + 0.02 * jax.random.normal(k, shape, F32)

    dt0 = jnp.exp(jax.random.uniform(ks[17], (L, 2, SSD_HEADS), F32,
                                     minval=math.log(1e-3), maxval=math.log(1e-1)))
    return dict(
        x_prompt=nrm(ks[0], (BATCH, SEQ, D_MODEL), 1.0),
        x_sample=nrm(ks[1], (DEC_BATCH, DEC_SEQ, D_MODEL), 1.0),
        c=nrm(ks[2], (DEC_BATCH, D_MODEL), 1.0),
        cache_ckv=nrm(ks[3], (DEC_BATCH, L, PAST_LEN, KV_RANK), 1.0),
        cache_krope=nrm(ks[4], (DEC_BATCH, L, PAST_LEN, QK_ROPE), 1.0),
        state_ssd=nrm(ks[5], (DEC_BATCH, L, 2, SSD_HEADS, SSD_HEAD_DIM, SSD_STATE), 0.5),
        c_ctx=nrm(ks[6], (D_MODEL,), 1.0),
        w_ada=nrm(ks[7], (L, D_MODEL, N_MOD * D_MODEL), 0.5 * D_MODEL ** -0.5),
        b_ada=nrm(ks[8], (L, N_MOD * D_MODEL), 0.02),
        g_mix=gain(ks[9], (L, D_MODEL)),
        w_in=nrm(ks[10], (L, D_MODEL, IN_WIDTH), D_MODEL ** -0.5),
        g_q=gain(ks[11], (L, Q_RANK)),
        w_uq=nrm(ks[12], (L, Q_RANK, MLA_HEADS * QK_HEAD), Q_RANK ** -0.5),
        g_kv=gain(ks[13], (L, KV_RANK)),
        w_ukv=nrm(ks[14], (L, KV_RANK, MLA_HEADS * (QK_NOPE + V_HEAD)), KV_RANK ** -0.5),
        ssd_conv_w=nrm(ks[15], (L, SSD_CONV, SSD_XBC), SSD_CONV ** -0.5),
        ssd_conv_b=nrm(ks[16], (L, SSD_XBC), 0.02),
        ssd_dt_bias=dt0 + jnp.log(-jnp.expm1(-dt0)),
        ssd_a_log=jnp.log(jax.random.uniform(ks[18], (L, 2, SSD_HEADS), F32, minval=1.0, maxval=16.0)),
        ssd_d=gain(ks[19], (L, 2, SSD_HEADS)),
        ssd_norm_g=gain(ks[20], (L, SSD_INNER)),
        cm_conv_w=nrm(ks[21], (L, CM_WIDTH, CM_CH), CM_WIDTH ** -0.5),
        cm_conv_b=nrm(ks[22], (L, CM_CH), 0.02),
        cm_ln_g=gain(ks[23], (L, CM_CH)),
        cm_ln_b=nrm(ks[24], (L, CM_CH), 0.02),
        w_out=nrm(ks[25], (L, MIX_WIDTH, D_MODEL), MIX_WIDTH ** -0.5),
        g_ffn=gain(ks[26], (L, D_MODEL)),
        w_gate=nrm(ks[27], (L, D_MODEL, D_FF), D_MODEL ** -0.5),
        w_up=nrm(ks[28], (L, D_MODEL, D_FF), D_MODEL ** -0.5),
        w_down=nrm(ks[29], (L, D_FF, D_MODEL), D_FF ** -0.5),
        g_final=gain(ks[30], (D_MODEL,)),
    )


def reference(x_prompt, x_sample, c, cache_ckv, cache_krope, state_ssd, c_ctx, w_ada, b_ada,
              g_mix, w_in, g_q, w_uq, g_kv, w_ukv, ssd_conv_w, ssd_conv_b, ssd_dt_bias,
              ssd_a_log, ssd_d, ssd_norm_g, cm_conv_w, cm_conv_b, cm_ln_g, cm_ln_b, w_out,
              g_ffn, w_gate, w_up, w_down, g_final):
    rope = rope_tables(x_sample.shape[1])
    cond_ctx = c_ctx[None, :]
    xp, xs = x_prompt, x_sample
    ckvs, krs, hss = [], [], []
    for l in range(DEPTH):
        lp = dict(w_ada=w_ada[l], b_ada=b_ada[l], g_mix=g_mix[l], w_in=w_in[l], g_q=g_q[l],
                  w_uq=w_uq[l], g_kv=g_kv[l], w_ukv=w_ukv[l], ssd_conv_w=ssd_conv_w[l],
                  ssd_conv_b=ssd_conv_b[l], ssd_dt_bias=ssd_dt_bias[l], ssd_a_log=ssd_a_log[l],
                  ssd_d=ssd_d[l], ssd_norm_g=ssd_norm_g[l], cm_conv_w=cm_conv_w[l],
                  cm_conv_b=cm_conv_b[l], cm_ln_g=cm_ln_g[l], cm_ln_b=cm_ln_b[l], w_out=w_out[l],
                  g_ffn=g_ffn[l], w_gate=w_gate[l], w_up=w_up[l], w_down=w_down[l])
        xp, ckv, kr, h_fin = layer(xp, cond_ctx, lp)
        ckvs.append(ckv)
        krs.append(kr)
        hss.append(h_fin)
        xs = layer(xs, c, lp, rope, (cache_ckv[:, l], cache_krope[:, l], state_ssd[:, l]))[0]
    y_prompt = rmsnorm(xp, g_final)
    y_sample = rmsnorm(xs, g_final)
    new_ckv = jnp.stack(ckvs, axis=1)
    new_krope = jnp.stack(krs, axis=1)
    new_ssd = jnp.stack(hss, axis=1)
    return (y_prompt, y_sample, new_ckv, new_krope, new_ssd)
```

```python
import math
from contextlib import ExitStack
import numpy as np
import concourse.bass as bass
import concourse.mybir as mybir
from concourse.bass_utils import run_bass_kernel_spmd

F32 = mybir.dt.float32
BF16 = mybir.dt.bfloat16
U8 = mybir.dt.uint8
AF = mybir.ActivationFunctionType
ALU = mybir.AluOpType

ENGS = ("pe", "act", "dve", "pool", "sp")

D = 1024
DEPTH = 4
TS = 2048
TPS = 256
NPS = 2
PAST = 256
DFF = 2816
NFG = DFF // 256
IN_W = 1704
SCALE = 96 ** -0.5
NPRM = 161


class View:
    __slots__ = ("name", "ap")

    def __init__(self, name, ap):
        self.name, self.ap = name, ap


class Prog:
    def __init__(self, nc):
        self.nc = nc
        self.ops = []
        self.regions = {}
        self.overlaps = {}

    def region(self, name, space, p0, p1, b0, b1):
        assert name not in self.regions, name
        self.regions[name] = (space, p0, p1, b0, b1)
        ov = [name]
        for n, (s, q0, q1, c0, c1) in self.regions.items():
            if n == name:
                continue
            if s == space and q0 < p1 and p0 < q1 and c0 < b1 and b0 < c1:
                ov.append(n)
                self.overlaps[n].append(name)
        self.overlaps[name] = ov

    def op(self, eng, fn, reads=(), writes=(), dma=False, chan=None, fresh=()):
        rs = [r if isinstance(r, str) else r.name for r in reads]
        ws = [w if isinstance(w, str) else w.name for w in writes]
        fr = [w if isinstance(w, str) else w.name for w in fresh]
        if dma and chan is None:
            chan = ws[0] if ws else rs[0]
        self.ops.append((eng, fn, rs, ws, dma, chan, fr))

    def finalize(self):
        ops = self.ops
        n = len(ops)
        last_w = {}
        readers = {}
        deps = [None] * n
        for i, (eng, fn, rs, ws, dma, chan, fr) in enumerate(ops):
            for f in fr:
                lw = last_w.get(f)
                assert lw is None or len(readers.get(f, ())) > 0, \
                    f"PSUM collision on {f} at op {i} (prev writer {lw} unread)"
            d = set()
            for r in rs:
                for rr in self.overlaps[r]:
                    w = last_w.get(rr)
                    if w is not None:
                        d.add(w)
                    if self.regions[rr][0] == "psum":
                        for x in readers.get(rr, {}).values():
                            d.add(x)
            for w_ in ws:
                for ww in self.overlaps[w_]:
                    w = last_w.get(ww)
                    if w is not None:
                        d.add(w)
                    for x in readers.get(ww, {}).values():
                        d.add(x)
            d.discard(i)
            dd = []
            mine = set(rs) | set(ws)
            for j in d:
                ej, _, rsj, wsj, dmaj, _, _ = ops[j]
                if not dmaj and not dma and ej == eng:
                    if eng == "pe":
                        continue
                    hit = False
                    for x in wsj:
                        for y in self.overlaps[x]:
                            if y in mine:
                                hit = True
                                break
                        if hit:
                            break
                    if not hit:
                        continue
                dd.append(j)
            deps[i] = dd
            key = ("c:" + chan) if dma else eng
            for r in rs:
                readers.setdefault(r, {})[key] = i
            for w_ in ws:
                last_w[w_] = i
                readers[w_] = {}
        needed = set()
        for dd in deps:
            needed.update(dd)
        eng_cnt = {e: 0 for e in ENGS}
        chan_cnt = {}
        ticket = {}
        for i, (eng, fn, rs, ws, dma, chan, fr) in enumerate(ops):
            if dma:
                chan_cnt[chan] = chan_cnt.get(chan, 0) + 16
                ticket[i] = ("c:" + chan, chan_cnt[chan])
            elif i in needed:
                eng_cnt[eng] += 1
                ticket[i] = ("e:" + eng, eng_cnt[eng])
        self.sem_names = ["e:" + e for e in ENGS] + ["c:" + c for c in chan_cnt]
        self.final_counts = {("e:" + e): eng_cnt[e] for e in ENGS}
        self.final_counts.update({("c:" + c): v for c, v in chan_cnt.items()})
        self.deps, self.ticket = deps, ticket
        return self.sem_names

    def emit(self, block, sems):
        ops, deps, ticket = self.ops, self.deps, self.ticket
        per_eng = {e: [] for e in ENGS}
        for i, o in enumerate(ops):
            per_eng[o[0]].append(i)
        final_counts = self.final_counts

        def make(engname):
            def body(e):
                waited = {}
                for i in per_eng[engname]:
                    _, fn, rs, ws, dma, chan, _ = ops[i]
                    need = {}
                    for j in deps[i]:
                        s, v = ticket[j]
                        if need.get(s, 0) < v:
                            need[s] = v
                    for s, v in need.items():
                        if waited.get(s, 0) < v:
                            e.wait_ge(sems[s], v)
                            waited[s] = v
                    ins = fn(e)
                    if i in ticket:
                        s, v = ticket[i]
                        ins.then_inc(sems[s], 16 if dma else 1)
                if engname == "sp":
                    for s, v in final_counts.items():
                        if v > 0 and waited.get(s, 0) < v:
                            e.wait_ge(sems[s], v)
            return body

        block.tensor(make("pe"))
        block.scalar(make("act"))
        block.vector(make("dve"))
        block.gpsimd(make("pool"))
        block.sync(make("sp"))


class Bump:
    def __init__(self, ranges):
        self.ranges = [list(r) for r in ranges]

    def take(self, nb):
        nb = (nb + 31) // 32 * 32
        for r in self.ranges:
            if r[1] - r[0] >= nb:
                b0 = r[0]
                r[0] += nb
                return b0
        raise MemoryError(f"bump pool exhausted need {nb} have {self.ranges}")


def esize(dt):
    return 4 if dt == F32 else 2


def build_program(n_layers=DEPTH, do_sample=True, do_prompt=True, stop=None):
    nc = bass.Bass("TRN2", target_bir_lowering=False)
    P = Prog(nc)

    def din(name, shape):
        return nc.dram_tensor(name, list(shape), F32, kind="ExternalInput").ap()

    def dout(name, shape):
        return nc.dram_tensor(name, list(shape), F32, kind="ExternalOutput").ap()

    d_xs = din("x_s", [TS, D])
    d_xp = din("x_p", [NPS * TPS, D])
    d_cond = din("cond", [2, D])
    d_cckv = din("cache_ckv", [DEPTH, PAST, 128])
    d_ckr = din("cache_krope", [DEPTH, PAST, 32])
    d_st = din("state_ssd", [DEPTH, 2, 4, 64, 64])
    d_wada = din("w_ada", [DEPTH, D, 6 * D])
    d_bada = din("b_ada", [DEPTH, 6 * D])
    d_gmix = din("g_mix", [DEPTH, D])
    d_win = din("w_in", [DEPTH, D, IN_W])
    d_gq = din("g_q", [DEPTH, 256])
    d_wuq = din("w_uq", [DEPTH, 256, 768])
    d_gkv = din("g_kv", [DEPTH, 128])
    d_wukv = din("w_ukv", [DEPTH, 128, 1024])
    d_scw = din("ssd_conv_w", [DEPTH, 5, 512])
    d_scb = din("ssd_conv_b", [DEPTH, 512])
    d_dtb = din("ssd_dt_bias", [DEPTH * 8])
    d_alog = din("ssd_a_log", [DEPTH * 8])
    d_sd = din("ssd_d", [DEPTH * 8])
    d_sng = din("ssd_norm_g", [DEPTH, 256])
    d_ccw = din("cm_conv_w", [DEPTH, 31, 256])
    d_ccb = din("cm_conv_b", [DEPTH, 256])
    d_clg = din("cm_ln_g", [DEPTH, 256])
    d_clb = din("cm_ln_b", [DEPTH, 256])
    d_wout = din("w_out", [DEPTH, D, D])
    d_gffn = din("g_ffn", [DEPTH, D])
    d_wg = din("w_gate", [DEPTH, D, DFF])
    d_wu = din("w_up", [DEPTH, D, DFF])
    d_wd = din("w_down", [DEPTH, DFF, D])
    d_gfin = din("g_final", [D])
    d_cst = din("cst", [128, 640])
    d_rope = din("rope", [2, 128, TS])

    o_ys = dout("y_s", [TS, D])
    o_yp = dout("y_p", [NPS * TPS, D])
    o_ckv = dout("o_ckv", [NPS, DEPTH, TPS, 128])
    o_kr = dout("o_kr", [NPS, DEPTH, TPS, 32])
    o_ssd = dout("o_ssd", [NPS, DEPTH, 2, 4, 64, 64])

    es = ExitStack()
    with es:
        SBYTES = 212000
        S = es.enter_context(nc.sbuf_tensor("S", [128, SBYTES], U8))
        banks = [es.enter_context(nc.psum_tensor(f"PS{i}", [128, 512], F32)) for i in range(8)]
        for i in range(8):
            P.region(f"ps{i}", "psum", 0, 128, i * 2048, (i + 1) * 2048)
        PS = [View(f"ps{i}", banks[i][:, :]) for i in range(8)]
        PSB = [View(f"ps{i}", banks[i][:, :].bitcast(BF16)) for i in range(8)]

        uid = [0]

        def alloc(pool, name, shape, dt, p0=0):
            nel = int(np.prod(shape[1:]))
            nb = nel * esize(dt)
            b0 = pool.take(nb)
            ap = S[p0:p0 + shape[0], b0:b0 + nb].bitcast(dt)
            if len(shape) == 3:
                ap = ap.rearrange("p (a b) -> p a b", a=shape[1])
            elif len(shape) == 4:
                ap = ap.rearrange("p (a b c) -> p a b c", a=shape[1], b=shape[2])
            uid[0] += 1
            nm = f"{name}.{uid[0]}"
            P.region(nm, "sbuf", p0, p0 + shape[0], b0, b0 + ((nb + 31) // 32 * 32))
            return View(nm, ap)

        pers = Bump([(0, SBYTES)])
        xT = alloc(pers, "xT", [128, 8, TS], F32)
        RING_N = 4
        ring = [alloc(pers, f"ring{i}", [128, 4096], BF16) for i in range(RING_N)]
        wuq = alloc(pers, "wuq", [128, 2, 1024], BF16)
        wukv = alloc(pers, "wukv", [128, 1024], BF16)
        cstf = alloc(pers, "cstf", [128, 640], F32)
        identf = View(cstf.name, cstf.ap[:, 0:128])
        SU = [View(cstf.name, cstf.ap[:, 128:256]), View(cstf.name, cstf.ap[:, 256:384])]
        TRI = [View(cstf.name, cstf.ap[:, 384:512]), View(cstf.name, cstf.ap[:, 512:640])]
        identb = alloc(pers, "identb", [128, 128], BF16)
        onesb = alloc(pers, "onesb", [128, 128], BF16)
        onesf = alloc(pers, "onesf", [128, 128], F32)
        ropeC = alloc(pers, "ropeC", [128, TS], BF16)
        ropeS = alloc(pers, "ropeS", [128, TS], BF16)
        prm = alloc(pers, "prm", [128, DEPTH, NPRM], F32)
        cnd = alloc(pers, "cnd", [128, 24], F32)
        scond = alloc(pers, "scond", [128, 8, 2], BF16)
        modv = alloc(pers, "modv", [128, DEPTH, 48, 2], F32)
        gsc = alloc(pers, "gsc", [128, 2, 8], F32)
        dtb_b = alloc(pers, "dtb_b", [128, 32], F32)
        alog_b = alloc(pers, "alog_b", [128, 32], F32)
        dd_b = alloc(pers, "dd_b", [128, 32], F32)
        a_b = alloc(pers, "a_b", [128, 32], F32)
        dsum_b = alloc(pers, "dsum_b", [128, DEPTH, 4], F32)
        gkv_b = alloc(pers, "gkv_b", [128, 128], F32)
        mhalf = alloc(pers, "mhalf", [128, 512], F32)
        xg = alloc(pers, "xg", [128, 8, 512], BF16)
        pers_end = pers.ranges[0][0]
        io = Bump([(pers_end, SBYTES)])
        gpad = alloc(io, "gpad", [128, 2, TS + 32], BF16)
        xbcp = alloc(io, "xbcp", [128, 4, TS + 4], BF16)
        szb = alloc(io, "szb", [128, 16, 256], BF16)
        qlat = alloc(io, "qlat", [128, 2, TS], BF16)
        ckvnT = alloc(io, "ckvnT", [128, PAST + TS], BF16)
        krT = alloc(io, "krT", [128, PAST + TS], BF16)
        dtr = alloc(io, "dtr", [128, 16, 8], F32)
        io_end = io.ranges[0][0]
        hT = alloc(Bump([(pers_end, SBYTES)]), "hT", [128, 8, TS], BF16)
        R = P.regions
        rg = lambda v: (R[v.name][3], R[v.name][4])
        SC0 = io_end
        print("mem: pers_end", pers_end, "io_end", io_end, "scratch", SBYTES - io_end)

        def dma(eng, out, in_, reads=(), writes=(), **kw):
            P.op(eng, lambda e: e.dma_start(out=out, in_=in_, **kw), reads=reads, writes=writes, dma=True)

        def mm(out, lhsT, rhs, start, stop, reads, w, fresh=False):
            P.op("pe", lambda e: e.matmul(out, lhsT=lhsT, rhs=rhs, start=start, stop=stop),
                 reads=reads, writes=[w], fresh=[w] if fresh else ())

        def tp(out, in_, ident, reads, w, fresh=False):
            P.op("pe", lambda e: e.transpose(out, in_, ident), reads=reads, writes=[w],
                 fresh=[w] if fresh else ())

        def act(out, in_, func, reads, writes, **kw):
            P.op("act", lambda e: e.activation(out=out, in_=in_, func=func, **kw), reads=reads, writes=writes)

        def tt(out, in0, in1, op, reads, writes, eng="dve"):
            P.op(eng, lambda e: e.tensor_tensor(out=out, in0=in0, in1=in1, op=op), reads=reads, writes=writes)

        def stt(out, in0, scalar, in1, op0, op1, reads, writes):
            P.op("dve", lambda e: e.scalar_tensor_tensor(out=out, in0=in0, scalar=scalar, in1=in1, op0=op0, op1=op1),
                 reads=reads, writes=writes)

        def ts(out, in0, s1, s2, op0, op1, reads, writes, eng="dve"):
            if op1 is None:
                P.op(eng, lambda e: e.tensor_scalar(out=out, in0=in0, scalar1=s1, scalar2=None, op0=op0),
                     reads=reads, writes=writes)
            else:
                P.op(eng, lambda e: e.tensor_scalar(out=out, in0=in0, scalar1=s1, scalar2=s2, op0=op0, op1=op1),
                     reads=reads, writes=writes)

        def cp(out, in_, reads, writes, eng="dve"):
            P.op(eng, lambda e: e.tensor_copy(out, in_), reads=reads, writes=writes)

        def memset(ap, val, writes, eng="dve"):
            P.op(eng, lambda e: e.memset(ap, val), writes=writes)

        def recip(out, in_, reads, writes):
            P.op("dve", lambda e: e.reciprocal(out=out, in_=in_), reads=reads, writes=writes)

        class Rot:
            def __init__(self, items):
                self.items, self.i = items, 0

            def next(self):
                v = self.items[self.i % len(self.items)]
                self.i += 1
                return v

        psA = Rot([0, 1, 2, 3])
        psBk = Rot([4, 5])

        dma("sp", cstf.ap, d_cst, writes=[cstf])
        dma("pool", ropeC.ap, d_rope[0], writes=[ropeC])
        dma("pool", ropeS.ap, d_rope[1], writes=[ropeS])
        memset(onesb.ap, 1.0, [onesb])
        memset(onesf.ap, 1.0, [onesf])
        memset(mhalf.ap, -0.5, [mhalf])
        cp(identb.ap, identf.ap, [cstf], [identb])
        dma("sp", dtb_b.ap, d_dtb.partition_broadcast(128), writes=[dtb_b])
        dma("sp", alog_b.ap, d_alog.partition_broadcast(128), writes=[alog_b])
        dma("sp", dd_b.ap, d_sd.partition_broadcast(128), writes=[dd_b])
        act(a_b.ap, alog_b.ap, AF.Exp, [alog_b], [a_b])
        ts(a_b.ap, a_b.ap, -1.0, None, ALU.mult, None, [a_b], [a_b])
        ddv = dd_b.ap.rearrange("p (l d h) -> p l d h", l=DEPTH, d=2)
        tt(dsum_b.ap, ddv[:, :, 0, :], ddv[:, :, 1, :], ALU.add, [dd_b], [dsum_b])

        prol = Bump([(SC0, SBYTES)])
        stg = [alloc(prol, f"stg{i}", [128, 128], F32) for i in range(2)]
        O_BADA, O_GMIX, O_GFFN, O_GQ, O_GKV, O_SCW, O_SCB, O_SNG, O_CCW, O_CCB, O_CLG, O_CLB = \
            0, 48, 56, 64, 66, 67, 87, 91, 93, 155, 157, 159
        for l in range(DEPTH):
            rows = [
                (d_bada[l].rearrange("(r c) -> r c", c=128), 48),
                (d_gmix[l].rearrange("(r c) -> r c", c=128), 8),
                (d_gffn[l].rearrange("(r c) -> r c", c=128), 8),
                (d_gq[l].rearrange("(r c) -> r c", c=128), 2),
                (d_gkv[l].rearrange("(r c) -> r c", c=128), 1),
                (d_scw[l].rearrange("j (r c) -> (j r) c", c=128), 20),
                (d_scb[l].rearrange("(r c) -> r c", c=128), 4),
                (d_sng[l].rearrange("(r c) -> r c", c=128), 2),
                (d_ccw[l].rearrange("j (r c) -> (j r) c", c=128), 62),
                (d_ccb[l].rearrange("(r c) -> r c", c=128), 2),
                (d_clg[l].rearrange("(r c) -> r c", c=128), 2),
                (d_clb[l].rearrange("(r c) -> r c", c=128), 2),
            ]
            r0 = 0
            for src, nr in rows:
                done = 0
                while done < nr:
                    si = (r0 + done) // 128
                    off = (r0 + done) % 128
                    k = min(nr - done, 128 - off)
                    dma("sp", stg[si].ap[off:off + k, :], src[done:done + k, :], writes=[stg[si]])
                    done += k
                r0 += nr
            assert r0 == NPRM
            for si, (c0, ncol) in enumerate([(0, 128), (128, NPRM - 128)]):
                pb = psA.next()
                tp(PS[pb].ap[:, 0:ncol], stg[si].ap[0:ncol, :], identf.ap[0:ncol, 0:ncol], [stg[si], cstf], PS[pb], fresh=True)
                cp(prm.ap[:, l, c0:c0 + ncol], PS[pb].ap[:, 0:ncol], [PS[pb]], [prm])
        dma("sp", stg[0].ap[0:16, :], d_cond.rearrange("a (r c) -> (a r) c", c=128), writes=[stg[0]])
        dma("sp", stg[0].ap[16:24, :], d_gfin.rearrange("(r c) -> r c", c=128), writes=[stg[0]])
        pb = psA.next()
        tp(PS[pb].ap[:, 0:24], stg[0].ap[0:24, :], identf.ap[0:24, 0:24], [stg[0], cstf], PS[pb], fresh=True)
        cp(cnd.ap, PS[pb].ap[:, 0:24], [PS[pb]], [cnd])
        act(scond.ap.rearrange("p k c -> p c k"), cnd.ap[:, 0:16].rearrange("p (c k) -> p c k", c=2), AF.Silu, [cnd], [scond])

        ring_i = [0]

        def load_piece(srcs):
            v = ring[ring_i[0] % RING_N]
            ring_i[0] += 1
            for dst_fn, src in srcs:
                dma("pool", dst_fn(v.ap), src, writes=[v])
            return v

        def r3(ap, a):
            return ap.rearrange("p (a b) -> p a b", a=a)

        for l in range(n_layers):
            for pc in range(12):
                v = load_piece([(lambda a: r3(a, 8), d_wada[l, :, pc * 512:(pc + 1) * 512].rearrange("(k p) c -> p k c", p=128))])
                w3 = r3(v.ap, 8)
                for q in range(4):
                    oc = pc * 4 + q
                    for kc in range(8):
                        mm(PS[7].ap[:, oc * 2:oc * 2 + 2], w3[:, kc, q * 128:(q + 1) * 128], scond.ap[:, kc, :],
                           kc == 0, kc == 7, [v, scond], PS[7], fresh=(oc == 0 and kc == 0))
            tt(modv.ap[:, l], PS[7].ap[:, 0:96].rearrange("p (o c) -> p o c", c=2),
               prm.ap[:, l, O_BADA:O_BADA + 48].unsqueeze(2).to_broadcast([128, 48, 2]), ALU.add, [PS[7], prm], [modv])

        class Cfg:
            pass

        def make_cfg(kind):
            c = Cfg()
            c.kind = kind
            if kind == "S":
                c.T, c.TT, c.seqs, c.rope, c.ctx, c.cond = TS, 512, [(0, TS)], True, True, 0
                c.dx, c.oy = d_xs, o_ys
            else:
                c.T, c.TT, c.seqs, c.rope, c.ctx, c.cond = NPS * TPS, 256, [(0, TPS), (TPS, TPS)], False, False, 1
                c.dx, c.oy = d_xp, o_yp
            c.tiles = []
            for si, (s0, sl) in enumerate(c.seqs):
                for t in range(s0, s0 + sl, c.TT):
                    c.tiles.append((si, t))
            c.nblk = c.T // 128
            return c

        def load_x(c):
            pool = Bump([(SC0, SBYTES)])
            xs_ = [alloc(pool, f"xstg{i}", [128, D], F32) for i in range(2)]
            for b in range(c.nblk):
                st = xs_[b % 2]
                dma("sp", st.ap, c.dx[b * 128:(b + 1) * 128, :], writes=[st])
                for half in range(2):
                    pb = psA.next()
                    for q in range(4):
                        kc = half * 4 + q
                        tp(PS[pb].ap[:, q * 128:(q + 1) * 128], st.ap[:, kc * 128:(kc + 1) * 128], identf.ap,
                           [st, cstf], PS[pb], fresh=(q == 0))
                    cp(xT.ap[:, half * 4:half * 4 + 4, b * 128:(b + 1) * 128],
                       PS[pb].ap.rearrange("p (q t) -> p q t", q=4), [PS[pb]], [xT])

        def norm_tile(c, pool_views, t0, n, gsc_ap, sh_ap, out_fn, out_v, dt_out_bf=True):
            sq, sd, rstd, tmpb = pool_views
            for kc in range(8):
                q = sq[kc % 2]
                act(q.ap[:, 0:n], xT.ap[:, kc, t0:t0 + n], AF.Square, [xT], [q])
                mm(PS[6].ap[:, 0:n], onesb.ap, q.ap[:, 0:n], kc == 0, kc == 7, [onesb, q], PS[6], fresh=(kc == 0))
            act(sd.ap[:, 0:n], PS[6].ap[:, 0:n], AF.Ln, [PS[6]], [sd], scale=1.0 / D, bias=1e-6)
            act(rstd.ap[:, 0:n], sd.ap[:, 0:n], AF.Exp, [sd], [rstd], scale=-0.5)
            for kc in range(8):
                tb = tmpb[kc % 2]
                tt(tb.ap[:, 0:n], xT.ap[:, kc, t0:t0 + n], rstd.ap[:, 0:n], ALU.mult, [xT, rstd], [tb])
                if sh_ap is None:
                    ts(out_fn(kc), tb.ap[:, 0:n], gsc_ap[:, kc:kc + 1], None, ALU.mult, None, [tb, cnd], [out_v])
                else:
                    act(out_fn(kc), tb.ap[:, 0:n], AF.Identity, [tb, gsc, modv], [out_v],
                        scale=gsc_ap[:, kc:kc + 1], bias=sh_ap[:, kc:kc + 1])

        def xupdate(c, l, t0, n, lhs_fn, rhs_list, g_off, reads):
            for oc in range(8):
                pb = psA.next()
                nj = len(rhs_list)
                for j in range(nj):
                    mm(PS[pb].ap[:, 0:n], lhs_fn(j, oc), rhs_list[j], j == 0, j == nj - 1, reads, PS[pb], fresh=(j == 0))
                stt(xT.ap[:, oc, t0:t0 + n], PS[pb].ap[:, 0:n], modv.ap[:, l, g_off + oc, c.cond:c.cond + 1],
                    xT.ap[:, oc, t0:t0 + n], ALU.mult, ALU.add, [PS[pb], modv, xT], [xT])

        def layer_pass(c, l):
            T, TT = c.T, c.TT
            cd = c.cond
            for i, (og, osc) in enumerate([(O_GMIX, 8), (O_GFFN, 32)]):
                stt(gsc.ap[:, i, :], modv.ap[:, l, osc:osc + 8, cd], 1.0, prm.ap[:, l, og:og + 8], ALU.add, ALU.mult,
                    [modv, prm], [gsc])
            sh1 = modv.ap[:, l, 0:8, cd]
            sh2 = modv.ap[:, l, 24:32, cd]
            dma("sp", gkv_b.ap, d_gkv[l].partition_broadcast(128), writes=[gkv_b])
            dma("pool", wuq.ap[:, :, 0:768], d_wuq[l].rearrange("(k p) c -> p k c", p=128), writes=[wuq])
            dma("pool", wukv.ap, d_wukv[l], writes=[wukv])
            for kc in range(2):
                ts(wuq.ap[:, kc, 0:768], wuq.ap[:, kc, 0:768], prm.ap[:, l, O_GQ + kc:O_GQ + kc + 1], None, ALU.mult, None,
                   [wuq, prm], [wuq])
                src = wuq.ap[:, kc, 0:768].rearrange("p (h c) -> p h c", h=8)[:, :, 64:96].rearrange("p h (a f) -> p h a f", a=2)
                dst = wuq.ap[:, kc, 768:1024].rearrange("p (h a f) -> p h a f", h=8, a=2)
                for a in range(2):
                    ts(dst[:, :, a, 0:8], src[:, :, a, 8:16], -1.0, None, ALU.mult, None, [wuq], [wuq])
                    cp(dst[:, :, a, 8:16], src[:, :, a, 0:8], [wuq], [wuq])

            wl = d_win[l]

            def wsl(c0, c1):
                return wl[:, c0:c1].rearrange("(k p) c -> p k c", p=128)

            v_cm = load_piece([(lambda a: r3(a, 8), wsl(1192, 1704))])
            v_ssd = load_piece([(lambda a: r3(a, 8), wsl(672, 1184))])
            v_misc = load_piece([(lambda a: r3(a, 8)[:, :, 0:256], wsl(416, 672)),
                                 (lambda a: r3(a, 8)[:, :, 256:416], wsl(256, 416)),
                                 (lambda a: r3(a, 8)[:, :, 448:456], wsl(1184, 1192))])
            v_q = load_piece([(lambda a: r3(a, 8)[:, :, 0:256], wsl(0, 256))])
            w_cm, w_ssd, w_misc, w_q = r3(v_cm.ap, 8), r3(v_ssd.ap, 8), r3(v_misc.ap, 8), r3(v_q.ap, 8)
            for kc in range(8):
                src = w_misc[:, kc, 384:416].rearrange("p (a f) -> p a f", a=2)
                dst = w_misc[:, kc, 416:448].rearrange("p (a f) -> p a f", a=2)
                ts(dst[:, :, 0:8], src[:, :, 8:16], -1.0, None, ALU.mult, None, [v_misc], [v_misc])
                cp(dst[:, :, 8:16], src[:, :, 0:8], [v_misc], [v_misc])

            sp = Bump([(SC0, SBYTES)])
            sq = [alloc(sp, f"sq{i}", [128, 512], BF16) for i in range(2)]
            sd = alloc(sp, "sd", [128, 512], F32)
            rstd = alloc(sp, "rstd", [128, 512], F32)
            tmpb = [alloc(sp, f"tmpb{i}", [128, 512], F32) for i in range(2)]
            sig = alloc(sp, "sig", [128, 512], F32)
            sqk = alloc(sp, "sqk", [128, 512], BF16)
            t1 = alloc(sp, "t1", [128, 512], F32)
            t2 = alloc(sp, "t2", [128, 512], F32)
            rq = alloc(sp, "rq", [128, 512], F32)
            tmo = alloc(sp, "tmo", [128, 160], F32)
            tmo2 = alloc(sp, "tmo2", [128, 128], F32)
            ssq = alloc(sp, "ssq", [128, 2], F32)
            junk = alloc(sp, "junk", [128, 256], F32)
            koff = PAST if c.ctx else 0

            for si, (s0, sl) in enumerate(c.seqs):
                g0 = s0 + si * 32
                memset(gpad.ap[:, :, g0:g0 + 16], 0.0, [gpad])
                memset(gpad.ap[:, :, g0 + 16 + sl:g0 + 32 + sl], 0.0, [gpad])
                x0 = s0 + si * 4
                memset(xbcp.ap[:, :, x0:x0 + 2], 0.0, [xbcp])
                memset(xbcp.ap[:, :, x0 + 2 + sl:x0 + 4 + sl], 0.0, [xbcp])

            if c.ctx:
                cst_ = alloc(sp, "cstg", [128, 128], F32)
                kst_ = alloc(sp, "kstg", [128, 96], F32)
                memset(kst_.ap, 0.0, [kst_])
                for b in range(2):
                    dma("sp", cst_.ap, d_cckv[l, b * 128:(b + 1) * 128, :], writes=[cst_])
                    pb = psA.next()
                    tp(PS[pb].ap[:, 0:128], cst_.ap, identf.ap, [cst_, cstf], PS[pb], fresh=True)
                    cp(ckvnT.ap[:, b * 128:(b + 1) * 128], PS[pb].ap[:, 0:128], [PS[pb]], [ckvnT])
                    dma("sp", kst_.ap[:, 64:96], d_ckr[l, b * 128:(b + 1) * 128, :], writes=[kst_])
                    pb = psA.next()
                    tp(PS[pb].ap[0:96, 0:128], kst_.ap, identf.ap, [kst_, cstf], PS[pb], fresh=True)
                    cp(krT.ap[64:96, b * 128:(b + 1) * 128], PS[pb].ap[64:96, 0:128], [PS[pb]], [krT])

            def fm(wap, c0, m, n, reads, out_rows=None):
                pb = psA.next()
                o = PS[pb].ap[0:m, 0:n] if out_rows is None else PS[pb].ap[out_rows[0]:out_rows[1], 0:n]
                for kc in range(8):
                    mm(o, wap[:, kc, c0:c0 + m], xg.ap[:, kc, 0:n], kc == 0, kc == 7, reads + [xg], PS[pb], fresh=(kc == 0))
                return pb

            for (si, t0) in c.tiles:
                n = TT
                s0, sl = c.seqs[si]
                norm_tile(c, (sq, sd, rstd, tmpb), t0, n, gsc.ap[:, 0, :], sh1, lambda kc: xg.ap[:, kc, 0:n], xg)
                gofs = si * 32 + 16 + t0
                for j in range(2):
                    pa = fm(w_cm, j * 128, 128, n, [v_cm])
                    pbk = fm(w_cm, 256 + j * 128, 128, n, [v_cm])
                    act(sig.ap[:, 0:n], PS[pbk].ap[:, 0:n], AF.Sigmoid, [PS[pbk]], [sig])
                    tt(gpad.ap[:, j, gofs:gofs + n], PS[pa].ap[:, 0:n], sig.ap[:, 0:n], ALU.mult, [PS[pa], sig], [gpad])
                xofs = si * 4 + 2 + t0
                for j in range(4):
                    pa = fm(w_ssd, j * 128, 128, n, [v_ssd])
                    act(xbcp.ap[:, j, xofs:xofs + n], PS[pa].ap[:, 0:n], AF.Copy, [PS[pa]], [xbcp])
                for b in range(n // 128):
                    blk = (t0 + b * 128) // 128
                    pb = psA.next()
                    for kc in range(8):
                        mm(PS[pb].ap[:, 0:256], xg.ap[:, kc, b * 128:(b + 1) * 128], w_misc[:, kc, 0:256], kc == 0, kc == 7,
                           [xg, v_misc], PS[pb], fresh=(kc == 0))
                    act(szb.ap[:, blk, :], PS[pb].ap[:, 0:256], AF.Silu, [PS[pb]], [szb])
                    pb = psA.next()
                    for kc in range(8):
                        mm(PS[pb].ap[:, 0:8], xg.ap[:, kc, b * 128:(b + 1) * 128], w_misc[:, kc, 448:456], kc == 0, kc == 7,
                           [xg, v_misc], PS[pb], fresh=(kc == 0))
                    tt(dtr.ap[:, blk, :], PS[pb].ap[:, 0:8], dtb_b.ap[:, l * 8:(l + 1) * 8], ALU.add, [PS[pb], dtb_b], [dtr])
                    if c.kind == "P":
                        pb = psA.next()
                        for kc in range(8):
                            mm(PS[pb].ap[:, 0:160], xg.ap[:, kc, b * 128:(b + 1) * 128], w_misc[:, kc, 256:416], kc == 0, kc == 7,
                               [xg, v_misc], PS[pb], fresh=(kc == 0))
                        cp(tmo.ap, PS[pb].ap[:, 0:160], [PS[pb]], [tmo])
                        tloc = t0 - s0 + b * 128
                        dma("sp", o_kr[si, l, tloc:tloc + 128, :], tmo.ap[:, 128:160], reads=[tmo])
                        memset(ssq.ap[:, 0:1], 0.0, [ssq])
                        act(junk.ap[:, 0:128], tmo.ap[:, 0:128], AF.Square, [tmo, ssq], [junk, ssq], accum_out=ssq.ap[:, 0:1])
                        act(ssq.ap[:, 1:2], ssq.ap[:, 0:1], AF.Ln, [ssq], [ssq], scale=1.0 / 128, bias=1e-6)
                        act(ssq.ap[:, 1:2], ssq.ap[:, 1:2], AF.Exp, [ssq], [ssq], scale=-0.5)
                        stt(tmo2.ap, tmo.ap[:, 0:128], ssq.ap[:, 1:2], gkv_b.ap, ALU.mult, ALU.mult, [tmo, ssq, gkv_b], [tmo2])
                        dma("sp", o_ckv[si, l, tloc:tloc + 128, :], tmo2.ap, reads=[tmo2])
                pa = fm(w_misc, 256, 128, n, [v_misc])
                act(sqk.ap[:, 0:n], PS[pa].ap[:, 0:n], AF.Square, [PS[pa]], [sqk])
                mm(PS[6].ap[:, 0:n], onesb.ap, sqk.ap[:, 0:n], True, True, [onesb, sqk], PS[6], fresh=True)
                act(sd.ap[:, 0:n], PS[6].ap[:, 0:n], AF.Ln, [PS[6]], [sd], scale=1.0 / 128, bias=1e-6)
                act(rstd.ap[:, 0:n], sd.ap[:, 0:n], AF.Exp, [sd], [rstd], scale=-0.5)
                kofs = koff + t0 if c.ctx else t0
                stt(ckvnT.ap[:, kofs:kofs + n], PS[pa].ap[:, 0:n], prm.ap[:, l, O_GKV:O_GKV + 1], rstd.ap[:, 0:n],
                    ALU.mult, ALU.mult, [PS[pa], prm, rstd], [ckvnT])
                pa = fm(w_misc, 384, 32, n, [v_misc], out_rows=(64, 96))
                if c.rope:
                    pbk = fm(w_misc, 416, 32, n, [v_misc], out_rows=(64, 96))
                    tt(t1.ap[64:96, 0:n], PS[pa].ap[64:96, 0:n], ropeC.ap[64:96, t0:t0 + n], ALU.mult, [PS[pa], ropeC], [t1])
                    tt(t2.ap[64:96, 0:n], PS[pbk].ap[64:96, 0:n], ropeS.ap[64:96, t0:t0 + n], ALU.mult, [PS[pbk], ropeS], [t2])
                    tt(krT.ap[64:96, kofs:kofs + n], t1.ap[64:96, 0:n], t2.ap[64:96, 0:n], ALU.add, [t1, t2], [krT])
                else:
                    act(krT.ap[64:96, kofs:kofs + n], PS[pa].ap[64:96, 0:n], AF.Copy, [PS[pa]], [krT])
                pq = [fm(w_q, j * 128, 128, n, [v_q]) for j in range(2)]
                for j in range(2):
                    act(sq[j].ap[:, 0:n], PS[pq[j]].ap[:, 0:n], AF.Square, [PS[pq[j]]], [sq[j]])
                    mm(PS[6].ap[:, 0:n], onesb.ap, sq[j].ap[:, 0:n], j == 0, j == 1, [onesb, sq[j]], PS[6], fresh=(j == 0))
                act(sd.ap[:, 0:n], PS[6].ap[:, 0:n], AF.Ln, [PS[6]], [sd], scale=1.0 / 256, bias=1e-6)
                act(rq.ap[:, 0:n], sd.ap[:, 0:n], AF.Exp, [sd], [rq], scale=-0.5)
                for j in range(2):
                    tt(qlat.ap[:, j, t0:t0 + n], PS[pq[j]].ap[:, 0:n], rq.ap[:, 0:n], ALU.mult, [PS[pq[j]], rq], [qlat])

            if stop == "I":
                return
            v_wor = load_piece([(lambda a: r3(a, 4), d_wout[l, 512:1024, :].rearrange("(k p) c -> p k c", p=128))])
            w_or = r3(v_wor.ap, 4)
            for j in range(2):
                ts(w_or[:, j, :], w_or[:, j, :], prm.ap[:, l, O_SNG + j:O_SNG + j + 1], None, ALU.mult, None, [v_wor, prm], [v_wor])

            sp = Bump([(SC0, SBYTES), rg(xg)])
            dg = alloc(sp, "dg", [128, 2, 31, 128], BF16)
            cvf = [alloc(sp, f"cvf{j}", [128, 512], F32) for j in range(2)]
            sqf = [alloc(sp, f"sqf{j}", [128, 512], F32) for j in range(2)]
            mean = alloc(sp, "mean", [128, 512], F32)
            var = alloc(sp, "var", [128, 512], F32)
            rr = alloc(sp, "rr", [128, 512], F32)
            uu = alloc(sp, "uu", [128, 512], F32)
            cmix = alloc(sp, "cmix", [128, 2, 512], BF16)
            for j in range(2):
                for tap in range(31):
                    col = O_CCW + tap * 2 + j
                    ts(dg.ap[:, j, tap, :], identb.ap, prm.ap[:, l, col:col + 1], None, ALU.mult, None, [identb, prm], [dg])
            for (si, t0) in c.tiles:
                n = TT
                gofs = si * 32 + 1 + t0
                for j in range(2):
                    pb = psBk.next()
                    for tap in range(31):
                        mm(PS[pb].ap[:, 0:n], dg.ap[:, j, tap, :], gpad.ap[:, j, gofs + tap:gofs + tap + n], tap == 0, tap == 30,
                           [dg, gpad], PS[pb], fresh=(tap == 0))
                    bcol = prm.ap[:, l, O_CCB + j:O_CCB + j + 1]
                    act(cvf[j].ap[:, 0:n], PS[pb].ap[:, 0:n], AF.Identity, [PS[pb], prm], [cvf[j]], bias=bcol)
                    act(sqf[j].ap[:, 0:n], PS[pb].ap[:, 0:n], AF.Square, [PS[pb], prm], [sqf[j]], bias=bcol)
                for j in range(2):
                    mm(PS[6].ap[:, 0:n], onesf.ap, cvf[j].ap[:, 0:n], j == 0, j == 1, [onesf, cvf[j]], PS[6], fresh=(j == 0))
                for j in range(2):
                    mm(PS[7].ap[:, 0:n], onesf.ap, sqf[j].ap[:, 0:n], j == 0, j == 1, [onesf, sqf[j]], PS[7], fresh=(j == 0))
                ts(mean.ap[:, 0:n], PS[6].ap[:, 0:n], 1.0 / 256, None, ALU.mult, None, [PS[6]], [mean])
                tt(var.ap[:, 0:n], mean.ap[:, 0:n], mean.ap[:, 0:n], ALU.mult, [mean], [var])
                stt(var.ap[:, 0:n], PS[7].ap[:, 0:n], 1.0 / 256, var.ap[:, 0:n], ALU.mult, ALU.subtract, [PS[7], var], [var])
                act(var.ap[:, 0:n], var.ap[:, 0:n], AF.Ln, [var], [var], bias=1e-5)
                act(rr.ap[:, 0:n], var.ap[:, 0:n], AF.Exp, [var], [rr], scale=-0.5)
                for j in range(2):
                    tt(uu.ap[:, 0:n], cvf[j].ap[:, 0:n], mean.ap[:, 0:n], ALU.subtract, [cvf[j], mean], [uu])
                    tt(uu.ap[:, 0:n], uu.ap[:, 0:n], rr.ap[:, 0:n], ALU.mult, [uu, rr], [uu])
                    act(cmix.ap[:, j, 0:n], uu.ap[:, 0:n], AF.Silu, [uu, prm], [cmix],
                        scale=prm.ap[:, l, O_CLG + j:O_CLG + j + 1], bias=prm.ap[:, l, O_CLB + j:O_CLB + j + 1])
                xupdate(c, l, t0, n, lambda j, oc: w_or[:, 2 + j, oc * 128:(oc + 1) * 128],
                        [cmix.ap[:, 0, 0:n], cmix.ap[:, 1, 0:n]], 16, [v_wor, cmix])

            if stop == "C":
                return
            ssd_phase(c, l, w_or, v_wor)
            if stop in ("S", "S1", "S2", "S3"):
                return
            v_woa = load_piece([(lambda a: r3(a, 4), d_wout[l, 0:512, :].rearrange("(k p) c -> p k c", p=128))])
            mla_phase(c, l, r3(v_woa.ap, 4), v_woa)
            if stop == "M":
                return
            ffn_phase(c, l)

        def ssd_phase(c, l, w_or, v_wor):
            T, TT = c.T, c.TT
            nblk = c.nblk
            g_r, x_r, z_r = rg(gpad), rg(xbcp), rg(szb)
            sp = Bump([(SC0, SBYTES), g_r])
            dg5 = alloc(sp, "dg5", [128, 4, 5, 128], BF16)
            xsT = alloc(sp, "xsT", [128, 2, 512], BF16)
            BCt = alloc(sp, "BCt", [128, 3, TS], BF16)
            xs_tm = alloc(sp, "xs_tm", [128, 16, 256], BF16)
            B_tm = alloc(sp, "B_tm", [128, 16, 128], BF16)
            dt = alloc(sp, "dt", [128, 16, 8], F32)
            dta = alloc(sp, "dta", [128, 16, 8], F32)
            hTf = [alloc(sp, f"hTf{d}", [128, 2, 64], F32) for d in range(2)]
            hTb = alloc(sp, "hTb", [128, 2, 64], BF16)
            sstg = alloc(sp, "sstg", [128, 128], F32)
            sp2 = Bump([tuple(r) for r in sp.ranges] + [x_r, rg(xg)])
            nb8 = nblk * 8
            dtf = dtr.ap.rearrange("p b e -> p (b e)")[:, 0:nb8]
            act(dt.ap.rearrange("p b e -> p (b e)")[:, 0:nb8], dtf, AF.Exp, [dtr], [dt])
            act(dt.ap.rearrange("p b e -> p (b e)")[:, 0:nb8], dt.ap.rearrange("p b e -> p (b e)")[:, 0:nb8], AF.Ln, [dt], [dt], bias=1.0)
            tt(dta.ap[:, 0:nblk, :], dt.ap[:, 0:nblk, :], a_b.ap[:, l * 8:(l + 1) * 8].unsqueeze(1).to_broadcast([128, nblk, 8]),
               ALU.mult, [dt, a_b], [dta])
            memset(BCt.ap[64:128, 1, 0:T], 0.0, [BCt])
            memset(BCt.ap[0:64, 2, 0:T], 0.0, [BCt])
            for ch in range(4):
                for tap in range(5):
                    col = O_SCW + tap * 4 + ch
                    ts(dg5.ap[:, ch, tap, :], identb.ap, prm.ap[:, l, col:col + 1], None, ALU.mult, None, [identb, prm], [dg5])
            for (si, t0) in c.tiles:
                n = TT
                xofs = si * 4 + t0
                for ch in range(4):
                    pb = psBk.next()
                    for tap in range(5):
                        mm(PS[pb].ap[:, 0:n], dg5.ap[:, ch, tap, :], xbcp.ap[:, ch, xofs + tap:xofs + tap + n], tap == 0, tap == 4,
                           [dg5, xbcp], PS[pb], fresh=(tap == 0))
                    bias_ = prm.ap[:, l, O_SCB + ch:O_SCB + ch + 1]
                    if ch < 2:
                        act(xsT.ap[:, ch, 0:n], PS[pb].ap[:, 0:n], AF.Silu, [PS[pb], prm], [xsT], bias=bias_)
                    elif ch == 2:
                        act(BCt.ap[:, 0, t0:t0 + n], PS[pb].ap[:, 0:n], AF.Silu, [PS[pb], prm], [BCt], bias=bias_)
                    else:
                        act(BCt.ap[0:64, 1, t0:t0 + n], PS[pb].ap[0:64, 0:n], AF.Silu, [PS[pb], prm], [BCt], bias=bias_[0:64])
                        act(BCt.ap[64:128, 2, t0:t0 + n], PS[pb].ap[64:128, 0:n], AF.Silu, [PS[pb], prm], [BCt], bias=bias_[64:128])
                for b in range(n // 128):
                    blk = (t0 + b * 128) // 128
                    pb = psA.next()
                    for j in range(2):
                        tp(PSB[pb].ap[:, j * 128:(j + 1) * 128], xsT.ap[:, j, b * 128:(b + 1) * 128], identb.ap, [xsT, identb], PS[pb], fresh=(j == 0))
                    tp(PSB[pb].ap[:, 256:384], BCt.ap[:, 0, t0 + b * 128:t0 + (b + 1) * 128], identb.ap, [BCt, identb], PS[pb])
                    cp(xs_tm.ap[:, blk, :], PSB[pb].ap[:, 0:256], [PS[pb]], [xs_tm])
                    cp(B_tm.ap[:, blk, :], PSB[pb].ap[:, 256:384], [PS[pb]], [B_tm])
            if stop == "S1":
                return
            hst = alloc(sp2, "hst", [128, 16, 2, 64], BF16)
            Gm = [alloc(sp2, f"Gm{d}", [128, 2, 128], F32) for d in range(2)]
            Lm = [alloc(sp2, f"Lm{d}", [128, 4, 128], F32) for d in range(2)]
            seg = [alloc(sp2, f"seg{d}", [128, 4, 128], F32) for d in range(2)]
            MT = [alloc(sp2, f"MT{d}", [128, 4, 128], BF16) for d in range(2)]
            xdt = [alloc(sp2, f"xdt{d}", [128, 4, 64], BF16) for d in range(2)]
            xdd = alloc(sp2, "xdd", [128, 4, 64], BF16)
            ee = alloc(sp2, "ee", [128, 16], F32)
            cds = alloc(sp2, "cds", [128, 2, 2], F32)
            wv = alloc(sp2, "wv", [128, 4], F32)
            yo = alloc(sp2, "yo", [128, 8, 64], F32)
            y1 = alloc(sp2, "y1", [128, 256], F32)
            y2 = alloc(sp2, "y2", [128, 256], F32)
            yn = alloc(sp2, "yn", [128, 256], BF16)
            ssq = alloc(sp2, "ssq2", [128, 2], F32)
            junk = alloc(sp2, "junk2", [128, 256], F32)
            smix = alloc(sp2, "smix", [128, 2, 512], BF16)

            def small_mm(blk, dirs):
                first = True
                for d in dirs:
                    A = dta.ap[:, blk, d * 4:(d + 1) * 4]
                    for (c0, lt, lv) in [(d * 4, SU[d].ap, cstf), (8 + d * 4, TRI[d].ap, cstf), (16 + d * 4, onesf.ap, onesf)]:
                        mm(PS[6].ap[:, c0:c0 + 4], lt, A, True, True, [lv, dta], PS[6], fresh=first)
                        first = False
                for d in dirs:
                    act(ee.ap[:, d * 4:d * 4 + 4], PS[6].ap[:, d * 4:d * 4 + 4], AF.Exp, [PS[6]], [ee])
                    act(ee.ap[:, 8 + d * 4:12 + d * 4], PS[6].ap[:, 8 + d * 4:12 + d * 4], AF.Exp, [PS[6]], [ee])
                    act(cds.ap[0:64, d, :], PS[6].ap[0:64, 16 + d * 4:18 + d * 4], AF.Exp, [PS[6]], [cds])
                    act(cds.ap[64:128, d, :], PS[6].ap[64:128, 18 + d * 4:20 + d * 4], AF.Exp, [PS[6]], [cds])

            def state_update(blk, d):
                tt(wv.ap, dt.ap[:, blk, d * 4:(d + 1) * 4], ee.ap[:, d * 4:d * 4 + 4], ALU.mult, [dt, ee], [wv])
                tt(xdd.ap, xs_tm.ap[:, blk, :].rearrange("p (h q) -> p h q", h=4), wv.ap.unsqueeze(2).to_broadcast([128, 4, 64]),
                   ALU.mult, [xs_tm, wv], [xdd])
                pb = psBk.next()
                for h in range(4):
                    g, j = h // 2, h % 2
                    mm(PS[pb].ap[g * 64:(g + 1) * 64, j * 64:(j + 1) * 64], B_tm.ap[:, blk, g * 64:(g + 1) * 64], xdd.ap[:, h, :],
                       True, True, [B_tm, xdd], PS[pb], fresh=(h == 0))
                for j in range(2):
                    stt(hTf[d].ap[:, j, :], hTf[d].ap[:, j, :], cds.ap[:, d, j:j + 1], PS[pb].ap[:, j * 64:(j + 1) * 64],
                        ALU.mult, ALU.add, [hTf[d], cds, PS[pb]], [hTf[d]])

            for si, (s0, sl) in enumerate(c.seqs):
                blks = list(range(s0 // 128, (s0 + sl) // 128))
                for d in range(2):
                    if c.ctx:
                        dma("sp", sstg.ap.rearrange("p (g n) -> p g n", g=2),
                            d_st[l, d].rearrange("(g j) p n -> (j p) g n", g=2), writes=[sstg])
                        pb = psA.next()
                        tp(PS[pb].ap[:, 0:128], sstg.ap, identf.ap, [sstg, cstf], PS[pb], fresh=True)
                        cp(hTf[d].ap.rearrange("p j q -> p (j q)"), PS[pb].ap[:, 0:128], [PS[pb]], [hTf[d]])
                    else:
                        memset(hTf[d].ap, 0.0, [hTf[d]])
                for blk in reversed(blks):
                    cp(hst.ap[:, blk], hTf[1].ap, [hTf[1]], [hst], eng="pool")
                    small_mm(blk, [1])
                    state_update(blk, 1)
                if stop == "S2":
                    return
                cp(hTb.ap, hTf[0].ap, [hTf[0]], [hTb], eng="pool")
                for blk in blks:
                    tk = slice(blk * 128, (blk + 1) * 128)
                    pg = psA.next()
                    mm(PS[pg].ap[:, 0:256], BCt.ap[:, 0, tk], BCt.ap[:, 1:3, tk], True, True, [BCt], PS[pg], fresh=True)
                    small_mm(blk, [0, 1])
                    for d in range(2):
                        tt(Gm[d].ap, PS[pg].ap[:, 0:256].rearrange("p (g s) -> p g s", g=2),
                           TRI[d].ap.unsqueeze(1).to_broadcast([128, 2, 128]), ALU.mult, [PS[pg], cstf], [Gm[d]])
                        tt(Lm[d].ap, SU[d].ap.unsqueeze(1).to_broadcast([128, 4, 128]),
                           dta.ap[:, blk, d * 4:(d + 1) * 4].unsqueeze(2).to_broadcast([128, 4, 128]), ALU.mult, [cstf, dta], [Lm[d]])
                        pdf = psA.next()
                        for h in range(4):
                            mm(PS[pdf].ap[:, h * 128:(h + 1) * 128], Lm[d].ap[:, h, :], TRI[d].ap, True, True, [Lm[d], cstf], PS[pdf], fresh=(h == 0))
                        act(seg[d].ap.rearrange("p h s -> p (h s)"), PS[pdf].ap, AF.Exp, [PS[pdf]], [seg[d]])
                        for g in range(2):
                            tt(MT[d].ap[:, 2 * g:2 * g + 2, :], seg[d].ap[:, 2 * g:2 * g + 2, :],
                               Gm[d].ap[:, g:g + 1, :].to_broadcast([128, 2, 128]), ALU.mult, [seg[d], Gm[d]], [MT[d]])
                        tt(xdt[d].ap, xs_tm.ap[:, blk, :].rearrange("p (h q) -> p h q", h=4),
                           dt.ap[:, blk, d * 4:(d + 1) * 4].unsqueeze(2).to_broadcast([128, 4, 64]), ALU.mult, [xs_tm, dt], [xdt[d]])
                    py = psBk.next()
                    for h in range(4):
                        for d in range(2):
                            mm(PS[py].ap[:, h * 64:(h + 1) * 64], MT[d].ap[:, h, :], xdt[d].ap[:, h, :], d == 0, d == 1,
                               [MT[d], xdt[d]], PS[py], fresh=(h == 0 and d == 0))
                    po = psBk.next()
                    for d in range(2):
                        hsrc_v = hTb if d == 0 else hst
                        hsrc = hTb.ap if d == 0 else hst.ap[:, blk]
                        for g in range(2):
                            mm(PS[po].ap[:, d * 256 + g * 128:d * 256 + (g + 1) * 128], BCt.ap[:, 1 + g, tk], hsrc,
                               True, True, [BCt, hsrc_v], PS[po], fresh=(d == 0 and g == 0))
                    tt(yo.ap, PS[po].ap.rearrange("p (e q) -> p e q", e=8), ee.ap[:, 8:16].unsqueeze(2).to_broadcast([128, 8, 64]),
                       ALU.mult, [PS[po], ee], [yo])
                    yof = yo.ap.rearrange("p e q -> p (e q)")
                    tt(y1.ap, yof[:, 0:256], yof[:, 256:512], ALU.add, [yo], [y1])
                    tt(y1.ap, y1.ap, PS[py].ap[:, 0:256], ALU.add, [y1, PS[py]], [y1])
                    tt(y2.ap.rearrange("p (h q) -> p h q", h=4), xs_tm.ap[:, blk, :].rearrange("p (h q) -> p h q", h=4),
                       dsum_b.ap[:, l, :].unsqueeze(2).to_broadcast([128, 4, 64]), ALU.mult, [xs_tm, dsum_b], [y2])
                    tt(y1.ap, y1.ap, y2.ap, ALU.add, [y1, y2], [y1])
                    tt(y2.ap, y1.ap, szb.ap[:, blk, :], ALU.mult, [y1, szb], [y2])
                    memset(ssq.ap[:, 0:1], 0.0, [ssq])
                    act(junk.ap, y2.ap, AF.Square, [y2, ssq], [junk, ssq], accum_out=ssq.ap[:, 0:1])
                    act(ssq.ap[:, 1:2], ssq.ap[:, 0:1], AF.Ln, [ssq], [ssq], scale=1.0 / 256, bias=1e-6)
                    act(ssq.ap[:, 1:2], ssq.ap[:, 1:2], AF.Exp, [ssq], [ssq], scale=-0.5)
                    ts(yn.ap, y2.ap, ssq.ap[:, 1:2], None, ALU.mult, None, [y2, ssq], [yn])
                    state_update(blk, 0)
                    cp(hTb.ap, hTf[0].ap, [hTf[0]], [hTb], eng="pool")
                    bt = (blk * 128) % TT
                    pt = psA.next()
                    for j in range(2):
                        tp(PSB[pt].ap[:, j * 128:(j + 1) * 128], yn.ap[:, j * 128:(j + 1) * 128], identb.ap, [yn, identb], PS[pt], fresh=(j == 0))
                    cp(smix.ap[:, :, bt:bt + 128], PSB[pt].ap[:, 0:256].rearrange("p (j t) -> p j t", j=2), [PS[pt]], [smix])
                    if bt + 128 == TT:
                        t0 = blk * 128 + 128 - TT
                        xupdate(c, l, t0, TT, lambda j, oc: w_or[:, j, oc * 128:(oc + 1) * 128],
                                [smix.ap[:, 0, 0:TT], smix.ap[:, 1, 0:TT]], 16, [v_wor, smix])
                if stop == "S3":
                    return
                if c.kind == "P":
                    for d in range(2):
                        pb = psA.next()
                        tp(PS[pb].ap[:, 0:128], hTf[d].ap.rearrange("p j q -> p (j q)"), identf.ap, [hTf[d], cstf], PS[pb], fresh=True)
                        cp(sstg.ap, PS[pb].ap[:, 0:128], [PS[pb]], [sstg])
                        dma("sp", o_ssd[si, l, d].rearrange("(g j) p n -> (j p) g n", g=2),
                            sstg.ap.rearrange("p (g n) -> p g n", g=2), reads=[sstg])

        def mla_phase(c, l, w_oa, v_woa):
            T, TT = c.T, c.TT
            g_r, x_r, z_r = rg(gpad), rg(xbcp), rg(szb)
            attnT = alloc(Bump([x_r]), "attnT", [128, 4, TS], BF16)
            sp = Bump([(SC0, SBYTES), g_r, z_r])
            NKB = (PAST + TS) // 128
            KT = [alloc(sp, f"KT{i}", [128, PAST + TS], BF16) for i in range(2)]
            VB = [(alloc(sp, f"Ve{i}", [128, NKB, 64], BF16), alloc(sp, f"Vo{i}", [128, NKB, 64], BF16)) for i in range(2)]
            QT = [alloc(sp, f"QT{i}", [128, TS], BF16) for i in range(2)]
            PT = [alloc(sp, f"PT{i}", [128, 512], BF16) for i in range(3)]
            t1 = alloc(sp, "mt1", [128, 512], F32)
            t2 = alloc(sp, "mt2", [128, 512], F32)
            rb = alloc(sp, "rb", [128, 512], F32)
            ptr = Rot(PT)
            LA = 2
            psS = Rot([0, 1, 2])
            PP = 3
            accb = Rot([(4, 5), (6, 7)])

            heads = []
            for si, (s0, sl) in enumerate(c.seqs):
                for h in range(8):
                    heads.append((si, s0, sl, h))

            def prep(idx):
                si, s0, sl, h = heads[idx]
                hp, hh = h // 2, h % 2
                Tk = (PAST if c.ctx else 0) + sl
                k0 = 0 if c.ctx else s0
                nkb = Tk // 128
                qtiles = [t for (s_, t) in c.tiles if s_ == si]
                kt, qt = KT[idx % 2], QT[idx % 2]
                Ve, Vo = VB[(idx // 2) % 2]
                if hh == 0:
                    vcols = wukv.ap.rearrange("p (h c) -> p h c", h=8)[:, 2 * hp:2 * hp + 2, 64:128]
                    for kb in range(nkb):
                        mm(PS[PP].ap[:, 0:128], ckvnT.ap[:, k0 + kb * 128:k0 + (kb + 1) * 128], vcols, True, True, [ckvnT, wukv], PS[PP], fresh=True)
                        cp(Ve.ap[:, kb, :], PS[PP].ap[:, 0:64], [PS[PP]], [Ve])
                        cp(Vo.ap[:, kb, :], PS[PP].ap[:, 64:128], [PS[PP]], [Vo])
                        yield
                for k1 in range(0, Tk, 512):
                    kn = min(512, Tk - k1)
                    mm(PS[PP].ap[0:64, 0:kn], wukv.ap[:, h * 128:h * 128 + 64], ckvnT.ap[:, k0 + k1:k0 + k1 + kn], True, True,
                       [wukv, ckvnT], PS[PP], fresh=True)
                    cp(kt.ap[0:64, k1:k1 + kn], PS[PP].ap[0:64, 0:kn], [PS[PP]], [kt])
                    yield
                cp(kt.ap[64:96, 0:Tk], krT.ap[64:96, k0:k0 + Tk], [krT], [kt], eng="pool")
                for t0 in qtiles:
                    n = TT
                    tl = t0 - s0
                    for kc in range(2):
                        mm(PS[PP].ap[0:96, 0:n], wuq.ap[:, kc, h * 96:(h + 1) * 96], qlat.ap[:, kc, t0:t0 + n], kc == 0, kc == 1,
                           [wuq, qlat], PS[PP], fresh=(kc == 0))
                    if c.rope:
                        cp(qt.ap[0:64, tl:tl + n], PS[PP].ap[0:64, 0:n], [PS[PP]], [qt])
                        tt(t1.ap[64:96, 0:n], PS[PP].ap[64:96, 0:n], ropeC.ap[64:96, t0:t0 + n], ALU.mult, [PS[PP], ropeC], [t1])
                        yield
                        for kc in range(2):
                            mm(PS[PP].ap[64:96, 0:n], wuq.ap[:, kc, 768 + h * 32:768 + (h + 1) * 32], qlat.ap[:, kc, t0:t0 + n],
                               kc == 0, kc == 1, [wuq, qlat], PS[PP], fresh=(kc == 0))
                        tt(t2.ap[64:96, 0:n], PS[PP].ap[64:96, 0:n], ropeS.ap[64:96, t0:t0 + n], ALU.mult, [PS[PP], ropeS], [t2])
                        tt(qt.ap[64:96, tl:tl + n], t1.ap[64:96, 0:n], t2.ap[64:96, 0:n], ALU.add, [t1, t2], [qt])
                    else:
                        cp(qt.ap[0:96, tl:tl + n], PS[PP].ap[0:96, 0:n], [PS[PP]], [qt])
                    yield

            def attn_unit(idx, t0, prev_tail, filler):
                si, s0, sl, h = heads[idx]
                hp, hh = h // 2, h % 2
                nkb = ((PAST if c.ctx else 0) + sl) // 128
                kt, qt = KT[idx % 2], QT[idx % 2]
                vv = VB[(idx // 2) % 2][hh]
                tl = t0 - s0
                n = c.TT
                po, pd = accb.next()
                r0 = 0 if hh == 0 else 64
                pts = []
                for kb in range(nkb + LA):
                    if kb < nkb:
                        pb = psS.next()
                        mm(PS[pb].ap[:, 0:n], kt.ap[0:96, kb * 128:(kb + 1) * 128], qt.ap[0:96, tl:tl + n], True, True,
                           [kt, qt], PS[pb], fresh=True)
                        pt_ = ptr.next()
                        pts.append(pt_)
                        act(pt_.ap[:, 0:n], PS[pb].ap[:, 0:n], AF.Exp, [PS[pb]], [pt_], scale=SCALE)
                    if kb == min(LA, nkb) - 1 and prev_tail is not None:
                        prev_tail()
                        prev_tail = None
                    if kb >= LA:
                        k2 = kb - LA
                        mm(PS[po].ap[r0:r0 + 64, 0:n], vv.ap[:, k2, :], pts[k2].ap[:, 0:n], k2 == 0, k2 == nkb - 1, [vv, pts[k2]], PS[po], fresh=(k2 == 0))
                        mm(PS[pd].ap[r0:r0 + 64, 0:n], onesb.ap[:, 0:64], pts[k2].ap[:, 0:n], k2 == 0, k2 == nkb - 1, [onesb, pts[k2]], PS[pd], fresh=(k2 == 0))
                    if filler is not None:
                        next(filler, None)

                def tail():
                    recip(rb.ap[r0:r0 + 64, 0:n], PS[pd].ap[r0:r0 + 64, 0:n], [PS[pd]], [rb])
                    tt(attnT.ap[r0:r0 + 64, hp, t0:t0 + n], PS[po].ap[r0:r0 + 64, 0:n], rb.ap[r0:r0 + 64, 0:n], ALU.mult, [PS[po], rb], [attnT])
                return tail

            for _ in prep(0):
                pass
            pend = None
            for idx in range(len(heads)):
                si = heads[idx][0]
                filler = prep(idx + 1) if idx + 1 < len(heads) else None
                for t0 in [t for (s_, t) in c.tiles if s_ == si]:
                    pend = attn_unit(idx, t0, pend, filler)
                if filler is not None:
                    for _ in filler:
                        pass
            if pend is not None:
                pend()
            for (si, t0) in c.tiles:
                xupdate(c, l, t0, TT, lambda j, oc: w_oa[:, j, oc * 128:(oc + 1) * 128],
                        [attnT.ap[:, j, t0:t0 + TT] for j in range(4)], 16, [v_woa, attnT])

        def ffn_phase(c, l):
            T, TT = c.T, c.TT
            sp = Bump([(max(SC0, rg(hT)[1]), SBYTES)])
            sq = [alloc(sp, f"fsq{i}", [128, 512], BF16) for i in range(2)]
            sd = alloc(sp, "fsd", [128, 512], F32)
            rstd = alloc(sp, "frstd", [128, 512], F32)
            tmpb = [alloc(sp, f"ftmpb{i}", [128, 512], F32) for i in range(2)]
            sg = [alloc(sp, f"sg{i}", [128, 512], F32) for i in range(2)]
            actb = [alloc(sp, f"actb{i}", [128, 2, 512], BF16) for i in range(2)]
            sh2 = modv.ap[:, l, 24:32, c.cond]
            for (si, t0) in c.tiles:
                norm_tile(c, (sq, sd, rstd, tmpb), t0, TT, gsc.ap[:, 1, :], sh2, lambda kc, t0=t0: hT.ap[:, kc, t0:t0 + TT], hT)
            def load_group(g):
                c0 = g * 256
                v1 = load_piece([(lambda a: r3(a, 8)[:, :, 0:256], d_wg[l, :, c0:c0 + 256].rearrange("(k p) c -> p k c", p=128)),
                                 (lambda a: r3(a, 8)[:, :, 256:512], d_wu[l, :, c0:c0 + 256].rearrange("(k p) c -> p k c", p=128))])
                v2 = load_piece([(lambda a: r3(a, 4)[:, 0:2, :], d_wd[l, c0:c0 + 256, :].rearrange("(k p) c -> p k c", p=128))])
                return (v1, v2)

            units = [(g, t0) for g in range(NFG) for (si, t0) in c.tiles]
            wts = {0: load_group(0), 1: load_group(1)}

            psF = Rot([4, 5, 6, 7])

            def stage_a_groups(u):
                g, t0 = units[u]
                v1, v2 = wts[g]
                wgu = r3(v1.ap, 8)
                n = TT
                ab = actb[u % 2]
                outs = []
                st = {}

                def mk(j, which):
                    def f():
                        pb = psA.next()
                        c0 = (0 if which == 0 else 256) + j * 128
                        for kc in range(8):
                            mm(PS[pb].ap[:, 0:n], wgu[:, kc, c0:c0 + 128], hT.ap[:, kc, t0:t0 + n], kc == 0, kc == 7, [v1, hT], PS[pb], fresh=(kc == 0))
                        st[(j, which)] = pb
                        if which == 1:
                            pg, pu = st[(j, 0)], pb
                            act(sg[j].ap[:, 0:n], PS[pg].ap[:, 0:n], AF.Silu, [PS[pg]], [sg[j]])
                            tt(ab.ap[:, j, 0:n], sg[j].ap[:, 0:n], PS[pu].ap[:, 0:n], ALU.mult, [sg[j], PS[pu]], [ab])
                    return f
                for j in range(2):
                    outs.append(mk(j, 0))
                    outs.append(mk(j, 1))
                return outs

            def stage_b_steps(u):
                g, t0 = units[u]
                v1, v2 = wts[g]
                wdn = r3(v2.ap, 4)
                n = TT
                ab = actb[u % 2]
                outs = []

                def mk(oc):
                    def f():
                        pb = psF.next()
                        for j in range(2):
                            mm(PS[pb].ap[:, 0:n], wdn[:, j, oc * 128:(oc + 1) * 128], ab.ap[:, j, 0:n], j == 0, j == 1, [v2, ab], PS[pb], fresh=(j == 0))
                        stt(xT.ap[:, oc, t0:t0 + n], PS[pb].ap[:, 0:n], modv.ap[:, l, 40 + oc, c.cond:c.cond + 1],
                            xT.ap[:, oc, t0:t0 + n], ALU.mult, ALU.add, [PS[pb], modv, xT], [xT])
                        if oc == 7 and (u + 1 == len(units) or units[u + 1][0] != g) and g + 2 < NFG:
                            wts[g + 2] = load_group(g + 2)
                    return f
                for oc in range(8):
                    outs.append(mk(oc))
                return outs

            for u in range(len(units) + 1):
                ga = stage_a_groups(u) if u < len(units) else []
                gb = stage_b_steps(u - 1) if u >= 1 else []
                for k in range(4):
                    if ga:
                        ga[k]()
                    if gb:
                        gb[2 * k]()
                        gb[2 * k + 1]()


        def final_out(c):
            sp = Bump([(SC0, SBYTES)])
            sq = [alloc(sp, f"osq{i}", [128, 512], BF16) for i in range(2)]
            sd = alloc(sp, "osd", [128, 512], F32)
            rstd = alloc(sp, "orstd", [128, 512], F32)
            tmpb = [alloc(sp, f"otmpb{i}", [128, 512], F32) for i in range(2)]
            yf = alloc(sp, "yf", [128, 8, 128], F32)
            ytm = [alloc(sp, f"ytm{i}", [128, D], F32) for i in range(2)]
            gf = cnd.ap[:, 16:24]
            for b in range(c.nblk):
                norm_tile(c, (sq, sd, rstd, tmpb), b * 128, 128, gf, None, lambda kc: yf.ap[:, kc, :], yf)
                y = ytm[b % 2]
                for half in range(2):
                    pb = psA.next()
                    for q in range(4):
                        kc = half * 4 + q
                        tp(PS[pb].ap[:, q * 128:(q + 1) * 128], yf.ap[:, kc, :], identf.ap, [yf, cstf], PS[pb], fresh=(q == 0))
                    act(y.ap[:, half * 512:(half + 1) * 512], PS[pb].ap, AF.Copy, [PS[pb]], [y])
                dma("sp", c.oy[b * 128:(b + 1) * 128, :], y.ap, reads=[y])

        passes = []
        if do_sample:
            passes.append(make_cfg("S"))
        if do_prompt:
            passes.append(make_cfg("P"))
        for c in passes:
            load_x(c)
            for l in range(n_layers):
                layer_pass(c, l)
            final_out(c)

        names = P.finalize()
        print("nops", len(P.ops), "nsems", len(names))
        sems = {nm: es.enter_context(nc.semaphore(f"s{i}")) for i, nm in enumerate(names)}
        with nc.Block() as block:
            P.emit(block, sems)
    return nc


def _consts():
    i = np.arange(128)
    ident = np.eye(128, dtype=np.float32)
    suf = (i[:, None] > i[None, :]).astype(np.float32)
    sub = (i[:, None] < i[None, :]).astype(np.float32)
    trif = (i[:, None] <= i[None, :]).astype(np.float32)
    trib = (i[:, None] >= i[None, :]).astype(np.float32)
    cst = np.concatenate([ident, suf, sub, trif, trib], axis=1).astype(np.float32)
    t = np.arange(TS)
    row = (t // 64).astype(np.float32)
    col = (t % 64).astype(np.float32)
    nf = 8
    inv = (10000.0 ** (-np.arange(nf, dtype=np.float32) / nf)).astype(np.float32)
    ang = np.stack([row[:, None] * inv, col[:, None] * inv], axis=1)
    cos = np.cos(ang).astype(np.float32)
    sin = np.sin(ang).astype(np.float32)
    rope = np.zeros((2, 128, TS), np.float32)
    for a in range(2):
        for half in range(2):
            for f in range(nf):
                r = 64 + a * 16 + half * 8 + f
                rope[0, r] = cos[:, a, f]
                rope[1, r] = sin[:, a, f]
    return cst, rope


_CACHE = {}


def kernel(x_prompt, x_sample, c, cache_ckv, cache_krope, state_ssd, c_ctx, w_ada, b_ada,
           g_mix, w_in, g_q, w_uq, g_kv, w_ukv, ssd_conv_w, ssd_conv_b, ssd_dt_bias,
           ssd_a_log, ssd_d, ssd_norm_g, cm_conv_w, cm_conv_b, cm_ln_g, cm_ln_b, w_out,
           g_ffn, w_gate, w_up, w_down, g_final, _n_layers=DEPTH, _do_sample=True, _do_prompt=True, _stop=None):
    f = lambda a: np.ascontiguousarray(np.asarray(a, dtype=np.float32))
    key = (_n_layers, _do_sample, _do_prompt, _stop)
    if key not in _CACHE:
        _CACHE[key] = build_program(_n_layers, _do_sample, _do_prompt, _stop)
    nc = _CACHE[key]
    cst, rope = _consts()
    shared = dict(
        w_ada=f(w_ada), b_ada=f(b_ada), g_mix=f(g_mix), w_in=f(w_in), g_q=f(g_q), w_uq=f(w_uq), g_kv=f(g_kv),
        w_ukv=f(w_ukv), ssd_conv_w=f(ssd_conv_w), ssd_conv_b=f(ssd_conv_b), ssd_dt_bias=f(ssd_dt_bias).reshape(-1),
        ssd_a_log=f(ssd_a_log).reshape(-1), ssd_d=f(ssd_d).reshape(-1), ssd_norm_g=f(ssd_norm_g), cm_conv_w=f(cm_conv_w),
        cm_conv_b=f(cm_conv_b), cm_ln_g=f(cm_ln_g), cm_ln_b=f(cm_ln_b), w_out=f(w_out), g_ffn=f(g_ffn),
        w_gate=f(w_gate), w_up=f(w_up), w_down=f(w_down), g_final=f(g_final), cst=cst, rope=rope)
    x_prompt, x_sample, c, c_ctx = f(x_prompt), f(x_sample), f(c), f(c_ctx)
    cache_ckv, cache_krope, state_ssd = f(cache_ckv), f(cache_krope), f(state_ssd)
    in_maps = []
    for i in range(8):
        b = i % 4
        m = dict(shared)
        m["x_s"] = x_sample[b]
        m["x_p"] = x_prompt[2 * i:2 * i + 2].reshape(NPS * TPS, D)
        m["cond"] = np.stack([c[b], c_ctx], axis=0)
        m["cache_ckv"] = cache_ckv[b]
        m["cache_krope"] = cache_krope[b]
        m["state_ssd"] = state_ssd[b]
        in_maps.append(m)
    res = run_bass_kernel_spmd(nc, in_maps, core_ids=list(range(8)))
    r = res.results
    y_sample = np.stack([r[b]["y_s"] for b in range(4)], axis=0).astype(np.float32)
    y_prompt = np.concatenate([r[i]["y_p"].reshape(NPS, TPS, D) for i in range(8)], axis=0).astype(np.float32)
    new_ckv = np.concatenate([r[i]["o_ckv"] for i in range(8)], axis=0).astype(np.float32)
    new_kr = np.concatenate([r[i]["o_kr"] for i in range(8)], axis=0).astype(np.float32)
    new_ssd = np.concatenate([r[i]["o_ssd"] for i in range(8)], axis=0).astype(np.float32)
    return (y_prompt, y_sample, new_ckv, new_kr, new_ssd)
```

```python
import math
from contextlib import ExitStack
import numpy as np
import concourse.bass as bass
import concourse.mybir as mybir
from concourse.bass_utils import run_bass_kernel_spmd

F32 = mybir.dt.float32
BF16 = mybir.dt.bfloat16
U8 = mybir.dt.uint8
AF = mybir.ActivationFunctionType
ALU = mybir.AluOpType

ENGS = ("pe", "act", "dve", "pool", "sp")

D = 1024
DEPTH = 4
TS = 2048
TPS = 256
NPS = 2
PAST = 256
DFF = 2816
NFG = DFF // 256
IN_W = 1704
SCALE = 96 ** -0.5
NPRM = 161


class View:
    __slots__ = ("name", "ap")

    def __init__(self, name, ap):
        self.name, self.ap = name, ap


class Prog:
    def __init__(self, nc):
        self.nc = nc
        self.ops = []
        self.regions = {}
        self.overlaps = {}

    def region(self, name, space, p0, p1, b0, b1):
        assert name not in self.regions, name
        self.regions[name] = (space, p0, p1, b0, b1)
        ov = [name]
        for n, (s, q0, q1, c0, c1) in self.regions.items():
            if n == name:
                continue
            if s == space and q0 < p1 and p0 < q1 and c0 < b1 and b0 < c1:
                ov.append(n)
                self.overlaps[n].append(name)
        self.overlaps[name] = ov

    def op(self, eng, fn, reads=(), writes=(), dma=False, chan=None, fresh=()):
        rs = [r if isinstance(r, str) else r.name for r in reads]
        ws = [w if isinstance(w, str) else w.name for w in writes]
        fr = [w if isinstance(w, str) else w.name for w in fresh]
        if dma and chan is None:
            chan = ws[0] if ws else rs[0]
        self.ops.append((eng, fn, rs, ws, dma, chan, fr))

    def finalize(self):
        ops = self.ops
        n = len(ops)
        last_w = {}
        readers = {}
        deps = [None] * n
        for i, (eng, fn, rs, ws, dma, chan, fr) in enumerate(ops):
            for f in fr:
                lw = last_w.get(f)
                assert lw is None or len(readers.get(f, ())) > 0, \
                    f"PSUM collision on {f} at op {i} (prev writer {lw} unread)"
            d = set()
            for r in rs:
                for rr in self.overlaps[r]:
                    w = last_w.get(rr)
                    if w is not None:
                        d.add(w)
                    if self.regions[rr][0] == "psum":
                        for x in readers.get(rr, {}).values():
                            d.add(x)
            for w_ in ws:
                for ww in self.overlaps[w_]:
                    w = last_w.get(ww)
                    if w is not None:
                        d.add(w)
                    for x in readers.get(ww, {}).values():
                        d.add(x)
            d.discard(i)
            dd = []
            mine = set(rs) | set(ws)
            for j in d:
                ej, _, rsj, wsj, dmaj, _, _ = ops[j]
                if not dmaj and not dma and ej == eng:
                    if eng == "pe":
                        continue
                    hit = False
                    for x in wsj:
                        for y in self.overlaps[x]:
                            if y in mine:
                                hit = True
                                break
                        if hit:
                            break
                    if not hit:
                        continue
                dd.append(j)
            deps[i] = dd
            key = ("c:" + chan) if dma else eng
            for r in rs:
                readers.setdefault(r, {})[key] = i
            for w_ in ws:
                last_w[w_] = i
                readers[w_] = {}
        needed = set()
        for dd in deps:
            needed.update(dd)
        eng_cnt = {e: 0 for e in ENGS}
        chan_cnt = {}
        ticket = {}
        for i, (eng, fn, rs, ws, dma, chan, fr) in enumerate(ops):
            if dma:
                chan_cnt[chan] = chan_cnt.get(chan, 0) + 16
                ticket[i] = ("c:" + chan, chan_cnt[chan])
            elif i in needed:
                eng_cnt[eng] += 1
                ticket[i] = ("e:" + eng, eng_cnt[eng])
        self.sem_names = ["e:" + e for e in ENGS] + ["c:" + c for c in chan_cnt]
        self.final_counts = {("e:" + e): eng_cnt[e] for e in ENGS}
        self.final_counts.update({("c:" + c): v for c, v in chan_cnt.items()})
        self.deps, self.ticket = deps, ticket
        return self.sem_names

    def emit(self, block, sems):
        ops, deps, ticket = self.ops, self.deps, self.ticket
        per_eng = {e: [] for e in ENGS}
        for i, o in enumerate(ops):
            per_eng[o[0]].append(i)
        final_counts = self.final_counts

        def make(engname):
            def body(e):
                waited = {}
                for i in per_eng[engname]:
                    _, fn, rs, ws, dma, chan, _ = ops[i]
                    need = {}
                    for j in deps[i]:
                        s, v = ticket[j]
                        if need.get(s, 0) < v:
                            need[s] = v
                    for s, v in need.items():
                        if waited.get(s, 0) < v:
                            e.wait_ge(sems[s], v)
                            waited[s] = v
                    ins = fn(e)
                    if i in ticket:
                        s, v = ticket[i]
                        ins.then_inc(sems[s], 16 if dma else 1)
                if engname == "sp":
                    for s, v in final_counts.items():
                        if v > 0 and waited.get(s, 0) < v:
                            e.wait_ge(sems[s], v)
            return body

        block.tensor(make("pe"))
        block.scalar(make("act"))
        block.vector(make("dve"))
        block.gpsimd(make("pool"))
        block.sync(make("sp"))


class Bump:
    def __init__(self, ranges):
        self.ranges = [list(r) for r in ranges]

    def take(self, nb):
        nb = (nb + 31) // 32 * 32
        for r in self.ranges:
            if r[1] - r[0] >= nb:
                b0 = r[0]
                r[0] += nb
                return b0
        raise MemoryError(f"bump pool exhausted need {nb} have {self.ranges}")


def esize(dt):
    return 4 if dt == F32 else 2


def build_program(n_layers=DEPTH, do_sample=True, do_prompt=True, stop=None):
    nc = bass.Bass("TRN2", target_bir_lowering=False)
    P = Prog(nc)

    def din(name, shape):
        return nc.dram_tensor(name, list(shape), F32, kind="ExternalInput").ap()

    def dout(name, shape):
        return nc.dram_tensor(name, list(shape), F32, kind="ExternalOutput").ap()

    d_xs = din("x_s", [TS, D])
    d_xp = din("x_p", [NPS * TPS, D])
    d_cond = din("cond", [2, D])
    d_cckv = din("cache_ckv", [DEPTH, PAST, 128])
    d_ckr = din("cache_krope", [DEPTH, PAST, 32])
    d_st = din("state_ssd", [DEPTH, 2, 4, 64, 64])
    d_wada = din("w_ada", [DEPTH, D, 6 * D])
    d_bada = din("b_ada", [DEPTH, 6 * D])
    d_gmix = din("g_mix", [DEPTH, D])
    d_win = din("w_in", [DEPTH, D, IN_W])
    d_gq = din("g_q", [DEPTH, 256])
    d_wuq = din("w_uq", [DEPTH, 256, 768])
    d_gkv = din("g_kv", [DEPTH, 128])
    d_wukv = din("w_ukv", [DEPTH, 128, 1024])
    d_scw = din("ssd_conv_w", [DEPTH, 5, 512])
    d_scb = din("ssd_conv_b", [DEPTH, 512])
    d_dtb = din("ssd_dt_bias", [DEPTH * 8])
    d_alog = din("ssd_a_log", [DEPTH * 8])
    d_sd = din("ssd_d", [DEPTH * 8])
    d_sng = din("ssd_norm_g", [DEPTH, 256])
    d_ccw = din("cm_conv_w", [DEPTH, 31, 256])
    d_ccb = din("cm_conv_b", [DEPTH, 256])
    d_clg = din("cm_ln_g", [DEPTH, 256])
    d_clb = din("cm_ln_b", [DEPTH, 256])
    d_wout = din("w_out", [DEPTH, D, D])
    d_gffn = din("g_ffn", [DEPTH, D])
    d_wg = din("w_gate", [DEPTH, D, DFF])
    d_wu = din("w_up", [DEPTH, D, DFF])
    d_wd = din("w_down", [DEPTH, DFF, D])
    d_gfin = din("g_final", [D])
    d_cst = din("cst", [128, 640])
    d_rope = din("rope", [2, 128, TS])

    o_ys = dout("y_s", [TS, D])
    o_yp = dout("y_p", [NPS * TPS, D])
    o_ckv = dout("o_ckv", [NPS, DEPTH, TPS, 128])
    o_kr = dout("o_kr", [NPS, DEPTH, TPS, 32])
    o_ssd = dout("o_ssd", [NPS, DEPTH, 2, 4, 64, 64])

    es = ExitStack()
    with es:
        SBYTES = 212000
        S = es.enter_context(nc.sbuf_tensor("S", [128, SBYTES], U8))
        banks = [es.enter_context(nc.psum_tensor(f"PS{i}", [128, 512], F32)) for i in range(8)]
        for i in range(8):
            P.region(f"ps{i}", "psum", 0, 128, i * 2048, (i + 1) * 2048)
        PS = [View(f"ps{i}", banks[i][:, :]) for i in range(8)]
        PSB = [View(f"ps{i}", banks[i][:, :].bitcast(BF16)) for i in range(8)]

        uid = [0]

        def alloc(pool, name, shape, dt, p0=0):
            nel = int(np.prod(shape[1:]))
            nb = nel * esize(dt)
            b0 = pool.take(nb)
            ap = S[p0:p0 + shape[0], b0:b0 + nb].bitcast(dt)
            if len(shape) == 3:
                ap = ap.rearrange("p (a b) -> p a b", a=shape[1])
            elif len(shape) == 4:
                ap = ap.rearrange("p (a b c) -> p a b c", a=shape[1], b=shape[2])
            uid[0] += 1
            nm = f"{name}.{uid[0]}"
            P.region(nm, "sbuf", p0, p0 + shape[0], b0, b0 + ((nb + 31) // 32 * 32))
            return View(nm, ap)

        pers = Bump([(0, SBYTES)])
        xT = alloc(pers, "xT", [128, 8, TS], F32)
        RING_N = 4
        ring = [alloc(pers, f"ring{i}", [128, 4096], BF16) for i in range(RING_N)]
        wuq = alloc(pers, "wuq", [128, 2, 1024], BF16)
        wukv = alloc(pers, "wukv", [128, 1024], BF16)
        cstf = alloc(pers, "cstf", [128, 640], F32)
        identf = View(cstf.name, cstf.ap[:, 0:128])
        SU = [View(cstf.name, cstf.ap[:, 128:256]), View(cstf.name, cstf.ap[:, 256:384])]
        TRI = [View(cstf.name, cstf.ap[:, 384:512]), View(cstf.name, cstf.ap[:, 512:640])]
        identb = alloc(pers, "identb", [128, 128], BF16)
        onesb = alloc(pers, "onesb", [128, 128], BF16)
        onesf = alloc(pers, "onesf", [128, 128], F32)
        ropeC = alloc(pers, "ropeC", [128, TS], BF16)
        ropeS = alloc(pers, "ropeS", [128, TS], BF16)
        prm = alloc(pers, "prm", [128, DEPTH, NPRM], F32)
        cnd = alloc(pers, "cnd", [128, 24], F32)
        scond = alloc(pers, "scond", [128, 8, 2], BF16)
        modv = alloc(pers, "modv", [128, DEPTH, 48, 2], F32)
        gsc = alloc(pers, "gsc", [128, 2, 8], F32)
        dtb_b = alloc(pers, "dtb_b", [128, 32], F32)
        alog_b = alloc(pers, "alog_b", [128, 32], F32)
        dd_b = alloc(pers, "dd_b", [128, 32], F32)
        a_b = alloc(pers, "a_b", [128, 32], F32)
        dsum_b = alloc(pers, "dsum_b", [128, DEPTH, 4], F32)
        gkv_b = alloc(pers, "gkv_b", [128, 128], F32)
        mhalf = alloc(pers, "mhalf", [128, 512], F32)
        xg = alloc(pers, "xg", [128, 8, 512], BF16)
        pers_end = pers.ranges[0][0]
        io = Bump([(pers_end, SBYTES)])
        gpad = alloc(io, "gpad", [128, 2, TS + 32], BF16)
        xbcp = alloc(io, "xbcp", [128, 4, TS + 4], BF16)
        szb = alloc(io, "szb", [128, 16, 256], BF16)
        qlat = alloc(io, "qlat", [128, 2, TS], BF16)
        ckvnT = alloc(io, "ckvnT", [128, PAST + TS], BF16)
        krT = alloc(io, "krT", [128, PAST + TS], BF16)
        dtr = alloc(io, "dtr", [128, 16, 8], F32)
        io_end = io.ranges[0][0]
        hT = alloc(Bump([(pers_end, SBYTES)]), "hT", [128, 8, TS], BF16)
        R = P.regions
        rg = lambda v: (R[v.name][3], R[v.name][4])
        SC0 = io_end
        print("mem: pers_end", pers_end, "io_end", io_end, "scratch", SBYTES - io_end)

        def dma(eng, out, in_, reads=(), writes=(), **kw):
            P.op(eng, lambda e: e.dma_start(out=out, in_=in_, **kw), reads=reads, writes=writes, dma=True)

        def mm(out, lhsT, rhs, start, stop, reads, w, fresh=False):
            P.op("pe", lambda e: e.matmul(out, lhsT=lhsT, rhs=rhs, start=start, stop=stop),
                 reads=reads, writes=[w], fresh=[w] if fresh else ())

        def tp(out, in_, ident, reads, w, fresh=False):
            P.op("pe", lambda e: e.transpose(out, in_, ident), reads=reads, writes=[w],
                 fresh=[w] if fresh else ())

        def act(out, in_, func, reads, writes, **kw):
            P.op("act", lambda e: e.activation(out=out, in_=in_, func=func, **kw), reads=reads, writes=writes)

        def tt(out, in0, in1, op, reads, writes, eng="dve"):
            P.op(eng, lambda e: e.tensor_tensor(out=out, in0=in0, in1=in1, op=op), reads=reads, writes=writes)

        def stt(out, in0, scalar, in1, op0, op1, reads, writes):
            P.op("dve", lambda e: e.scalar_tensor_tensor(out=out, in0=in0, scalar=scalar, in1=in1, op0=op0, op1=op1),
                 reads=reads, writes=writes)

        def ts(out, in0, s1, s2, op0, op1, reads, writes, eng="dve"):
            if op1 is None:
                P.op(eng, lambda e: e.tensor_scalar(out=out, in0=in0, scalar1=s1, scalar2=None, op0=op0),
                     reads=reads, writes=writes)
            else:
                P.op(eng, lambda e: e.tensor_scalar(out=out, in0=in0, scalar1=s1, scalar2=s2, op0=op0, op1=op1),
                     reads=reads, writes=writes)

        def cp(out, in_, reads, writes, eng="dve"):
            P.op(eng, lambda e: e.tensor_copy(out, in_), reads=reads, writes=writes)

        def memset(ap, val, writes, eng="dve"):
            P.op(eng, lambda e: e.memset(ap, val), writes=writes)

        def recip(out, in_, reads, writes):
            P.op("dve", lambda e: e.reciprocal(out=out, in_=in_), reads=reads, writes=writes)

        class Rot:
            def __init__(self, items):
                self.items, self.i = items, 0

            def next(self):
                v = self.items[self.i % len(self.items)]
                self.i += 1
                return v

        psA = Rot([0, 1, 2, 3])
        psBk = Rot([4, 5])

        dma("sp", cstf.ap, d_cst, writes=[cstf])
        dma("pool", ropeC.ap, d_rope[0], writes=[ropeC])
        dma("pool", ropeS.ap, d_rope[1], writes=[ropeS])
        memset(onesb.ap, 1.0, [onesb])
        memset(onesf.ap, 1.0, [onesf])
        memset(mhalf.ap, -0.5, [mhalf])
        cp(identb.ap, identf.ap, [cstf], [identb])
        dma("sp", dtb_b.ap, d_dtb.partition_broadcast(128), writes=[dtb_b])
        dma("sp", alog_b.ap, d_alog.partition_broadcast(128), writes=[alog_b])
        dma("sp", dd_b.ap, d_sd.partition_broadcast(128), writes=[dd_b])
        act(a_b.ap, alog_b.ap, AF.Exp, [alog_b], [a_b])
        ts(a_b.ap, a_b.ap, -1.0, None, ALU.mult, None, [a_b], [a_b])
        ddv = dd_b.ap.rearrange("p (l d h) -> p l d h", l=DEPTH, d=2)
        tt(dsum_b.ap, ddv[:, :, 0, :], ddv[:, :, 1, :], ALU.add, [dd_b], [dsum_b])

        prol = Bump([(SC0, SBYTES)])
        stg = [alloc(prol, f"stg{i}", [128, 128], F32) for i in range(2)]
        O_BADA, O_GMIX, O_GFFN, O_GQ, O_GKV, O_SCW, O_SCB, O_SNG, O_CCW, O_CCB, O_CLG, O_CLB = \
            0, 48, 56, 64, 66, 67, 87, 91, 93, 155, 157, 159
        for l in range(DEPTH):
            rows = [
                (d_bada[l].rearrange("(r c) -> r c", c=128), 48),
                (d_gmix[l].rearrange("(r c) -> r c", c=128), 8),
                (d_gffn[l].rearrange("(r c) -> r c", c=128), 8),
                (d_gq[l].rearrange("(r c) -> r c", c=128), 2),
                (d_gkv[l].rearrange("(r c) -> r c", c=128), 1),
                (d_scw[l].rearrange("j (r c) -> (j r) c", c=128), 20),
                (d_scb[l].rearrange("(r c) -> r c", c=128), 4),
                (d_sng[l].rearrange("(r c) -> r c", c=128), 2),
                (d_ccw[l].rearrange("j (r c) -> (j r) c", c=128), 62),
                (d_ccb[l].rearrange("(r c) -> r c", c=128), 2),
                (d_clg[l].rearrange("(r c) -> r c", c=128), 2),
                (d_clb[l].rearrange("(r c) -> r c", c=128), 2),
            ]
            r0 = 0
            for src, nr in rows:
                done = 0
                while done < nr:
                    si = (r0 + done) // 128
                    off = (r0 + done) % 128
                    k = min(nr - done, 128 - off)
                    dma("sp", stg[si].ap[off:off + k, :], src[done:done + k, :], writes=[stg[si]])
                    done += k
                r0 += nr
            assert r0 == NPRM
            for si, (c0, ncol) in enumerate([(0, 128), (128, NPRM - 128)]):
                pb = psA.next()
                tp(PS[pb].ap[:, 0:ncol], stg[si].ap[0:ncol, :], identf.ap[0:ncol, 0:ncol], [stg[si], cstf], PS[pb], fresh=True)
                cp(prm.ap[:, l, c0:c0 + ncol], PS[pb].ap[:, 0:ncol], [PS[pb]], [prm])
        dma("sp", stg[0].ap[0:16, :], d_cond.rearrange("a (r c) -> (a r) c", c=128), writes=[stg[0]])
        dma("sp", stg[0].ap[16:24, :], d_gfin.rearrange("(r c) -> r c", c=128), writes=[stg[0]])
        pb = psA.next()
        tp(PS[pb].ap[:, 0:24], stg[0].ap[0:24, :], identf.ap[0:24, 0:24], [stg[0], cstf], PS[pb], fresh=True)
        cp(cnd.ap, PS[pb].ap[:, 0:24], [PS[pb]], [cnd])
        act(scond.ap.rearrange("p k c -> p c k"), cnd.ap[:, 0:16].rearrange("p (c k) -> p c k", c=2), AF.Silu, [cnd], [scond])

        ring_i = [0]

        def load_piece(srcs):
            v = ring[ring_i[0] % RING_N]
            ring_i[0] += 1
            for dst_fn, src in srcs:
                dma("pool", dst_fn(v.ap), src, writes=[v])
            return v

        def r3(ap, a):
            return ap.rearrange("p (a b) -> p a b", a=a)

        for l in range(n_layers):
            for pc in range(12):
                v = load_piece([(lambda a: r3(a, 8), d_wada[l, :, pc * 512:(pc + 1) * 512].rearrange("(k p) c -> p k c", p=128))])
                w3 = r3(v.ap, 8)
                for q in range(4):
                    oc = pc * 4 + q
                    for kc in range(8):
                        mm(PS[7].ap[:, oc * 2:oc * 2 + 2], w3[:, kc, q * 128:(q + 1) * 128], scond.ap[:, kc, :],
                           kc == 0, kc == 7, [v, scond], PS[7], fresh=(oc == 0 and kc == 0))
            tt(modv.ap[:, l], PS[7].ap[:, 0:96].rearrange("p (o c) -> p o c", c=2),
               prm.ap[:, l, O_BADA:O_BADA + 48].unsqueeze(2).to_broadcast([128, 48, 2]), ALU.add, [PS[7], prm], [modv])

        class Cfg:
            pass

        def make_cfg(kind):
            c = Cfg()
            c.kind = kind
            if kind == "S":
                c.T, c.TT, c.seqs, c.rope, c.ctx, c.cond = TS, 512, [(0, TS)], True, True, 0
                c.dx, c.oy = d_xs, o_ys
            else:
                c.T, c.TT, c.seqs, c.rope, c.ctx, c.cond = NPS * TPS, 256, [(0, TPS), (TPS, TPS)], False, False, 1
                c.dx, c.oy = d_xp, o_yp
            c.tiles = []
            for si, (s0, sl) in enumerate(c.seqs):
                for t in range(s0, s0 + sl, c.TT):
                    c.tiles.append((si, t))
            c.nblk = c.T // 128
            return c

        def load_x(c):
            pool = Bump([(SC0, SBYTES)])
            xs_ = [alloc(pool, f"xstg{i}", [128, D], F32) for i in range(2)]
            for b in range(c.nblk):
                st = xs_[b % 2]
                dma("sp", st.ap, c.dx[b * 128:(b + 1) * 128, :], writes=[st])
                for half in range(2):
                    pb = psA.next()
                    for q in range(4):
                        kc = half * 4 + q
                        tp(PS[pb].ap[:, q * 128:(q + 1) * 128], st.ap[:, kc * 128:(kc + 1) * 128], identf.ap,
                           [st, cstf], PS[pb], fresh=(q == 0))
                    cp(xT.ap[:, half * 4:half * 4 + 4, b * 128:(b + 1) * 128],
                       PS[pb].ap.rearrange("p (q t) -> p q t", q=4), [PS[pb]], [xT])

        def norm_tile(c, pool_views, t0, n, gsc_ap, sh_ap, out_fn, out_v, dt_out_bf=True):
            sq, sd, rstd, tmpb = pool_views
            for kc in range(8):
                q = sq[kc % 2]
                act(q.ap[:, 0:n], xT.ap[:, kc, t0:t0 + n], AF.Square, [xT], [q])
                mm(PS[6].ap[:, 0:n], onesb.ap, q.ap[:, 0:n], kc == 0, kc == 7, [onesb, q], PS[6], fresh=(kc == 0))
            act(sd.ap[:, 0:n], PS[6].ap[:, 0:n], AF.Ln, [PS[6]], [sd], scale=1.0 / D, bias=1e-6)
            act(rstd.ap[:, 0:n], sd.ap[:, 0:n], AF.Exp, [sd], [rstd], scale=-0.5)
            for kc in range(8):
                tb = tmpb[kc % 2]
                tt(tb.ap[:, 0:n], xT.ap[:, kc, t0:t0 + n], rstd.ap[:, 0:n], ALU.mult, [xT, rstd], [tb])
                if sh_ap is None:
                    ts(out_fn(kc), tb.ap[:, 0:n], gsc_ap[:, kc:kc + 1], None, ALU.mult, None, [tb, cnd], [out_v])
                else:
                    act(out_fn(kc), tb.ap[:, 0:n], AF.Identity, [tb, gsc, modv], [out_v],
                        scale=gsc_ap[:, kc:kc + 1], bias=sh_ap[:, kc:kc + 1])

        def xupdate(c, l, t0, n, lhs_fn, rhs_list, g_off, reads):
            for oc in range(8):
                pb = psA.next()
                nj = len(rhs_list)
                for j in range(nj):
                    mm(PS[pb].ap[:, 0:n], lhs_fn(j, oc), rhs_list[j], j == 0, j == nj - 1, reads, PS[pb], fresh=(j == 0))
                stt(xT.ap[:, oc, t0:t0 + n], PS[pb].ap[:, 0:n], modv.ap[:, l, g_off + oc, c.cond:c.cond + 1],
                    xT.ap[:, oc, t0:t0 + n], ALU.mult, ALU.add, [PS[pb], modv, xT], [xT])

        def layer_pass(c, l):
            T, TT = c.T, c.TT
            cd = c.cond
            for i, (og, osc) in enumerate([(O_GMIX, 8), (O_GFFN, 32)]):
                stt(gsc.ap[:, i, :], modv.ap[:, l, osc:osc + 8, cd], 1.0, prm.ap[:, l, og:og + 8], ALU.add, ALU.mult,
                    [modv, prm], [gsc])
            sh1 = modv.ap[:, l, 0:8, cd]
            sh2 = modv.ap[:, l, 24:32, cd]
            dma("sp", gkv_b.ap, d_gkv[l].partition_broadcast(128), writes=[gkv_b])
            dma("pool", wuq.ap[:, :, 0:768], d_wuq[l].rearrange("(k p) c -> p k c", p=128), writes=[wuq])
            dma("pool", wukv.ap, d_wukv[l], writes=[wukv])
            for kc in range(2):
                ts(wuq.ap[:, kc, 0:768], wuq.ap[:, kc, 0:768], prm.ap[:, l, O_GQ + kc:O_GQ + kc + 1], None, ALU.mult, None,
                   [wuq, prm], [wuq])
                src = wuq.ap[:, kc, 0:768].rearrange("p (h c) -> p h c", h=8)[:, :, 64:96].rearrange("p h (a f) -> p h a f", a=2)
                dst = wuq.ap[:, kc, 768:1024].rearrange("p (h a f) -> p h a f", h=8, a=2)
                for a in range(2):
                    ts(dst[:, :, a, 0:8], src[:, :, a, 8:16], -1.0, None, ALU.mult, None, [wuq], [wuq])
                    cp(dst[:, :, a, 8:16], src[:, :, a, 0:8], [wuq], [wuq])

            wl = d_win[l]

            def wsl(c0, c1):
                return wl[:, c0:c1].rearrange("(k p) c -> p k c", p=128)

            v_cm = load_piece([(lambda a: r3(a, 8), wsl(1192, 1704))])
            v_ssd = load_piece([(lambda a: r3(a, 8), wsl(672, 1184))])
            v_misc = load_piece([(lambda a: r3(a, 8)[:, :, 0:256], wsl(416, 672)),
                                 (lambda a: r3(a, 8)[:, :, 256:416], wsl(256, 416)),
                                 (lambda a: r3(a, 8)[:, :, 448:456], wsl(1184, 1192))])
            v_q = load_piece([(lambda a: r3(a, 8)[:, :, 0:256], wsl(0, 256))])
            w_cm, w_ssd, w_misc, w_q = r3(v_cm.ap, 8), r3(v_ssd.ap, 8), r3(v_misc.ap, 8), r3(v_q.ap, 8)
            for kc in range(8):
                src = w_misc[:, kc, 384:416].rearrange("p (a f) -> p a f", a=2)
                dst = w_misc[:, kc, 416:448].rearrange("p (a f) -> p a f", a=2)
                ts(dst[:, :, 0:8], src[:, :, 8:16], -1.0, None, ALU.mult, None, [v_misc], [v_misc])
                cp(dst[:, :, 8:16], src[:, :, 0:8], [v_misc], [v_misc])

            sp = Bump([(SC0, SBYTES)])
            sq = [alloc(sp, f"sq{i}", [128, 512], BF16) for i in range(2)]
            sd = alloc(sp, "sd", [128, 512], F32)
            rstd = alloc(sp, "rstd", [128, 512], F32)
            tmpb = [alloc(sp, f"tmpb{i}", [128, 512], F32) for i in range(2)]
            sig = alloc(sp, "sig", [128, 512], F32)
            sqk = alloc(sp, "sqk", [128, 512], BF16)
            t1 = alloc(sp, "t1", [128, 512], F32)
            t2 = alloc(sp, "t2", [128, 512], F32)
            rq = alloc(sp, "rq", [128, 512], F32)
            tmo = alloc(sp, "tmo", [128, 160], F32)
            tmo2 = alloc(sp, "tmo2", [128, 128], F32)
            ssq = alloc(sp, "ssq", [128, 2], F32)
            junk = alloc(sp, "junk", [128, 256], F32)
            koff = PAST if c.ctx else 0

            for si, (s0, sl) in enumerate(c.seqs):
                g0 = s0 + si * 32
                memset(gpad.ap[:, :, g0:g0 + 16], 0.0, [gpad])
                memset(gpad.ap[:, :, g0 + 16 + sl:g0 + 32 + sl], 0.0, [gpad])
                x0 = s0 + si * 4
                memset(xbcp.ap[:, :, x0:x0 + 2], 0.0, [xbcp])
                memset(xbcp.ap[:, :, x0 + 2 + sl:x0 + 4 + sl], 0.0, [xbcp])

            if c.ctx:
                cst_ = alloc(sp, "cstg", [128, 128], F32)
                kst_ = alloc(sp, "kstg", [128, 96], F32)
                memset(kst_.ap, 0.0, [kst_])
                for b in range(2):
                    dma("sp", cst_.ap, d_cckv[l, b * 128:(b + 1) * 128, :], writes=[cst_])
                    pb = psA.next()
                    tp(PS[pb].ap[:, 0:128], cst_.ap, identf.ap, [cst_, cstf], PS[pb], fresh=True)
                    cp(ckvnT.ap[:, b * 128:(b + 1) * 128], PS[pb].ap[:, 0:128], [PS[pb]], [ckvnT])
                    dma("sp", kst_.ap[:, 64:96], d_ckr[l, b * 128:(b + 1) * 128, :], writes=[kst_])
                    pb = psA.next()
                    tp(PS[pb].ap[0:96, 0:128], kst_.ap, identf.ap, [kst_, cstf], PS[pb], fresh=True)
                    cp(krT.ap[64:96, b * 128:(b + 1) * 128], PS[pb].ap[64:96, 0:128], [PS[pb]], [krT])

            def fm(wap, c0, m, n, reads, out_rows=None):
                pb = psA.next()
                o = PS[pb].ap[0:m, 0:n] if out_rows is None else PS[pb].ap[out_rows[0]:out_rows[1], 0:n]
                for kc in range(8):
                    mm(o, wap[:, kc, c0:c0 + m], xg.ap[:, kc, 0:n], kc == 0, kc == 7, reads + [xg], PS[pb], fresh=(kc == 0))
                return pb

            for (si, t0) in c.tiles:
                n = TT
                s0, sl = c.seqs[si]
                norm_tile(c, (sq, sd, rstd, tmpb), t0, n, gsc.ap[:, 0, :], sh1, lambda kc: xg.ap[:, kc, 0:n], xg)
                gofs = si * 32 + 16 + t0
                for j in range(2):
                    pa = fm(w_cm, j * 128, 128, n, [v_cm])
                    pbk = fm(w_cm, 256 + j * 128, 128, n, [v_cm])
                    act(sig.ap[:, 0:n], PS[pbk].ap[:, 0:n], AF.Sigmoid, [PS[pbk]], [sig])
                    tt(gpad.ap[:, j, gofs:gofs + n], PS[pa].ap[:, 0:n], sig.ap[:, 0:n], ALU.mult, [PS[pa], sig], [gpad])
                xofs = si * 4 + 2 + t0
                for j in range(4):
                    pa = fm(w_ssd, j * 128, 128, n, [v_ssd])
                    act(xbcp.ap[:, j, xofs:xofs + n], PS[pa].ap[:, 0:n], AF.Copy, [PS[pa]], [xbcp])
                for b in range(n // 128):
                    blk = (t0 + b * 128) // 128
                    pb = psA.next()
                    for kc in range(8):
                        mm(PS[pb].ap[:, 0:256], xg.ap[:, kc, b * 128:(b + 1) * 128], w_misc[:, kc, 0:256], kc == 0, kc == 7,
                           [xg, v_misc], PS[pb], fresh=(kc == 0))
                    act(szb.ap[:, blk, :], PS[pb].ap[:, 0:256], AF.Silu, [PS[pb]], [szb])
                    pb = psA.next()
                    for kc in range(8):
                        mm(PS[pb].ap[:, 0:8], xg.ap[:, kc, b * 128:(b + 1) * 128], w_misc[:, kc, 448:456], kc == 0, kc == 7,
                           [xg, v_misc], PS[pb], fresh=(kc == 0))
                    tt(dtr.ap[:, blk, :], PS[pb].ap[:, 0:8], dtb_b.ap[:, l * 8:(l + 1) * 8], ALU.add, [PS[pb], dtb_b], [dtr])
                    if c.kind == "P":
                        pb = psA.next()
                        for kc in range(8):
                            mm(PS[pb].ap[:, 0:160], xg.ap[:, kc, b * 128:(b + 1) * 128], w_misc[:, kc, 256:416], kc == 0, kc == 7,
                               [xg, v_misc], PS[pb], fresh=(kc == 0))
                        cp(tmo.ap, PS[pb].ap[:, 0:160], [PS[pb]], [tmo])
                        tloc = t0 - s0 + b * 128
                        dma("sp", o_kr[si, l, tloc:tloc + 128, :], tmo.ap[:, 128:160], reads=[tmo])
                        memset(ssq.ap[:, 0:1], 0.0, [ssq])
                        act(junk.ap[:, 0:128], tmo.ap[:, 0:128], AF.Square, [tmo, ssq], [junk, ssq], accum_out=ssq.ap[:, 0:1])
                        act(ssq.ap[:, 1:2], ssq.ap[:, 0:1], AF.Ln, [ssq], [ssq], scale=1.0 / 128, bias=1e-6)
                        act(ssq.ap[:, 1:2], ssq.ap[:, 1:2], AF.Exp, [ssq], [ssq], scale=-0.5)
                        stt(tmo2.ap, tmo.ap[:, 0:128], ssq.ap[:, 1:2], gkv_b.ap, ALU.mult, ALU.mult, [tmo, ssq, gkv_b], [tmo2])
                        dma("sp", o_ckv[si, l, tloc:tloc + 128, :], tmo2.ap, reads=[tmo2])
                pa = fm(w_misc, 256, 128, n, [v_misc])
                act(sqk.ap[:, 0:n], PS[pa].ap[:, 0:n], AF.Square, [PS[pa]], [sqk])
                mm(PS[6].ap[:, 0:n], onesb.ap, sqk.ap[:, 0:n], True, True, [onesb, sqk], PS[6], fresh=True)
                act(sd.ap[:, 0:n], PS[6].ap[:, 0:n], AF.Ln, [PS[6]], [sd], scale=1.0 / 128, bias=1e-6)
                act(rstd.ap[:, 0:n], sd.ap[:, 0:n], AF.Exp, [sd], [rstd], scale=-0.5)
                kofs = koff + t0 if c.ctx else t0
                stt(ckvnT.ap[:, kofs:kofs + n], PS[pa].ap[:, 0:n], prm.ap[:, l, O_GKV:O_GKV + 1], rstd.ap[:, 0:n],
                    ALU.mult, ALU.mult, [PS[pa], prm, rstd], [ckvnT])
                pa = fm(w_misc, 384, 32, n, [v_misc], out_rows=(64, 96))
                if c.rope:
                    pbk = fm(w_misc, 416, 32, n, [v_misc], out_rows=(64, 96))
                    tt(t1.ap[64:96, 0:n], PS[pa].ap[64:96, 0:n], ropeC.ap[64:96, t0:t0 + n], ALU.mult, [PS[pa], ropeC], [t1])
                    tt(t2.ap[64:96, 0:n], PS[pbk].ap[64:96, 0:n], ropeS.ap[64:96, t0:t0 + n], ALU.mult, [PS[pbk], ropeS], [t2])
                    tt(krT.ap[64:96, kofs:kofs + n], t1.ap[64:96, 0:n], t2.ap[64:96, 0:n], ALU.add, [t1, t2], [krT])
                else:
                    act(krT.ap[64:96, kofs:kofs + n], PS[pa].ap[64:96, 0:n], AF.Copy, [PS[pa]], [krT])
                pq = [fm(w_q, j * 128, 128, n, [v_q]) for j in range(2)]
                for j in range(2):
                    act(sq[j].ap[:, 0:n], PS[pq[j]].ap[:, 0:n], AF.Square, [PS[pq[j]]], [sq[j]])
                    mm(PS[6].ap[:, 0:n], onesb.ap, sq[j].ap[:, 0:n], j == 0, j == 1, [onesb, sq[j]], PS[6], fresh=(j == 0))
                act(sd.ap[:, 0:n], PS[6].ap[:, 0:n], AF.Ln, [PS[6]], [sd], scale=1.0 / 256, bias=1e-6)
                act(rq.ap[:, 0:n], sd.ap[:, 0:n], AF.Exp, [sd], [rq], scale=-0.5)
                for j in range(2):
                    tt(qlat.ap[:, j, t0:t0 + n], PS[pq[j]].ap[:, 0:n], rq.ap[:, 0:n], ALU.mult, [PS[pq[j]], rq], [qlat])

            if stop == "I":
                return
            v_wor = load_piece([(lambda a: r3(a, 4), d_wout[l, 512:1024, :].rearrange("(k p) c -> p k c", p=128))])
            w_or = r3(v_wor.ap, 4)
            for j in range(2):
                ts(w_or[:, j, :], w_or[:, j, :], prm.ap[:, l, O_SNG + j:O_SNG + j + 1], None, ALU.mult, None, [v_wor, prm], [v_wor])

            sp = Bump([(SC0, SBYTES), rg(xg)])
            dg = alloc(sp, "dg", [128, 2, 31, 128], BF16)
            cvf = [alloc(sp, f"cvf{j}", [128, 512], F32) for j in range(2)]
            sqf = [alloc(sp, f"sqf{j}", [128, 512], F32) for j in range(2)]
            mean = alloc(sp, "mean", [128, 512], F32)
            var = alloc(sp, "var", [128, 512], F32)
            rr = alloc(sp, "rr", [128, 512], F32)
            uu = alloc(sp, "uu", [128, 512], F32)
            cmix = alloc(sp, "cmix", [128, 2, 512], BF16)
            for j in range(2):
                for tap in range(31):
                    col = O_CCW + tap * 2 + j
                    ts(dg.ap[:, j, tap, :], identb.ap, prm.ap[:, l, col:col + 1], None, ALU.mult, None, [identb, prm], [dg])
            for (si, t0) in c.tiles:
                n = TT
                gofs = si * 32 + 1 + t0
                for j in range(2):
                    pb = psBk.next()
                    for tap in range(31):
                        mm(PS[pb].ap[:, 0:n], dg.ap[:, j, tap, :], gpad.ap[:, j, gofs + tap:gofs + tap + n], tap == 0, tap == 30,
                           [dg, gpad], PS[pb], fresh=(tap == 0))
                    bcol = prm.ap[:, l, O_CCB + j:O_CCB + j + 1]
                    act(cvf[j].ap[:, 0:n], PS[pb].ap[:, 0:n], AF.Identity, [PS[pb], prm], [cvf[j]], bias=bcol)
                    act(sqf[j].ap[:, 0:n], PS[pb].ap[:, 0:n], AF.Square, [PS[pb], prm], [sqf[j]], bias=bcol)
                for j in range(2):
                    mm(PS[6].ap[:, 0:n], onesf.ap, cvf[j].ap[:, 0:n], j == 0, j == 1, [onesf, cvf[j]], PS[6], fresh=(j == 0))
                for j in range(2):
                    mm(PS[7].ap[:, 0:n], onesf.ap, sqf[j].ap[:, 0:n], j == 0, j == 1, [onesf, sqf[j]], PS[7], fresh=(j == 0))
                ts(mean.ap[:, 0:n], PS[6].ap[:, 0:n], 1.0 / 256, None, ALU.mult, None, [PS[6]], [mean])
                tt(var.ap[:, 0:n], mean.ap[:, 0:n], mean.ap[:, 0:n], ALU.mult, [mean], [var])
                stt(var.ap[:, 0:n], PS[7].ap[:, 0:n], 1.0 / 256, var.ap[:, 0:n], ALU.mult, ALU.subtract, [PS[7], var], [var])
                act(var.ap[:, 0:n], var.ap[:, 0:n], AF.Ln, [var], [var], bias=1e-5)
                act(rr.ap[:, 0:n], var.ap[:, 0:n], AF.Exp, [var], [rr], scale=-0.5)
                for j in range(2):
                    tt(uu.ap[:, 0:n], cvf[j].ap[:, 0:n], mean.ap[:, 0:n], ALU.subtract, [cvf[j], mean], [uu])
                    tt(uu.ap[:, 0:n], uu.ap[:, 0:n], rr.ap[:, 0:n], ALU.mult, [uu, rr], [uu])
                    act(cmix.ap[:, j, 0:n], uu.ap[:, 0:n], AF.Silu, [uu, prm], [cmix],
                        scale=prm.ap[:, l, O_CLG + j:O_CLG + j + 1], bias=prm.ap[:, l, O_CLB + j:O_CLB + j + 1])
                xupdate(c, l, t0, n, lambda j, oc: w_or[:, 2 + j, oc * 128:(oc + 1) * 128],
                        [cmix.ap[:, 0, 0:n], cmix.ap[:, 1, 0:n]], 16, [v_wor, cmix])

            if stop == "C":
                return
            ssd_phase(c, l, w_or, v_wor)
            if stop in ("S", "S1", "S2", "S3"):
                return
            v_woa = load_piece([(lambda a: r3(a, 4), d_wout[l, 0:512, :].rearrange("(k p) c -> p k c", p=128))])
            mla_phase(c, l, r3(v_woa.ap, 4), v_woa)
            if stop == "M":
                return
            ffn_phase(c, l)

        def ssd_phase(c, l, w_or, v_wor):
            T, TT = c.T, c.TT
            nblk = c.nblk
            g_r, x_r, z_r = rg(gpad), rg(xbcp), rg(szb)
            sp = Bump([(SC0, SBYTES), g_r])
            dg5 = alloc(sp, "dg5", [128, 4, 5, 128], BF16)
            xsT = alloc(sp, "xsT", [128, 2, 512], BF16)
            BCt = alloc(sp, "BCt", [128, 3, TS], BF16)
            xs_tm = alloc(sp, "xs_tm", [128, 16, 256], BF16)
            B_tm = alloc(sp, "B_tm", [128, 16, 128], BF16)
            dt = alloc(sp, "dt", [128, 16, 8], F32)
            dta = alloc(sp, "dta", [128, 16, 8], F32)
            hTf = [alloc(sp, f"hTf{d}", [128, 2, 64], F32) for d in range(2)]
            hTb = alloc(sp, "hTb", [128, 2, 64], BF16)
            sstg = alloc(sp, "sstg", [128, 128], F32)
            sp2 = Bump([tuple(r) for r in sp.ranges] + [x_r, rg(xg)])
            nb8 = nblk * 8
            dtf = dtr.ap.rearrange("p b e -> p (b e)")[:, 0:nb8]
            act(dt.ap.rearrange("p b e -> p (b e)")[:, 0:nb8], dtf, AF.Exp, [dtr], [dt])
            act(dt.ap.rearrange("p b e -> p (b e)")[:, 0:nb8], dt.ap.rearrange("p b e -> p (b e)")[:, 0:nb8], AF.Ln, [dt], [dt], bias=1.0)
            tt(dta.ap[:, 0:nblk, :], dt.ap[:, 0:nblk, :], a_b.ap[:, l * 8:(l + 1) * 8].unsqueeze(1).to_broadcast([128, nblk, 8]),
               ALU.mult, [dt, a_b], [dta])
            memset(BCt.ap[64:128, 1, 0:T], 0.0, [BCt])
            memset(BCt.ap[0:64, 2, 0:T], 0.0, [BCt])
            for ch in range(4):
                for tap in range(5):
                    col = O_SCW + tap * 4 + ch
                    ts(dg5.ap[:, ch, tap, :], identb.ap, prm.ap[:, l, col:col + 1], None, ALU.mult, None, [identb, prm], [dg5])
            for (si, t0) in c.tiles:
                n = TT
                xofs = si * 4 + t0
                for ch in range(4):
                    pb = psBk.next()
                    for tap in range(5):
                        mm(PS[pb].ap[:, 0:n], dg5.ap[:, ch, tap, :], xbcp.ap[:, ch, xofs + tap:xofs + tap + n], tap == 0, tap == 4,
                           [dg5, xbcp], PS[pb], fresh=(tap == 0))
                    bias_ = prm.ap[:, l, O_SCB + ch:O_SCB + ch + 1]
                    if ch < 2:
                        act(xsT.ap[:, ch, 0:n], PS[pb].ap[:, 0:n], AF.Silu, [PS[pb], prm], [xsT], bias=bias_)
                    elif ch == 2:
                        act(BCt.ap[:, 0, t0:t0 + n], PS[pb].ap[:, 0:n], AF.Silu, [PS[pb], prm], [BCt], bias=bias_)
                    else:
                        act(BCt.ap[0:64, 1, t0:t0 + n], PS[pb].ap[0:64, 0:n], AF.Silu, [PS[pb], prm], [BCt], bias=bias_[0:64])
                        act(BCt.ap[64:128, 2, t0:t0 + n], PS[pb].ap[64:128, 0:n], AF.Silu, [PS[pb], prm], [BCt], bias=bias_[64:128])
                for b in range(n // 128):
                    blk = (t0 + b * 128) // 128
                    pb = psA.next()
                    for j in range(2):
                        tp(PSB[pb].ap[:, j * 128:(j + 1) * 128], xsT.ap[:, j, b * 128:(b + 1) * 128], identb.ap, [xsT, identb], PS[pb], fresh=(j == 0))
                    tp(PSB[pb].ap[:, 256:384], BCt.ap[:, 0, t0 + b * 128:t0 + (b + 1) * 128], identb.ap, [BCt, identb], PS[pb])
                    cp(xs_tm.ap[:, blk, :], PSB[pb].ap[:, 0:256], [PS[pb]], [xs_tm])
                    cp(B_tm.ap[:, blk, :], PSB[pb].ap[:, 256:384], [PS[pb]], [B_tm])
            if stop == "S1":
                return
            hst = alloc(sp2, "hst", [128, 16, 2, 64], BF16)
            Gm = [alloc(sp2, f"Gm{d}", [128, 2, 128], F32) for d in range(2)]
            Lm = alloc(sp2, "Lm", [128, 4, 128], F32)
            seg = alloc(sp2, "seg", [128, 4, 128], F32)
            MT = [[alloc(sp2, f"MT{i}{d}", [128, 4, 128], BF16) for d in range(2)] for i in range(2)]
            xdt = [[alloc(sp2, f"xdt{i}{d}", [128, 4, 64], BF16) for d in range(2)] for i in range(2)]
            ee = [alloc(sp2, f"ee{i}", [128, 16], F32) for i in range(2)]
            cds = [alloc(sp2, f"cds{i}", [128, 2, 2], F32) for i in range(2)]
            xdd = [alloc(sp2, f"xdd{i}", [128, 4, 64], BF16) for i in range(2)]
            wv = alloc(sp2, "wv", [128, 4], F32)
            yo = alloc(sp2, "yo", [128, 8, 64], F32)
            y1 = alloc(sp2, "y1", [128, 256], F32)
            y2 = alloc(sp2, "y2", [128, 256], F32)
            yn = alloc(sp2, "yn", [128, 256], BF16)
            ssq = alloc(sp2, "ssq2", [128, 2], F32)
            smix = alloc(sp2, "smix", [128, 2, 512], BF16)
            junk = yo

            def small_mm(blk, dirs, ee_, cds_):
                first = True
                for d in dirs:
                    A = dta.ap[:, blk, d * 4:(d + 1) * 4]
                    for (c0, lt, lv) in [(d * 4, SU[d].ap, cstf), (8 + d * 4, TRI[d].ap, cstf), (16 + d * 4, onesf.ap, onesf)]:
                        mm(PS[6].ap[:, c0:c0 + 4], lt, A, True, True, [lv, dta], PS[6], fresh=first)
                        first = False
                if len(dirs) == 2:
                    act(ee_.ap[:, 0:16], PS[6].ap[:, 0:16], AF.Exp, [PS[6]], [ee_])
                else:
                    d = dirs[0]
                    act(ee_.ap[:, d * 4:d * 4 + 4], PS[6].ap[:, d * 4:d * 4 + 4], AF.Exp, [PS[6]], [ee_])
                for d in dirs:
                    act(cds_.ap[0:64, d, :], PS[6].ap[0:64, 16 + d * 4:18 + d * 4], AF.Exp, [PS[6]], [cds_])
                    act(cds_.ap[64:128, d, :], PS[6].ap[64:128, 18 + d * 4:20 + d * 4], AF.Exp, [PS[6]], [cds_])

            def state_pre(blk, d, ee_, xdd_):
                tt(wv.ap, dt.ap[:, blk, d * 4:(d + 1) * 4], ee_.ap[:, d * 4:d * 4 + 4], ALU.mult, [dt, ee_], [wv])
                tt(xdd_.ap, xs_tm.ap[:, blk, :].rearrange("p (h q) -> p h q", h=4), wv.ap.unsqueeze(2).to_broadcast([128, 4, 64]),
                   ALU.mult, [xs_tm, wv], [xdd_])

            def state_post(blk, d, cds_, xdd_):
                pb = 7
                for h in range(4):
                    g, j = h // 2, h % 2
                    mm(PS[pb].ap[g * 64:(g + 1) * 64, j * 64:(j + 1) * 64], B_tm.ap[:, blk, g * 64:(g + 1) * 64], xdd_.ap[:, h, :],
                       True, True, [B_tm, xdd_], PS[pb], fresh=(h == 0))
                for j in range(2):
                    stt(hTf[d].ap[:, j, :], hTf[d].ap[:, j, :], cds_.ap[:, d, j:j + 1], PS[pb].ap[:, j * 64:(j + 1) * 64],
                        ALU.mult, ALU.add, [hTf[d], cds_, PS[pb]], [hTf[d]])

            def main_pre(blk, i):
                tk = slice(blk * 128, (blk + 1) * 128)
                pg = psA.next()
                mm(PS[pg].ap[:, 0:256], BCt.ap[:, 0, tk], BCt.ap[:, 1:3, tk], True, True, [BCt], PS[pg], fresh=True)
                small_mm(blk, [0, 1], ee[i], cds[i])
                for d in range(2):
                    tt(Gm[d].ap, PS[pg].ap[:, 0:256].rearrange("p (g s) -> p g s", g=2),
                       TRI[d].ap.unsqueeze(1).to_broadcast([128, 2, 128]), ALU.mult, [PS[pg], cstf], [Gm[d]])
                for d in range(2):
                    tt(Lm.ap, SU[d].ap.unsqueeze(1).to_broadcast([128, 4, 128]),
                       dta.ap[:, blk, d * 4:(d + 1) * 4].unsqueeze(2).to_broadcast([128, 4, 128]), ALU.mult, [cstf, dta], [Lm])
                    pdf = psA.next()
                    for h in range(4):
                        mm(PS[pdf].ap[:, h * 128:(h + 1) * 128], Lm.ap[:, h, :], TRI[d].ap, True, True, [Lm, cstf], PS[pdf], fresh=(h == 0))
                    act(seg.ap.rearrange("p h s -> p (h s)"), PS[pdf].ap, AF.Exp, [PS[pdf]], [seg])
                    for g in range(2):
                        tt(MT[i][d].ap[:, 2 * g:2 * g + 2, :], seg.ap[:, 2 * g:2 * g + 2, :],
                           Gm[d].ap[:, g:g + 1, :].to_broadcast([128, 2, 128]), ALU.mult, [seg, Gm[d]], [MT[i][d]])
                    tt(xdt[i][d].ap, xs_tm.ap[:, blk, :].rearrange("p (h q) -> p h q", h=4),
                       dt.ap[:, blk, d * 4:(d + 1) * 4].unsqueeze(2).to_broadcast([128, 4, 64]), ALU.mult, [xs_tm, dt], [xdt[i][d]])
                state_pre(blk, 0, ee[i], xdd[i])

            def main_post(blk, i):
                tk = slice(blk * 128, (blk + 1) * 128)
                py = psBk.next()
                for h in range(4):
                    for d in range(2):
                        mm(PS[py].ap[:, h * 64:(h + 1) * 64], MT[i][d].ap[:, h, :], xdt[i][d].ap[:, h, :], d == 0, d == 1,
                           [MT[i][d], xdt[i][d]], PS[py], fresh=(h == 0 and d == 0))
                po = psBk.next()
                for d in range(2):
                    hsrc_v = hTb if d == 0 else hst
                    hsrc = hTb.ap if d == 0 else hst.ap[:, blk]
                    for g in range(2):
                        mm(PS[po].ap[:, d * 256 + g * 128:d * 256 + (g + 1) * 128], BCt.ap[:, 1 + g, tk], hsrc,
                           True, True, [BCt, hsrc_v], PS[po], fresh=(d == 0 and g == 0))
                state_post(blk, 0, cds[i], xdd[i])
                cp(hTb.ap, hTf[0].ap, [hTf[0]], [hTb], eng="pool")
                tt(yo.ap, PS[po].ap.rearrange("p (e q) -> p e q", e=8), ee[i].ap[:, 8:16].unsqueeze(2).to_broadcast([128, 8, 64]),
                   ALU.mult, [PS[po], ee[i]], [yo])
                yof = yo.ap.rearrange("p e q -> p (e q)")
                tt(y1.ap, yof[:, 0:256], yof[:, 256:512], ALU.add, [yo], [y1])
                tt(y1.ap, y1.ap, PS[py].ap[:, 0:256], ALU.add, [y1, PS[py]], [y1])
                tt(y2.ap.rearrange("p (h q) -> p h q", h=4), xs_tm.ap[:, blk, :].rearrange("p (h q) -> p h q", h=4),
                   dsum_b.ap[:, l, :].unsqueeze(2).to_broadcast([128, 4, 64]), ALU.mult, [xs_tm, dsum_b], [y2])
                tt(y1.ap, y1.ap, y2.ap, ALU.add, [y1, y2], [y1])
                tt(y2.ap, y1.ap, szb.ap[:, blk, :], ALU.mult, [y1, szb], [y2])
                memset(ssq.ap[:, 0:1], 0.0, [ssq])
                act(junk.ap.rearrange("p e q -> p (e q)")[:, 0:256], y2.ap, AF.Square, [y2, ssq], [junk, ssq], accum_out=ssq.ap[:, 0:1])
                act(ssq.ap[:, 1:2], ssq.ap[:, 0:1], AF.Ln, [ssq], [ssq], scale=1.0 / 256, bias=1e-6)
                act(ssq.ap[:, 1:2], ssq.ap[:, 1:2], AF.Exp, [ssq], [ssq], scale=-0.5)
                ts(yn.ap, y2.ap, ssq.ap[:, 1:2], None, ALU.mult, None, [y2, ssq], [yn])
                bt = (blk * 128) % TT
                pt = psA.next()
                for j in range(2):
                    tp(PSB[pt].ap[:, j * 128:(j + 1) * 128], yn.ap[:, j * 128:(j + 1) * 128], identb.ap, [yn, identb], PS[pt], fresh=(j == 0))
                cp(smix.ap[:, :, bt:bt + 128], PSB[pt].ap[:, 0:256].rearrange("p (j t) -> p j t", j=2), [PS[pt]], [smix])
                if bt + 128 == TT:
                    t0 = blk * 128 + 128 - TT
                    xupdate(c, l, t0, TT, lambda j, oc: w_or[:, j, oc * 128:(oc + 1) * 128],
                            [smix.ap[:, 0, 0:TT], smix.ap[:, 1, 0:TT]], 16, [v_wor, smix])

            for si, (s0, sl) in enumerate(c.seqs):
                blks = list(range(s0 // 128, (s0 + sl) // 128))
                for d in range(2):
                    if c.ctx:
                        dma("sp", sstg.ap.rearrange("p (g n) -> p g n", g=2),
                            d_st[l, d].rearrange("(g j) p n -> (j p) g n", g=2), writes=[sstg])
                        pb = psA.next()
                        tp(PS[pb].ap[:, 0:128], sstg.ap, identf.ap, [sstg, cstf], PS[pb], fresh=True)
                        cp(hTf[d].ap.rearrange("p j q -> p (j q)"), PS[pb].ap[:, 0:128], [PS[pb]], [hTf[d]])
                    else:
                        memset(hTf[d].ap, 0.0, [hTf[d]])
                rb_ = list(reversed(blks))
                small_mm(rb_[0], [1], ee[0], cds[0])
                state_pre(rb_[0], 1, ee[0], xdd[0])
                for k, blk in enumerate(rb_):
                    i = k % 2
                    if k + 1 < len(rb_):
                        small_mm(rb_[k + 1], [1], ee[1 - i], cds[1 - i])
                        state_pre(rb_[k + 1], 1, ee[1 - i], xdd[1 - i])
                    cp(hst.ap[:, blk], hTf[1].ap, [hTf[1]], [hst], eng="pool")
                    state_post(blk, 1, cds[i], xdd[i])
                if stop == "S2":
                    return
                cp(hTb.ap, hTf[0].ap, [hTf[0]], [hTb], eng="pool")
                main_pre(blks[0], 0)
                for k, blk in enumerate(blks):
                    if k + 1 < len(blks):
                        main_pre(blks[k + 1], (k + 1) % 2)
                    main_post(blk, k % 2)
                if stop == "S3":
                    return
                if c.kind == "P":
                    for d in range(2):
                        pb = psA.next()
                        tp(PS[pb].ap[:, 0:128], hTf[d].ap.rearrange("p j q -> p (j q)"), identf.ap, [hTf[d], cstf], PS[pb], fresh=True)
                        cp(sstg.ap, PS[pb].ap[:, 0:128], [PS[pb]], [sstg])
                        dma("sp", o_ssd[si, l, d].rearrange("(g j) p n -> (j p) g n", g=2),
                            sstg.ap.rearrange("p (g n) -> p g n", g=2), reads=[sstg])

        def mla_phase(c, l, w_oa, v_woa):
            T, TT = c.T, c.TT
            g_r, x_r, z_r = rg(gpad), rg(xbcp), rg(szb)
            attnT = alloc(Bump([x_r]), "attnT", [128, 4, TS], BF16)
            sp = Bump([(SC0, SBYTES), g_r, z_r])
            NKB = (PAST + TS) // 128
            KT = [alloc(sp, f"KT{i}", [128, PAST + TS], BF16) for i in range(2)]
            VB = [(alloc(sp, f"Ve{i}", [128, NKB, 64], BF16), alloc(sp, f"Vo{i}", [128, NKB, 64], BF16)) for i in range(2)]
            QT = [alloc(sp, f"QT{i}", [128, TS], BF16) for i in range(2)]
            PT = [alloc(sp, f"PT{i}", [128, 512], BF16) for i in range(3)]
            t1 = alloc(sp, "mt1", [128, 512], F32)
            t2 = alloc(sp, "mt2", [128, 512], F32)
            rb = alloc(sp, "rb", [128, 512], F32)
            ptr = Rot(PT)
            LA = 2
            psS = Rot([0, 1, 2])
            PP = 3
            accb = Rot([(4, 5), (6, 7)])

            heads = []
            for si, (s0, sl) in enumerate(c.seqs):
                for h in range(8):
                    heads.append((si, s0, sl, h))

            def prep(idx):
                si, s0, sl, h = heads[idx]
                hp, hh = h // 2, h % 2
                Tk = (PAST if c.ctx else 0) + sl
                k0 = 0 if c.ctx else s0
                nkb = Tk // 128
                qtiles = [t for (s_, t) in c.tiles if s_ == si]
                kt, qt = KT[idx % 2], QT[idx % 2]
                Ve, Vo = VB[(idx // 2) % 2]
                if hh == 0:
                    vcols = wukv.ap.rearrange("p (h c) -> p h c", h=8)[:, 2 * hp:2 * hp + 2, 64:128]
                    for kb in range(nkb):
                        mm(PS[PP].ap[:, 0:128], ckvnT.ap[:, k0 + kb * 128:k0 + (kb + 1) * 128], vcols, True, True, [ckvnT, wukv], PS[PP], fresh=True)
                        cp(Ve.ap[:, kb, :], PS[PP].ap[:, 0:64], [PS[PP]], [Ve])
                        cp(Vo.ap[:, kb, :], PS[PP].ap[:, 64:128], [PS[PP]], [Vo])
                        yield
                for k1 in range(0, Tk, 512):
                    kn = min(512, Tk - k1)
                    mm(PS[PP].ap[0:64, 0:kn], wukv.ap[:, h * 128:h * 128 + 64], ckvnT.ap[:, k0 + k1:k0 + k1 + kn], True, True,
                       [wukv, ckvnT], PS[PP], fresh=True)
                    cp(kt.ap[0:64, k1:k1 + kn], PS[PP].ap[0:64, 0:kn], [PS[PP]], [kt])
                    yield
                cp(kt.ap[64:96, 0:Tk], krT.ap[64:96, k0:k0 + Tk], [krT], [kt], eng="pool")
                for t0 in qtiles:
                    n = TT
                    tl = t0 - s0
                    for kc in range(2):
                        mm(PS[PP].ap[0:96, 0:n], wuq.ap[:, kc, h * 96:(h + 1) * 96], qlat.ap[:, kc, t0:t0 + n], kc == 0, kc == 1,
                           [wuq, qlat], PS[PP], fresh=(kc == 0))
                    if c.rope:
                        cp(qt.ap[0:64, tl:tl + n], PS[PP].ap[0:64, 0:n], [PS[PP]], [qt])
                        tt(t1.ap[64:96, 0:n], PS[PP].ap[64:96, 0:n], ropeC.ap[64:96, t0:t0 + n], ALU.mult, [PS[PP], ropeC], [t1])
                        yield
                        for kc in range(2):
                            mm(PS[PP].ap[64:96, 0:n], wuq.ap[:, kc, 768 + h * 32:768 + (h + 1) * 32], qlat.ap[:, kc, t0:t0 + n],
                               kc == 0, kc == 1, [wuq, qlat], PS[PP], fresh=(kc == 0))
                        tt(t2.ap[64:96, 0:n], PS[PP].ap[64:96, 0:n], ropeS.ap[64:96, t0:t0 + n], ALU.mult, [PS[PP], ropeS], [t2])
                        tt(qt.ap[64:96, tl:tl + n], t1.ap[64:96, 0:n], t2.ap[64:96, 0:n], ALU.add, [t1, t2], [qt])
                    else:
                        cp(qt.ap[0:96, tl:tl + n], PS[PP].ap[0:96, 0:n], [PS[PP]], [qt])
                    yield

            def attn_unit(idx, t0, prev_tail, filler):
                si, s0, sl, h = heads[idx]
                hp, hh = h // 2, h % 2
                nkb = ((PAST if c.ctx else 0) + sl) // 128
                kt, qt = KT[idx % 2], QT[idx % 2]
                vv = VB[(idx // 2) % 2][hh]
                tl = t0 - s0
                n = c.TT
                po, pd = accb.next()
                r0 = 0 if hh == 0 else 64
                pts = []
                for kb in range(nkb + LA):
                    if kb < nkb:
                        pb = psS.next()
                        mm(PS[pb].ap[:, 0:n], kt.ap[0:96, kb * 128:(kb + 1) * 128], qt.ap[0:96, tl:tl + n], True, True,
                           [kt, qt], PS[pb], fresh=True)
                        pt_ = ptr.next()
                        pts.append(pt_)
                        act(pt_.ap[:, 0:n], PS[pb].ap[:, 0:n], AF.Exp, [PS[pb]], [pt_], scale=SCALE)
                    if kb == min(LA, nkb) - 1 and prev_tail is not None:
                        prev_tail()
                        prev_tail = None
                    if kb >= LA:
                        k2 = kb - LA
                        mm(PS[po].ap[r0:r0 + 64, 0:n], vv.ap[:, k2, :], pts[k2].ap[:, 0:n], k2 == 0, k2 == nkb - 1, [vv, pts[k2]], PS[po], fresh=(k2 == 0))
                        mm(PS[pd].ap[r0:r0 + 64, 0:n], onesb.ap[:, 0:64], pts[k2].ap[:, 0:n], k2 == 0, k2 == nkb - 1, [onesb, pts[k2]], PS[pd], fresh=(k2 == 0))
                    if filler is not None:
                        next(filler, None)

                def tail():
                    recip(rb.ap[r0:r0 + 64, 0:n], PS[pd].ap[r0:r0 + 64, 0:n], [PS[pd]], [rb])
                    tt(attnT.ap[r0:r0 + 64, hp, t0:t0 + n], PS[po].ap[r0:r0 + 64, 0:n], rb.ap[r0:r0 + 64, 0:n], ALU.mult, [PS[po], rb], [attnT])
                return tail

            for _ in prep(0):
                pass
            pend = None
            for idx in range(len(heads)):
                si = heads[idx][0]
                filler = prep(idx + 1) if idx + 1 < len(heads) else None
                for t0 in [t for (s_, t) in c.tiles if s_ == si]:
                    pend = attn_unit(idx, t0, pend, filler)
                if filler is not None:
                    for _ in filler:
                        pass
            if pend is not None:
                pend()
            for (si, t0) in c.tiles:
                xupdate(c, l, t0, TT, lambda j, oc: w_oa[:, j, oc * 128:(oc + 1) * 128],
                        [attnT.ap[:, j, t0:t0 + TT] for j in range(4)], 16, [v_woa, attnT])

        def ffn_phase(c, l):
            T, TT = c.T, c.TT
            sp = Bump([(max(SC0, rg(hT)[1]), SBYTES)])
            sq = [alloc(sp, f"fsq{i}", [128, 512], BF16) for i in range(2)]
            sd = alloc(sp, "fsd", [128, 512], F32)
            rstd = alloc(sp, "frstd", [128, 512], F32)
            tmpb = [alloc(sp, f"ftmpb{i}", [128, 512], F32) for i in range(2)]
            sg = [alloc(sp, f"sg{i}", [128, 512], F32) for i in range(2)]
            actb = [alloc(sp, f"actb{i}", [128, 2, 512], BF16) for i in range(2)]
            sh2 = modv.ap[:, l, 24:32, c.cond]
            for (si, t0) in c.tiles:
                norm_tile(c, (sq, sd, rstd, tmpb), t0, TT, gsc.ap[:, 1, :], sh2, lambda kc, t0=t0: hT.ap[:, kc, t0:t0 + TT], hT)
            def load_group(g):
                c0 = g * 256
                v1 = load_piece([(lambda a: r3(a, 8)[:, :, 0:256], d_wg[l, :, c0:c0 + 256].rearrange("(k p) c -> p k c", p=128)),
                                 (lambda a: r3(a, 8)[:, :, 256:512], d_wu[l, :, c0:c0 + 256].rearrange("(k p) c -> p k c", p=128))])
                v2 = load_piece([(lambda a: r3(a, 4)[:, 0:2, :], d_wd[l, c0:c0 + 256, :].rearrange("(k p) c -> p k c", p=128))])
                return (v1, v2)

            units = [(g, t0) for g in range(NFG) for (si, t0) in c.tiles]
            wts = {0: load_group(0), 1: load_group(1)}

            psF = Rot([4, 5, 6, 7])

            def stage_a_groups(u):
                g, t0 = units[u]
                v1, v2 = wts[g]
                wgu = r3(v1.ap, 8)
                n = TT
                ab = actb[u % 2]
                outs = []
                st = {}

                def mk(j, which):
                    def f():
                        pb = psA.next()
                        c0 = (0 if which == 0 else 256) + j * 128
                        for kc in range(8):
                            mm(PS[pb].ap[:, 0:n], wgu[:, kc, c0:c0 + 128], hT.ap[:, kc, t0:t0 + n], kc == 0, kc == 7, [v1, hT], PS[pb], fresh=(kc == 0))
                        st[(j, which)] = pb
                        if which == 1:
                            pg, pu = st[(j, 0)], pb
                            act(sg[j].ap[:, 0:n], PS[pg].ap[:, 0:n], AF.Silu, [PS[pg]], [sg[j]])
                            tt(ab.ap[:, j, 0:n], sg[j].ap[:, 0:n], PS[pu].ap[:, 0:n], ALU.mult, [sg[j], PS[pu]], [ab])
                    return f
                for j in range(2):
                    outs.append(mk(j, 0))
                    outs.append(mk(j, 1))
                return outs

            def stage_b_steps(u):
                g, t0 = units[u]
                v1, v2 = wts[g]
                wdn = r3(v2.ap, 4)
                n = TT
                ab = actb[u % 2]
                outs = []

                def mk(oc):
                    def f():
                        pb = psF.next()
                        for j in range(2):
                            mm(PS[pb].ap[:, 0:n], wdn[:, j, oc * 128:(oc + 1) * 128], ab.ap[:, j, 0:n], j == 0, j == 1, [v2, ab], PS[pb], fresh=(j == 0))
                        stt(xT.ap[:, oc, t0:t0 + n], PS[pb].ap[:, 0:n], modv.ap[:, l, 40 + oc, c.cond:c.cond + 1],
                            xT.ap[:, oc, t0:t0 + n], ALU.mult, ALU.add, [PS[pb], modv, xT], [xT])
                        if oc == 7 and (u + 1 == len(units) or units[u + 1][0] != g) and g + 2 < NFG:
                            wts[g + 2] = load_group(g + 2)
                    return f
                for oc in range(8):
                    outs.append(mk(oc))
                return outs

            for u in range(len(units) + 1):
                ga = stage_a_groups(u) if u < len(units) else []
                gb = stage_b_steps(u - 1) if u >= 1 else []
                for k in range(4):
                    if ga:
                        ga[k]()
                    if gb:
                        gb[2 * k]()
                        gb[2 * k + 1]()


        def final_out(c):
            sp = Bump([(SC0, SBYTES)])
            sq = [alloc(sp, f"osq{i}", [128, 512], BF16) for i in range(2)]
            sd = alloc(sp, "osd", [128, 512], F32)
            rstd = alloc(sp, "orstd", [128, 512], F32)
            tmpb = [alloc(sp, f"otmpb{i}", [128, 512], F32) for i in range(2)]
            yf = alloc(sp, "yf", [128, 8, 128], F32)
            ytm = [alloc(sp, f"ytm{i}", [128, D], F32) for i in range(2)]
            gf = cnd.ap[:, 16:24]
            for b in range(c.nblk):
                norm_tile(c, (sq, sd, rstd, tmpb), b * 128, 128, gf, None, lambda kc: yf.ap[:, kc, :], yf)
                y = ytm[b % 2]
                for half in range(2):
                    pb = psA.next()
                    for q in range(4):
                        kc = half * 4 + q
                        tp(PS[pb].ap[:, q * 128:(q + 1) * 128], yf.ap[:, kc, :], identf.ap, [yf, cstf], PS[pb], fresh=(q == 0))
                    act(y.ap[:, half * 512:(half + 1) * 512], PS[pb].ap, AF.Copy, [PS[pb]], [y])
                dma("sp", c.oy[b * 128:(b + 1) * 128, :], y.ap, reads=[y])

        passes = []
        if do_sample:
            passes.append(make_cfg("S"))
        if do_prompt:
            passes.append(make_cfg("P"))
        for c in passes:
            load_x(c)
            for l in range(n_layers):
                layer_pass(c, l)
            final_out(c)

        names = P.finalize()
        print("nops", len(P.ops), "nsems", len(names))
        sems = {nm: es.enter_context(nc.semaphore(f"s{i}")) for i, nm in enumerate(names)}
        with nc.Block() as block:
            P.emit(block, sems)
    return nc


def _consts():
    i = np.arange(128)
    ident = np.eye(128, dtype=np.float32)
    suf = (i[:, None] > i[None, :]).astype(np.float32)
    sub = (i[:, None] < i[None, :]).astype(np.float32)
    trif = (i[:, None] <= i[None, :]).astype(np.float32)
    trib = (i[:, None] >= i[None, :]).astype(np.float32)
    cst = np.concatenate([ident, suf, sub, trif, trib], axis=1).astype(np.float32)
    t = np.arange(TS)
    row = (t // 64).astype(np.float32)
    col = (t % 64).astype(np.float32)
    nf = 8
    inv = (10000.0 ** (-np.arange(nf, dtype=np.float32) / nf)).astype(np.float32)
    ang = np.stack([row[:, None] * inv, col[:, None] * inv], axis=1)
    cos = np.cos(ang).astype(np.float32)
    sin = np.sin(ang).astype(np.float32)
    rope = np.zeros((2, 128, TS), np.float32)
    for a in range(2):
        for half in range(2):
            for f in range(nf):
                r = 64 + a * 16 + half * 8 + f
                rope[0, r] = cos[:, a, f]
                rope[1, r] = sin[:, a, f]
    return cst, rope


_CACHE = {}


def kernel(x_prompt, x_sample, c, cache_ckv, cache_krope, state_ssd, c_ctx, w_ada, b_ada,
           g_mix, w_in, g_q, w_uq, g_kv, w_ukv, ssd_conv_w, ssd_conv_b, ssd_dt_bias,
           ssd_a_log, ssd_d, ssd_norm_g, cm_conv_w, cm_conv_b, cm_ln_g, cm_ln_b, w_out,
           g_ffn, w_gate, w_up, w_down, g_final, _n_layers=DEPTH, _do_sample=True, _do_prompt=True, _stop=None):
    f = lambda a: np.ascontiguousarray(np.asarray(a, dtype=np.float32))
    key = (_n_layers, _do_sample, _do_prompt, _stop)
    if key not in _CACHE:
        _CACHE[key] = build_program(_n_layers, _do_sample, _do_prompt, _stop)
    nc = _CACHE[key]
    cst, rope = _consts()
    shared = dict(
        w_ada=f(w_ada), b_ada=f(b_ada), g_mix=f(g_mix), w_in=f(w_in), g_q=f(g_q), w_uq=f(w_uq), g_kv=f(g_kv),
        w_ukv=f(w_ukv), ssd_conv_w=f(ssd_conv_w), ssd_conv_b=f(ssd_conv_b), ssd_dt_bias=f(ssd_dt_bias).reshape(-1),
        ssd_a_log=f(ssd_a_log).reshape(-1), ssd_d=f(ssd_d).reshape(-1), ssd_norm_g=f(ssd_norm_g), cm_conv_w=f(cm_conv_w),
        cm_conv_b=f(cm_conv_b), cm_ln_g=f(cm_ln_g), cm_ln_b=f(cm_ln_b), w_out=f(w_out), g_ffn=f(g_ffn),
        w_gate=f(w_gate), w_up=f(w_up), w_down=f(w_down), g_final=f(g_final), cst=cst, rope=rope)
    x_prompt, x_sample, c, c_ctx = f(x_prompt), f(x_sample), f(c), f(c_ctx)
    cache_ckv, cache_krope, state_ssd = f(cache_ckv), f(cache_krope), f(state_ssd)
    in_maps = []
    for i in range(8):
        b = i % 4
        m = dict(shared)
        m["x_s"] = x_sample[b]
        m["x_p"] = x_prompt[2 * i:2 * i + 2].reshape(NPS * TPS, D)
        m["cond"] = np.stack([c[b], c_ctx], axis=0)
        m["cache_ckv"] = cache_ckv[b]
        m["cache_krope"] = cache_krope[b]
        m["state_ssd"] = state_ssd[b]
        in_maps.append(m)
    res = run_bass_kernel_spmd(nc, in_maps, core_ids=list(range(8)))
    r = res.results
    y_sample = np.stack([r[b]["y_s"] for b in range(4)], axis=0).astype(np.float32)
    y_prompt = np.concatenate([r[i]["y_p"].reshape(NPS, TPS, D) for i in range(8)], axis=0).astype(np.float32)
    new_ckv = np.concatenate([r[i]["o_ckv"] for i in range(8)], axis=0).astype(np.float32)
    new_kr = np.concatenate([r[i]["o_kr"] for i in range(8)], axis=0).astype(np.float32)
    new_ssd = np.concatenate([r[i]["o_ssd"] for i in range(8)], axis=0).astype(np.float32)
    return (y_prompt, y_sample, new_ckv, new_kr, new_ssd)
```

```python
import math
from contextlib import ExitStack
import numpy as np
import concourse.bass as bass
import concourse.mybir as mybir
from concourse.bass_utils import run_bass_kernel_spmd

F32 = mybir.dt.float32
BF16 = mybir.dt.bfloat16
U8 = mybir.dt.uint8
AF = mybir.ActivationFunctionType
ALU = mybir.AluOpType

ENGS = ("pe", "act", "dve", "pool", "sp")

D = 1024
DEPTH = 4
TS = 2048
TPS = 256
NPS = 2
PAST = 256
DFF = 2816
NFG = DFF // 256
IN_W = 1704
SCALE = 96 ** -0.5
NPRM = 161


class View:
    __slots__ = ("name", "ap")

    def __init__(self, name, ap):
        self.name, self.ap = name, ap


class Prog:
    def __init__(self, nc):
        self.nc = nc
        self.ops = []
        self.regions = {}
        self.overlaps = {}

    def region(self, name, space, p0, p1, b0, b1):
        assert name not in self.regions, name
        self.regions[name] = (space, p0, p1, b0, b1)
        ov = [name]
        for n, (s, q0, q1, c0, c1) in self.regions.items():
            if n == name:
                continue
            if s == space and q0 < p1 and p0 < q1 and c0 < b1 and b0 < c1:
                ov.append(n)
                self.overlaps[n].append(name)
        self.overlaps[name] = ov

    def op(self, eng, fn, reads=(), writes=(), dma=False, chan=None, fresh=()):
        rs = [r if isinstance(r, str) else r.name for r in reads]
        ws = [w if isinstance(w, str) else w.name for w in writes]
        fr = [w if isinstance(w, str) else w.name for w in fresh]
        if dma and chan is None:
            chan = ws[0] if ws else rs[0]
        self.ops.append((eng, fn, rs, ws, dma, chan, fr))

    def finalize(self):
        ops = self.ops
        n = len(ops)
        last_w = {}
        readers = {}
        deps = [None] * n
        for i, (eng, fn, rs, ws, dma, chan, fr) in enumerate(ops):
            for f in fr:
                lw = last_w.get(f)
                assert lw is None or len(readers.get(f, ())) > 0, \
                    f"PSUM collision on {f} at op {i} (prev writer {lw} unread)"
            d = set()
            for r in rs:
                for rr in self.overlaps[r]:
                    w = last_w.get(rr)
                    if w is not None:
                        d.add(w)
                    if self.regions[rr][0] == "psum":
                        for x in readers.get(rr, {}).values():
                            d.add(x)
            for w_ in ws:
                for ww in self.overlaps[w_]:
                    w = last_w.get(ww)
                    if w is not None:
                        d.add(w)
                    for x in readers.get(ww, {}).values():
                        d.add(x)
            d.discard(i)
            dd = []
            mine = set(rs) | set(ws)
            for j in d:
                ej, _, rsj, wsj, dmaj, _, _ = ops[j]
                if not dmaj and not dma and ej == eng:
                    if eng == "pe":
                        continue
                    hit = False
                    for x in wsj:
                        for y in self.overlaps[x]:
                            if y in mine:
                                hit = True
                                break
                        if hit:
                            break
                    if not hit:
                        continue
                dd.append(j)
            deps[i] = dd
            key = ("c:" + chan) if dma else eng
            for r in rs:
                readers.setdefault(r, {})[key] = i
            for w_ in ws:
                last_w[w_] = i
                readers[w_] = {}
        needed = set()
        for dd in deps:
            needed.update(dd)
        eng_cnt = {e: 0 for e in ENGS}
        chan_cnt = {}
        ticket = {}
        for i, (eng, fn, rs, ws, dma, chan, fr) in enumerate(ops):
            if dma:
                chan_cnt[chan] = chan_cnt.get(chan, 0) + 16
                ticket[i] = ("c:" + chan, chan_cnt[chan])
            elif i in needed:
                eng_cnt[eng] += 1
                ticket[i] = ("e:" + eng, eng_cnt[eng])
        self.sem_names = ["e:" + e for e in ENGS] + ["c:" + c for c in chan_cnt]
        self.final_counts = {("e:" + e): eng_cnt[e] for e in ENGS}
        self.final_counts.update({("c:" + c): v for c, v in chan_cnt.items()})
        self.deps, self.ticket = deps, ticket
        return self.sem_names

    def emit(self, block, sems):
        ops, deps, ticket = self.ops, self.deps, self.ticket
        per_eng = {e: [] for e in ENGS}
        for i, o in enumerate(ops):
            per_eng[o[0]].append(i)
        final_counts = self.final_counts

        def make(engname):
            def body(e):
                waited = {}
                for i in per_eng[engname]:
                    _, fn, rs, ws, dma, chan, _ = ops[i]
                    need = {}
                    for j in deps[i]:
                        s, v = ticket[j]
                        if need.get(s, 0) < v:
                            need[s] = v
                    for s, v in need.items():
                        if waited.get(s, 0) < v:
                            e.wait_ge(sems[s], v)
                            waited[s] = v
                    ins = fn(e)
                    if i in ticket:
                        s, v = ticket[i]
                        ins.then_inc(sems[s], 16 if dma else 1)
                if engname == "sp":
                    for s, v in final_counts.items():
                        if v > 0 and waited.get(s, 0) < v:
                            e.wait_ge(sems[s], v)
            return body

        block.tensor(make("pe"))
        block.scalar(make("act"))
        block.vector(make("dve"))
        block.gpsimd(make("pool"))
        block.sync(make("sp"))


class Bump:
    def __init__(self, ranges):
        self.ranges = [list(r) for r in ranges]

    def take(self, nb):
        nb = (nb + 31) // 32 * 32
        for r in self.ranges:
            if r[1] - r[0] >= nb:
                b0 = r[0]
                r[0] += nb
                return b0
        raise MemoryError(f"bump pool exhausted need {nb} have {self.ranges}")


def esize(dt):
    return 4 if dt == F32 else 2


def build_program(n_layers=DEPTH, do_sample=True, do_prompt=True, stop=None):
    nc = bass.Bass("TRN2", target_bir_lowering=False)
    P = Prog(nc)

    def din(name, shape):
        return nc.dram_tensor(name, list(shape), F32, kind="ExternalInput").ap()

    def dout(name, shape):
        return nc.dram_tensor(name, list(shape), F32, kind="ExternalOutput").ap()

    d_xs = din("x_s", [TS, D])
    d_xp = din("x_p", [NPS * TPS, D])
    d_cond = din("cond", [2, D])
    d_cckv = din("cache_ckv", [DEPTH, PAST, 128])
    d_ckr = din("cache_krope", [DEPTH, PAST, 32])
    d_st = din("state_ssd", [DEPTH, 2, 4, 64, 64])
    d_wada = din("w_ada", [DEPTH, D, 6 * D])
    d_bada = din("b_ada", [DEPTH, 6 * D])
    d_gmix = din("g_mix", [DEPTH, D])
    d_win = din("w_in", [DEPTH, D, IN_W])
    d_gq = din("g_q", [DEPTH, 256])
    d_wuq = din("w_uq", [DEPTH, 256, 768])
    d_gkv = din("g_kv", [DEPTH, 128])
    d_wukv = din("w_ukv", [DEPTH, 128, 1024])
    d_scw = din("ssd_conv_w", [DEPTH, 5, 512])
    d_scb = din("ssd_conv_b", [DEPTH, 512])
    d_dtb = din("ssd_dt_bias", [DEPTH * 8])
    d_alog = din("ssd_a_log", [DEPTH * 8])
    d_sd = din("ssd_d", [DEPTH * 8])
    d_sng = din("ssd_norm_g", [DEPTH, 256])
    d_ccw = din("cm_conv_w", [DEPTH, 31, 256])
    d_ccb = din("cm_conv_b", [DEPTH, 256])
    d_clg = din("cm_ln_g", [DEPTH, 256])
    d_clb = din("cm_ln_b", [DEPTH, 256])
    d_wout = din("w_out", [DEPTH, D, D])
    d_gffn = din("g_ffn", [DEPTH, D])
    d_wg = din("w_gate", [DEPTH, D, DFF])
    d_wu = din("w_up", [DEPTH, D, DFF])
    d_wd = din("w_down", [DEPTH, DFF, D])
    d_gfin = din("g_final", [D])
    d_cst = din("cst", [128, 640])
    d_rope = din("rope", [2, 128, TS])

    o_ys = dout("y_s", [TS, D])
    o_yp = dout("y_p", [NPS * TPS, D])
    o_ckv = dout("o_ckv", [NPS, DEPTH, TPS, 128])
    o_kr = dout("o_kr", [NPS, DEPTH, TPS, 32])
    o_ssd = dout("o_ssd", [NPS, DEPTH, 2, 4, 64, 64])

    es = ExitStack()
    with es:
        SBYTES = 212000
        S = es.enter_context(nc.sbuf_tensor("S", [128, SBYTES], U8))
        banks = [es.enter_context(nc.psum_tensor(f"PS{i}", [128, 512], F32)) for i in range(8)]
        for i in range(8):
            P.region(f"ps{i}", "psum", 0, 128, i * 2048, (i + 1) * 2048)
        PS = [View(f"ps{i}", banks[i][:, :]) for i in range(8)]
        PSB = [View(f"ps{i}", banks[i][:, :].bitcast(BF16)) for i in range(8)]

        uid = [0]

        def alloc(pool, name, shape, dt, p0=0):
            nel = int(np.prod(shape[1:]))
            nb = nel * esize(dt)
            b0 = pool.take(nb)
            ap = S[p0:p0 + shape[0], b0:b0 + nb].bitcast(dt)
            if len(shape) == 3:
                ap = ap.rearrange("p (a b) -> p a b", a=shape[1])
            elif len(shape) == 4:
                ap = ap.rearrange("p (a b c) -> p a b c", a=shape[1], b=shape[2])
            uid[0] += 1
            nm = f"{name}.{uid[0]}"
            P.region(nm, "sbuf", p0, p0 + shape[0], b0, b0 + ((nb + 31) // 32 * 32))
            return View(nm, ap)

        pers = Bump([(0, SBYTES)])
        xT = alloc(pers, "xT", [128, 8, TS], F32)
        RING_N = 4
        ring = [alloc(pers, f"ring{i}", [128, 4096], BF16) for i in range(RING_N)]
        wuq = alloc(pers, "wuq", [128, 2, 1024], BF16)
        wukv = alloc(pers, "wukv", [128, 1024], BF16)
        cstf = alloc(pers, "cstf", [128, 640], F32)
        identf = View(cstf.name, cstf.ap[:, 0:128])
        SU = [View(cstf.name, cstf.ap[:, 128:256]), View(cstf.name, cstf.ap[:, 256:384])]
        TRI = [View(cstf.name, cstf.ap[:, 384:512]), View(cstf.name, cstf.ap[:, 512:640])]
        identb = alloc(pers, "identb", [128, 128], BF16)
        onesb = alloc(pers, "onesb", [128, 128], BF16)
        onesf = alloc(pers, "onesf", [128, 128], F32)
        ropeC = alloc(pers, "ropeC", [128, TS], BF16)
        ropeS = alloc(pers, "ropeS", [128, TS], BF16)
        prm = alloc(pers, "prm", [128, DEPTH, NPRM], F32)
        cnd = alloc(pers, "cnd", [128, 24], F32)
        scond = alloc(pers, "scond", [128, 8, 2], BF16)
        modv = alloc(pers, "modv", [128, DEPTH, 48, 2], F32)
        gsc = alloc(pers, "gsc", [128, 2, 8], F32)
        dtb_b = alloc(pers, "dtb_b", [128, 32], F32)
        alog_b = alloc(pers, "alog_b", [128, 32], F32)
        dd_b = alloc(pers, "dd_b", [128, 32], F32)
        a_b = alloc(pers, "a_b", [128, 32], F32)
        dsum_b = alloc(pers, "dsum_b", [128, DEPTH, 4], F32)
        gkv_b = alloc(pers, "gkv_b", [128, 128], F32)
        xg = alloc(pers, "xg", [128, 8, 512], BF16)
        pers_end = pers.ranges[0][0]
        io = Bump([(pers_end, SBYTES)])
        gpad = alloc(io, "gpad", [128, 2, TS + 32], BF16)
        xbcp = alloc(io, "xbcp", [128, 4, TS + 4], BF16)
        szb = alloc(io, "szb", [128, 16, 256], BF16)
        qlat = alloc(io, "qlat", [128, 2, TS], BF16)
        ckvnT = alloc(io, "ckvnT", [128, PAST + TS], BF16)
        krT = alloc(io, "krT", [128, PAST + TS], BF16)
        dtr = alloc(io, "dtr", [128, 16, 8], F32)
        io_end = io.ranges[0][0]
        hT = alloc(Bump([(pers_end, SBYTES)]), "hT", [128, 8, TS], BF16)
        R = P.regions
        rg = lambda v: (R[v.name][3], R[v.name][4])
        SC0 = io_end
        print("mem: pers_end", pers_end, "io_end", io_end, "scratch", SBYTES - io_end)

        def dma(eng, out, in_, reads=(), writes=(), **kw):
            P.op(eng, lambda e: e.dma_start(out=out, in_=in_, **kw), reads=reads, writes=writes, dma=True)

        def mm(out, lhsT, rhs, start, stop, reads, w, fresh=False):
            P.op("pe", lambda e: e.matmul(out, lhsT=lhsT, rhs=rhs, start=start, stop=stop),
                 reads=reads, writes=[w], fresh=[w] if fresh else ())

        def tp(out, in_, ident, reads, w, fresh=False):
            P.op("pe", lambda e: e.transpose(out, in_, ident), reads=reads, writes=[w],
                 fresh=[w] if fresh else ())

        def act(out, in_, func, reads, writes, **kw):
            P.op("act", lambda e: e.activation(out=out, in_=in_, func=func, **kw), reads=reads, writes=writes)

        def tt(out, in0, in1, op, reads, writes, eng="dve"):
            P.op(eng, lambda e: e.tensor_tensor(out=out, in0=in0, in1=in1, op=op), reads=reads, writes=writes)

        def stt(out, in0, scalar, in1, op0, op1, reads, writes):
            P.op("dve", lambda e: e.scalar_tensor_tensor(out=out, in0=in0, scalar=scalar, in1=in1, op0=op0, op1=op1),
                 reads=reads, writes=writes)

        def ts(out, in0, s1, s2, op0, op1, reads, writes, eng="dve"):
            if op1 is None:
                P.op(eng, lambda e: e.tensor_scalar(out=out, in0=in0, scalar1=s1, scalar2=None, op0=op0),
                     reads=reads, writes=writes)
            else:
                P.op(eng, lambda e: e.tensor_scalar(out=out, in0=in0, scalar1=s1, scalar2=s2, op0=op0, op1=op1),
                     reads=reads, writes=writes)

        def cp(out, in_, reads, writes, eng="dve"):
            P.op(eng, lambda e: e.tensor_copy(out, in_), reads=reads, writes=writes)

        def memset(ap, val, writes, eng="dve"):
            P.op(eng, lambda e: e.memset(ap, val), writes=writes)

        def recip(out, in_, reads, writes):
            P.op("dve", lambda e: e.reciprocal(out=out, in_=in_), reads=reads, writes=writes)

        class Rot:
            def __init__(self, items):
                self.items, self.i = items, 0

            def next(self):
                v = self.items[self.i % len(self.items)]
                self.i += 1
                return v

        psA = Rot([0, 1, 2, 3])
        psBk = Rot([4, 5])

        dma("sp", cstf.ap, d_cst, writes=[cstf])
        dma("pool", ropeC.ap, d_rope[0], writes=[ropeC])
        dma("pool", ropeS.ap, d_rope[1], writes=[ropeS])
        memset(onesb.ap, 1.0, [onesb])
        memset(onesf.ap, 1.0, [onesf])
        cp(identb.ap, identf.ap, [cstf], [identb])
        dma("sp", dtb_b.ap, d_dtb.partition_broadcast(128), writes=[dtb_b])
        dma("sp", alog_b.ap, d_alog.partition_broadcast(128), writes=[alog_b])
        dma("sp", dd_b.ap, d_sd.partition_broadcast(128), writes=[dd_b])
        act(a_b.ap, alog_b.ap, AF.Exp, [alog_b], [a_b])
        ts(a_b.ap, a_b.ap, -1.0, None, ALU.mult, None, [a_b], [a_b])
        ddv = dd_b.ap.rearrange("p (l d h) -> p l d h", l=DEPTH, d=2)
        tt(dsum_b.ap, ddv[:, :, 0, :], ddv[:, :, 1, :], ALU.add, [dd_b], [dsum_b])

        prol = Bump([(SC0, SBYTES)])
        stg = [alloc(prol, f"stg{i}", [128, 128], F32) for i in range(2)]
        O_BADA, O_GMIX, O_GFFN, O_GQ, O_GKV, O_SCW, O_SCB, O_SNG, O_CCW, O_CCB, O_CLG, O_CLB = \
            0, 48, 56, 64, 66, 67, 87, 91, 93, 155, 157, 159
        for l in range(DEPTH):
            rows = [
                (d_bada[l].rearrange("(r c) -> r c", c=128), 48),
                (d_gmix[l].rearrange("(r c) -> r c", c=128), 8),
                (d_gffn[l].rearrange("(r c) -> r c", c=128), 8),
                (d_gq[l].rearrange("(r c) -> r c", c=128), 2),
                (d_gkv[l].rearrange("(r c) -> r c", c=128), 1),
                (d_scw[l].rearrange("j (r c) -> (j r) c", c=128), 20),
                (d_scb[l].rearrange("(r c) -> r c", c=128), 4),
                (d_sng[l].rearrange("(r c) -> r c", c=128), 2),
                (d_ccw[l].rearrange("j (r c) -> (j r) c", c=128), 62),
                (d_ccb[l].rearrange("(r c) -> r c", c=128), 2),
                (d_clg[l].rearrange("(r c) -> r c", c=128), 2),
                (d_clb[l].rearrange("(r c) -> r c", c=128), 2),
            ]
            r0 = 0
            for src, nr in rows:
                done = 0
                while done < nr:
                    si = (r0 + done) // 128
                    off = (r0 + done) % 128
                    k = min(nr - done, 128 - off)
                    dma("sp", stg[si].ap[off:off + k, :], src[done:done + k, :], writes=[stg[si]])
                    done += k
                r0 += nr
            assert r0 == NPRM
            for si, (c0, ncol) in enumerate([(0, 128), (128, NPRM - 128)]):
                pb = psA.next()
                tp(PS[pb].ap[:, 0:ncol], stg[si].ap[0:ncol, :], identf.ap[0:ncol, 0:ncol], [stg[si], cstf], PS[pb], fresh=True)
                cp(prm.ap[:, l, c0:c0 + ncol], PS[pb].ap[:, 0:ncol], [PS[pb]], [prm])
        dma("sp", stg[0].ap[0:16, :], d_cond.rearrange("a (r c) -> (a r) c", c=128), writes=[stg[0]])
        dma("sp", stg[0].ap[16:24, :], d_gfin.rearrange("(r c) -> r c", c=128), writes=[stg[0]])
        pb = psA.next()
        tp(PS[pb].ap[:, 0:24], stg[0].ap[0:24, :], identf.ap[0:24, 0:24], [stg[0], cstf], PS[pb], fresh=True)
        cp(cnd.ap, PS[pb].ap[:, 0:24], [PS[pb]], [cnd])
        act(scond.ap.rearrange("p k c -> p c k"), cnd.ap[:, 0:16].rearrange("p (c k) -> p c k", c=2), AF.Silu, [cnd], [scond])

        ring_i = [0]

        def load_piece(srcs):
            v = ring[ring_i[0] % RING_N]
            ring_i[0] += 1
            for dst_fn, src in srcs:
                dma("pool", dst_fn(v.ap), src, writes=[v])
            return v

        def r3(ap, a):
            return ap.rearrange("p (a b) -> p a b", a=a)

        for l in range(n_layers):
            for pc in range(12):
                v = load_piece([(lambda a: r3(a, 8), d_wada[l, :, pc * 512:(pc + 1) * 512].rearrange("(k p) c -> p k c", p=128))])
                w3 = r3(v.ap, 8)
                for q in range(4):
                    oc = pc * 4 + q
                    for kc in range(8):
                        mm(PS[7].ap[:, oc * 2:oc * 2 + 2], w3[:, kc, q * 128:(q + 1) * 128], scond.ap[:, kc, :],
                           kc == 0, kc == 7, [v, scond], PS[7], fresh=(oc == 0 and kc == 0))
            tt(modv.ap[:, l], PS[7].ap[:, 0:96].rearrange("p (o c) -> p o c", c=2),
               prm.ap[:, l, O_BADA:O_BADA + 48].unsqueeze(2).to_broadcast([128, 48, 2]), ALU.add, [PS[7], prm], [modv])

        class Cfg:
            pass

        def make_cfg(kind):
            c = Cfg()
            c.kind = kind
            if kind == "S":
                c.T, c.TT, c.seqs, c.rope, c.ctx, c.cond = TS, 512, [(0, TS)], True, True, 0
                c.dx, c.oy = d_xs, o_ys
            else:
                c.T, c.TT, c.seqs, c.rope, c.ctx, c.cond = NPS * TPS, 256, [(0, TPS), (TPS, TPS)], False, False, 1
                c.dx, c.oy = d_xp, o_yp
            c.tiles = []
            for si, (s0, sl) in enumerate(c.seqs):
                for t in range(s0, s0 + sl, c.TT):
                    c.tiles.append((si, t))
            c.nblk = c.T // 128
            return c

        def load_x(c):
            pool = Bump([(SC0, SBYTES)])
            xs_ = [alloc(pool, f"xstg{i}", [128, D], F32) for i in range(2)]
            for b in range(c.nblk):
                st = xs_[b % 2]
                dma("sp", st.ap, c.dx[b * 128:(b + 1) * 128, :], writes=[st])
                for half in range(2):
                    pb = psA.next()
                    for q in range(4):
                        kc = half * 4 + q
                        tp(PS[pb].ap[:, q * 128:(q + 1) * 128], st.ap[:, kc * 128:(kc + 1) * 128], identf.ap,
                           [st, cstf], PS[pb], fresh=(q == 0))
                    cp(xT.ap[:, half * 4:half * 4 + 4, b * 128:(b + 1) * 128],
                       PS[pb].ap.rearrange("p (q t) -> p q t", q=4), [PS[pb]], [xT])

        def norm_tile(c, pool_views, t0, n, gsc_ap, sh_ap, out_fn, out_v, dt_out_bf=True):
            sq, sd, rstd, tmpb = pool_views
            for kc in range(8):
                q = sq[kc % 2]
                act(q.ap[:, 0:n], xT.ap[:, kc, t0:t0 + n], AF.Square, [xT], [q])
                mm(PS[6].ap[:, 0:n], onesb.ap, q.ap[:, 0:n], kc == 0, kc == 7, [onesb, q], PS[6], fresh=(kc == 0))
            act(sd.ap[:, 0:n], PS[6].ap[:, 0:n], AF.Ln, [PS[6]], [sd], scale=1.0 / D, bias=1e-6)
            act(rstd.ap[:, 0:n], sd.ap[:, 0:n], AF.Exp, [sd], [rstd], scale=-0.5)
            for kc in range(8):
                tb = tmpb[kc % 2]
                tt(tb.ap[:, 0:n], xT.ap[:, kc, t0:t0 + n], rstd.ap[:, 0:n], ALU.mult, [xT, rstd], [tb])
                if sh_ap is None:
                    ts(out_fn(kc), tb.ap[:, 0:n], gsc_ap[:, kc:kc + 1], None, ALU.mult, None, [tb, cnd], [out_v])
                else:
                    act(out_fn(kc), tb.ap[:, 0:n], AF.Identity, [tb, gsc, modv], [out_v],
                        scale=gsc_ap[:, kc:kc + 1], bias=sh_ap[:, kc:kc + 1])

        def xupdate(c, l, t0, n, lhs_fn, rhs_list, g_off, reads, evac=None):
            for oc in range(8):
                pb = psA.next()
                nj = len(rhs_list)
                for j in range(nj):
                    mm(PS[pb].ap[:, 0:n], lhs_fn(j, oc), rhs_list[j], j == 0, j == nj - 1, reads, PS[pb], fresh=(j == 0))
                gcol = modv.ap[:, l, g_off + oc, c.cond:c.cond + 1]
                if evac is None:
                    stt(xT.ap[:, oc, t0:t0 + n], PS[pb].ap[:, 0:n], gcol,
                        xT.ap[:, oc, t0:t0 + n], ALU.mult, ALU.add, [PS[pb], modv, xT], [xT])
                else:
                    ev = evac[oc % len(evac)]
                    act(ev.ap[:, 0:n], PS[pb].ap[:, 0:n], AF.Copy, [PS[pb], modv], [ev], scale=gcol)
                    tt(xT.ap[:, oc, t0:t0 + n], xT.ap[:, oc, t0:t0 + n], ev.ap[:, 0:n], ALU.add, [xT, ev], [xT], eng="pool")

        def layer_pass(c, l):
            T, TT = c.T, c.TT
            cd = c.cond
            for i, (og, osc) in enumerate([(O_GMIX, 8), (O_GFFN, 32)]):
                stt(gsc.ap[:, i, :], modv.ap[:, l, osc:osc + 8, cd], 1.0, prm.ap[:, l, og:og + 8], ALU.add, ALU.mult,
                    [modv, prm], [gsc])
            sh1 = modv.ap[:, l, 0:8, cd]
            sh2 = modv.ap[:, l, 24:32, cd]
            dma("sp", gkv_b.ap, d_gkv[l].partition_broadcast(128), writes=[gkv_b])
            dma("pool", wuq.ap[:, :, 0:768], d_wuq[l].rearrange("(k p) c -> p k c", p=128), writes=[wuq])
            dma("pool", wukv.ap, d_wukv[l], writes=[wukv])
            for kc in range(2):
                ts(wuq.ap[:, kc, 0:768], wuq.ap[:, kc, 0:768], prm.ap[:, l, O_GQ + kc:O_GQ + kc + 1], None, ALU.mult, None,
                   [wuq, prm], [wuq])
                src = wuq.ap[:, kc, 0:768].rearrange("p (h c) -> p h c", h=8)[:, :, 64:96].rearrange("p h (a f) -> p h a f", a=2)
                dst = wuq.ap[:, kc, 768:1024].rearrange("p (h a f) -> p h a f", h=8, a=2)
                for a in range(2):
                    ts(dst[:, :, a, 0:8], src[:, :, a, 8:16], -1.0, None, ALU.mult, None, [wuq], [wuq])
                    cp(dst[:, :, a, 8:16], src[:, :, a, 0:8], [wuq], [wuq])

            wl = d_win[l]

            def wsl(c0, c1):
                return wl[:, c0:c1].rearrange("(k p) c -> p k c", p=128)

            v_cm = load_piece([(lambda a: r3(a, 8), wsl(1192, 1704))])
            v_ssd = load_piece([(lambda a: r3(a, 8), wsl(672, 1184))])
            v_misc = load_piece([(lambda a: r3(a, 8)[:, :, 0:256], wsl(416, 672)),
                                 (lambda a: r3(a, 8)[:, :, 256:416], wsl(256, 416)),
                                 (lambda a: r3(a, 8)[:, :, 448:456], wsl(1184, 1192))])
            v_q = load_piece([(lambda a: r3(a, 8)[:, :, 0:256], wsl(0, 256))])
            w_cm, w_ssd, w_misc, w_q = r3(v_cm.ap, 8), r3(v_ssd.ap, 8), r3(v_misc.ap, 8), r3(v_q.ap, 8)
            for kc in range(8):
                src = w_misc[:, kc, 384:416].rearrange("p (a f) -> p a f", a=2)
                dst = w_misc[:, kc, 416:448].rearrange("p (a f) -> p a f", a=2)
                ts(dst[:, :, 0:8], src[:, :, 8:16], -1.0, None, ALU.mult, None, [v_misc], [v_misc])
                cp(dst[:, :, 8:16], src[:, :, 0:8], [v_misc], [v_misc])

            sp = Bump([(SC0, SBYTES)])
            sq = [alloc(sp, f"sq{i}", [128, 512], BF16) for i in range(2)]
            sd = alloc(sp, "sd", [128, 512], F32)
            rstd = alloc(sp, "rstd", [128, 512], F32)
            tmpb = [alloc(sp, f"tmpb{i}", [128, 512], F32) for i in range(2)]
            sig = alloc(sp, "sig", [128, 512], F32)
            sqk = alloc(sp, "sqk", [128, 512], BF16)
            t1 = alloc(sp, "t1", [128, 512], F32)
            t2 = alloc(sp, "t2", [128, 512], F32)
            rq = alloc(sp, "rq", [128, 512], F32)
            tmo = alloc(sp, "tmo", [128, 160], F32)
            tmo2 = alloc(sp, "tmo2", [128, 128], F32)
            ssq = alloc(sp, "ssq", [128, 2], F32)
            junk = alloc(sp, "junk", [128, 256], F32)
            koff = PAST if c.ctx else 0

            for si, (s0, sl) in enumerate(c.seqs):
                g0 = s0 + si * 32
                memset(gpad.ap[:, :, g0:g0 + 16], 0.0, [gpad])
                memset(gpad.ap[:, :, g0 + 16 + sl:g0 + 32 + sl], 0.0, [gpad])
                x0 = s0 + si * 4
                memset(xbcp.ap[:, :, x0:x0 + 2], 0.0, [xbcp])
                memset(xbcp.ap[:, :, x0 + 2 + sl:x0 + 4 + sl], 0.0, [xbcp])

            if c.ctx:
                cst_ = alloc(sp, "cstg", [128, 128], F32)
                kst_ = alloc(sp, "kstg", [128, 96], F32)
                memset(kst_.ap, 0.0, [kst_])
                for b in range(2):
                    dma("sp", cst_.ap, d_cckv[l, b * 128:(b + 1) * 128, :], writes=[cst_])
                    pb = psA.next()
                    tp(PS[pb].ap[:, 0:128], cst_.ap, identf.ap, [cst_, cstf], PS[pb], fresh=True)
                    cp(ckvnT.ap[:, b * 128:(b + 1) * 128], PS[pb].ap[:, 0:128], [PS[pb]], [ckvnT])
                    dma("sp", kst_.ap[:, 64:96], d_ckr[l, b * 128:(b + 1) * 128, :], writes=[kst_])
                    pb = psA.next()
                    tp(PS[pb].ap[0:96, 0:128], kst_.ap, identf.ap, [kst_, cstf], PS[pb], fresh=True)
                    cp(krT.ap[64:96, b * 128:(b + 1) * 128], PS[pb].ap[64:96, 0:128], [PS[pb]], [krT])

            def fm(wap, c0, m, n, reads, out_rows=None):
                pb = psA.next()
                o = PS[pb].ap[0:m, 0:n] if out_rows is None else PS[pb].ap[out_rows[0]:out_rows[1], 0:n]
                for kc in range(8):
                    mm(o, wap[:, kc, c0:c0 + m], xg.ap[:, kc, 0:n], kc == 0, kc == 7, reads + [xg], PS[pb], fresh=(kc == 0))
                return pb

            for (si, t0) in c.tiles:
                n = TT
                s0, sl = c.seqs[si]
                norm_tile(c, (sq, sd, rstd, tmpb), t0, n, gsc.ap[:, 0, :], sh1, lambda kc: xg.ap[:, kc, 0:n], xg)
                gofs = si * 32 + 16 + t0
                for j in range(2):
                    pa = fm(w_cm, j * 128, 128, n, [v_cm])
                    pbk = fm(w_cm, 256 + j * 128, 128, n, [v_cm])
                    act(sig.ap[:, 0:n], PS[pbk].ap[:, 0:n], AF.Sigmoid, [PS[pbk]], [sig])
                    tt(gpad.ap[:, j, gofs:gofs + n], PS[pa].ap[:, 0:n], sig.ap[:, 0:n], ALU.mult, [PS[pa], sig], [gpad])
                xofs = si * 4 + 2 + t0
                for j in range(4):
                    pa = fm(w_ssd, j * 128, 128, n, [v_ssd])
                    act(xbcp.ap[:, j, xofs:xofs + n], PS[pa].ap[:, 0:n], AF.Copy, [PS[pa]], [xbcp])
                for b in range(n // 128):
                    blk = (t0 + b * 128) // 128
                    pb = psA.next()
                    for kc in range(8):
                        mm(PS[pb].ap[:, 0:256], xg.ap[:, kc, b * 128:(b + 1) * 128], w_misc[:, kc, 0:256], kc == 0, kc == 7,
                           [xg, v_misc], PS[pb], fresh=(kc == 0))
                    act(szb.ap[:, blk, :], PS[pb].ap[:, 0:256], AF.Silu, [PS[pb]], [szb])
                    pb = psA.next()
                    for kc in range(8):
                        mm(PS[pb].ap[:, 0:8], xg.ap[:, kc, b * 128:(b + 1) * 128], w_misc[:, kc, 448:456], kc == 0, kc == 7,
                           [xg, v_misc], PS[pb], fresh=(kc == 0))
                    tt(dtr.ap[:, blk, :], PS[pb].ap[:, 0:8], dtb_b.ap[:, l * 8:(l + 1) * 8], ALU.add, [PS[pb], dtb_b], [dtr])
                    if c.kind == "P":
                        pb = psA.next()
                        for kc in range(8):
                            mm(PS[pb].ap[:, 0:160], xg.ap[:, kc, b * 128:(b + 1) * 128], w_misc[:, kc, 256:416], kc == 0, kc == 7,
                               [xg, v_misc], PS[pb], fresh=(kc == 0))
                        cp(tmo.ap, PS[pb].ap[:, 0:160], [PS[pb]], [tmo])
                        tloc = t0 - s0 + b * 128
                        dma("sp", o_kr[si, l, tloc:tloc + 128, :], tmo.ap[:, 128:160], reads=[tmo])
                        memset(ssq.ap[:, 0:1], 0.0, [ssq])
                        act(junk.ap[:, 0:128], tmo.ap[:, 0:128], AF.Square, [tmo, ssq], [junk, ssq], accum_out=ssq.ap[:, 0:1])
                        act(ssq.ap[:, 1:2], ssq.ap[:, 0:1], AF.Ln, [ssq], [ssq], scale=1.0 / 128, bias=1e-6)
                        act(ssq.ap[:, 1:2], ssq.ap[:, 1:2], AF.Exp, [ssq], [ssq], scale=-0.5)
                        stt(tmo2.ap, tmo.ap[:, 0:128], ssq.ap[:, 1:2], gkv_b.ap, ALU.mult, ALU.mult, [tmo, ssq, gkv_b], [tmo2])
                        dma("sp", o_ckv[si, l, tloc:tloc + 128, :], tmo2.ap, reads=[tmo2])
                pa = fm(w_misc, 256, 128, n, [v_misc])
                act(sqk.ap[:, 0:n], PS[pa].ap[:, 0:n], AF.Square, [PS[pa]], [sqk])
                mm(PS[6].ap[:, 0:n], onesb.ap, sqk.ap[:, 0:n], True, True, [onesb, sqk], PS[6], fresh=True)
                act(sd.ap[:, 0:n], PS[6].ap[:, 0:n], AF.Ln, [PS[6]], [sd], scale=1.0 / 128, bias=1e-6)
                act(rstd.ap[:, 0:n], sd.ap[:, 0:n], AF.Exp, [sd], [rstd], scale=-0.5)
                kofs = koff + t0 if c.ctx else t0
                stt(ckvnT.ap[:, kofs:kofs + n], PS[pa].ap[:, 0:n], prm.ap[:, l, O_GKV:O_GKV + 1], rstd.ap[:, 0:n],
                    ALU.mult, ALU.mult, [PS[pa], prm, rstd], [ckvnT])
                pa = fm(w_misc, 384, 32, n, [v_misc], out_rows=(64, 96))
                if c.rope:
                    pbk = fm(w_misc, 416, 32, n, [v_misc], out_rows=(64, 96))
                    tt(t1.ap[64:96, 0:n], PS[pa].ap[64:96, 0:n], ropeC.ap[64:96, t0:t0 + n], ALU.mult, [PS[pa], ropeC], [t1])
                    tt(t2.ap[64:96, 0:n], PS[pbk].ap[64:96, 0:n], ropeS.ap[64:96, t0:t0 + n], ALU.mult, [PS[pbk], ropeS], [t2])
                    tt(krT.ap[64:96, kofs:kofs + n], t1.ap[64:96, 0:n], t2.ap[64:96, 0:n], ALU.add, [t1, t2], [krT])
                else:
                    act(krT.ap[64:96, kofs:kofs + n], PS[pa].ap[64:96, 0:n], AF.Copy, [PS[pa]], [krT])
                pq = [fm(w_q, j * 128, 128, n, [v_q]) for j in range(2)]
                for j in range(2):
                    act(sq[j].ap[:, 0:n], PS[pq[j]].ap[:, 0:n], AF.Square, [PS[pq[j]]], [sq[j]])
                    mm(PS[6].ap[:, 0:n], onesb.ap, sq[j].ap[:, 0:n], j == 0, j == 1, [onesb, sq[j]], PS[6], fresh=(j == 0))
                act(sd.ap[:, 0:n], PS[6].ap[:, 0:n], AF.Ln, [PS[6]], [sd], scale=1.0 / 256, bias=1e-6)
                act(rq.ap[:, 0:n], sd.ap[:, 0:n], AF.Exp, [sd], [rq], scale=-0.5)
                for j in range(2):
                    tt(qlat.ap[:, j, t0:t0 + n], PS[pq[j]].ap[:, 0:n], rq.ap[:, 0:n], ALU.mult, [PS[pq[j]], rq], [qlat])

            if stop == "I":
                return
            v_wor = load_piece([(lambda a: r3(a, 4), d_wout[l, 512:1024, :].rearrange("(k p) c -> p k c", p=128))])
            w_or = r3(v_wor.ap, 4)
            for j in range(2):
                ts(w_or[:, j, :], w_or[:, j, :], prm.ap[:, l, O_SNG + j:O_SNG + j + 1], None, ALU.mult, None, [v_wor, prm], [v_wor])

            sp = Bump([(SC0, SBYTES), rg(xg)])
            dg = alloc(sp, "dg", [128, 2, 31, 128], BF16)
            cvf = [alloc(sp, f"cvf{j}", [128, 512], F32) for j in range(2)]
            sqf = [alloc(sp, f"sqf{j}", [128, 512], F32) for j in range(2)]
            mean = alloc(sp, "mean", [128, 512], F32)
            var = alloc(sp, "var", [128, 512], F32)
            rr = alloc(sp, "rr", [128, 512], F32)
            uu = alloc(sp, "uu", [128, 512], F32)
            cmix = alloc(sp, "cmix", [128, 2, 512], BF16)
            for j in range(2):
                for tap in range(31):
                    col = O_CCW + tap * 2 + j
                    ts(dg.ap[:, j, tap, :], identb.ap, prm.ap[:, l, col:col + 1], None, ALU.mult, None, [identb, prm], [dg])
            for (si, t0) in c.tiles:
                n = TT
                gofs = si * 32 + 1 + t0
                for j in range(2):
                    pb = psBk.next()
                    for tap in range(31):
                        mm(PS[pb].ap[:, 0:n], dg.ap[:, j, tap, :], gpad.ap[:, j, gofs + tap:gofs + tap + n], tap == 0, tap == 30,
                           [dg, gpad], PS[pb], fresh=(tap == 0))
                    bcol = prm.ap[:, l, O_CCB + j:O_CCB + j + 1]
                    act(cvf[j].ap[:, 0:n], PS[pb].ap[:, 0:n], AF.Identity, [PS[pb], prm], [cvf[j]], bias=bcol)
                    act(sqf[j].ap[:, 0:n], PS[pb].ap[:, 0:n], AF.Square, [PS[pb], prm], [sqf[j]], bias=bcol)
                for j in range(2):
                    mm(PS[6].ap[:, 0:n], onesf.ap, cvf[j].ap[:, 0:n], j == 0, j == 1, [onesf, cvf[j]], PS[6], fresh=(j == 0))
                for j in range(2):
                    mm(PS[7].ap[:, 0:n], onesf.ap, sqf[j].ap[:, 0:n], j == 0, j == 1, [onesf, sqf[j]], PS[7], fresh=(j == 0))
                ts(mean.ap[:, 0:n], PS[6].ap[:, 0:n], 1.0 / 256, None, ALU.mult, None, [PS[6]], [mean])
                tt(var.ap[:, 0:n], mean.ap[:, 0:n], mean.ap[:, 0:n], ALU.mult, [mean], [var])
                stt(var.ap[:, 0:n], PS[7].ap[:, 0:n], 1.0 / 256, var.ap[:, 0:n], ALU.mult, ALU.subtract, [PS[7], var], [var])
                act(var.ap[:, 0:n], var.ap[:, 0:n], AF.Ln, [var], [var], bias=1e-5)
                act(rr.ap[:, 0:n], var.ap[:, 0:n], AF.Exp, [var], [rr], scale=-0.5)
                for j in range(2):
                    tt(uu.ap[:, 0:n], cvf[j].ap[:, 0:n], mean.ap[:, 0:n], ALU.subtract, [cvf[j], mean], [uu])
                    tt(uu.ap[:, 0:n], uu.ap[:, 0:n], rr.ap[:, 0:n], ALU.mult, [uu, rr], [uu])
                    act(cmix.ap[:, j, 0:n], uu.ap[:, 0:n], AF.Silu, [uu, prm], [cmix],
                        scale=prm.ap[:, l, O_CLG + j:O_CLG + j + 1], bias=prm.ap[:, l, O_CLB + j:O_CLB + j + 1])
                xupdate(c, l, t0, n, lambda j, oc: w_or[:, 2 + j, oc * 128:(oc + 1) * 128],
                        [cmix.ap[:, 0, 0:n], cmix.ap[:, 1, 0:n]], 16, [v_wor, cmix])

            if stop == "C":
                return
            ssd_phase(c, l, w_or, v_wor)
            if stop in ("S", "S1", "S2", "S3"):
                return
            v_woa = load_piece([(lambda a: r3(a, 4), d_wout[l, 0:512, :].rearrange("(k p) c -> p k c", p=128))])
            mla_phase(c, l, r3(v_woa.ap, 4), v_woa)
            if stop == "M":
                return
            ffn_phase(c, l)

        def ssd_phase(c, l, w_or, v_wor):
            T, TT = c.T, c.TT
            nblk = c.nblk
            g_r, x_r, z_r = rg(gpad), rg(xbcp), rg(szb)
            sp = Bump([(SC0, SBYTES), g_r])
            dg5 = alloc(sp, "dg5", [128, 4, 5, 128], BF16)
            xsT = alloc(sp, "xsT", [128, 2, 512], BF16)
            BCt = alloc(sp, "BCt", [128, 3, TS], BF16)
            xs_tm = alloc(sp, "xs_tm", [128, 16, 256], BF16)
            B_tm = alloc(sp, "B_tm", [128, 16, 128], BF16)
            dt = alloc(sp, "dt", [128, 16, 8], F32)
            dta = alloc(sp, "dta", [128, 16, 8], F32)
            hTf = [alloc(sp, f"hTf{d}", [128, 2, 64], F32) for d in range(2)]
            hTb = alloc(sp, "hTb", [128, 2, 64], BF16)
            sstg = alloc(sp, "sstg", [128, 128], F32)
            sp2 = Bump([tuple(r) for r in sp.ranges] + [x_r, rg(xg)])
            nb8 = nblk * 8
            dtf = dtr.ap.rearrange("p b e -> p (b e)")[:, 0:nb8]
            act(dt.ap.rearrange("p b e -> p (b e)")[:, 0:nb8], dtf, AF.Exp, [dtr], [dt])
            act(dt.ap.rearrange("p b e -> p (b e)")[:, 0:nb8], dt.ap.rearrange("p b e -> p (b e)")[:, 0:nb8], AF.Ln, [dt], [dt], bias=1.0)
            tt(dta.ap[:, 0:nblk, :], dt.ap[:, 0:nblk, :], a_b.ap[:, l * 8:(l + 1) * 8].unsqueeze(1).to_broadcast([128, nblk, 8]),
               ALU.mult, [dt, a_b], [dta])
            memset(BCt.ap[64:128, 1, 0:T], 0.0, [BCt])
            memset(BCt.ap[0:64, 2, 0:T], 0.0, [BCt])
            for ch in range(4):
                for tap in range(5):
                    col = O_SCW + tap * 4 + ch
                    ts(dg5.ap[:, ch, tap, :], identb.ap, prm.ap[:, l, col:col + 1], None, ALU.mult, None, [identb, prm], [dg5])
            for (si, t0) in c.tiles:
                n = TT
                xofs = si * 4 + t0
                for ch in range(4):
                    pb = psBk.next()
                    for tap in range(5):
                        mm(PS[pb].ap[:, 0:n], dg5.ap[:, ch, tap, :], xbcp.ap[:, ch, xofs + tap:xofs + tap + n], tap == 0, tap == 4,
                           [dg5, xbcp], PS[pb], fresh=(tap == 0))
                    bias_ = prm.ap[:, l, O_SCB + ch:O_SCB + ch + 1]
                    if ch < 2:
                        act(xsT.ap[:, ch, 0:n], PS[pb].ap[:, 0:n], AF.Silu, [PS[pb], prm], [xsT], bias=bias_)
                    elif ch == 2:
                        act(BCt.ap[:, 0, t0:t0 + n], PS[pb].ap[:, 0:n], AF.Silu, [PS[pb], prm], [BCt], bias=bias_)
                    else:
                        act(BCt.ap[0:64, 1, t0:t0 + n], PS[pb].ap[0:64, 0:n], AF.Silu, [PS[pb], prm], [BCt], bias=bias_[0:64])
                        act(BCt.ap[64:128, 2, t0:t0 + n], PS[pb].ap[64:128, 0:n], AF.Silu, [PS[pb], prm], [BCt], bias=bias_[64:128])
                for b in range(n // 128):
                    blk = (t0 + b * 128) // 128
                    pb = psA.next()
                    for j in range(2):
                        tp(PSB[pb].ap[:, j * 128:(j + 1) * 128], xsT.ap[:, j, b * 128:(b + 1) * 128], identb.ap, [xsT, identb], PS[pb], fresh=(j == 0))
                    tp(PSB[pb].ap[:, 256:384], BCt.ap[:, 0, t0 + b * 128:t0 + (b + 1) * 128], identb.ap, [BCt, identb], PS[pb])
                    cp(xs_tm.ap[:, blk, :], PSB[pb].ap[:, 0:256], [PS[pb]], [xs_tm])
                    cp(B_tm.ap[:, blk, :], PSB[pb].ap[:, 256:384], [PS[pb]], [B_tm])
            if stop == "S1":
                return
            hst = alloc(sp2, "hst", [128, 16, 2, 64], BF16)
            Gm = [alloc(sp2, f"Gm{d}", [128, 2, 128], F32) for d in range(2)]
            Lm = alloc(sp2, "Lm", [128, 4, 128], F32)
            seg = alloc(sp2, "seg", [128, 4, 128], F32)
            MT = [[alloc(sp2, f"MT{i}{d}", [128, 4, 128], BF16) for d in range(2)] for i in range(2)]
            xdt = [[alloc(sp2, f"xdt{i}{d}", [128, 4, 64], BF16) for d in range(2)] for i in range(2)]
            ee = [alloc(sp2, f"ee{i}", [128, 32], F32) for i in range(2)]
            cds = [alloc(sp2, f"cds{i}", [128, 2, 2], F32) for i in range(2)]
            xdd = [alloc(sp2, f"xdd{i}", [128, 4, 64], BF16) for i in range(2)]
            wv = alloc(sp2, "wv", [128, 4], F32)
            yo = alloc(sp2, "yo", [128, 8, 64], F32)
            y1 = alloc(sp2, "y1", [128, 256], F32)
            y2 = alloc(sp2, "y2", [128, 256], F32)
            y3 = alloc(sp2, "y3", [128, 256], F32)
            yn = [alloc(sp2, f"yn{i}", [128, 256], BF16) for i in range(2)]
            ssq = alloc(sp2, "ssq2", [128, 2], F32)
            smix = alloc(sp2, "smix", [128, 2, 512], BF16)
            junk = yo
            evb = [View(Lm.name, Lm.ap.rearrange("p h s -> p (h s)")), View(seg.name, seg.ap.rearrange("p h s -> p (h s)"))]

            def small_mm(blk, dirs, ee_, cds_):
                if len(dirs) == 2:
                    A = dta.ap[:, blk, 0:8]
                    for k, (lt, lv) in enumerate([(SU[0].ap, cstf), (SU[1].ap, cstf), (TRI[0].ap, cstf), (TRI[1].ap, cstf), (onesf.ap, onesf)]):
                        mm(PS[6].ap[:, k * 8:k * 8 + 8], lt, A, True, True, [lv, dta], PS[6], fresh=(k == 0))
                    act(ee_.ap[:, 0:32], PS[6].ap[:, 0:32], AF.Exp, [PS[6]], [ee_])
                else:
                    A = dta.ap[:, blk, 4:8]
                    mm(PS[6].ap[:, 12:16], SU[1].ap, A, True, True, [cstf, dta], PS[6], fresh=True)
                    mm(PS[6].ap[:, 36:40], onesf.ap, A, True, True, [onesf, dta], PS[6])
                    act(ee_.ap[:, 12:16], PS[6].ap[:, 12:16], AF.Exp, [PS[6]], [ee_])
                for d in dirs:
                    act(cds_.ap[0:64, d, :], PS[6].ap[0:64, 32 + d * 4:34 + d * 4], AF.Exp, [PS[6]], [cds_])
                    act(cds_.ap[64:128, d, :], PS[6].ap[64:128, 34 + d * 4:36 + d * 4], AF.Exp, [PS[6]], [cds_])

            DEC = [slice(0, 4), slice(12, 16)]

            def state_pre(blk, d, ee_, xdd_):
                tt(wv.ap, dt.ap[:, blk, d * 4:(d + 1) * 4], ee_.ap[:, DEC[d]], ALU.mult, [dt, ee_], [wv])
                tt(xdd_.ap, xs_tm.ap[:, blk, :].rearrange("p (h q) -> p h q", h=4), wv.ap.unsqueeze(2).to_broadcast([128, 4, 64]),
                   ALU.mult, [xs_tm, wv], [xdd_])

            def state_post(blk, d, cds_, xdd_):
                pb = 7
                for h in range(4):
                    g, j = h // 2, h % 2
                    mm(PS[pb].ap[g * 64:(g + 1) * 64, j * 64:(j + 1) * 64], B_tm.ap[:, blk, g * 64:(g + 1) * 64], xdd_.ap[:, h, :],
                       True, True, [B_tm, xdd_], PS[pb], fresh=(h == 0))
                for j in range(2):
                    stt(hTf[d].ap[:, j, :], hTf[d].ap[:, j, :], cds_.ap[:, d, j:j + 1], PS[pb].ap[:, j * 64:(j + 1) * 64],
                        ALU.mult, ALU.add, [hTf[d], cds_, PS[pb]], [hTf[d]])

            def main_pre(blk, i):
                tk = slice(blk * 128, (blk + 1) * 128)
                pg = psA.next()
                mm(PS[pg].ap[:, 0:256], BCt.ap[:, 0, tk], BCt.ap[:, 1:3, tk], True, True, [BCt], PS[pg], fresh=True)
                small_mm(blk, [0, 1], ee[i], cds[i])
                for d in range(2):
                    tt(Gm[d].ap, PS[pg].ap[:, 0:256].rearrange("p (g s) -> p g s", g=2),
                       TRI[d].ap.unsqueeze(1).to_broadcast([128, 2, 128]), ALU.mult, [PS[pg], cstf], [Gm[d]])
                for d in range(2):
                    tt(Lm.ap, SU[d].ap.unsqueeze(1).to_broadcast([128, 4, 128]),
                       dta.ap[:, blk, d * 4:(d + 1) * 4].unsqueeze(2).to_broadcast([128, 4, 128]), ALU.mult, [cstf, dta], [Lm], eng="pool")
                    pdf = psA.next()
                    for h in range(4):
                        mm(PS[pdf].ap[:, h * 128:(h + 1) * 128], Lm.ap[:, h, :], TRI[d].ap, True, True, [Lm, cstf], PS[pdf], fresh=(h == 0))
                    act(seg.ap.rearrange("p h s -> p (h s)"), PS[pdf].ap, AF.Exp, [PS[pdf]], [seg])
                    for g in range(2):
                        tt(MT[i][d].ap[:, 2 * g:2 * g + 2, :], seg.ap[:, 2 * g:2 * g + 2, :],
                           Gm[d].ap[:, g:g + 1, :].to_broadcast([128, 2, 128]), ALU.mult, [seg, Gm[d]], [MT[i][d]])
                    tt(xdt[i][d].ap, xs_tm.ap[:, blk, :].rearrange("p (h q) -> p h q", h=4),
                       dt.ap[:, blk, d * 4:(d + 1) * 4].unsqueeze(2).to_broadcast([128, 4, 64]), ALU.mult, [xs_tm, dt], [xdt[i][d]], eng="pool")
                state_pre(blk, 0, ee[i], xdd[i])

            def main_post(blk, i):
                tk = slice(blk * 128, (blk + 1) * 128)
                tt(y3.ap.rearrange("p (h q) -> p h q", h=4), xs_tm.ap[:, blk, :].rearrange("p (h q) -> p h q", h=4),
                   dsum_b.ap[:, l, :].unsqueeze(2).to_broadcast([128, 4, 64]), ALU.mult, [xs_tm, dsum_b], [y3], eng="pool")
                py = psBk.next()
                for h in range(4):
                    for d in range(2):
                        mm(PS[py].ap[:, h * 64:(h + 1) * 64], MT[i][d].ap[:, h, :], xdt[i][d].ap[:, h, :], d == 0, d == 1,
                           [MT[i][d], xdt[i][d]], PS[py], fresh=(h == 0 and d == 0))
                po = psBk.next()
                for d in range(2):
                    hsrc_v = hTb if d == 0 else hst
                    hsrc = hTb.ap if d == 0 else hst.ap[:, blk]
                    for g in range(2):
                        mm(PS[po].ap[:, d * 256 + g * 128:d * 256 + (g + 1) * 128], BCt.ap[:, 1 + g, tk], hsrc,
                           True, True, [BCt, hsrc_v], PS[po], fresh=(d == 0 and g == 0))
                state_post(blk, 0, cds[i], xdd[i])
                cp(hTb.ap, hTf[0].ap, [hTf[0]], [hTb], eng="pool")
                for d in range(2):
                    e0 = 16 + 12 * d
                    tt(yo.ap[:, 4 * d:4 * d + 4, :], PS[po].ap[:, d * 256:(d + 1) * 256].rearrange("p (e q) -> p e q", e=4),
                       ee[i].ap[:, e0:e0 + 4].unsqueeze(2).to_broadcast([128, 4, 64]), ALU.mult, [PS[po], ee[i]], [yo])
                yof = yo.ap.rearrange("p e q -> p (e q)")
                tt(y1.ap, yof[:, 0:256], yof[:, 256:512], ALU.add, [yo], [y1])
                tt(y1.ap, y1.ap, PS[py].ap[:, 0:256], ALU.add, [y1, PS[py]], [y1])
                tt(y1.ap, y1.ap, y3.ap, ALU.add, [y1, y3], [y1])
                tt(y2.ap, y1.ap, szb.ap[:, blk, :], ALU.mult, [y1, szb], [y2])
                memset(ssq.ap[:, 0:1], 0.0, [ssq])
                act(junk.ap.rearrange("p e q -> p (e q)")[:, 0:256], y2.ap, AF.Square, [y2, ssq], [junk, ssq], accum_out=ssq.ap[:, 0:1])
                act(ssq.ap[:, 1:2], ssq.ap[:, 0:1], AF.Ln, [ssq], [ssq], scale=1.0 / 256, bias=1e-6)
                act(ssq.ap[:, 1:2], ssq.ap[:, 1:2], AF.Exp, [ssq], [ssq], scale=-0.5)
                ts(yn[i].ap, y2.ap, ssq.ap[:, 1:2], None, ALU.mult, None, [y2, ssq], [yn[i]])

            def main_tail(blk, i):
                bt = (blk * 128) % TT
                pt = psA.next()
                for j in range(2):
                    tp(PSB[pt].ap[:, j * 128:(j + 1) * 128], yn[i].ap[:, j * 128:(j + 1) * 128], identb.ap, [yn[i], identb], PS[pt], fresh=(j == 0))
                cp(smix.ap[:, :, bt:bt + 128], PSB[pt].ap[:, 0:256].rearrange("p (j t) -> p j t", j=2), [PS[pt]], [smix])
                if bt + 128 == TT:
                    t0 = blk * 128 + 128 - TT
                    xupdate(c, l, t0, TT, lambda j, oc: w_or[:, j, oc * 128:(oc + 1) * 128],
                            [smix.ap[:, 0, 0:TT], smix.ap[:, 1, 0:TT]], 16, [v_wor, smix], evac=evb)

            for si, (s0, sl) in enumerate(c.seqs):
                blks = list(range(s0 // 128, (s0 + sl) // 128))
                for d in range(2):
                    if c.ctx:
                        dma("sp", sstg.ap.rearrange("p (g n) -> p g n", g=2),
                            d_st[l, d].rearrange("(g j) p n -> (j p) g n", g=2), writes=[sstg])
                        pb = psA.next()
                        tp(PS[pb].ap[:, 0:128], sstg.ap, identf.ap, [sstg, cstf], PS[pb], fresh=True)
                        cp(hTf[d].ap.rearrange("p j q -> p (j q)"), PS[pb].ap[:, 0:128], [PS[pb]], [hTf[d]])
                    else:
                        memset(hTf[d].ap, 0.0, [hTf[d]])
                rb_ = list(reversed(blks))
                small_mm(rb_[0], [1], ee[0], cds[0])
                state_pre(rb_[0], 1, ee[0], xdd[0])
                for k, blk in enumerate(rb_):
                    i = k % 2
                    if k + 1 < len(rb_):
                        small_mm(rb_[k + 1], [1], ee[1 - i], cds[1 - i])
                        state_pre(rb_[k + 1], 1, ee[1 - i], xdd[1 - i])
                    cp(hst.ap[:, blk], hTf[1].ap, [hTf[1]], [hst], eng="pool")
                    state_post(blk, 1, cds[i], xdd[i])
                if stop == "S2":
                    return
                cp(hTb.ap, hTf[0].ap, [hTf[0]], [hTb], eng="pool")
                main_pre(blks[0], 0)
                for k, blk in enumerate(blks):
                    if k + 1 < len(blks):
                        main_pre(blks[k + 1], (k + 1) % 2)
                    main_post(blk, k % 2)
                    if k >= 1:
                        main_tail(blks[k - 1], (k - 1) % 2)
                main_tail(blks[-1], (len(blks) - 1) % 2)
                if stop == "S3":
                    return
                if c.kind == "P":
                    for d in range(2):
                        pb = psA.next()
                        tp(PS[pb].ap[:, 0:128], hTf[d].ap.rearrange("p j q -> p (j q)"), identf.ap, [hTf[d], cstf], PS[pb], fresh=True)
                        cp(sstg.ap, PS[pb].ap[:, 0:128], [PS[pb]], [sstg])
                        dma("sp", o_ssd[si, l, d].rearrange("(g j) p n -> (j p) g n", g=2),
                            sstg.ap.rearrange("p (g n) -> p g n", g=2), reads=[sstg])

        def mla_phase(c, l, w_oa, v_woa):
            T, TT = c.T, c.TT
            g_r, x_r, z_r = rg(gpad), rg(xbcp), rg(szb)
            attnT = alloc(Bump([x_r]), "attnT", [128, 4, TS], BF16)
            sp = Bump([(SC0, SBYTES), g_r, z_r])
            NKB = (PAST + TS) // 128
            KT = [alloc(sp, f"KT{i}", [128, PAST + TS], BF16) for i in range(2)]
            VB = [(alloc(sp, f"Ve{i}", [128, NKB, 64], BF16), alloc(sp, f"Vo{i}", [128, NKB, 64], BF16)) for i in range(2)]
            QT = [alloc(sp, f"QT{i}", [128, TS], BF16) for i in range(2)]
            PT = [alloc(sp, f"PT{i}", [128, 512], BF16) for i in range(3)]
            t1 = alloc(sp, "mt1", [128, 512], F32)
            t2 = alloc(sp, "mt2", [128, 512], F32)
            rb = alloc(sp, "rb", [128, 512], F32)
            ptr = Rot(PT)
            LA = 2
            psS = Rot([0, 1, 2])
            PP = 3
            accb = Rot([(4, 5), (6, 7)])

            heads = []
            for si, (s0, sl) in enumerate(c.seqs):
                for h in range(8):
                    heads.append((si, s0, sl, h))

            def prep(idx):
                si, s0, sl, h = heads[idx]
                hp, hh = h // 2, h % 2
                Tk = (PAST if c.ctx else 0) + sl
                k0 = 0 if c.ctx else s0
                nkb = Tk // 128
                qtiles = [t for (s_, t) in c.tiles if s_ == si]
                kt, qt = KT[idx % 2], QT[idx % 2]
                Ve, Vo = VB[(idx // 2) % 2]
                if hh == 0:
                    vcols = wukv.ap.rearrange("p (h c) -> p h c", h=8)[:, 2 * hp:2 * hp + 2, 64:128]
                    for kb in range(nkb):
                        mm(PS[PP].ap[:, 0:128], ckvnT.ap[:, k0 + kb * 128:k0 + (kb + 1) * 128], vcols, True, True, [ckvnT, wukv], PS[PP], fresh=True)
                        cp(Ve.ap[:, kb, :], PS[PP].ap[:, 0:64], [PS[PP]], [Ve])
                        cp(Vo.ap[:, kb, :], PS[PP].ap[:, 64:128], [PS[PP]], [Vo])
                        yield
                for k1 in range(0, Tk, 512):
                    kn = min(512, Tk - k1)
                    mm(PS[PP].ap[0:64, 0:kn], wukv.ap[:, h * 128:h * 128 + 64], ckvnT.ap[:, k0 + k1:k0 + k1 + kn], True, True,
                       [wukv, ckvnT], PS[PP], fresh=True)
                    cp(kt.ap[0:64, k1:k1 + kn], PS[PP].ap[0:64, 0:kn], [PS[PP]], [kt])
                    yield
                cp(kt.ap[64:96, 0:Tk], krT.ap[64:96, k0:k0 + Tk], [krT], [kt], eng="pool")
                for t0 in qtiles:
                    n = TT
                    tl = t0 - s0
                    for kc in range(2):
                        mm(PS[PP].ap[0:96, 0:n], wuq.ap[:, kc, h * 96:(h + 1) * 96], qlat.ap[:, kc, t0:t0 + n], kc == 0, kc == 1,
                           [wuq, qlat], PS[PP], fresh=(kc == 0))
                    if c.rope:
                        cp(qt.ap[0:64, tl:tl + n], PS[PP].ap[0:64, 0:n], [PS[PP]], [qt])
                        tt(t1.ap[64:96, 0:n], PS[PP].ap[64:96, 0:n], ropeC.ap[64:96, t0:t0 + n], ALU.mult, [PS[PP], ropeC], [t1])
                        yield
                        for kc in range(2):
                            mm(PS[PP].ap[64:96, 0:n], wuq.ap[:, kc, 768 + h * 32:768 + (h + 1) * 32], qlat.ap[:, kc, t0:t0 + n],
                               kc == 0, kc == 1, [wuq, qlat], PS[PP], fresh=(kc == 0))
                        tt(t2.ap[64:96, 0:n], PS[PP].ap[64:96, 0:n], ropeS.ap[64:96, t0:t0 + n], ALU.mult, [PS[PP], ropeS], [t2])
                        tt(qt.ap[64:96, tl:tl + n], t1.ap[64:96, 0:n], t2.ap[64:96, 0:n], ALU.add, [t1, t2], [qt])
                    else:
                        cp(qt.ap[0:96, tl:tl + n], PS[PP].ap[0:96, 0:n], [PS[PP]], [qt])
                    yield

            def attn_unit(idx, t0, prev_tail, filler):
                si, s0, sl, h = heads[idx]
                hp, hh = h // 2, h % 2
                nkb = ((PAST if c.ctx else 0) + sl) // 128
                kt, qt = KT[idx % 2], QT[idx % 2]
                vv = VB[(idx // 2) % 2][hh]
                tl = t0 - s0
                n = c.TT
                po, pd = accb.next()
                r0 = 0 if hh == 0 else 64
                pts = []
                for kb in range(nkb + LA):
                    if kb < nkb:
                        pb = psS.next()
                        mm(PS[pb].ap[:, 0:n], kt.ap[0:96, kb * 128:(kb + 1) * 128], qt.ap[0:96, tl:tl + n], True, True,
                           [kt, qt], PS[pb], fresh=True)
                        pt_ = ptr.next()
                        pts.append(pt_)
                        act(pt_.ap[:, 0:n], PS[pb].ap[:, 0:n], AF.Exp, [PS[pb]], [pt_], scale=SCALE)
                    if kb == min(LA, nkb) - 1 and prev_tail is not None:
                        prev_tail()
                        prev_tail = None
                    if kb >= LA:
                        k2 = kb - LA
                        mm(PS[po].ap[r0:r0 + 64, 0:n], vv.ap[:, k2, :], pts[k2].ap[:, 0:n], k2 == 0, k2 == nkb - 1, [vv, pts[k2]], PS[po], fresh=(k2 == 0))
                        mm(PS[pd].ap[r0:r0 + 64, 0:n], onesb.ap[:, 0:64], pts[k2].ap[:, 0:n], k2 == 0, k2 == nkb - 1, [onesb, pts[k2]], PS[pd], fresh=(k2 == 0))
                    if filler is not None:
                        next(filler, None)

                def tail():
                    recip(rb.ap[r0:r0 + 64, 0:n], PS[pd].ap[r0:r0 + 64, 0:n], [PS[pd]], [rb])
                    tt(attnT.ap[r0:r0 + 64, hp, t0:t0 + n], PS[po].ap[r0:r0 + 64, 0:n], rb.ap[r0:r0 + 64, 0:n], ALU.mult, [PS[po], rb], [attnT])
                return tail

            for _ in prep(0):
                pass
            pend = None
            for idx in range(len(heads)):
                si = heads[idx][0]
                filler = prep(idx + 1) if idx + 1 < len(heads) else None
                for t0 in [t for (s_, t) in c.tiles if s_ == si]:
                    pend = attn_unit(idx, t0, pend, filler)
                if filler is not None:
                    for _ in filler:
                        pass
            if pend is not None:
                pend()
            for (si, t0) in c.tiles:
                xupdate(c, l, t0, TT, lambda j, oc: w_oa[:, j, oc * 128:(oc + 1) * 128],
                        [attnT.ap[:, j, t0:t0 + TT] for j in range(4)], 16, [v_woa, attnT])

        def ffn_phase(c, l):
            T, TT = c.T, c.TT
            sp = Bump([(max(SC0, rg(hT)[1]), SBYTES)])
            sq = [alloc(sp, f"fsq{i}", [128, 512], BF16) for i in range(2)]
            sd = alloc(sp, "fsd", [128, 512], F32)
            rstd = alloc(sp, "frstd", [128, 512], F32)
            tmpb = [alloc(sp, f"ftmpb{i}", [128, 512], F32) for i in range(2)]
            sg = [alloc(sp, f"sg{i}", [128, 512], F32) for i in range(2)]
            actb = [alloc(sp, f"actb{i}", [128, 2, 512], BF16) for i in range(2)]
            sh2 = modv.ap[:, l, 24:32, c.cond]
            for (si, t0) in c.tiles:
                norm_tile(c, (sq, sd, rstd, tmpb), t0, TT, gsc.ap[:, 1, :], sh2, lambda kc, t0=t0: hT.ap[:, kc, t0:t0 + TT], hT)
            def load_group(g):
                c0 = g * 256
                v1 = load_piece([(lambda a: r3(a, 8)[:, :, 0:256], d_wg[l, :, c0:c0 + 256].rearrange("(k p) c -> p k c", p=128)),
                                 (lambda a: r3(a, 8)[:, :, 256:512], d_wu[l, :, c0:c0 + 256].rearrange("(k p) c -> p k c", p=128))])
                v2 = load_piece([(lambda a: r3(a, 4)[:, 0:2, :], d_wd[l, c0:c0 + 256, :].rearrange("(k p) c -> p k c", p=128))])
                return (v1, v2)

            units = [(g, t0) for g in range(NFG) for (si, t0) in c.tiles]
            wts = {0: load_group(0), 1: load_group(1)}

            psF = Rot([4, 5, 6, 7])

            def stage_a_groups(u):
                g, t0 = units[u]
                v1, v2 = wts[g]
                wgu = r3(v1.ap, 8)
                n = TT
                ab = actb[u % 2]
                outs = []
                st = {}

                def mk(j, which):
                    def f():
                        pb = psA.next()
                        c0 = (0 if which == 0 else 256) + j * 128
                        for kc in range(8):
                            mm(PS[pb].ap[:, 0:n], wgu[:, kc, c0:c0 + 128], hT.ap[:, kc, t0:t0 + n], kc == 0, kc == 7, [v1, hT], PS[pb], fresh=(kc == 0))
                        st[(j, which)] = pb
                        if which == 1:
                            pg, pu = st[(j, 0)], pb
                            act(sg[j].ap[:, 0:n], PS[pg].ap[:, 0:n], AF.Silu, [PS[pg]], [sg[j]])
                            tt(ab.ap[:, j, 0:n], sg[j].ap[:, 0:n], PS[pu].ap[:, 0:n], ALU.mult, [sg[j], PS[pu]], [ab])
                    return f
                for j in range(2):
                    outs.append(mk(j, 0))
                    outs.append(mk(j, 1))
                return outs

            def stage_b_steps(u):
                g, t0 = units[u]
                v1, v2 = wts[g]
                wdn = r3(v2.ap, 4)
                n = TT
                ab = actb[u % 2]
                outs = []

                def mk(oc):
                    def f():
                        pb = psF.next()
                        for j in range(2):
                            mm(PS[pb].ap[:, 0:n], wdn[:, j, oc * 128:(oc + 1) * 128], ab.ap[:, j, 0:n], j == 0, j == 1, [v2, ab], PS[pb], fresh=(j == 0))
                        stt(xT.ap[:, oc, t0:t0 + n], PS[pb].ap[:, 0:n], modv.ap[:, l, 40 + oc, c.cond:c.cond + 1],
                            xT.ap[:, oc, t0:t0 + n], ALU.mult, ALU.add, [PS[pb], modv, xT], [xT])
                        if oc == 7 and (u + 1 == len(units) or units[u + 1][0] != g) and g + 2 < NFG:
                            wts[g + 2] = load_group(g + 2)
                    return f
                for oc in range(8):
                    outs.append(mk(oc))
                return outs

            for u in range(len(units) + 1):
                ga = stage_a_groups(u) if u < len(units) else []
                gb = stage_b_steps(u - 1) if u >= 1 else []
                for k in range(4):
                    if ga:
                        ga[k]()
                    if gb:
                        gb[2 * k]()
                        gb[2 * k + 1]()


        def final_out(c):
            sp = Bump([(SC0, SBYTES)])
            sq = [alloc(sp, f"osq{i}", [128, 512], BF16) for i in range(2)]
            sd = alloc(sp, "osd", [128, 512], F32)
            rstd = alloc(sp, "orstd", [128, 512], F32)
            tmpb = [alloc(sp, f"otmpb{i}", [128, 512], F32) for i in range(2)]
            yf = alloc(sp, "yf", [128, 8, 128], F32)
            ytm = [alloc(sp, f"ytm{i}", [128, D], F32) for i in range(2)]
            gf = cnd.ap[:, 16:24]
            for b in range(c.nblk):
                norm_tile(c, (sq, sd, rstd, tmpb), b * 128, 128, gf, None, lambda kc: yf.ap[:, kc, :], yf)
                y = ytm[b % 2]
                for half in range(2):
                    pb = psA.next()
                    for q in range(4):
                        kc = half * 4 + q
                        tp(PS[pb].ap[:, q * 128:(q + 1) * 128], yf.ap[:, kc, :], identf.ap, [yf, cstf], PS[pb], fresh=(q == 0))
                    act(y.ap[:, half * 512:(half + 1) * 512], PS[pb].ap, AF.Copy, [PS[pb]], [y])
                dma("sp", c.oy[b * 128:(b + 1) * 128, :], y.ap, reads=[y])

        passes = []
        if do_sample:
            passes.append(make_cfg("S"))
        if do_prompt:
            passes.append(make_cfg("P"))
        for c in passes:
            load_x(c)
            for l in range(n_layers):
                layer_pass(c, l)
            final_out(c)

        names = P.finalize()
        print("nops", len(P.ops), "nsems", len(names))
        sems = {nm: es.enter_context(nc.semaphore(f"s{i}")) for i, nm in enumerate(names)}
        with nc.Block() as block:
            P.emit(block, sems)
    return nc


def _consts():
    i = np.arange(128)
    ident = np.eye(128, dtype=np.float32)
    suf = (i[:, None] > i[None, :]).astype(np.float32)
    sub = (i[:, None] < i[None, :]).astype(np.float32)
    trif = (i[:, None] <= i[None, :]).astype(np.float32)
    trib = (i[:, None] >= i[None, :]).astype(np.float32)
    cst = np.concatenate([ident, suf, sub, trif, trib], axis=1).astype(np.float32)
    t = np.arange(TS)
    row = (t // 64).astype(np.float32)
    col = (t % 64).astype(np.float32)
    nf = 8
    inv = (10000.0 ** (-np.arange(nf, dtype=np.float32) / nf)).astype(np.float32)
    ang = np.stack([row[:, None] * inv, col[:, None] * inv], axis=1)
    cos = np.cos(ang).astype(np.float32)
    sin = np.sin(ang).astype(np.float32)
    rope = np.zeros((2, 128, TS), np.float32)
    for a in range(2):
        for half in range(2):
            for f in range(nf):
                r = 64 + a * 16 + half * 8 + f
                rope[0, r] = cos[:, a, f]
                rope[1, r] = sin[:, a, f]
    return cst, rope


_CACHE = {}


def kernel(x_prompt, x_sample, c, cache_ckv, cache_krope, state_ssd, c_ctx, w_ada, b_ada,
           g_mix, w_in, g_q, w_uq, g_kv, w_ukv, ssd_conv_w, ssd_conv_b, ssd_dt_bias,
           ssd_a_log, ssd_d, ssd_norm_g, cm_conv_w, cm_conv_b, cm_ln_g, cm_ln_b, w_out,
           g_ffn, w_gate, w_up, w_down, g_final, _n_layers=DEPTH, _do_sample=True, _do_prompt=True, _stop=None):
    f = lambda a: np.ascontiguousarray(np.asarray(a, dtype=np.float32))
    key = (_n_layers, _do_sample, _do_prompt, _stop)
    if key not in _CACHE:
        _CACHE[key] = build_program(_n_layers, _do_sample, _do_prompt, _stop)
    nc = _CACHE[key]
    cst, rope = _consts()
    shared = dict(
        w_ada=f(w_ada), b_ada=f(b_ada), g_mix=f(g_mix), w_in=f(w_in), g_q=f(g_q), w_uq=f(w_uq), g_kv=f(g_kv),
        w_ukv=f(w_ukv), ssd_conv_w=f(ssd_conv_w), ssd_conv_b=f(ssd_conv_b), ssd_dt_bias=f(ssd_dt_bias).reshape(-1),
        ssd_a_log=f(ssd_a_log).reshape(-1), ssd_d=f(ssd_d).reshape(-1), ssd_norm_g=f(ssd_norm_g), cm_conv_w=f(cm_conv_w),
        cm_conv_b=f(cm_conv_b), cm_ln_g=f(cm_ln_g), cm_ln_b=f(cm_ln_b), w_out=f(w_out), g_ffn=f(g_ffn),
        w_gate=f(w_gate), w_up=f(w_up), w_down=f(w_down), g_final=f(g_final), cst=cst, rope=rope)
    x_prompt, x_sample, c, c_ctx = f(x_prompt), f(x_sample), f(c), f(c_ctx)
    cache_ckv, cache_krope, state_ssd = f(cache_ckv), f(cache_krope), f(state_ssd)
    in_maps = []
    for i in range(8):
        b = i % 4
        m = dict(shared)
        m["x_s"] = x_sample[b]
        m["x_p"] = x_prompt[2 * i:2 * i + 2].reshape(NPS * TPS, D)
        m["cond"] = np.stack([c[b], c_ctx], axis=0)
        m["cache_ckv"] = cache_ckv[b]
        m["cache_krope"] = cache_krope[b]
        m["state_ssd"] = state_ssd[b]
        in_maps.append(m)
    res = run_bass_kernel_spmd(nc, in_maps, core_ids=list(range(8)))
    r = res.results
    y_sample = np.stack([r[b]["y_s"] for b in range(4)], axis=0).astype(np.float32)
    y_prompt = np.concatenate([r[i]["y_p"].reshape(NPS, TPS, D) for i in range(8)], axis=0).astype(np.float32)
    new_ckv = np.concatenate([r[i]["o_ckv"] for i in range(8)], axis=0).astype(np.float32)
    new_kr = np.concatenate([r[i]["o_kr"] for i in range(8)], axis=0).astype(np.float32)
    new_ssd = np.concatenate([r[i]["o_ssd"] for i in range(8)], axis=0).astype(np.float32)
    return (y_prompt, y_sample, new_ckv, new_kr, new_ssd)
```

```python
import math
from contextlib import ExitStack
import numpy as np
import concourse.bass as bass
import concourse.mybir as mybir
from concourse.bass_utils import run_bass_kernel_spmd

F32 = mybir.dt.float32
BF16 = mybir.dt.bfloat16
U8 = mybir.dt.uint8
AF = mybir.ActivationFunctionType
ALU = mybir.AluOpType

ENGS = ("pe", "act", "dve", "pool", "sp")

D = 1024
DEPTH = 4
TS = 2048
TPS = 256
NPS = 2
PAST = 256
DFF = 2816
NFG = DFF // 256
IN_W = 1704
SCALE = 96 ** -0.5
NPRM = 161


class View:
    __slots__ = ("name", "ap")

    def __init__(self, name, ap):
        self.name, self.ap = name, ap


class Prog:
    def __init__(self, nc):
        self.nc = nc
        self.ops = []
        self.regions = {}
        self.overlaps = {}

    def region(self, name, space, p0, p1, b0, b1):
        assert name not in self.regions, name
        self.regions[name] = (space, p0, p1, b0, b1)
        ov = [name]
        for n, (s, q0, q1, c0, c1) in self.regions.items():
            if n == name:
                continue
            if s == space and q0 < p1 and p0 < q1 and c0 < b1 and b0 < c1:
                ov.append(n)
                self.overlaps[n].append(name)
        self.overlaps[name] = ov

    def op(self, eng, fn, reads=(), writes=(), dma=False, chan=None, fresh=()):
        rs = [r if isinstance(r, str) else r.name for r in reads]
        ws = [w if isinstance(w, str) else w.name for w in writes]
        fr = [w if isinstance(w, str) else w.name for w in fresh]
        if dma and chan is None:
            chan = ws[0] if ws else rs[0]
        self.ops.append((eng, fn, rs, ws, dma, chan, fr))

    def finalize(self):
        ops = self.ops
        n = len(ops)
        last_w = {}
        readers = {}
        deps = [None] * n
        for i, (eng, fn, rs, ws, dma, chan, fr) in enumerate(ops):
            for f in fr:
                lw = last_w.get(f)
                assert lw is None or len(readers.get(f, ())) > 0, \
                    f"PSUM collision on {f} at op {i} (prev writer {lw} unread)"
            d = set()
            for r in rs:
                for rr in self.overlaps[r]:
                    w = last_w.get(rr)
                    if w is not None:
                        d.add(w)
                    if self.regions[rr][0] == "psum":
                        for x in readers.get(rr, {}).values():
                            d.add(x)
            for w_ in ws:
                for ww in self.overlaps[w_]:
                    w = last_w.get(ww)
                    if w is not None:
                        d.add(w)
                    for x in readers.get(ww, {}).values():
                        d.add(x)
            d.discard(i)
            dd = []
            mine = set(rs) | set(ws)
            for j in d:
                ej, _, rsj, wsj, dmaj, _, _ = ops[j]
                if not dmaj and not dma and ej == eng:
                    if eng == "pe":
                        continue
                    hit = False
                    for x in wsj:
                        for y in self.overlaps[x]:
                            if y in mine:
                                hit = True
                                break
                        if hit:
                            break
                    if not hit:
                        continue
                dd.append(j)
            deps[i] = dd
            key = ("c:" + chan) if dma else eng
            for r in rs:
                readers.setdefault(r, {})[key] = i
            for w_ in ws:
                last_w[w_] = i
                readers[w_] = {}
        needed = set()
        for dd in deps:
            needed.update(dd)
        eng_cnt = {e: 0 for e in ENGS}
        chan_cnt = {}
        ticket = {}
        for i, (eng, fn, rs, ws, dma, chan, fr) in enumerate(ops):
            if dma:
                chan_cnt[chan] = chan_cnt.get(chan, 0) + 16
                ticket[i] = ("c:" + chan, chan_cnt[chan])
            elif i in needed:
                eng_cnt[eng] += 1
                ticket[i] = ("e:" + eng, eng_cnt[eng])
        self.sem_names = ["e:" + e for e in ENGS] + ["c:" + c for c in chan_cnt]
        self.final_counts = {("e:" + e): eng_cnt[e] for e in ENGS}
        self.final_counts.update({("c:" + c): v for c, v in chan_cnt.items()})
        self.deps, self.ticket = deps, ticket
        return self.sem_names

    def emit(self, block, sems):
        ops, deps, ticket = self.ops, self.deps, self.ticket
        per_eng = {e: [] for e in ENGS}
        for i, o in enumerate(ops):
            per_eng[o[0]].append(i)
        final_counts = self.final_counts

        def make(engname):
            def body(e):
                waited = {}
                for i in per_eng[engname]:
                    _, fn, rs, ws, dma, chan, _ = ops[i]
                    need = {}
                    for j in deps[i]:
                        s, v = ticket[j]
                        if need.get(s, 0) < v:
                            need[s] = v
                    for s, v in need.items():
                        if waited.get(s, 0) < v:
                            e.wait_ge(sems[s], v)
                            waited[s] = v
                    ins = fn(e)
                    if i in ticket:
                        s, v = ticket[i]
                        ins.then_inc(sems[s], 16 if dma else 1)
                if engname == "sp":
                    for s, v in final_counts.items():
                        if v > 0 and waited.get(s, 0) < v:
                            e.wait_ge(sems[s], v)
            return body

        block.tensor(make("pe"))
        block.scalar(make("act"))
        block.vector(make("dve"))
        block.gpsimd(make("pool"))
        block.sync(make("sp"))


class Bump:
    def __init__(self, ranges):
        self.ranges = [list(r) for r in ranges]

    def take(self, nb):
        nb = (nb + 31) // 32 * 32
        for r in self.ranges:
            if r[1] - r[0] >= nb:
                b0 = r[0]
                r[0] += nb
                return b0
        raise MemoryError(f"bump pool exhausted need {nb} have {self.ranges}")


def esize(dt):
    return 4 if dt == F32 else 2


def build_program(n_layers=DEPTH, do_sample=True, do_prompt=True, stop=None):
    nc = bass.Bass("TRN2", target_bir_lowering=False)
    P = Prog(nc)

    def din(name, shape):
        return nc.dram_tensor(name, list(shape), F32, kind="ExternalInput").ap()

    def dout(name, shape):
        return nc.dram_tensor(name, list(shape), F32, kind="ExternalOutput").ap()

    d_xs = din("x_s", [TS, D])
    d_xp = din("x_p", [NPS * TPS, D])
    d_cond = din("cond", [2, D])
    d_cckv = din("cache_ckv", [DEPTH, PAST, 128])
    d_ckr = din("cache_krope", [DEPTH, PAST, 32])
    d_st = din("state_ssd", [DEPTH, 2, 4, 64, 64])
    d_wada = din("w_ada", [DEPTH, D, 6 * D])
    d_bada = din("b_ada", [DEPTH, 6 * D])
    d_gmix = din("g_mix", [DEPTH, D])
    d_win = din("w_in", [DEPTH, D, IN_W])
    d_gq = din("g_q", [DEPTH, 256])
    d_wuq = din("w_uq", [DEPTH, 256, 768])
    d_gkv = din("g_kv", [DEPTH, 128])
    d_wukv = din("w_ukv", [DEPTH, 128, 1024])
    d_scw = din("ssd_conv_w", [DEPTH, 5, 512])
    d_scb = din("ssd_conv_b", [DEPTH, 512])
    d_dtb = din("ssd_dt_bias", [DEPTH * 8])
    d_alog = din("ssd_a_log", [DEPTH * 8])
    d_sd = din("ssd_d", [DEPTH * 8])
    d_sng = din("ssd_norm_g", [DEPTH, 256])
    d_ccw = din("cm_conv_w", [DEPTH, 31, 256])
    d_ccb = din("cm_conv_b", [DEPTH, 256])
    d_clg = din("cm_ln_g", [DEPTH, 256])
    d_clb = din("cm_ln_b", [DEPTH, 256])
    d_wout = din("w_out", [DEPTH, D, D])
    d_gffn = din("g_ffn", [DEPTH, D])
    d_wg = din("w_gate", [DEPTH, D, DFF])
    d_wu = din("w_up", [DEPTH, D, DFF])
    d_wd = din("w_down", [DEPTH, DFF, D])
    d_gfin = din("g_final", [D])
    d_cst = din("cst", [128, 640])
    d_rope = din("rope", [2, 128, TS])

    o_ys = dout("y_s", [TS, D])
    o_yp = dout("y_p", [NPS * TPS, D])
    o_ckv = dout("o_ckv", [NPS, DEPTH, TPS, 128])
    o_kr = dout("o_kr", [NPS, DEPTH, TPS, 32])
    o_ssd = dout("o_ssd", [NPS, DEPTH, 2, 4, 64, 64])

    es = ExitStack()
    with es:
        SBYTES = 212000
        S = es.enter_context(nc.sbuf_tensor("S", [128, SBYTES], U8))
        banks = [es.enter_context(nc.psum_tensor(f"PS{i}", [128, 512], F32)) for i in range(8)]
        for i in range(8):
            P.region(f"ps{i}", "psum", 0, 128, i * 2048, (i + 1) * 2048)
        PS = [View(f"ps{i}", banks[i][:, :]) for i in range(8)]
        PSB = [View(f"ps{i}", banks[i][:, :].bitcast(BF16)) for i in range(8)]

        uid = [0]

        def alloc(pool, name, shape, dt, p0=0):
            nel = int(np.prod(shape[1:]))
            nb = nel * esize(dt)
            b0 = pool.take(nb)
            ap = S[p0:p0 + shape[0], b0:b0 + nb].bitcast(dt)
            if len(shape) == 3:
                ap = ap.rearrange("p (a b) -> p a b", a=shape[1])
            elif len(shape) == 4:
                ap = ap.rearrange("p (a b c) -> p a b c", a=shape[1], b=shape[2])
            uid[0] += 1
            nm = f"{name}.{uid[0]}"
            P.region(nm, "sbuf", p0, p0 + shape[0], b0, b0 + ((nb + 31) // 32 * 32))
            return View(nm, ap)

        pers = Bump([(0, SBYTES)])
        xT = alloc(pers, "xT", [128, 8, TS], F32)
        RING_N = 4
        ring = [alloc(pers, f"ring{i}", [128, 4096], BF16) for i in range(RING_N)]
        wuq = alloc(pers, "wuq", [128, 2, 1024], BF16)
        wukv = alloc(pers, "wukv", [128, 1024], BF16)
        cstf = alloc(pers, "cstf", [128, 640], F32)
        identf = View(cstf.name, cstf.ap[:, 0:128])
        SU = [View(cstf.name, cstf.ap[:, 128:256]), View(cstf.name, cstf.ap[:, 256:384])]
        TRI = [View(cstf.name, cstf.ap[:, 384:512]), View(cstf.name, cstf.ap[:, 512:640])]
        identb = alloc(pers, "identb", [128, 128], BF16)
        onesb = alloc(pers, "onesb", [128, 128], BF16)
        onesf = alloc(pers, "onesf", [128, 128], F32)
        ropeC = alloc(pers, "ropeC", [128, TS], BF16)
        ropeS = alloc(pers, "ropeS", [128, TS], BF16)
        prm = alloc(pers, "prm", [128, DEPTH, NPRM], F32)
        cnd = alloc(pers, "cnd", [128, 24], F32)
        scond = alloc(pers, "scond", [128, 8, 2], BF16)
        modv = alloc(pers, "modv", [128, DEPTH, 48, 2], F32)
        gsc = alloc(pers, "gsc", [128, 2, 8], F32)
        dtb_b = alloc(pers, "dtb_b", [128, 32], F32)
        alog_b = alloc(pers, "alog_b", [128, 32], F32)
        dd_b = alloc(pers, "dd_b", [128, 32], F32)
        a_b = alloc(pers, "a_b", [128, 32], F32)
        dsum_b = alloc(pers, "dsum_b", [128, DEPTH, 4], F32)
        gkv_b = alloc(pers, "gkv_b", [128, 128], F32)
        xg = alloc(pers, "xg", [128, 8, 512], BF16)
        pers_end = pers.ranges[0][0]
        io = Bump([(pers_end, SBYTES)])
        gpad = alloc(io, "gpad", [128, 2, TS + 32], BF16)
        xbcp = alloc(io, "xbcp", [128, 4, TS + 4], BF16)
        szb = alloc(io, "szb", [128, 16, 256], BF16)
        qlat = alloc(io, "qlat", [128, 2, TS], BF16)
        ckvnT = alloc(io, "ckvnT", [128, PAST + TS], BF16)
        krT = alloc(io, "krT", [128, PAST + TS], BF16)
        dtr = alloc(io, "dtr", [128, 16, 8], F32)
        io_end = io.ranges[0][0]
        hT = alloc(Bump([(pers_end, SBYTES)]), "hT", [128, 8, TS], BF16)
        R = P.regions
        rg = lambda v: (R[v.name][3], R[v.name][4])
        SC0 = io_end
        print("mem: pers_end", pers_end, "io_end", io_end, "scratch", SBYTES - io_end)

        def dma(eng, out, in_, reads=(), writes=(), **kw):
            P.op(eng, lambda e: e.dma_start(out=out, in_=in_, **kw), reads=reads, writes=writes, dma=True)

        def mm(out, lhsT, rhs, start, stop, reads, w, fresh=False):
            P.op("pe", lambda e: e.matmul(out, lhsT=lhsT, rhs=rhs, start=start, stop=stop),
                 reads=reads, writes=[w], fresh=[w] if fresh else ())

        def tp(out, in_, ident, reads, w, fresh=False):
            P.op("pe", lambda e: e.transpose(out, in_, ident), reads=reads, writes=[w],
                 fresh=[w] if fresh else ())

        def act(out, in_, func, reads, writes, **kw):
            P.op("act", lambda e: e.activation(out=out, in_=in_, func=func, **kw), reads=reads, writes=writes)

        def tt(out, in0, in1, op, reads, writes, eng="dve"):
            P.op(eng, lambda e: e.tensor_tensor(out=out, in0=in0, in1=in1, op=op), reads=reads, writes=writes)

        def stt(out, in0, scalar, in1, op0, op1, reads, writes):
            P.op("dve", lambda e: e.scalar_tensor_tensor(out=out, in0=in0, scalar=scalar, in1=in1, op0=op0, op1=op1),
                 reads=reads, writes=writes)

        def ts(out, in0, s1, s2, op0, op1, reads, writes, eng="dve"):
            if op1 is None:
                P.op(eng, lambda e: e.tensor_scalar(out=out, in0=in0, scalar1=s1, scalar2=None, op0=op0),
                     reads=reads, writes=writes)
            else:
                P.op(eng, lambda e: e.tensor_scalar(out=out, in0=in0, scalar1=s1, scalar2=s2, op0=op0, op1=op1),
                     reads=reads, writes=writes)

        def cp(out, in_, reads, writes, eng="dve"):
            P.op(eng, lambda e: e.tensor_copy(out, in_), reads=reads, writes=writes)

        def memset(ap, val, writes, eng="dve"):
            P.op(eng, lambda e: e.memset(ap, val), writes=writes)

        def recip(out, in_, reads, writes):
            P.op("dve", lambda e: e.reciprocal(out=out, in_=in_), reads=reads, writes=writes)

        class Rot:
            def __init__(self, items):
                self.items, self.i = items, 0

            def next(self):
                v = self.items[self.i % len(self.items)]
                self.i += 1
                return v

        psA = Rot([0, 1, 2, 3])
        psBk = Rot([4, 5])

        dma("sp", cstf.ap, d_cst, writes=[cstf])
        dma("pool", ropeC.ap, d_rope[0], writes=[ropeC])
        dma("pool", ropeS.ap, d_rope[1], writes=[ropeS])
        memset(onesb.ap, 1.0, [onesb])
        memset(onesf.ap, 1.0, [onesf])
        cp(identb.ap, identf.ap, [cstf], [identb])
        dma("sp", dtb_b.ap, d_dtb.partition_broadcast(128), writes=[dtb_b])
        dma("sp", alog_b.ap, d_alog.partition_broadcast(128), writes=[alog_b])
        dma("sp", dd_b.ap, d_sd.partition_broadcast(128), writes=[dd_b])
        act(a_b.ap, alog_b.ap, AF.Exp, [alog_b], [a_b])
        ts(a_b.ap, a_b.ap, -1.0, None, ALU.mult, None, [a_b], [a_b])
        ddv = dd_b.ap.rearrange("p (l d h) -> p l d h", l=DEPTH, d=2)
        tt(dsum_b.ap, ddv[:, :, 0, :], ddv[:, :, 1, :], ALU.add, [dd_b], [dsum_b])

        prol = Bump([(SC0, SBYTES)])
        stg = [alloc(prol, f"stg{i}", [128, 128], F32) for i in range(2)]
        O_BADA, O_GMIX, O_GFFN, O_GQ, O_GKV, O_SCW, O_SCB, O_SNG, O_CCW, O_CCB, O_CLG, O_CLB = \
            0, 48, 56, 64, 66, 67, 87, 91, 93, 155, 157, 159
        for l in range(DEPTH):
            rows = [
                (d_bada[l].rearrange("(r c) -> r c", c=128), 48),
                (d_gmix[l].rearrange("(r c) -> r c", c=128), 8),
                (d_gffn[l].rearrange("(r c) -> r c", c=128), 8),
                (d_gq[l].rearrange("(r c) -> r c", c=128), 2),
                (d_gkv[l].rearrange("(r c) -> r c", c=128), 1),
                (d_scw[l].rearrange("j (r c) -> (j r) c", c=128), 20),
                (d_scb[l].rearrange("(r c) -> r c", c=128), 4),
                (d_sng[l].rearrange("(r c) -> r c", c=128), 2),
                (d_ccw[l].rearrange("j (r c) -> (j r) c", c=128), 62),
                (d_ccb[l].rearrange("(r c) -> r c", c=128), 2),
                (d_clg[l].rearrange("(r c) -> r c", c=128), 2),
                (d_clb[l].rearrange("(r c) -> r c", c=128), 2),
            ]
            r0 = 0
            for src, nr in rows:
                done = 0
                while done < nr:
                    si = (r0 + done) // 128
                    off = (r0 + done) % 128
                    k = min(nr - done, 128 - off)
                    dma("sp", stg[si].ap[off:off + k, :], src[done:done + k, :], writes=[stg[si]])
                    done += k
                r0 += nr
            assert r0 == NPRM
            for si, (c0, ncol) in enumerate([(0, 128), (128, NPRM - 128)]):
                pb = psA.next()
                tp(PS[pb].ap[:, 0:ncol], stg[si].ap[0:ncol, :], identf.ap[0:ncol, 0:ncol], [stg[si], cstf], PS[pb], fresh=True)
                cp(prm.ap[:, l, c0:c0 + ncol], PS[pb].ap[:, 0:ncol], [PS[pb]], [prm])
        dma("sp", stg[0].ap[0:16, :], d_cond.rearrange("a (r c) -> (a r) c", c=128), writes=[stg[0]])
        dma("sp", stg[0].ap[16:24, :], d_gfin.rearrange("(r c) -> r c", c=128), writes=[stg[0]])
        pb = psA.next()
        tp(PS[pb].ap[:, 0:24], stg[0].ap[0:24, :], identf.ap[0:24, 0:24], [stg[0], cstf], PS[pb], fresh=True)
        cp(cnd.ap, PS[pb].ap[:, 0:24], [PS[pb]], [cnd])
        act(scond.ap.rearrange("p k c -> p c k"), cnd.ap[:, 0:16].rearrange("p (c k) -> p c k", c=2), AF.Silu, [cnd], [scond])

        ring_i = [0]

        def load_piece(srcs):
            v = ring[ring_i[0] % RING_N]
            ring_i[0] += 1
            for dst_fn, src in srcs:
                dma("pool", dst_fn(v.ap), src, writes=[v])
            return v

        def r3(ap, a):
            return ap.rearrange("p (a b) -> p a b", a=a)

        for l in range(n_layers):
            for pc in range(12):
                v = load_piece([(lambda a: r3(a, 8), d_wada[l, :, pc * 512:(pc + 1) * 512].rearrange("(k p) c -> p k c", p=128))])
                w3 = r3(v.ap, 8)
                for q in range(4):
                    oc = pc * 4 + q
                    for kc in range(8):
                        mm(PS[7].ap[:, oc * 2:oc * 2 + 2], w3[:, kc, q * 128:(q + 1) * 128], scond.ap[:, kc, :],
                           kc == 0, kc == 7, [v, scond], PS[7], fresh=(oc == 0 and kc == 0))
            tt(modv.ap[:, l], PS[7].ap[:, 0:96].rearrange("p (o c) -> p o c", c=2),
               prm.ap[:, l, O_BADA:O_BADA + 48].unsqueeze(2).to_broadcast([128, 48, 2]), ALU.add, [PS[7], prm], [modv])

        class Cfg:
            pass

        def make_cfg(kind):
            c = Cfg()
            c.kind = kind
            if kind == "S":
                c.T, c.TT, c.seqs, c.rope, c.ctx, c.cond = TS, 512, [(0, TS)], True, True, 0
                c.dx, c.oy = d_xs, o_ys
            else:
                c.T, c.TT, c.seqs, c.rope, c.ctx, c.cond = NPS * TPS, 256, [(0, TPS), (TPS, TPS)], False, False, 1
                c.dx, c.oy = d_xp, o_yp
            c.tiles = []
            for si, (s0, sl) in enumerate(c.seqs):
                for t in range(s0, s0 + sl, c.TT):
                    c.tiles.append((si, t))
            c.nblk = c.T // 128
            return c

        def load_x(c):
            pool = Bump([(SC0, SBYTES)])
            xs_ = [alloc(pool, f"xstg{i}", [128, D], F32) for i in range(2)]
            for b in range(c.nblk):
                st = xs_[b % 2]
                dma("sp", st.ap, c.dx[b * 128:(b + 1) * 128, :], writes=[st])
                for half in range(2):
                    pb = psA.next()
                    for q in range(4):
                        kc = half * 4 + q
                        tp(PS[pb].ap[:, q * 128:(q + 1) * 128], st.ap[:, kc * 128:(kc + 1) * 128], identf.ap,
                           [st, cstf], PS[pb], fresh=(q == 0))
                    cp(xT.ap[:, half * 4:half * 4 + 4, b * 128:(b + 1) * 128],
                       PS[pb].ap.rearrange("p (q t) -> p q t", q=4), [PS[pb]], [xT])

        def norm_tile(c, pool_views, t0, n, gsc_ap, sh_ap, out_fn, out_v, dt_out_bf=True):
            sq, sd, rstd, tmpb = pool_views
            for kc in range(8):
                q = sq[kc % 2]
                act(q.ap[:, 0:n], xT.ap[:, kc, t0:t0 + n], AF.Square, [xT], [q])
                mm(PS[6].ap[:, 0:n], onesb.ap, q.ap[:, 0:n], kc == 0, kc == 7, [onesb, q], PS[6], fresh=(kc == 0))
            act(sd.ap[:, 0:n], PS[6].ap[:, 0:n], AF.Ln, [PS[6]], [sd], scale=1.0 / D, bias=1e-6)
            act(rstd.ap[:, 0:n], sd.ap[:, 0:n], AF.Exp, [sd], [rstd], scale=-0.5)
            for kc in range(8):
                tb = tmpb[kc % 2]
                tt(tb.ap[:, 0:n], xT.ap[:, kc, t0:t0 + n], rstd.ap[:, 0:n], ALU.mult, [xT, rstd], [tb])
                if sh_ap is None:
                    ts(out_fn(kc), tb.ap[:, 0:n], gsc_ap[:, kc:kc + 1], None, ALU.mult, None, [tb, cnd], [out_v])
                else:
                    act(out_fn(kc), tb.ap[:, 0:n], AF.Identity, [tb, gsc, modv], [out_v],
                        scale=gsc_ap[:, kc:kc + 1], bias=sh_ap[:, kc:kc + 1])

        def xupdate(c, l, t0, n, lhs_fn, rhs_list, g_off, reads, evac=None):
            for oc in range(8):
                pb = psA.next()
                nj = len(rhs_list)
                for j in range(nj):
                    mm(PS[pb].ap[:, 0:n], lhs_fn(j, oc), rhs_list[j], j == 0, j == nj - 1, reads, PS[pb], fresh=(j == 0))
                gcol = modv.ap[:, l, g_off + oc, c.cond:c.cond + 1]
                if evac is None:
                    stt(xT.ap[:, oc, t0:t0 + n], PS[pb].ap[:, 0:n], gcol,
                        xT.ap[:, oc, t0:t0 + n], ALU.mult, ALU.add, [PS[pb], modv, xT], [xT])
                else:
                    ev = evac[oc % len(evac)]
                    act(ev.ap[:, 0:n], PS[pb].ap[:, 0:n], AF.Copy, [PS[pb], modv], [ev], scale=gcol)
                    tt(xT.ap[:, oc, t0:t0 + n], xT.ap[:, oc, t0:t0 + n], ev.ap[:, 0:n], ALU.add, [xT, ev], [xT], eng="pool")

        def layer_pass(c, l):
            T, TT = c.T, c.TT
            cd = c.cond
            for i, (og, osc) in enumerate([(O_GMIX, 8), (O_GFFN, 32)]):
                stt(gsc.ap[:, i, :], modv.ap[:, l, osc:osc + 8, cd], 1.0, prm.ap[:, l, og:og + 8], ALU.add, ALU.mult,
                    [modv, prm], [gsc])
            sh1 = modv.ap[:, l, 0:8, cd]
            sh2 = modv.ap[:, l, 24:32, cd]
            dma("sp", gkv_b.ap, d_gkv[l].partition_broadcast(128), writes=[gkv_b])
            dma("pool", wuq.ap[:, :, 0:768], d_wuq[l].rearrange("(k p) c -> p k c", p=128), writes=[wuq])
            dma("pool", wukv.ap, d_wukv[l], writes=[wukv])
            for kc in range(2):
                ts(wuq.ap[:, kc, 0:768], wuq.ap[:, kc, 0:768], prm.ap[:, l, O_GQ + kc:O_GQ + kc + 1], None, ALU.mult, None,
                   [wuq, prm], [wuq])
                src = wuq.ap[:, kc, 0:768].rearrange("p (h c) -> p h c", h=8)[:, :, 64:96].rearrange("p h (a f) -> p h a f", a=2)
                dst = wuq.ap[:, kc, 768:1024].rearrange("p (h a f) -> p h a f", h=8, a=2)
                for a in range(2):
                    ts(dst[:, :, a, 0:8], src[:, :, a, 8:16], -1.0, None, ALU.mult, None, [wuq], [wuq])
                    cp(dst[:, :, a, 8:16], src[:, :, a, 0:8], [wuq], [wuq])

            wl = d_win[l]

            def wsl(c0, c1):
                return wl[:, c0:c1].rearrange("(k p) c -> p k c", p=128)

            v_cm = load_piece([(lambda a: r3(a, 8), wsl(1192, 1704))])
            v_ssd = load_piece([(lambda a: r3(a, 8), wsl(672, 1184))])
            v_misc = load_piece([(lambda a: r3(a, 8)[:, :, 0:256], wsl(416, 672)),
                                 (lambda a: r3(a, 8)[:, :, 256:416], wsl(256, 416)),
                                 (lambda a: r3(a, 8)[:, :, 448:456], wsl(1184, 1192))])
            v_q = load_piece([(lambda a: r3(a, 8)[:, :, 0:256], wsl(0, 256))])
            w_cm, w_ssd, w_misc, w_q = r3(v_cm.ap, 8), r3(v_ssd.ap, 8), r3(v_misc.ap, 8), r3(v_q.ap, 8)
            for kc in range(8):
                src = w_misc[:, kc, 384:416].rearrange("p (a f) -> p a f", a=2)
                dst = w_misc[:, kc, 416:448].rearrange("p (a f) -> p a f", a=2)
                ts(dst[:, :, 0:8], src[:, :, 8:16], -1.0, None, ALU.mult, None, [v_misc], [v_misc])
                cp(dst[:, :, 8:16], src[:, :, 0:8], [v_misc], [v_misc])

            sp = Bump([(SC0, SBYTES)])
            sq = [alloc(sp, f"sq{i}", [128, 512], BF16) for i in range(2)]
            sd = alloc(sp, "sd", [128, 512], F32)
            rstd = alloc(sp, "rstd", [128, 512], F32)
            tmpb = [alloc(sp, f"tmpb{i}", [128, 512], F32) for i in range(2)]
            sig = alloc(sp, "sig", [128, 512], F32)
            sqk = alloc(sp, "sqk", [128, 512], BF16)
            t1 = alloc(sp, "t1", [128, 512], F32)
            t2 = alloc(sp, "t2", [128, 512], F32)
            rq = alloc(sp, "rq", [128, 512], F32)
            tmo = alloc(sp, "tmo", [128, 160], F32)
            tmo2 = alloc(sp, "tmo2", [128, 128], F32)
            ssq = alloc(sp, "ssq", [128, 2], F32)
            junk = alloc(sp, "junk", [128, 256], F32)
            koff = PAST if c.ctx else 0

            for si, (s0, sl) in enumerate(c.seqs):
                g0 = s0 + si * 32
                memset(gpad.ap[:, :, g0:g0 + 16], 0.0, [gpad])
                memset(gpad.ap[:, :, g0 + 16 + sl:g0 + 32 + sl], 0.0, [gpad])
                x0 = s0 + si * 4
                memset(xbcp.ap[:, :, x0:x0 + 2], 0.0, [xbcp])
                memset(xbcp.ap[:, :, x0 + 2 + sl:x0 + 4 + sl], 0.0, [xbcp])

            if c.ctx:
                cst_ = alloc(sp, "cstg", [128, 128], F32)
                kst_ = alloc(sp, "kstg", [128, 96], F32)
                memset(kst_.ap, 0.0, [kst_])
                for b in range(2):
                    dma("sp", cst_.ap, d_cckv[l, b * 128:(b + 1) * 128, :], writes=[cst_])
                    pb = psA.next()
                    tp(PS[pb].ap[:, 0:128], cst_.ap, identf.ap, [cst_, cstf], PS[pb], fresh=True)
                    cp(ckvnT.ap[:, b * 128:(b + 1) * 128], PS[pb].ap[:, 0:128], [PS[pb]], [ckvnT])
                    dma("sp", kst_.ap[:, 64:96], d_ckr[l, b * 128:(b + 1) * 128, :], writes=[kst_])
                    pb = psA.next()
                    tp(PS[pb].ap[0:96, 0:128], kst_.ap, identf.ap, [kst_, cstf], PS[pb], fresh=True)
                    cp(krT.ap[64:96, b * 128:(b + 1) * 128], PS[pb].ap[64:96, 0:128], [PS[pb]], [krT])

            def fm(wap, c0, m, n, reads, out_rows=None):
                pb = psA.next()
                o = PS[pb].ap[0:m, 0:n] if out_rows is None else PS[pb].ap[out_rows[0]:out_rows[1], 0:n]
                for kc in range(8):
                    mm(o, wap[:, kc, c0:c0 + m], xg.ap[:, kc, 0:n], kc == 0, kc == 7, reads + [xg], PS[pb], fresh=(kc == 0))
                return pb

            for (si, t0) in c.tiles:
                n = TT
                s0, sl = c.seqs[si]
                norm_tile(c, (sq, sd, rstd, tmpb), t0, n, gsc.ap[:, 0, :], sh1, lambda kc: xg.ap[:, kc, 0:n], xg)
                gofs = si * 32 + 16 + t0
                for j in range(2):
                    pa = fm(w_cm, j * 128, 128, n, [v_cm])
                    pbk = fm(w_cm, 256 + j * 128, 128, n, [v_cm])
                    act(sig.ap[:, 0:n], PS[pbk].ap[:, 0:n], AF.Sigmoid, [PS[pbk]], [sig])
                    tt(gpad.ap[:, j, gofs:gofs + n], PS[pa].ap[:, 0:n], sig.ap[:, 0:n], ALU.mult, [PS[pa], sig], [gpad])
                xofs = si * 4 + 2 + t0
                for j in range(4):
                    pa = fm(w_ssd, j * 128, 128, n, [v_ssd])
                    act(xbcp.ap[:, j, xofs:xofs + n], PS[pa].ap[:, 0:n], AF.Copy, [PS[pa]], [xbcp])
                for b in range(n // 128):
                    blk = (t0 + b * 128) // 128
                    pb = psA.next()
                    for kc in range(8):
                        mm(PS[pb].ap[:, 0:256], xg.ap[:, kc, b * 128:(b + 1) * 128], w_misc[:, kc, 0:256], kc == 0, kc == 7,
                           [xg, v_misc], PS[pb], fresh=(kc == 0))
                    act(szb.ap[:, blk, :], PS[pb].ap[:, 0:256], AF.Silu, [PS[pb]], [szb])
                    pb = psA.next()
                    for kc in range(8):
                        mm(PS[pb].ap[:, 0:8], xg.ap[:, kc, b * 128:(b + 1) * 128], w_misc[:, kc, 448:456], kc == 0, kc == 7,
                           [xg, v_misc], PS[pb], fresh=(kc == 0))
                    tt(dtr.ap[:, blk, :], PS[pb].ap[:, 0:8], dtb_b.ap[:, l * 8:(l + 1) * 8], ALU.add, [PS[pb], dtb_b], [dtr])
                    if c.kind == "P":
                        pb = psA.next()
                        for kc in range(8):
                            mm(PS[pb].ap[:, 0:160], xg.ap[:, kc, b * 128:(b + 1) * 128], w_misc[:, kc, 256:416], kc == 0, kc == 7,
                               [xg, v_misc], PS[pb], fresh=(kc == 0))
                        cp(tmo.ap, PS[pb].ap[:, 0:160], [PS[pb]], [tmo])
                        tloc = t0 - s0 + b * 128
                        dma("sp", o_kr[si, l, tloc:tloc + 128, :], tmo.ap[:, 128:160], reads=[tmo])
                        memset(ssq.ap[:, 0:1], 0.0, [ssq])
                        act(junk.ap[:, 0:128], tmo.ap[:, 0:128], AF.Square, [tmo, ssq], [junk, ssq], accum_out=ssq.ap[:, 0:1])
                        act(ssq.ap[:, 1:2], ssq.ap[:, 0:1], AF.Ln, [ssq], [ssq], scale=1.0 / 128, bias=1e-6)
                        act(ssq.ap[:, 1:2], ssq.ap[:, 1:2], AF.Exp, [ssq], [ssq], scale=-0.5)
                        stt(tmo2.ap, tmo.ap[:, 0:128], ssq.ap[:, 1:2], gkv_b.ap, ALU.mult, ALU.mult, [tmo, ssq, gkv_b], [tmo2])
                        dma("sp", o_ckv[si, l, tloc:tloc + 128, :], tmo2.ap, reads=[tmo2])
                pa = fm(w_misc, 256, 128, n, [v_misc])
                act(sqk.ap[:, 0:n], PS[pa].ap[:, 0:n], AF.Square, [PS[pa]], [sqk])
                mm(PS[6].ap[:, 0:n], onesb.ap, sqk.ap[:, 0:n], True, True, [onesb, sqk], PS[6], fresh=True)
                act(sd.ap[:, 0:n], PS[6].ap[:, 0:n], AF.Ln, [PS[6]], [sd], scale=1.0 / 128, bias=1e-6)
                act(rstd.ap[:, 0:n], sd.ap[:, 0:n], AF.Exp, [sd], [rstd], scale=-0.5)
                kofs = koff + t0 if c.ctx else t0
                stt(ckvnT.ap[:, kofs:kofs + n], PS[pa].ap[:, 0:n], prm.ap[:, l, O_GKV:O_GKV + 1], rstd.ap[:, 0:n],
                    ALU.mult, ALU.mult, [PS[pa], prm, rstd], [ckvnT])
                pa = fm(w_misc, 384, 32, n, [v_misc], out_rows=(64, 96))
                if c.rope:
                    pbk = fm(w_misc, 416, 32, n, [v_misc], out_rows=(64, 96))
                    tt(t1.ap[64:96, 0:n], PS[pa].ap[64:96, 0:n], ropeC.ap[64:96, t0:t0 + n], ALU.mult, [PS[pa], ropeC], [t1])
                    tt(t2.ap[64:96, 0:n], PS[pbk].ap[64:96, 0:n], ropeS.ap[64:96, t0:t0 + n], ALU.mult, [PS[pbk], ropeS], [t2])
                    tt(krT.ap[64:96, kofs:kofs + n], t1.ap[64:96, 0:n], t2.ap[64:96, 0:n], ALU.add, [t1, t2], [krT])
                else:
                    act(krT.ap[64:96, kofs:kofs + n], PS[pa].ap[64:96, 0:n], AF.Copy, [PS[pa]], [krT])
                pq = [fm(w_q, j * 128, 128, n, [v_q]) for j in range(2)]
                for j in range(2):
                    act(sq[j].ap[:, 0:n], PS[pq[j]].ap[:, 0:n], AF.Square, [PS[pq[j]]], [sq[j]])
                    mm(PS[6].ap[:, 0:n], onesb.ap, sq[j].ap[:, 0:n], j == 0, j == 1, [onesb, sq[j]], PS[6], fresh=(j == 0))
                act(sd.ap[:, 0:n], PS[6].ap[:, 0:n], AF.Ln, [PS[6]], [sd], scale=1.0 / 256, bias=1e-6)
                act(rq.ap[:, 0:n], sd.ap[:, 0:n], AF.Exp, [sd], [rq], scale=-0.5)
                for j in range(2):
                    tt(qlat.ap[:, j, t0:t0 + n], PS[pq[j]].ap[:, 0:n], rq.ap[:, 0:n], ALU.mult, [PS[pq[j]], rq], [qlat])

            if stop == "I":
                return
            v_wor = load_piece([(lambda a: r3(a, 4), d_wout[l, 512:1024, :].rearrange("(k p) c -> p k c", p=128))])
            w_or = r3(v_wor.ap, 4)
            for j in range(2):
                ts(w_or[:, j, :], w_or[:, j, :], prm.ap[:, l, O_SNG + j:O_SNG + j + 1], None, ALU.mult, None, [v_wor, prm], [v_wor])

            sp = Bump([(SC0, SBYTES), rg(xg)])
            dg = alloc(sp, "dg", [128, 2, 31, 128], BF16)
            cvf = [alloc(sp, f"cvf{j}", [128, 512], F32) for j in range(2)]
            sqf = [alloc(sp, f"sqf{j}", [128, 512], F32) for j in range(2)]
            mean = alloc(sp, "mean", [128, 512], F32)
            var = alloc(sp, "var", [128, 512], F32)
            rr = alloc(sp, "rr", [128, 512], F32)
            uu = alloc(sp, "uu", [128, 512], F32)
            cmix = alloc(sp, "cmix", [128, 2, 512], BF16)
            for j in range(2):
                for tap in range(31):
                    col = O_CCW + tap * 2 + j
                    ts(dg.ap[:, j, tap, :], identb.ap, prm.ap[:, l, col:col + 1], None, ALU.mult, None, [identb, prm], [dg])
            for (si, t0) in c.tiles:
                n = TT
                gofs = si * 32 + 1 + t0
                for j in range(2):
                    pb = psBk.next()
                    for tap in range(31):
                        mm(PS[pb].ap[:, 0:n], dg.ap[:, j, tap, :], gpad.ap[:, j, gofs + tap:gofs + tap + n], tap == 0, tap == 30,
                           [dg, gpad], PS[pb], fresh=(tap == 0))
                    bcol = prm.ap[:, l, O_CCB + j:O_CCB + j + 1]
                    act(cvf[j].ap[:, 0:n], PS[pb].ap[:, 0:n], AF.Identity, [PS[pb], prm], [cvf[j]], bias=bcol)
                    act(sqf[j].ap[:, 0:n], PS[pb].ap[:, 0:n], AF.Square, [PS[pb], prm], [sqf[j]], bias=bcol)
                for j in range(2):
                    mm(PS[6].ap[:, 0:n], onesf.ap, cvf[j].ap[:, 0:n], j == 0, j == 1, [onesf, cvf[j]], PS[6], fresh=(j == 0))
                for j in range(2):
                    mm(PS[7].ap[:, 0:n], onesf.ap, sqf[j].ap[:, 0:n], j == 0, j == 1, [onesf, sqf[j]], PS[7], fresh=(j == 0))
                ts(mean.ap[:, 0:n], PS[6].ap[:, 0:n], 1.0 / 256, None, ALU.mult, None, [PS[6]], [mean])
                tt(var.ap[:, 0:n], mean.ap[:, 0:n], mean.ap[:, 0:n], ALU.mult, [mean], [var])
                stt(var.ap[:, 0:n], PS[7].ap[:, 0:n], 1.0 / 256, var.ap[:, 0:n], ALU.mult, ALU.subtract, [PS[7], var], [var])
                act(var.ap[:, 0:n], var.ap[:, 0:n], AF.Ln, [var], [var], bias=1e-5)
                act(rr.ap[:, 0:n], var.ap[:, 0:n], AF.Exp, [var], [rr], scale=-0.5)
                for j in range(2):
                    tt(uu.ap[:, 0:n], cvf[j].ap[:, 0:n], mean.ap[:, 0:n], ALU.subtract, [cvf[j], mean], [uu])
                    tt(uu.ap[:, 0:n], uu.ap[:, 0:n], rr.ap[:, 0:n], ALU.mult, [uu, rr], [uu])
                    act(cmix.ap[:, j, 0:n], uu.ap[:, 0:n], AF.Silu, [uu, prm], [cmix],
                        scale=prm.ap[:, l, O_CLG + j:O_CLG + j + 1], bias=prm.ap[:, l, O_CLB + j:O_CLB + j + 1])
                xupdate(c, l, t0, n, lambda j, oc: w_or[:, 2 + j, oc * 128:(oc + 1) * 128],
                        [cmix.ap[:, 0, 0:n], cmix.ap[:, 1, 0:n]], 16, [v_wor, cmix])

            if stop == "C":
                return
            ssd_phase(c, l, w_or, v_wor)
            if stop in ("S", "S1", "S2", "S3"):
                return
            v_woa = load_piece([(lambda a: r3(a, 4), d_wout[l, 0:512, :].rearrange("(k p) c -> p k c", p=128))])
            mla_phase(c, l, r3(v_woa.ap, 4), v_woa)
            if stop == "M":
                return
            ffn_phase(c, l)

        def ssd_phase(c, l, w_or, v_wor):
            T, TT = c.T, c.TT
            nblk = c.nblk
            g_r, x_r, z_r = rg(gpad), rg(xbcp), rg(szb)
            sp = Bump([(SC0, SBYTES), g_r])
            dg5 = alloc(sp, "dg5", [128, 4, 5, 128], BF16)
            xsT = alloc(sp, "xsT", [128, 2, 512], BF16)
            BCt = alloc(sp, "BCt", [128, 3, TS], BF16)
            xs_tm = alloc(sp, "xs_tm", [128, 16, 256], BF16)
            B_tm = alloc(sp, "B_tm", [128, 16, 128], BF16)
            dt = alloc(sp, "dt", [128, 16, 8], F32)
            dta = alloc(sp, "dta", [128, 16, 8], F32)
            hTf = [alloc(sp, f"hTf{d}", [128, 2, 64], F32) for d in range(2)]
            hTb = alloc(sp, "hTb", [128, 2, 64], BF16)
            sstg = alloc(sp, "sstg", [128, 128], F32)
            sp2 = Bump([tuple(r) for r in sp.ranges] + [x_r, rg(xg)])
            nb8 = nblk * 8
            dtf = dtr.ap.rearrange("p b e -> p (b e)")[:, 0:nb8]
            act(dt.ap.rearrange("p b e -> p (b e)")[:, 0:nb8], dtf, AF.Exp, [dtr], [dt])
            act(dt.ap.rearrange("p b e -> p (b e)")[:, 0:nb8], dt.ap.rearrange("p b e -> p (b e)")[:, 0:nb8], AF.Ln, [dt], [dt], bias=1.0)
            tt(dta.ap[:, 0:nblk, :], dt.ap[:, 0:nblk, :], a_b.ap[:, l * 8:(l + 1) * 8].unsqueeze(1).to_broadcast([128, nblk, 8]),
               ALU.mult, [dt, a_b], [dta])
            memset(BCt.ap[64:128, 1, 0:T], 0.0, [BCt])
            memset(BCt.ap[0:64, 2, 0:T], 0.0, [BCt])
            for ch in range(4):
                for tap in range(5):
                    col = O_SCW + tap * 4 + ch
                    ts(dg5.ap[:, ch, tap, :], identb.ap, prm.ap[:, l, col:col + 1], None, ALU.mult, None, [identb, prm], [dg5])
            for (si, t0) in c.tiles:
                n = TT
                xofs = si * 4 + t0
                for ch in range(4):
                    pb = psBk.next()
                    for tap in range(5):
                        mm(PS[pb].ap[:, 0:n], dg5.ap[:, ch, tap, :], xbcp.ap[:, ch, xofs + tap:xofs + tap + n], tap == 0, tap == 4,
                           [dg5, xbcp], PS[pb], fresh=(tap == 0))
                    bias_ = prm.ap[:, l, O_SCB + ch:O_SCB + ch + 1]
                    if ch < 2:
                        act(xsT.ap[:, ch, 0:n], PS[pb].ap[:, 0:n], AF.Silu, [PS[pb], prm], [xsT], bias=bias_)
                    elif ch == 2:
                        act(BCt.ap[:, 0, t0:t0 + n], PS[pb].ap[:, 0:n], AF.Silu, [PS[pb], prm], [BCt], bias=bias_)
                    else:
                        act(BCt.ap[0:64, 1, t0:t0 + n], PS[pb].ap[0:64, 0:n], AF.Silu, [PS[pb], prm], [BCt], bias=bias_[0:64])
                        act(BCt.ap[64:128, 2, t0:t0 + n], PS[pb].ap[64:128, 0:n], AF.Silu, [PS[pb], prm], [BCt], bias=bias_[64:128])
                for b in range(n // 128):
                    blk = (t0 + b * 128) // 128
                    pb = psA.next()
                    for j in range(2):
                        tp(PSB[pb].ap[:, j * 128:(j + 1) * 128], xsT.ap[:, j, b * 128:(b + 1) * 128], identb.ap, [xsT, identb], PS[pb], fresh=(j == 0))
                    tp(PSB[pb].ap[:, 256:384], BCt.ap[:, 0, t0 + b * 128:t0 + (b + 1) * 128], identb.ap, [BCt, identb], PS[pb])
                    cp(xs_tm.ap[:, blk, :], PSB[pb].ap[:, 0:256], [PS[pb]], [xs_tm])
                    cp(B_tm.ap[:, blk, :], PSB[pb].ap[:, 256:384], [PS[pb]], [B_tm])
            if stop == "S1":
                return
            hst = alloc(sp2, "hst", [128, 16, 2, 64], BF16)
            Gm = [alloc(sp2, f"Gm{d}", [128, 2, 128], F32) for d in range(2)]
            Lm = alloc(sp2, "Lm", [128, 4, 128], F32)
            seg = alloc(sp2, "seg", [128, 4, 128], F32)
            MT = [[alloc(sp2, f"MT{i}{d}", [128, 4, 128], BF16) for d in range(2)] for i in range(2)]
            xdt = [[alloc(sp2, f"xdt{i}{d}", [128, 4, 64], BF16) for d in range(2)] for i in range(2)]
            ee = [alloc(sp2, f"ee{i}", [128, 32], F32) for i in range(2)]
            cds = [alloc(sp2, f"cds{i}", [128, 2, 2], F32) for i in range(2)]
            xdd = [alloc(sp2, f"xdd{i}", [128, 4, 64], BF16) for i in range(2)]
            wv = alloc(sp2, "wv", [128, 4], F32)
            yo = alloc(sp2, "yo", [128, 8, 64], F32)
            y1 = alloc(sp2, "y1", [128, 256], F32)
            y2 = alloc(sp2, "y2", [128, 256], F32)
            y3 = alloc(sp2, "y3", [128, 256], F32)
            yn = [alloc(sp2, f"yn{i}", [128, 256], BF16) for i in range(2)]
            ssq = alloc(sp2, "ssq2", [128, 2], F32)
            smix = alloc(sp2, "smix", [128, 2, 512], BF16)
            junk = yo
            evb = [View(Lm.name, Lm.ap.rearrange("p h s -> p (h s)")), View(seg.name, seg.ap.rearrange("p h s -> p (h s)"))]

            def small_mm(blk, dirs, ee_, cds_):
                if len(dirs) == 2:
                    A = dta.ap[:, blk, 0:8]
                    for k, (lt, lv) in enumerate([(SU[0].ap, cstf), (SU[1].ap, cstf), (TRI[0].ap, cstf), (TRI[1].ap, cstf), (onesf.ap, onesf)]):
                        mm(PS[6].ap[:, k * 8:k * 8 + 8], lt, A, True, True, [lv, dta], PS[6], fresh=(k == 0))
                    act(ee_.ap[:, 0:32], PS[6].ap[:, 0:32], AF.Exp, [PS[6]], [ee_])
                else:
                    A = dta.ap[:, blk, 4:8]
                    mm(PS[6].ap[:, 12:16], SU[1].ap, A, True, True, [cstf, dta], PS[6], fresh=True)
                    mm(PS[6].ap[:, 36:40], onesf.ap, A, True, True, [onesf, dta], PS[6])
                    act(ee_.ap[:, 12:16], PS[6].ap[:, 12:16], AF.Exp, [PS[6]], [ee_])
                for d in dirs:
                    act(cds_.ap[0:64, d, :], PS[6].ap[0:64, 32 + d * 4:34 + d * 4], AF.Exp, [PS[6]], [cds_])
                    act(cds_.ap[64:128, d, :], PS[6].ap[64:128, 34 + d * 4:36 + d * 4], AF.Exp, [PS[6]], [cds_])

            DEC = [slice(0, 4), slice(12, 16)]

            def state_pre(blk, d, ee_, xdd_):
                tt(wv.ap, dt.ap[:, blk, d * 4:(d + 1) * 4], ee_.ap[:, DEC[d]], ALU.mult, [dt, ee_], [wv])
                tt(xdd_.ap, xs_tm.ap[:, blk, :].rearrange("p (h q) -> p h q", h=4), wv.ap.unsqueeze(2).to_broadcast([128, 4, 64]),
                   ALU.mult, [xs_tm, wv], [xdd_])

            def state_post(blk, d, cds_, xdd_):
                pb = 7
                for h in range(4):
                    g, j = h // 2, h % 2
                    mm(PS[pb].ap[g * 64:(g + 1) * 64, j * 64:(j + 1) * 64], B_tm.ap[:, blk, g * 64:(g + 1) * 64], xdd_.ap[:, h, :],
                       True, True, [B_tm, xdd_], PS[pb], fresh=(h == 0))
                for j in range(2):
                    stt(hTf[d].ap[:, j, :], hTf[d].ap[:, j, :], cds_.ap[:, d, j:j + 1], PS[pb].ap[:, j * 64:(j + 1) * 64],
                        ALU.mult, ALU.add, [hTf[d], cds_, PS[pb]], [hTf[d]])

            def main_pre(blk, i):
                tk = slice(blk * 128, (blk + 1) * 128)
                pg = psA.next()
                mm(PS[pg].ap[:, 0:256], BCt.ap[:, 0, tk], BCt.ap[:, 1:3, tk], True, True, [BCt], PS[pg], fresh=True)
                small_mm(blk, [0, 1], ee[i], cds[i])
                for d in range(2):
                    tt(Gm[d].ap, PS[pg].ap[:, 0:256].rearrange("p (g s) -> p g s", g=2),
                       TRI[d].ap.unsqueeze(1).to_broadcast([128, 2, 128]), ALU.mult, [PS[pg], cstf], [Gm[d]])
                for d in range(2):
                    tt(Lm.ap, SU[d].ap.unsqueeze(1).to_broadcast([128, 4, 128]),
                       dta.ap[:, blk, d * 4:(d + 1) * 4].unsqueeze(2).to_broadcast([128, 4, 128]), ALU.mult, [cstf, dta], [Lm], eng="pool")
                    pdf = psA.next()
                    for h in range(4):
                        mm(PS[pdf].ap[:, h * 128:(h + 1) * 128], Lm.ap[:, h, :], TRI[d].ap, True, True, [Lm, cstf], PS[pdf], fresh=(h == 0))
                    act(seg.ap.rearrange("p h s -> p (h s)"), PS[pdf].ap, AF.Exp, [PS[pdf]], [seg])
                    for g in range(2):
                        tt(MT[i][d].ap[:, 2 * g:2 * g + 2, :], seg.ap[:, 2 * g:2 * g + 2, :],
                           Gm[d].ap[:, g:g + 1, :].to_broadcast([128, 2, 128]), ALU.mult, [seg, Gm[d]], [MT[i][d]])
                    tt(xdt[i][d].ap, xs_tm.ap[:, blk, :].rearrange("p (h q) -> p h q", h=4),
                       dt.ap[:, blk, d * 4:(d + 1) * 4].unsqueeze(2).to_broadcast([128, 4, 64]), ALU.mult, [xs_tm, dt], [xdt[i][d]], eng="pool")
                state_pre(blk, 0, ee[i], xdd[i])

            def main_post(blk, i):
                tk = slice(blk * 128, (blk + 1) * 128)
                tt(y3.ap.rearrange("p (h q) -> p h q", h=4), xs_tm.ap[:, blk, :].rearrange("p (h q) -> p h q", h=4),
                   dsum_b.ap[:, l, :].unsqueeze(2).to_broadcast([128, 4, 64]), ALU.mult, [xs_tm, dsum_b], [y3], eng="pool")
                py = psBk.next()
                for h in range(4):
                    for d in range(2):
                        mm(PS[py].ap[:, h * 64:(h + 1) * 64], MT[i][d].ap[:, h, :], xdt[i][d].ap[:, h, :], d == 0, d == 1,
                           [MT[i][d], xdt[i][d]], PS[py], fresh=(h == 0 and d == 0))
                po = psBk.next()
                for d in range(2):
                    hsrc_v = hTb if d == 0 else hst
                    hsrc = hTb.ap if d == 0 else hst.ap[:, blk]
                    for g in range(2):
                        mm(PS[po].ap[:, d * 256 + g * 128:d * 256 + (g + 1) * 128], BCt.ap[:, 1 + g, tk], hsrc,
                           True, True, [BCt, hsrc_v], PS[po], fresh=(d == 0 and g == 0))
                state_post(blk, 0, cds[i], xdd[i])
                cp(hTb.ap, hTf[0].ap, [hTf[0]], [hTb], eng="pool")
                for d in range(2):
                    e0 = 16 + 12 * d
                    tt(yo.ap[:, 4 * d:4 * d + 4, :], PS[po].ap[:, d * 256:(d + 1) * 256].rearrange("p (e q) -> p e q", e=4),
                       ee[i].ap[:, e0:e0 + 4].unsqueeze(2).to_broadcast([128, 4, 64]), ALU.mult, [PS[po], ee[i]], [yo])
                yof = yo.ap.rearrange("p e q -> p (e q)")
                tt(y1.ap, yof[:, 0:256], yof[:, 256:512], ALU.add, [yo], [y1])
                tt(y1.ap, y1.ap, PS[py].ap[:, 0:256], ALU.add, [y1, PS[py]], [y1])
                tt(y1.ap, y1.ap, y3.ap, ALU.add, [y1, y3], [y1])
                tt(y2.ap, y1.ap, szb.ap[:, blk, :], ALU.mult, [y1, szb], [y2])
                memset(ssq.ap[:, 0:1], 0.0, [ssq])
                act(junk.ap.rearrange("p e q -> p (e q)")[:, 0:256], y2.ap, AF.Square, [y2, ssq], [junk, ssq], accum_out=ssq.ap[:, 0:1])
                act(ssq.ap[:, 1:2], ssq.ap[:, 0:1], AF.Ln, [ssq], [ssq], scale=1.0 / 256, bias=1e-6)
                act(ssq.ap[:, 1:2], ssq.ap[:, 1:2], AF.Exp, [ssq], [ssq], scale=-0.5)
                ts(yn[i].ap, y2.ap, ssq.ap[:, 1:2], None, ALU.mult, None, [y2, ssq], [yn[i]])

            def main_tail(blk, i):
                bt = (blk * 128) % TT
                pt = psA.next()
                for j in range(2):
                    tp(PSB[pt].ap[:, j * 128:(j + 1) * 128], yn[i].ap[:, j * 128:(j + 1) * 128], identb.ap, [yn[i], identb], PS[pt], fresh=(j == 0))
                cp(smix.ap[:, :, bt:bt + 128], PSB[pt].ap[:, 0:256].rearrange("p (j t) -> p j t", j=2), [PS[pt]], [smix])
                if bt + 128 == TT:
                    t0 = blk * 128 + 128 - TT
                    xupdate(c, l, t0, TT, lambda j, oc: w_or[:, j, oc * 128:(oc + 1) * 128],
                            [smix.ap[:, 0, 0:TT], smix.ap[:, 1, 0:TT]], 16, [v_wor, smix], evac=evb)

            for si, (s0, sl) in enumerate(c.seqs):
                blks = list(range(s0 // 128, (s0 + sl) // 128))
                for d in range(2):
                    if c.ctx:
                        dma("sp", sstg.ap.rearrange("p (g n) -> p g n", g=2),
                            d_st[l, d].rearrange("(g j) p n -> (j p) g n", g=2), writes=[sstg])
                        pb = psA.next()
                        tp(PS[pb].ap[:, 0:128], sstg.ap, identf.ap, [sstg, cstf], PS[pb], fresh=True)
                        cp(hTf[d].ap.rearrange("p j q -> p (j q)"), PS[pb].ap[:, 0:128], [PS[pb]], [hTf[d]])
                    else:
                        memset(hTf[d].ap, 0.0, [hTf[d]])
                rb_ = list(reversed(blks))
                small_mm(rb_[0], [1], ee[0], cds[0])
                state_pre(rb_[0], 1, ee[0], xdd[0])
                for k, blk in enumerate(rb_):
                    i = k % 2
                    if k + 1 < len(rb_):
                        small_mm(rb_[k + 1], [1], ee[1 - i], cds[1 - i])
                        state_pre(rb_[k + 1], 1, ee[1 - i], xdd[1 - i])
                    cp(hst.ap[:, blk], hTf[1].ap, [hTf[1]], [hst], eng="pool")
                    state_post(blk, 1, cds[i], xdd[i])
                if stop == "S2":
                    return
                cp(hTb.ap, hTf[0].ap, [hTf[0]], [hTb], eng="pool")
                main_pre(blks[0], 0)
                for k, blk in enumerate(blks):
                    if k + 1 < len(blks):
                        main_pre(blks[k + 1], (k + 1) % 2)
                    main_post(blk, k % 2)
                    if k >= 1:
                        main_tail(blks[k - 1], (k - 1) % 2)
                main_tail(blks[-1], (len(blks) - 1) % 2)
                if stop == "S3":
                    return
                if c.kind == "P":
                    for d in range(2):
                        pb = psA.next()
                        tp(PS[pb].ap[:, 0:128], hTf[d].ap.rearrange("p j q -> p (j q)"), identf.ap, [hTf[d], cstf], PS[pb], fresh=True)
                        cp(sstg.ap, PS[pb].ap[:, 0:128], [PS[pb]], [sstg])
                        dma("sp", o_ssd[si, l, d].rearrange("(g j) p n -> (j p) g n", g=2),
                            sstg.ap.rearrange("p (g n) -> p g n", g=2), reads=[sstg])

        def mla_phase(c, l, w_oa, v_woa):
            T, TT = c.T, c.TT
            g_r, x_r, z_r = rg(gpad), rg(xbcp), rg(szb)
            attnT = alloc(Bump([x_r]), "attnT", [128, 4, TS], BF16)
            sp = Bump([(SC0, SBYTES), g_r, z_r])
            NKB = (PAST + TS) // 128
            KT = [alloc(sp, f"KT{i}", [128, PAST + TS], BF16) for i in range(2)]
            VB = [(alloc(sp, f"Ve{i}", [128, NKB, 128], BF16), alloc(sp, f"Vo{i}", [128, NKB, 128], BF16)) for i in range(2)]
            for i in range(2):
                memset(VB[i][0].ap[:, :, 64:128], 1.0, [VB[i][0]])
                memset(VB[i][1].ap[:, :, 0:64], 1.0, [VB[i][1]])
            QT = [alloc(sp, f"QT{i}", [128, TS], BF16) for i in range(2)]
            PT = [alloc(sp, f"PT{i}", [128, 512], BF16) for i in range(3)]
            t1 = alloc(sp, "mt1", [128, 512], F32)
            t2 = alloc(sp, "mt2", [128, 512], F32)
            rbs = Bump([(sp.take(2048),) * 2])
            _b0 = rbs.ranges[0][0]
            rbs = None
            def half_view(nm, b0, p0):
                ap = S[p0:p0 + 64, b0:b0 + 2048].bitcast(F32)
                uid[0] += 1
                n2 = f"{nm}.{uid[0]}"
                P.region(n2, "sbuf", p0, p0 + 64, b0, b0 + 2048)
                return View(n2, ap)
            _b1 = sp.take(2048)
            rb_src = {0: half_view("rbsE", _b0, 64), 1: half_view("rbsO", _b0, 0)}
            rb_dst = {0: half_view("rbdE", _b1, 0), 1: half_view("rbdO", _b1, 64)}
            ptr = Rot(PT)
            LA = 2
            psS = Rot([0, 1, 2])
            PP = 3
            accb = Rot([4, 5, 6, 7])

            heads = []
            for si, (s0, sl) in enumerate(c.seqs):
                for h in range(8):
                    heads.append((si, s0, sl, h))

            def prep(idx):
                si, s0, sl, h = heads[idx]
                hp, hh = h // 2, h % 2
                Tk = (PAST if c.ctx else 0) + sl
                k0 = 0 if c.ctx else s0
                nkb = Tk // 128
                qtiles = [t for (s_, t) in c.tiles if s_ == si]
                kt, qt = KT[idx % 2], QT[idx % 2]
                Ve, Vo = VB[(idx // 2) % 2]
                if hh == 0:
                    vcols = wukv.ap.rearrange("p (h c) -> p h c", h=8)[:, 2 * hp:2 * hp + 2, 64:128]
                    for kb in range(nkb):
                        mm(PS[PP].ap[:, 0:128], ckvnT.ap[:, k0 + kb * 128:k0 + (kb + 1) * 128], vcols, True, True, [ckvnT, wukv], PS[PP], fresh=True)
                        cp(Ve.ap[:, kb, 0:64], PS[PP].ap[:, 0:64], [PS[PP]], [Ve])
                        cp(Vo.ap[:, kb, 64:128], PS[PP].ap[:, 64:128], [PS[PP]], [Vo])
                        yield
                for k1 in range(0, Tk, 512):
                    kn = min(512, Tk - k1)
                    mm(PS[PP].ap[0:64, 0:kn], wukv.ap[:, h * 128:h * 128 + 64], ckvnT.ap[:, k0 + k1:k0 + k1 + kn], True, True,
                       [wukv, ckvnT], PS[PP], fresh=True)
                    cp(kt.ap[0:64, k1:k1 + kn], PS[PP].ap[0:64, 0:kn], [PS[PP]], [kt])
                    yield
                cp(kt.ap[64:96, 0:Tk], krT.ap[64:96, k0:k0 + Tk], [krT], [kt], eng="pool")
                for t0 in qtiles:
                    n = TT
                    tl = t0 - s0
                    for kc in range(2):
                        mm(PS[PP].ap[0:96, 0:n], wuq.ap[:, kc, h * 96:(h + 1) * 96], qlat.ap[:, kc, t0:t0 + n], kc == 0, kc == 1,
                           [wuq, qlat], PS[PP], fresh=(kc == 0))
                    if c.rope:
                        cp(qt.ap[0:64, tl:tl + n], PS[PP].ap[0:64, 0:n], [PS[PP]], [qt])
                        tt(t1.ap[64:96, 0:n], PS[PP].ap[64:96, 0:n], ropeC.ap[64:96, t0:t0 + n], ALU.mult, [PS[PP], ropeC], [t1])
                        yield
                        for kc in range(2):
                            mm(PS[PP].ap[64:96, 0:n], wuq.ap[:, kc, 768 + h * 32:768 + (h + 1) * 32], qlat.ap[:, kc, t0:t0 + n],
                               kc == 0, kc == 1, [wuq, qlat], PS[PP], fresh=(kc == 0))
                        tt(t2.ap[64:96, 0:n], PS[PP].ap[64:96, 0:n], ropeS.ap[64:96, t0:t0 + n], ALU.mult, [PS[PP], ropeS], [t2])
                        tt(qt.ap[64:96, tl:tl + n], t1.ap[64:96, 0:n], t2.ap[64:96, 0:n], ALU.add, [t1, t2], [qt])
                    else:
                        cp(qt.ap[0:96, tl:tl + n], PS[PP].ap[0:96, 0:n], [PS[PP]], [qt])
                    yield

            def attn_unit(idx, t0, prev_tail, filler):
                si, s0, sl, h = heads[idx]
                hp, hh = h // 2, h % 2
                nkb = ((PAST if c.ctx else 0) + sl) // 128
                kt, qt = KT[idx % 2], QT[idx % 2]
                vv = VB[(idx // 2) % 2][hh]
                tl = t0 - s0
                n = c.TT
                po = accb.next()
                r0 = 0 if hh == 0 else 64
                d0 = 64 - r0
                pts = []
                for kb in range(nkb + LA):
                    if kb < nkb:
                        pb = psS.next()
                        mm(PS[pb].ap[:, 0:n], kt.ap[0:96, kb * 128:(kb + 1) * 128], qt.ap[0:96, tl:tl + n], True, True,
                           [kt, qt], PS[pb], fresh=True)
                        pt_ = ptr.next()
                        pts.append(pt_)
                        act(pt_.ap[:, 0:n], PS[pb].ap[:, 0:n], AF.Exp, [PS[pb]], [pt_], scale=SCALE)
                    if kb == min(LA, nkb) - 1 and prev_tail is not None:
                        prev_tail()
                        prev_tail = None
                    if kb >= LA:
                        k2 = kb - LA
                        mm(PS[po].ap[:, 0:n], vv.ap[:, k2, :], pts[k2].ap[:, 0:n], k2 == 0, k2 == nkb - 1, [vv, pts[k2]], PS[po], fresh=(k2 == 0))
                    if filler is not None:
                        next(filler, None)

                def tail():
                    rs_, rd_ = rb_src[hh], rb_dst[hh]
                    recip(rs_.ap[:, 0:n], PS[po].ap[d0:d0 + 64, 0:n], [PS[po]], [rs_])
                    dma("sp", rd_.ap[:, 0:n], rs_.ap[:, 0:n], reads=[rs_], writes=[rd_])
                    tt(attnT.ap[r0:r0 + 64, hp, t0:t0 + n], PS[po].ap[r0:r0 + 64, 0:n], rd_.ap[:, 0:n], ALU.mult, [PS[po], rd_], [attnT])
                return tail

            for _ in prep(0):
                pass
            pend = None
            for idx in range(len(heads)):
                si = heads[idx][0]
                filler = prep(idx + 1) if idx + 1 < len(heads) else None
                for t0 in [t for (s_, t) in c.tiles if s_ == si]:
                    pend = attn_unit(idx, t0, pend, filler)
                if filler is not None:
                    for _ in filler:
                        pass
            if pend is not None:
                pend()
            for (si, t0) in c.tiles:
                xupdate(c, l, t0, TT, lambda j, oc: w_oa[:, j, oc * 128:(oc + 1) * 128],
                        [attnT.ap[:, j, t0:t0 + TT] for j in range(4)], 16, [v_woa, attnT])

        def ffn_phase(c, l):
            T, TT = c.T, c.TT
            sp = Bump([(max(SC0, rg(hT)[1]), SBYTES)])
            sq = [alloc(sp, f"fsq{i}", [128, 512], BF16) for i in range(2)]
            sd = alloc(sp, "fsd", [128, 512], F32)
            rstd = alloc(sp, "frstd", [128, 512], F32)
            tmpb = [alloc(sp, f"ftmpb{i}", [128, 512], F32) for i in range(2)]
            sg = [alloc(sp, f"sg{i}", [128, 512], F32) for i in range(2)]
            actb = [alloc(sp, f"actb{i}", [128, 2, 512], BF16) for i in range(2)]
            sh2 = modv.ap[:, l, 24:32, c.cond]
            for (si, t0) in c.tiles:
                norm_tile(c, (sq, sd, rstd, tmpb), t0, TT, gsc.ap[:, 1, :], sh2, lambda kc, t0=t0: hT.ap[:, kc, t0:t0 + TT], hT)
            def load_group(g):
                c0 = g * 256
                v1 = load_piece([(lambda a: r3(a, 8)[:, :, 0:256], d_wg[l, :, c0:c0 + 256].rearrange("(k p) c -> p k c", p=128)),
                                 (lambda a: r3(a, 8)[:, :, 256:512], d_wu[l, :, c0:c0 + 256].rearrange("(k p) c -> p k c", p=128))])
                v2 = load_piece([(lambda a: r3(a, 4)[:, 0:2, :], d_wd[l, c0:c0 + 256, :].rearrange("(k p) c -> p k c", p=128))])
                return (v1, v2)

            units = [(g, t0) for g in range(NFG) for (si, t0) in c.tiles]
            wts = {0: load_group(0), 1: load_group(1)}

            psF = Rot([4, 5, 6, 7])

            def stage_a_groups(u):
                g, t0 = units[u]
                v1, v2 = wts[g]
                wgu = r3(v1.ap, 8)
                n = TT
                ab = actb[u % 2]
                outs = []
                st = {}

                def mk(j, which):
                    def f():
                        pb = psA.next()
                        c0 = (0 if which == 0 else 256) + j * 128
                        for kc in range(8):
                            mm(PS[pb].ap[:, 0:n], wgu[:, kc, c0:c0 + 128], hT.ap[:, kc, t0:t0 + n], kc == 0, kc == 7, [v1, hT], PS[pb], fresh=(kc == 0))
                        st[(j, which)] = pb
                        if which == 1:
                            pg, pu = st[(j, 0)], pb
                            act(sg[j].ap[:, 0:n], PS[pg].ap[:, 0:n], AF.Silu, [PS[pg]], [sg[j]])
                            tt(ab.ap[:, j, 0:n], sg[j].ap[:, 0:n], PS[pu].ap[:, 0:n], ALU.mult, [sg[j], PS[pu]], [ab])
                    return f
                for j in range(2):
                    outs.append(mk(j, 0))
                    outs.append(mk(j, 1))
                return outs

            def stage_b_steps(u):
                g, t0 = units[u]
                v1, v2 = wts[g]
                wdn = r3(v2.ap, 4)
                n = TT
                ab = actb[u % 2]
                outs = []

                def mk(oc):
                    def f():
                        pb = psF.next()
                        for j in range(2):
                            mm(PS[pb].ap[:, 0:n], wdn[:, j, oc * 128:(oc + 1) * 128], ab.ap[:, j, 0:n], j == 0, j == 1, [v2, ab], PS[pb], fresh=(j == 0))
                        stt(xT.ap[:, oc, t0:t0 + n], PS[pb].ap[:, 0:n], modv.ap[:, l, 40 + oc, c.cond:c.cond + 1],
                            xT.ap[:, oc, t0:t0 + n], ALU.mult, ALU.add, [PS[pb], modv, xT], [xT])
                        if oc == 7 and (u + 1 == len(units) or units[u + 1][0] != g) and g + 2 < NFG:
                            wts[g + 2] = load_group(g + 2)
                    return f
                for oc in range(8):
                    outs.append(mk(oc))
                return outs

            for u in range(len(units) + 1):
                ga = stage_a_groups(u) if u < len(units) else []
                gb = stage_b_steps(u - 1) if u >= 1 else []
                for k in range(4):
                    if ga:
                        ga[k]()
                    if gb:
                        gb[2 * k]()
                        gb[2 * k + 1]()


        def final_out(c):
            sp = Bump([(SC0, SBYTES)])
            sq = [alloc(sp, f"osq{i}", [128, 512], BF16) for i in range(2)]
            sd = alloc(sp, "osd", [128, 512], F32)
            rstd = alloc(sp, "orstd", [128, 512], F32)
            tmpb = [alloc(sp, f"otmpb{i}", [128, 512], F32) for i in range(2)]
            yf = alloc(sp, "yf", [128, 8, 128], F32)
            ytm = [alloc(sp, f"ytm{i}", [128, D], F32) for i in range(2)]
            gf = cnd.ap[:, 16:24]
            for b in range(c.nblk):
                norm_tile(c, (sq, sd, rstd, tmpb), b * 128, 128, gf, None, lambda kc: yf.ap[:, kc, :], yf)
                y = ytm[b % 2]
                for half in range(2):
                    pb = psA.next()
                    for q in range(4):
                        kc = half * 4 + q
                        tp(PS[pb].ap[:, q * 128:(q + 1) * 128], yf.ap[:, kc, :], identf.ap, [yf, cstf], PS[pb], fresh=(q == 0))
                    act(y.ap[:, half * 512:(half + 1) * 512], PS[pb].ap, AF.Copy, [PS[pb]], [y])
                dma("sp", c.oy[b * 128:(b + 1) * 128, :], y.ap, reads=[y])

        passes = []
        if do_sample:
            passes.append(make_cfg("S"))
        if do_prompt:
            passes.append(make_cfg("P"))
        for c in passes:
            load_x(c)
            for l in range(n_layers):
                layer_pass(c, l)
            final_out(c)

        names = P.finalize()
        print("nops", len(P.ops), "nsems", len(names))
        sems = {nm: es.enter_context(nc.semaphore(f"s{i}")) for i, nm in enumerate(names)}
        with nc.Block() as block:
            P.emit(block, sems)
    return nc


def _consts():
    i = np.arange(128)
    ident = np.eye(128, dtype=np.float32)
    suf = (i[:, None] > i[None, :]).astype(np.float32)
    sub = (i[:, None] < i[None, :]).astype(np.float32)
    trif = (i[:, None] <= i[None, :]).astype(np.float32)
    trib = (i[:, None] >= i[None, :]).astype(np.float32)
    cst = np.concatenate([ident, suf, sub, trif, trib], axis=1).astype(np.float32)
    t = np.arange(TS)
    row = (t // 64).astype(np.float32)
    col = (t % 64).astype(np.float32)
    nf = 8
    inv = (10000.0 ** (-np.arange(nf, dtype=np.float32) / nf)).astype(np.float32)
    ang = np.stack([row[:, None] * inv, col[:, None] * inv], axis=1)
    cos = np.cos(ang).astype(np.float32)
    sin = np.sin(ang).astype(np.float32)
    rope = np.zeros((2, 128, TS), np.float32)
    for a in range(2):
        for half in range(2):
            for f in range(nf):
                r = 64 + a * 16 + half * 8 + f
                rope[0, r] = cos[:, a, f]
                rope[1, r] = sin[:, a, f]
    return cst, rope


_CACHE = {}


def kernel(x_prompt, x_sample, c, cache_ckv, cache_krope, state_ssd, c_ctx, w_ada, b_ada,
           g_mix, w_in, g_q, w_uq, g_kv, w_ukv, ssd_conv_w, ssd_conv_b, ssd_dt_bias,
           ssd_a_log, ssd_d, ssd_norm_g, cm_conv_w, cm_conv_b, cm_ln_g, cm_ln_b, w_out,
           g_ffn, w_gate, w_up, w_down, g_final, _n_layers=DEPTH, _do_sample=True, _do_prompt=True, _stop=None):
    f = lambda a: np.ascontiguousarray(np.asarray(a, dtype=np.float32))
    key = (_n_layers, _do_sample, _do_prompt, _stop)
    if key not in _CACHE:
        _CACHE[key] = build_program(_n_layers, _do_sample, _do_prompt, _stop)
    nc = _CACHE[key]
    cst, rope = _consts()
    shared = dict(
        w_ada=f(w_ada), b_ada=f(b_ada), g_mix=f(g_mix), w_in=f(w_in), g_q=f(g_q), w_uq=f(w_uq), g_kv=f(g_kv),
        w_ukv=f(w_ukv), ssd_conv_w=f(ssd_conv_w), ssd_conv_b=f(ssd_conv_b), ssd_dt_bias=f(ssd_dt_bias).reshape(-1),
        ssd_a_log=f(ssd_a_log).reshape(-1), ssd_d=f(ssd_d).reshape(-1), ssd_norm_g=f(ssd_norm_g), cm_conv_w=f(cm_conv_w),
        cm_conv_b=f(cm_conv_b), cm_ln_g=f(cm_ln_g), cm_ln_b=f(cm_ln_b), w_out=f(w_out), g_ffn=f(g_ffn),
        w_gate=f(w_gate), w_up=f(w_up), w_down=f(w_down), g_final=f(g_final), cst=cst, rope=rope)
    x_prompt, x_sample, c, c_ctx = f(x_prompt), f(x_sample), f(c), f(c_ctx)
    cache_ckv, cache_krope, state_ssd = f(cache_ckv), f(cache_krope), f(state_ssd)
    in_maps = []
    for i in range(8):
        b = i % 4
        m = dict(shared)
        m["x_s"] = x_sample[b]
        m["x_p"] = x_prompt[2 * i:2 * i + 2].reshape(NPS * TPS, D)
        m["cond"] = np.stack([c[b], c_ctx], axis=0)
        m["cache_ckv"] = cache_ckv[b]
        m["cache_krope"] = cache_krope[b]
        m["state_ssd"] = state_ssd[b]
        in_maps.append(m)
    res = run_bass_kernel_spmd(nc, in_maps, core_ids=list(range(8)))
    r = res.results
    y_sample = np.stack([r[b]["y_s"] for b in range(4)], axis=0).astype(np.float32)
    y_prompt = np.concatenate([r[i]["y_p"].reshape(NPS, TPS, D) for i in range(8)], axis=0).astype(np.float32)
    new_ckv = np.concatenate([r[i]["o_ckv"] for i in range(8)], axis=0).astype(np.float32)
    new_kr = np.concatenate([r[i]["o_kr"] for i in range(8)], axis=0).astype(np.float32)
    new_ssd = np.concatenate([r[i]["o_ssd"] for i in range(8)], axis=0).astype(np.float32)
    return (y_prompt, y_sample, new_ckv, new_kr, new_ssd)
```

```python
import math
from contextlib import ExitStack
import numpy as np
import concourse.bass as bass
import concourse.mybir as mybir
from concourse.bass_utils import run_bass_kernel_spmd

F32 = mybir.dt.float32
BF16 = mybir.dt.bfloat16
U8 = mybir.dt.uint8
AF = mybir.ActivationFunctionType
ALU = mybir.AluOpType

ENGS = ("pe", "act", "dve", "pool", "sp")

D = 1024
DEPTH = 4
TS = 2048
TPS = 256
NPS = 2
PAST = 256
DFF = 2816
NFG = DFF // 256
IN_W = 1704
SCALE = 96 ** -0.5
NPRM = 161


class View:
    __slots__ = ("name", "ap")

    def __init__(self, name, ap):
        self.name, self.ap = name, ap


class Prog:
    def __init__(self, nc):
        self.nc = nc
        self.ops = []
        self.regions = {}
        self.overlaps = {}

    def region(self, name, space, p0, p1, b0, b1):
        assert name not in self.regions, name
        self.regions[name] = (space, p0, p1, b0, b1)
        ov = [name]
        for n, (s, q0, q1, c0, c1) in self.regions.items():
            if n == name:
                continue
            if s == space and q0 < p1 and p0 < q1 and c0 < b1 and b0 < c1:
                ov.append(n)
                self.overlaps[n].append(name)
        self.overlaps[name] = ov

    def op(self, eng, fn, reads=(), writes=(), dma=False, chan=None, fresh=()):
        rs = [r if isinstance(r, str) else r.name for r in reads]
        ws = [w if isinstance(w, str) else w.name for w in writes]
        fr = [w if isinstance(w, str) else w.name for w in fresh]
        if dma and chan is None:
            chan = ws[0] if ws else rs[0]
        self.ops.append((eng, fn, rs, ws, dma, chan, fr))

    def finalize(self):
        ops = self.ops
        n = len(ops)
        last_w = {}
        readers = {}
        deps = [None] * n
        for i, (eng, fn, rs, ws, dma, chan, fr) in enumerate(ops):
            for f in fr:
                lw = last_w.get(f)
                assert lw is None or len(readers.get(f, ())) > 0, \
                    f"PSUM collision on {f} at op {i} (prev writer {lw} unread)"
            d = set()
            for r in rs:
                for rr in self.overlaps[r]:
                    w = last_w.get(rr)
                    if w is not None:
                        d.add(w)
                    if self.regions[rr][0] == "psum":
                        for x in readers.get(rr, {}).values():
                            d.add(x)
            for w_ in ws:
                for ww in self.overlaps[w_]:
                    w = last_w.get(ww)
                    if w is not None:
                        d.add(w)
                    for x in readers.get(ww, {}).values():
                        d.add(x)
            d.discard(i)
            dd = []
            mine = set(rs) | set(ws)
            for j in d:
                ej, _, rsj, wsj, dmaj, _, _ = ops[j]
                if not dmaj and not dma and ej == eng:
                    if eng == "pe":
                        continue
                    hit = False
                    for x in wsj:
                        for y in self.overlaps[x]:
                            if y in mine:
                                hit = True
                                break
                        if hit:
                            break
                    if not hit:
                        continue
                dd.append(j)
            deps[i] = dd
            key = ("c:" + chan) if dma else eng
            for r in rs:
                readers.setdefault(r, {})[key] = i
            for w_ in ws:
                last_w[w_] = i
                readers[w_] = {}
        needed = set()
        for dd in deps:
            needed.update(dd)
        eng_cnt = {e: 0 for e in ENGS}
        chan_cnt = {}
        ticket = {}
        for i, (eng, fn, rs, ws, dma, chan, fr) in enumerate(ops):
            if dma:
                chan_cnt[chan] = chan_cnt.get(chan, 0) + 16
                ticket[i] = ("c:" + chan, chan_cnt[chan])
            elif i in needed:
                eng_cnt[eng] += 1
                ticket[i] = ("e:" + eng, eng_cnt[eng])
        self.sem_names = ["e:" + e for e in ENGS] + ["c:" + c for c in chan_cnt]
        self.final_counts = {("e:" + e): eng_cnt[e] for e in ENGS}
        self.final_counts.update({("c:" + c): v for c, v in chan_cnt.items()})
        self.deps, self.ticket = deps, ticket
        return self.sem_names

    def emit(self, block, sems):
        ops, deps, ticket = self.ops, self.deps, self.ticket
        per_eng = {e: [] for e in ENGS}
        for i, o in enumerate(ops):
            per_eng[o[0]].append(i)
        final_counts = self.final_counts

        def make(engname):
            def body(e):
                waited = {}
                for i in per_eng[engname]:
                    _, fn, rs, ws, dma, chan, _ = ops[i]
                    need = {}
                    for j in deps[i]:
                        s, v = ticket[j]
                        if need.get(s, 0) < v:
                            need[s] = v
                    for s, v in need.items():
                        if waited.get(s, 0) < v:
                            e.wait_ge(sems[s], v)
                            waited[s] = v
                    ins = fn(e)
                    if i in ticket:
                        s, v = ticket[i]
                        ins.then_inc(sems[s], 16 if dma else 1)
                if engname == "sp":
                    for s, v in final_counts.items():
                        if v > 0 and waited.get(s, 0) < v:
                            e.wait_ge(sems[s], v)
            return body

        block.tensor(make("pe"))
        block.scalar(make("act"))
        block.vector(make("dve"))
        block.gpsimd(make("pool"))
        block.sync(make("sp"))


class Bump:
    def __init__(self, ranges):
        self.ranges = [list(r) for r in ranges]

    def take(self, nb):
        nb = (nb + 31) // 32 * 32
        for r in self.ranges:
            if r[1] - r[0] >= nb:
                b0 = r[0]
                r[0] += nb
                return b0
        raise MemoryError(f"bump pool exhausted need {nb} have {self.ranges}")


def esize(dt):
    return 4 if dt == F32 else 2


def build_program(n_layers=DEPTH, do_sample=True, do_prompt=True, stop=None):
    nc = bass.Bass("TRN2", target_bir_lowering=False)
    P = Prog(nc)

    def din(name, shape):
        return nc.dram_tensor(name, list(shape), F32, kind="ExternalInput").ap()

    def dout(name, shape):
        return nc.dram_tensor(name, list(shape), F32, kind="ExternalOutput").ap()

    d_xs = din("x_s", [TS, D])
    d_xp = din("x_p", [NPS * TPS, D])
    d_cond = din("cond", [2, D])
    d_cckv = din("cache_ckv", [DEPTH, PAST, 128])
    d_ckr = din("cache_krope", [DEPTH, PAST, 32])
    d_st = din("state_ssd", [DEPTH, 2, 4, 64, 64])
    d_wada = din("w_ada", [DEPTH, D, 6 * D])
    d_bada = din("b_ada", [DEPTH, 6 * D])
    d_gmix = din("g_mix", [DEPTH, D])
    d_win = din("w_in", [DEPTH, D, IN_W])
    d_gq = din("g_q", [DEPTH, 256])
    d_wuq = din("w_uq", [DEPTH, 256, 768])
    d_gkv = din("g_kv", [DEPTH, 128])
    d_wukv = din("w_ukv", [DEPTH, 128, 1024])
    d_scw = din("ssd_conv_w", [DEPTH, 5, 512])
    d_scb = din("ssd_conv_b", [DEPTH, 512])
    d_dtb = din("ssd_dt_bias", [DEPTH * 8])
    d_alog = din("ssd_a_log", [DEPTH * 8])
    d_sd = din("ssd_d", [DEPTH * 8])
    d_sng = din("ssd_norm_g", [DEPTH, 256])
    d_ccw = din("cm_conv_w", [DEPTH, 31, 256])
    d_ccb = din("cm_conv_b", [DEPTH, 256])
    d_clg = din("cm_ln_g", [DEPTH, 256])
    d_clb = din("cm_ln_b", [DEPTH, 256])
    d_wout = din("w_out", [DEPTH, D, D])
    d_gffn = din("g_ffn", [DEPTH, D])
    d_wg = din("w_gate", [DEPTH, D, DFF])
    d_wu = din("w_up", [DEPTH, D, DFF])
    d_wd = din("w_down", [DEPTH, DFF, D])
    d_gfin = din("g_final", [D])
    d_cst = din("cst", [128, 640])
    d_rope = din("rope", [2, 128, TS])

    o_ys = dout("y_s", [TS, D])
    o_yp = dout("y_p", [NPS * TPS, D])
    o_ckv = dout("o_ckv", [NPS, DEPTH, TPS, 128])
    o_kr = dout("o_kr", [NPS, DEPTH, TPS, 32])
    o_ssd = dout("o_ssd", [NPS, DEPTH, 2, 4, 64, 64])

    es = ExitStack()
    with es:
        SBYTES = 212000
        S = es.enter_context(nc.sbuf_tensor("S", [128, SBYTES], U8))
        banks = [es.enter_context(nc.psum_tensor(f"PS{i}", [128, 512], F32)) for i in range(8)]
        for i in range(8):
            P.region(f"ps{i}", "psum", 0, 128, i * 2048, (i + 1) * 2048)
        PS = [View(f"ps{i}", banks[i][:, :]) for i in range(8)]
        PSB = [View(f"ps{i}", banks[i][:, :].bitcast(BF16)) for i in range(8)]

        uid = [0]

        def alloc(pool, name, shape, dt, p0=0):
            nel = int(np.prod(shape[1:]))
            nb = nel * esize(dt)
            b0 = pool.take(nb)
            ap = S[p0:p0 + shape[0], b0:b0 + nb].bitcast(dt)
            if len(shape) == 3:
                ap = ap.rearrange("p (a b) -> p a b", a=shape[1])
            elif len(shape) == 4:
                ap = ap.rearrange("p (a b c) -> p a b c", a=shape[1], b=shape[2])
            uid[0] += 1
            nm = f"{name}.{uid[0]}"
            P.region(nm, "sbuf", p0, p0 + shape[0], b0, b0 + ((nb + 31) // 32 * 32))
            return View(nm, ap)

        pers = Bump([(0, SBYTES)])
        xT = alloc(pers, "xT", [128, 8, TS], F32)
        RING_N = 4
        ring = [alloc(pers, f"ring{i}", [128, 4096], BF16) for i in range(RING_N)]
        wuq = alloc(pers, "wuq", [128, 2, 1024], BF16)
        wukv = alloc(pers, "wukv", [128, 1024], BF16)
        cstf = alloc(pers, "cstf", [128, 640], F32)
        identf = View(cstf.name, cstf.ap[:, 0:128])
        SU = [View(cstf.name, cstf.ap[:, 128:256]), View(cstf.name, cstf.ap[:, 256:384])]
        TRI = [View(cstf.name, cstf.ap[:, 384:512]), View(cstf.name, cstf.ap[:, 512:640])]
        identb = alloc(pers, "identb", [128, 128], BF16)
        onesb = alloc(pers, "onesb", [128, 128], BF16)
        onesf = alloc(pers, "onesf", [128, 128], F32)
        ropeC = alloc(pers, "ropeC", [128, TS], BF16)
        ropeS = alloc(pers, "ropeS", [128, TS], BF16)
        prm = alloc(pers, "prm", [128, DEPTH, NPRM], F32)
        cnd = alloc(pers, "cnd", [128, 24], F32)
        scond = alloc(pers, "scond", [128, 8, 2], BF16)
        modv = alloc(pers, "modv", [128, DEPTH, 48, 2], F32)
        gsc = alloc(pers, "gsc", [128, 2, 8], F32)
        dtb_b = alloc(pers, "dtb_b", [128, 32], F32)
        alog_b = alloc(pers, "alog_b", [128, 32], F32)
        dd_b = alloc(pers, "dd_b", [128, 32], F32)
        a_b = alloc(pers, "a_b", [128, 32], F32)
        dsum_b = alloc(pers, "dsum_b", [128, DEPTH, 4], F32)
        gkv_b = alloc(pers, "gkv_b", [128, 128], F32)
        xg = alloc(pers, "xg", [128, 8, 512], BF16)
        xg0 = xg
        pers_end = pers.ranges[0][0]
        io = Bump([(pers_end, SBYTES)])
        gpad = alloc(io, "gpad", [128, 2, TS + 32], BF16)
        xbcp = alloc(io, "xbcp", [128, 4, TS + 4], BF16)
        szb = alloc(io, "szb", [128, 16, 256], BF16)
        qlat = alloc(io, "qlat", [128, 2, TS], BF16)
        ckvnT = alloc(io, "ckvnT", [128, PAST + TS], BF16)
        krT = alloc(io, "krT", [128, PAST + TS], BF16)
        dtr = alloc(io, "dtr", [128, 16, 8], F32)
        io_end = io.ranges[0][0]
        hT = alloc(Bump([(pers_end, SBYTES)]), "hT", [128, 8, TS], BF16)
        R = P.regions
        rg = lambda v: (R[v.name][3], R[v.name][4])
        SC0 = io_end
        print("mem: pers_end", pers_end, "io_end", io_end, "scratch", SBYTES - io_end)

        def dma(eng, out, in_, reads=(), writes=(), **kw):
            P.op(eng, lambda e: e.dma_start(out=out, in_=in_, **kw), reads=reads, writes=writes, dma=True)

        def mm(out, lhsT, rhs, start, stop, reads, w, fresh=False):
            P.op("pe", lambda e: e.matmul(out, lhsT=lhsT, rhs=rhs, start=start, stop=stop),
                 reads=reads, writes=[w], fresh=[w] if fresh else ())

        def tp(out, in_, ident, reads, w, fresh=False):
            P.op("pe", lambda e: e.transpose(out, in_, ident), reads=reads, writes=[w],
                 fresh=[w] if fresh else ())

        def act(out, in_, func, reads, writes, **kw):
            P.op("act", lambda e: e.activation(out=out, in_=in_, func=func, **kw), reads=reads, writes=writes)

        def tt(out, in0, in1, op, reads, writes, eng="dve"):
            P.op(eng, lambda e: e.tensor_tensor(out=out, in0=in0, in1=in1, op=op), reads=reads, writes=writes)

        def stt(out, in0, scalar, in1, op0, op1, reads, writes):
            P.op("dve", lambda e: e.scalar_tensor_tensor(out=out, in0=in0, scalar=scalar, in1=in1, op0=op0, op1=op1),
                 reads=reads, writes=writes)

        def ts(out, in0, s1, s2, op0, op1, reads, writes, eng="dve"):
            if op1 is None:
                P.op(eng, lambda e: e.tensor_scalar(out=out, in0=in0, scalar1=s1, scalar2=None, op0=op0),
                     reads=reads, writes=writes)
            else:
                P.op(eng, lambda e: e.tensor_scalar(out=out, in0=in0, scalar1=s1, scalar2=s2, op0=op0, op1=op1),
                     reads=reads, writes=writes)

        def cp(out, in_, reads, writes, eng="dve"):
            P.op(eng, lambda e: e.tensor_copy(out, in_), reads=reads, writes=writes)

        def memset(ap, val, writes, eng="dve"):
            P.op(eng, lambda e: e.memset(ap, val), writes=writes)

        def recip(out, in_, reads, writes):
            P.op("dve", lambda e: e.reciprocal(out=out, in_=in_), reads=reads, writes=writes)

        class Rot:
            def __init__(self, items):
                self.items, self.i = items, 0

            def next(self):
                v = self.items[self.i % len(self.items)]
                self.i += 1
                return v

        psA = Rot([0, 1, 2, 3])
        psBk = Rot([4, 5])

        dma("sp", cstf.ap, d_cst, writes=[cstf])
        dma("pool", ropeC.ap, d_rope[0], writes=[ropeC])
        dma("pool", ropeS.ap, d_rope[1], writes=[ropeS])
        memset(onesb.ap, 1.0, [onesb])
        memset(onesf.ap, 1.0, [onesf])
        cp(identb.ap, identf.ap, [cstf], [identb])
        dma("sp", dtb_b.ap, d_dtb.partition_broadcast(128), writes=[dtb_b])
        dma("sp", alog_b.ap, d_alog.partition_broadcast(128), writes=[alog_b])
        dma("sp", dd_b.ap, d_sd.partition_broadcast(128), writes=[dd_b])
        act(a_b.ap, alog_b.ap, AF.Exp, [alog_b], [a_b])
        ts(a_b.ap, a_b.ap, -1.0, None, ALU.mult, None, [a_b], [a_b])
        ddv = dd_b.ap.rearrange("p (l d h) -> p l d h", l=DEPTH, d=2)
        tt(dsum_b.ap, ddv[:, :, 0, :], ddv[:, :, 1, :], ALU.add, [dd_b], [dsum_b])

        prol = Bump([(SC0, SBYTES)])
        stg = [alloc(prol, f"stg{i}", [128, 128], F32) for i in range(2)]
        O_BADA, O_GMIX, O_GFFN, O_GQ, O_GKV, O_SCW, O_SCB, O_SNG, O_CCW, O_CCB, O_CLG, O_CLB = \
            0, 48, 56, 64, 66, 67, 87, 91, 93, 155, 157, 159
        for l in range(DEPTH):
            rows = [
                (d_bada[l].rearrange("(r c) -> r c", c=128), 48),
                (d_gmix[l].rearrange("(r c) -> r c", c=128), 8),
                (d_gffn[l].rearrange("(r c) -> r c", c=128), 8),
                (d_gq[l].rearrange("(r c) -> r c", c=128), 2),
                (d_gkv[l].rearrange("(r c) -> r c", c=128), 1),
                (d_scw[l].rearrange("j (r c) -> (j r) c", c=128), 20),
                (d_scb[l].rearrange("(r c) -> r c", c=128), 4),
                (d_sng[l].rearrange("(r c) -> r c", c=128), 2),
                (d_ccw[l].rearrange("j (r c) -> (j r) c", c=128), 62),
                (d_ccb[l].rearrange("(r c) -> r c", c=128), 2),
                (d_clg[l].rearrange("(r c) -> r c", c=128), 2),
                (d_clb[l].rearrange("(r c) -> r c", c=128), 2),
            ]
            r0 = 0
            for src, nr in rows:
                done = 0
                while done < nr:
                    si = (r0 + done) // 128
                    off = (r0 + done) % 128
                    k = min(nr - done, 128 - off)
                    dma("sp", stg[si].ap[off:off + k, :], src[done:done + k, :], writes=[stg[si]])
                    done += k
                r0 += nr
            assert r0 == NPRM
            for si, (c0, ncol) in enumerate([(0, 128), (128, NPRM - 128)]):
                pb = psA.next()
                tp(PS[pb].ap[:, 0:ncol], stg[si].ap[0:ncol, :], identf.ap[0:ncol, 0:ncol], [stg[si], cstf], PS[pb], fresh=True)
                cp(prm.ap[:, l, c0:c0 + ncol], PS[pb].ap[:, 0:ncol], [PS[pb]], [prm])
        dma("sp", stg[0].ap[0:16, :], d_cond.rearrange("a (r c) -> (a r) c", c=128), writes=[stg[0]])
        dma("sp", stg[0].ap[16:24, :], d_gfin.rearrange("(r c) -> r c", c=128), writes=[stg[0]])
        pb = psA.next()
        tp(PS[pb].ap[:, 0:24], stg[0].ap[0:24, :], identf.ap[0:24, 0:24], [stg[0], cstf], PS[pb], fresh=True)
        cp(cnd.ap, PS[pb].ap[:, 0:24], [PS[pb]], [cnd])
        act(scond.ap.rearrange("p k c -> p c k"), cnd.ap[:, 0:16].rearrange("p (c k) -> p c k", c=2), AF.Silu, [cnd], [scond])

        ring_i = [0]

        def load_piece(srcs):
            v = ring[ring_i[0] % RING_N]
            ring_i[0] += 1
            for dst_fn, src in srcs:
                dma("pool", dst_fn(v.ap), src, writes=[v])
            return v

        def r3(ap, a):
            return ap.rearrange("p (a b) -> p a b", a=a)

        for l in range(n_layers):
            for pc in range(12):
                v = load_piece([(lambda a: r3(a, 8), d_wada[l, :, pc * 512:(pc + 1) * 512].rearrange("(k p) c -> p k c", p=128))])
                w3 = r3(v.ap, 8)
                for q in range(4):
                    oc = pc * 4 + q
                    for kc in range(8):
                        mm(PS[7].ap[:, oc * 2:oc * 2 + 2], w3[:, kc, q * 128:(q + 1) * 128], scond.ap[:, kc, :],
                           kc == 0, kc == 7, [v, scond], PS[7], fresh=(oc == 0 and kc == 0))
            tt(modv.ap[:, l], PS[7].ap[:, 0:96].rearrange("p (o c) -> p o c", c=2),
               prm.ap[:, l, O_BADA:O_BADA + 48].unsqueeze(2).to_broadcast([128, 48, 2]), ALU.add, [PS[7], prm], [modv])

        class Cfg:
            pass

        def make_cfg(kind):
            c = Cfg()
            c.kind = kind
            if kind == "S":
                c.T, c.TT, c.seqs, c.rope, c.ctx, c.cond = TS, 512, [(0, TS)], True, True, 0
                c.dx, c.oy = d_xs, o_ys
            else:
                c.T, c.TT, c.seqs, c.rope, c.ctx, c.cond = NPS * TPS, 256, [(0, TPS), (TPS, TPS)], False, False, 1
                c.dx, c.oy = d_xp, o_yp
            c.tiles = []
            for si, (s0, sl) in enumerate(c.seqs):
                for t in range(s0, s0 + sl, c.TT):
                    c.tiles.append((si, t))
            c.nblk = c.T // 128
            return c

        def load_x(c):
            pool = Bump([(SC0, SBYTES)])
            xs_ = [alloc(pool, f"xstg{i}", [128, D], F32) for i in range(2)]
            for b in range(c.nblk):
                st = xs_[b % 2]
                dma("sp", st.ap, c.dx[b * 128:(b + 1) * 128, :], writes=[st])
                for half in range(2):
                    pb = psA.next()
                    for q in range(4):
                        kc = half * 4 + q
                        tp(PS[pb].ap[:, q * 128:(q + 1) * 128], st.ap[:, kc * 128:(kc + 1) * 128], identf.ap,
                           [st, cstf], PS[pb], fresh=(q == 0))
                    cp(xT.ap[:, half * 4:half * 4 + 4, b * 128:(b + 1) * 128],
                       PS[pb].ap.rearrange("p (q t) -> p q t", q=4), [PS[pb]], [xT])

        def norm_tile(c, pool_views, t0, n, gsc_ap, sh_ap, out_fn, out_v, dt_out_bf=True, sq_dve=False):
            sq, sd, rstd, tmpb = pool_views
            for kc in range(8):
                q = sq[kc % 2]
                if sq_dve:
                    tt(q.ap[:, 0:n], xT.ap[:, kc, t0:t0 + n], xT.ap[:, kc, t0:t0 + n], ALU.mult, [xT], [q])
                else:
                    act(q.ap[:, 0:n], xT.ap[:, kc, t0:t0 + n], AF.Square, [xT], [q])
                mm(PS[6].ap[:, 0:n], onesb.ap, q.ap[:, 0:n], kc == 0, kc == 7, [onesb, q], PS[6], fresh=(kc == 0))
            act(sd.ap[:, 0:n], PS[6].ap[:, 0:n], AF.Ln, [PS[6]], [sd], scale=1.0 / D, bias=1e-6)
            act(rstd.ap[:, 0:n], sd.ap[:, 0:n], AF.Exp, [sd], [rstd], scale=-0.5)
            for kc in range(8):
                tb = tmpb[kc % 2]
                tt(tb.ap[:, 0:n], xT.ap[:, kc, t0:t0 + n], rstd.ap[:, 0:n], ALU.mult, [xT, rstd], [tb])
                if sh_ap is None:
                    ts(out_fn(kc), tb.ap[:, 0:n], gsc_ap[:, kc:kc + 1], None, ALU.mult, None, [tb, cnd], [out_v])
                else:
                    act(out_fn(kc), tb.ap[:, 0:n], AF.Identity, [tb, gsc, modv], [out_v],
                        scale=gsc_ap[:, kc:kc + 1], bias=sh_ap[:, kc:kc + 1])

        def xupdate(c, l, t0, n, lhs_fn, rhs_list, g_off, reads, evac=None):
            for oc in range(8):
                pb = psA.next()
                nj = len(rhs_list)
                for j in range(nj):
                    mm(PS[pb].ap[:, 0:n], lhs_fn(j, oc), rhs_list[j], j == 0, j == nj - 1, reads, PS[pb], fresh=(j == 0))
                gcol = modv.ap[:, l, g_off + oc, c.cond:c.cond + 1]
                if evac is None:
                    stt(xT.ap[:, oc, t0:t0 + n], PS[pb].ap[:, 0:n], gcol,
                        xT.ap[:, oc, t0:t0 + n], ALU.mult, ALU.add, [PS[pb], modv, xT], [xT])
                else:
                    ev = evac[oc % len(evac)]
                    act(ev.ap[:, 0:n], PS[pb].ap[:, 0:n], AF.Copy, [PS[pb], modv], [ev], scale=gcol)
                    tt(xT.ap[:, oc, t0:t0 + n], xT.ap[:, oc, t0:t0 + n], ev.ap[:, 0:n], ALU.add, [xT, ev], [xT], eng="pool")

        def layer_pass(c, l):
            T, TT = c.T, c.TT
            cd = c.cond
            for i, (og, osc) in enumerate([(O_GMIX, 8), (O_GFFN, 32)]):
                stt(gsc.ap[:, i, :], modv.ap[:, l, osc:osc + 8, cd], 1.0, prm.ap[:, l, og:og + 8], ALU.add, ALU.mult,
                    [modv, prm], [gsc])
            sh1 = modv.ap[:, l, 0:8, cd]
            sh2 = modv.ap[:, l, 24:32, cd]
            dma("sp", gkv_b.ap, d_gkv[l].partition_broadcast(128), writes=[gkv_b])
            dma("pool", wuq.ap[:, :, 0:768], d_wuq[l].rearrange("(k p) c -> p k c", p=128), writes=[wuq])
            dma("pool", wukv.ap, d_wukv[l], writes=[wukv])
            for kc in range(2):
                ts(wuq.ap[:, kc, 0:768], wuq.ap[:, kc, 0:768], prm.ap[:, l, O_GQ + kc:O_GQ + kc + 1], None, ALU.mult, None,
                   [wuq, prm], [wuq])
                src = wuq.ap[:, kc, 0:768].rearrange("p (h c) -> p h c", h=8)[:, :, 64:96].rearrange("p h (a f) -> p h a f", a=2)
                dst = wuq.ap[:, kc, 768:1024].rearrange("p (h a f) -> p h a f", h=8, a=2)
                for a in range(2):
                    ts(dst[:, :, a, 0:8], src[:, :, a, 8:16], -1.0, None, ALU.mult, None, [wuq], [wuq])
                    cp(dst[:, :, a, 8:16], src[:, :, a, 0:8], [wuq], [wuq])

            wl = d_win[l]

            def wsl(c0, c1):
                return wl[:, c0:c1].rearrange("(k p) c -> p k c", p=128)

            v_cm = load_piece([(lambda a: r3(a, 8), wsl(1192, 1704))])
            v_ssd = load_piece([(lambda a: r3(a, 8), wsl(672, 1184))])
            v_misc = load_piece([(lambda a: r3(a, 8)[:, :, 0:256], wsl(416, 672)),
                                 (lambda a: r3(a, 8)[:, :, 256:416], wsl(256, 416)),
                                 (lambda a: r3(a, 8)[:, :, 448:456], wsl(1184, 1192))])
            v_q = load_piece([(lambda a: r3(a, 8)[:, :, 0:256], wsl(0, 256))])
            w_cm, w_ssd, w_misc, w_q = r3(v_cm.ap, 8), r3(v_ssd.ap, 8), r3(v_misc.ap, 8), r3(v_q.ap, 8)
            for kc in range(8):
                src = w_misc[:, kc, 384:416].rearrange("p (a f) -> p a f", a=2)
                dst = w_misc[:, kc, 416:448].rearrange("p (a f) -> p a f", a=2)
                ts(dst[:, :, 0:8], src[:, :, 8:16], -1.0, None, ALU.mult, None, [v_misc], [v_misc])
                cp(dst[:, :, 8:16], src[:, :, 0:8], [v_misc], [v_misc])

            sp = Bump([(SC0, SBYTES)])
            xg2 = alloc(sp, "xg2", [128, 8, 512], BF16)
            xgs = [xg0, xg2]
            sq = [alloc(sp, f"sq{i}", [128, 512], BF16) for i in range(2)]
            sd = alloc(sp, "sd", [128, 512], F32)
            rstd = alloc(sp, "rstd", [128, 512], F32)
            tmpb = [alloc(sp, f"tmpb{i}", [128, 512], F32) for i in range(2)]
            sig = alloc(sp, "sig", [128, 512], F32)
            sqk = alloc(sp, "sqk", [128, 512], BF16)
            t1 = alloc(sp, "t1", [128, 512], F32)
            t2 = alloc(sp, "t2", [128, 512], F32)
            rq = alloc(sp, "rq", [128, 512], F32)
            tmo = alloc(sp, "tmo", [128, 160], F32)
            tmo2 = alloc(sp, "tmo2", [128, 128], F32)
            ssq = alloc(sp, "ssq", [128, 2], F32)
            junk = alloc(sp, "junk", [128, 256], F32)
            koff = PAST if c.ctx else 0

            for si, (s0, sl) in enumerate(c.seqs):
                g0 = s0 + si * 32
                memset(gpad.ap[:, :, g0:g0 + 16], 0.0, [gpad])
                memset(gpad.ap[:, :, g0 + 16 + sl:g0 + 32 + sl], 0.0, [gpad])
                x0 = s0 + si * 4
                memset(xbcp.ap[:, :, x0:x0 + 2], 0.0, [xbcp])
                memset(xbcp.ap[:, :, x0 + 2 + sl:x0 + 4 + sl], 0.0, [xbcp])

            if c.ctx:
                cst_ = alloc(sp, "cstg", [128, 128], F32)
                kst_ = alloc(sp, "kstg", [128, 96], F32)
                memset(kst_.ap, 0.0, [kst_])
                for b in range(2):
                    dma("sp", cst_.ap, d_cckv[l, b * 128:(b + 1) * 128, :], writes=[cst_])
                    pb = psA.next()
                    tp(PS[pb].ap[:, 0:128], cst_.ap, identf.ap, [cst_, cstf], PS[pb], fresh=True)
                    cp(ckvnT.ap[:, b * 128:(b + 1) * 128], PS[pb].ap[:, 0:128], [PS[pb]], [ckvnT])
                    dma("sp", kst_.ap[:, 64:96], d_ckr[l, b * 128:(b + 1) * 128, :], writes=[kst_])
                    pb = psA.next()
                    tp(PS[pb].ap[0:96, 0:128], kst_.ap, identf.ap, [kst_, cstf], PS[pb], fresh=True)
                    cp(krT.ap[64:96, b * 128:(b + 1) * 128], PS[pb].ap[64:96, 0:128], [PS[pb]], [krT])

            cur = [xg0]

            def fm(wap, c0, m, n, reads, out_rows=None):
                xg = cur[0]
                pb = psA.next()
                o = PS[pb].ap[0:m, 0:n] if out_rows is None else PS[pb].ap[out_rows[0]:out_rows[1], 0:n]
                for kc in range(8):
                    mm(o, wap[:, kc, c0:c0 + m], xg.ap[:, kc, 0:n], kc == 0, kc == 7, reads + [xg], PS[pb], fresh=(kc == 0))
                return pb

            def do_norm(ti):
                si_, t0_ = c.tiles[ti]
                xv = xgs[ti % 2]
                norm_tile(c, (sq, sd, rstd, tmpb), t0_, TT, gsc.ap[:, 0, :], sh1, lambda kc: xv.ap[:, kc, 0:TT], xv, sq_dve=True)

            do_norm(0)
            for ti, (si, t0) in enumerate(c.tiles):
                n = TT
                s0, sl = c.seqs[si]
                if ti + 1 < len(c.tiles):
                    do_norm(ti + 1)
                xg = xgs[ti % 2]
                cur[0] = xg
                gofs = si * 32 + 16 + t0
                for j in range(2):
                    pa = fm(w_cm, j * 128, 128, n, [v_cm])
                    pbk = fm(w_cm, 256 + j * 128, 128, n, [v_cm])
                    act(sig.ap[:, 0:n], PS[pbk].ap[:, 0:n], AF.Sigmoid, [PS[pbk]], [sig])
                    tt(gpad.ap[:, j, gofs:gofs + n], PS[pa].ap[:, 0:n], sig.ap[:, 0:n], ALU.mult, [PS[pa], sig], [gpad])
                xofs = si * 4 + 2 + t0
                for j in range(4):
                    pa = fm(w_ssd, j * 128, 128, n, [v_ssd])
                    act(xbcp.ap[:, j, xofs:xofs + n], PS[pa].ap[:, 0:n], AF.Copy, [PS[pa]], [xbcp])
                for b in range(n // 128):
                    blk = (t0 + b * 128) // 128
                    pb = psA.next()
                    for kc in range(8):
                        mm(PS[pb].ap[:, 0:256], xg.ap[:, kc, b * 128:(b + 1) * 128], w_misc[:, kc, 0:256], kc == 0, kc == 7,
                           [xg, v_misc], PS[pb], fresh=(kc == 0))
                    act(szb.ap[:, blk, :], PS[pb].ap[:, 0:256], AF.Silu, [PS[pb]], [szb])
                    pb = psA.next()
                    for kc in range(8):
                        mm(PS[pb].ap[:, 0:8], xg.ap[:, kc, b * 128:(b + 1) * 128], w_misc[:, kc, 448:456], kc == 0, kc == 7,
                           [xg, v_misc], PS[pb], fresh=(kc == 0))
                    tt(dtr.ap[:, blk, :], PS[pb].ap[:, 0:8], dtb_b.ap[:, l * 8:(l + 1) * 8], ALU.add, [PS[pb], dtb_b], [dtr])
                    if c.kind == "P":
                        pb = psA.next()
                        for kc in range(8):
                            mm(PS[pb].ap[:, 0:160], xg.ap[:, kc, b * 128:(b + 1) * 128], w_misc[:, kc, 256:416], kc == 0, kc == 7,
                               [xg, v_misc], PS[pb], fresh=(kc == 0))
                        cp(tmo.ap, PS[pb].ap[:, 0:160], [PS[pb]], [tmo])
                        tloc = t0 - s0 + b * 128
                        dma("sp", o_kr[si, l, tloc:tloc + 128, :], tmo.ap[:, 128:160], reads=[tmo])
                        memset(ssq.ap[:, 0:1], 0.0, [ssq])
                        act(junk.ap[:, 0:128], tmo.ap[:, 0:128], AF.Square, [tmo, ssq], [junk, ssq], accum_out=ssq.ap[:, 0:1])
                        act(ssq.ap[:, 1:2], ssq.ap[:, 0:1], AF.Ln, [ssq], [ssq], scale=1.0 / 128, bias=1e-6)
                        act(ssq.ap[:, 1:2], ssq.ap[:, 1:2], AF.Exp, [ssq], [ssq], scale=-0.5)
                        stt(tmo2.ap, tmo.ap[:, 0:128], ssq.ap[:, 1:2], gkv_b.ap, ALU.mult, ALU.mult, [tmo, ssq, gkv_b], [tmo2])
                        dma("sp", o_ckv[si, l, tloc:tloc + 128, :], tmo2.ap, reads=[tmo2])
                pa = fm(w_misc, 256, 128, n, [v_misc])
                act(sqk.ap[:, 0:n], PS[pa].ap[:, 0:n], AF.Square, [PS[pa]], [sqk])
                mm(PS[6].ap[:, 0:n], onesb.ap, sqk.ap[:, 0:n], True, True, [onesb, sqk], PS[6], fresh=True)
                act(sd.ap[:, 0:n], PS[6].ap[:, 0:n], AF.Ln, [PS[6]], [sd], scale=1.0 / 128, bias=1e-6)
                act(rstd.ap[:, 0:n], sd.ap[:, 0:n], AF.Exp, [sd], [rstd], scale=-0.5)
                kofs = koff + t0 if c.ctx else t0
                stt(ckvnT.ap[:, kofs:kofs + n], PS[pa].ap[:, 0:n], prm.ap[:, l, O_GKV:O_GKV + 1], rstd.ap[:, 0:n],
                    ALU.mult, ALU.mult, [PS[pa], prm, rstd], [ckvnT])
                pa = fm(w_misc, 384, 32, n, [v_misc], out_rows=(64, 96))
                if c.rope:
                    pbk = fm(w_misc, 416, 32, n, [v_misc], out_rows=(64, 96))
                    tt(t1.ap[64:96, 0:n], PS[pa].ap[64:96, 0:n], ropeC.ap[64:96, t0:t0 + n], ALU.mult, [PS[pa], ropeC], [t1])
                    tt(t2.ap[64:96, 0:n], PS[pbk].ap[64:96, 0:n], ropeS.ap[64:96, t0:t0 + n], ALU.mult, [PS[pbk], ropeS], [t2])
                    tt(krT.ap[64:96, kofs:kofs + n], t1.ap[64:96, 0:n], t2.ap[64:96, 0:n], ALU.add, [t1, t2], [krT])
                else:
                    act(krT.ap[64:96, kofs:kofs + n], PS[pa].ap[64:96, 0:n], AF.Copy, [PS[pa]], [krT])
                pq = [fm(w_q, j * 128, 128, n, [v_q]) for j in range(2)]
                for j in range(2):
                    act(sq[j].ap[:, 0:n], PS[pq[j]].ap[:, 0:n], AF.Square, [PS[pq[j]]], [sq[j]])
                    mm(PS[6].ap[:, 0:n], onesb.ap, sq[j].ap[:, 0:n], j == 0, j == 1, [onesb, sq[j]], PS[6], fresh=(j == 0))
                act(sd.ap[:, 0:n], PS[6].ap[:, 0:n], AF.Ln, [PS[6]], [sd], scale=1.0 / 256, bias=1e-6)
                act(rq.ap[:, 0:n], sd.ap[:, 0:n], AF.Exp, [sd], [rq], scale=-0.5)
                for j in range(2):
                    tt(qlat.ap[:, j, t0:t0 + n], PS[pq[j]].ap[:, 0:n], rq.ap[:, 0:n], ALU.mult, [PS[pq[j]], rq], [qlat])

            if stop == "I":
                return
            v_wor = load_piece([(lambda a: r3(a, 4), d_wout[l, 512:1024, :].rearrange("(k p) c -> p k c", p=128))])
            w_or = r3(v_wor.ap, 4)
            for j in range(2):
                ts(w_or[:, j, :], w_or[:, j, :], prm.ap[:, l, O_SNG + j:O_SNG + j + 1], None, ALU.mult, None, [v_wor, prm], [v_wor])

            sp = Bump([(SC0, SBYTES), rg(xg0)])
            dg = alloc(sp, "dg", [128, 2, 31, 128], BF16)
            cvf = [alloc(sp, f"cvf{j}", [128, 512], F32) for j in range(2)]
            sqf = [alloc(sp, f"sqf{j}", [128, 512], F32) for j in range(2)]
            mean = alloc(sp, "mean", [128, 512], F32)
            var = alloc(sp, "var", [128, 512], F32)
            rr = alloc(sp, "rr", [128, 512], F32)
            uu = alloc(sp, "uu", [128, 512], F32)
            cmix = alloc(sp, "cmix", [128, 2, 512], BF16)
            for j in range(2):
                for tap in range(31):
                    col = O_CCW + tap * 2 + j
                    ts(dg.ap[:, j, tap, :], identb.ap, prm.ap[:, l, col:col + 1], None, ALU.mult, None, [identb, prm], [dg])
            for (si, t0) in c.tiles:
                n = TT
                gofs = si * 32 + 1 + t0
                for j in range(2):
                    pb = psBk.next()
                    for tap in range(31):
                        mm(PS[pb].ap[:, 0:n], dg.ap[:, j, tap, :], gpad.ap[:, j, gofs + tap:gofs + tap + n], tap == 0, tap == 30,
                           [dg, gpad], PS[pb], fresh=(tap == 0))
                    bcol = prm.ap[:, l, O_CCB + j:O_CCB + j + 1]
                    act(cvf[j].ap[:, 0:n], PS[pb].ap[:, 0:n], AF.Identity, [PS[pb], prm], [cvf[j]], bias=bcol)
                    act(sqf[j].ap[:, 0:n], PS[pb].ap[:, 0:n], AF.Square, [PS[pb], prm], [sqf[j]], bias=bcol)
                for j in range(2):
                    mm(PS[6].ap[:, 0:n], onesf.ap, cvf[j].ap[:, 0:n], j == 0, j == 1, [onesf, cvf[j]], PS[6], fresh=(j == 0))
                for j in range(2):
                    mm(PS[7].ap[:, 0:n], onesf.ap, sqf[j].ap[:, 0:n], j == 0, j == 1, [onesf, sqf[j]], PS[7], fresh=(j == 0))
                ts(mean.ap[:, 0:n], PS[6].ap[:, 0:n], 1.0 / 256, None, ALU.mult, None, [PS[6]], [mean])
                tt(var.ap[:, 0:n], mean.ap[:, 0:n], mean.ap[:, 0:n], ALU.mult, [mean], [var])
                stt(var.ap[:, 0:n], PS[7].ap[:, 0:n], 1.0 / 256, var.ap[:, 0:n], ALU.mult, ALU.subtract, [PS[7], var], [var])
                act(var.ap[:, 0:n], var.ap[:, 0:n], AF.Ln, [var], [var], bias=1e-5)
                act(rr.ap[:, 0:n], var.ap[:, 0:n], AF.Exp, [var], [rr], scale=-0.5)
                for j in range(2):
                    tt(uu.ap[:, 0:n], cvf[j].ap[:, 0:n], mean.ap[:, 0:n], ALU.subtract, [cvf[j], mean], [uu])
                    tt(uu.ap[:, 0:n], uu.ap[:, 0:n], rr.ap[:, 0:n], ALU.mult, [uu, rr], [uu])
                    act(cmix.ap[:, j, 0:n], uu.ap[:, 0:n], AF.Silu, [uu, prm], [cmix],
                        scale=prm.ap[:, l, O_CLG + j:O_CLG + j + 1], bias=prm.ap[:, l, O_CLB + j:O_CLB + j + 1])
                xupdate(c, l, t0, n, lambda j, oc: w_or[:, 2 + j, oc * 128:(oc + 1) * 128],
                        [cmix.ap[:, 0, 0:n], cmix.ap[:, 1, 0:n]], 16, [v_wor, cmix])

            if stop == "C":
                return
            ssd_phase(c, l, w_or, v_wor)
            if stop in ("S", "S1", "S2", "S3"):
                return
            v_woa = load_piece([(lambda a: r3(a, 4), d_wout[l, 0:512, :].rearrange("(k p) c -> p k c", p=128))])
            mla_phase(c, l, r3(v_woa.ap, 4), v_woa)
            if stop == "M":
                return
            ffn_phase(c, l)

        def ssd_phase(c, l, w_or, v_wor):
            T, TT = c.T, c.TT
            nblk = c.nblk
            g_r, x_r, z_r = rg(gpad), rg(xbcp), rg(szb)
            sp = Bump([(SC0, SBYTES), g_r])
            dg5 = alloc(sp, "dg5", [128, 4, 5, 128], BF16)
            xsT = alloc(sp, "xsT", [128, 2, 512], BF16)
            BCt = alloc(sp, "BCt", [128, 3, TS], BF16)
            xs_tm = alloc(sp, "xs_tm", [128, 16, 256], BF16)
            B_tm = alloc(sp, "B_tm", [128, 16, 128], BF16)
            dt = alloc(sp, "dt", [128, 16, 8], F32)
            dta = alloc(sp, "dta", [128, 16, 8], F32)
            hTf = [alloc(sp, f"hTf{d}", [128, 2, 64], F32) for d in range(2)]
            hTb = alloc(sp, "hTb", [128, 2, 64], BF16)
            sstg = alloc(sp, "sstg", [128, 128], F32)
            sp2 = Bump([tuple(r) for r in sp.ranges] + [x_r, rg(xg)])
            nb8 = nblk * 8
            dtf = dtr.ap.rearrange("p b e -> p (b e)")[:, 0:nb8]
            act(dt.ap.rearrange("p b e -> p (b e)")[:, 0:nb8], dtf, AF.Exp, [dtr], [dt])
            act(dt.ap.rearrange("p b e -> p (b e)")[:, 0:nb8], dt.ap.rearrange("p b e -> p (b e)")[:, 0:nb8], AF.Ln, [dt], [dt], bias=1.0)
            tt(dta.ap[:, 0:nblk, :], dt.ap[:, 0:nblk, :], a_b.ap[:, l * 8:(l + 1) * 8].unsqueeze(1).to_broadcast([128, nblk, 8]),
               ALU.mult, [dt, a_b], [dta])
            memset(BCt.ap[64:128, 1, 0:T], 0.0, [BCt])
            memset(BCt.ap[0:64, 2, 0:T], 0.0, [BCt])
            for ch in range(4):
                for tap in range(5):
                    col = O_SCW + tap * 4 + ch
                    ts(dg5.ap[:, ch, tap, :], identb.ap, prm.ap[:, l, col:col + 1], None, ALU.mult, None, [identb, prm], [dg5])
            for (si, t0) in c.tiles:
                n = TT
                xofs = si * 4 + t0
                for ch in range(4):
                    pb = psBk.next()
                    for tap in range(5):
                        mm(PS[pb].ap[:, 0:n], dg5.ap[:, ch, tap, :], xbcp.ap[:, ch, xofs + tap:xofs + tap + n], tap == 0, tap == 4,
                           [dg5, xbcp], PS[pb], fresh=(tap == 0))
                    bias_ = prm.ap[:, l, O_SCB + ch:O_SCB + ch + 1]
                    if ch < 2:
                        act(xsT.ap[:, ch, 0:n], PS[pb].ap[:, 0:n], AF.Silu, [PS[pb], prm], [xsT], bias=bias_)
                    elif ch == 2:
                        act(BCt.ap[:, 0, t0:t0 + n], PS[pb].ap[:, 0:n], AF.Silu, [PS[pb], prm], [BCt], bias=bias_)
                    else:
                        act(BCt.ap[0:64, 1, t0:t0 + n], PS[pb].ap[0:64, 0:n], AF.Silu, [PS[pb], prm], [BCt], bias=bias_[0:64])
                        act(BCt.ap[64:128, 2, t0:t0 + n], PS[pb].ap[64:128, 0:n], AF.Silu, [PS[pb], prm], [BCt], bias=bias_[64:128])
                for b in range(n // 128):
                    blk = (t0 + b * 128) // 128
                    pb = psA.next()
                    for j in range(2):
                        tp(PSB[pb].ap[:, j * 128:(j + 1) * 128], xsT.ap[:, j, b * 128:(b + 1) * 128], identb.ap, [xsT, identb], PS[pb], fresh=(j == 0))
                    tp(PSB[pb].ap[:, 256:384], BCt.ap[:, 0, t0 + b * 128:t0 + (b + 1) * 128], identb.ap, [BCt, identb], PS[pb])
                    cp(xs_tm.ap[:, blk, :], PSB[pb].ap[:, 0:256], [PS[pb]], [xs_tm])
                    cp(B_tm.ap[:, blk, :], PSB[pb].ap[:, 256:384], [PS[pb]], [B_tm])
            if stop == "S1":
                return
            hst = alloc(sp2, "hst", [128, 16, 2, 64], BF16)
            Gm = [alloc(sp2, f"Gm{d}", [128, 2, 128], F32) for d in range(2)]
            Lm = alloc(sp2, "Lm", [128, 4, 128], F32)
            seg = alloc(sp2, "seg", [128, 4, 128], F32)
            MT = [[alloc(sp2, f"MT{i}{d}", [128, 4, 128], BF16) for d in range(2)] for i in range(2)]
            xdt = [[alloc(sp2, f"xdt{i}{d}", [128, 4, 64], BF16) for d in range(2)] for i in range(2)]
            ee = [alloc(sp2, f"ee{i}", [128, 32], F32) for i in range(2)]
            cds = [alloc(sp2, f"cds{i}", [128, 2, 2], F32) for i in range(2)]
            xdd = [alloc(sp2, f"xdd{i}", [128, 4, 64], BF16) for i in range(2)]
            wv = alloc(sp2, "wv", [128, 4], F32)
            yo = alloc(sp2, "yo", [128, 8, 64], F32)
            y1 = alloc(sp2, "y1", [128, 256], F32)
            y2 = alloc(sp2, "y2", [128, 256], F32)
            y3 = alloc(sp2, "y3", [128, 256], F32)
            yn = [alloc(sp2, f"yn{i}", [128, 256], BF16) for i in range(2)]
            ssq = alloc(sp2, "ssq2", [128, 2], F32)
            smix = alloc(sp2, "smix", [128, 2, 512], BF16)
            junk = yo
            evb = [View(Lm.name, Lm.ap.rearrange("p h s -> p (h s)")), View(seg.name, seg.ap.rearrange("p h s -> p (h s)"))]

            def small_mm(blk, dirs, ee_, cds_):
                if len(dirs) == 2:
                    A = dta.ap[:, blk, 0:8]
                    for k, (lt, lv) in enumerate([(SU[0].ap, cstf), (SU[1].ap, cstf), (TRI[0].ap, cstf), (TRI[1].ap, cstf), (onesf.ap, onesf)]):
                        mm(PS[6].ap[:, k * 8:k * 8 + 8], lt, A, True, True, [lv, dta], PS[6], fresh=(k == 0))
                    act(ee_.ap[:, 0:32], PS[6].ap[:, 0:32], AF.Exp, [PS[6]], [ee_])
                else:
                    A = dta.ap[:, blk, 4:8]
                    mm(PS[6].ap[:, 12:16], SU[1].ap, A, True, True, [cstf, dta], PS[6], fresh=True)
                    mm(PS[6].ap[:, 36:40], onesf.ap, A, True, True, [onesf, dta], PS[6])
                    act(ee_.ap[:, 12:16], PS[6].ap[:, 12:16], AF.Exp, [PS[6]], [ee_])
                for d in dirs:
                    act(cds_.ap[0:64, d, :], PS[6].ap[0:64, 32 + d * 4:34 + d * 4], AF.Exp, [PS[6]], [cds_])
                    act(cds_.ap[64:128, d, :], PS[6].ap[64:128, 34 + d * 4:36 + d * 4], AF.Exp, [PS[6]], [cds_])

            DEC = [slice(0, 4), slice(12, 16)]

            def state_pre(blk, d, ee_, xdd_):
                tt(wv.ap, dt.ap[:, blk, d * 4:(d + 1) * 4], ee_.ap[:, DEC[d]], ALU.mult, [dt, ee_], [wv])
                tt(xdd_.ap, xs_tm.ap[:, blk, :].rearrange("p (h q) -> p h q", h=4), wv.ap.unsqueeze(2).to_broadcast([128, 4, 64]),
                   ALU.mult, [xs_tm, wv], [xdd_])

            def state_post(blk, d, cds_, xdd_):
                pb = 7
                for h in range(4):
                    g, j = h // 2, h % 2
                    mm(PS[pb].ap[g * 64:(g + 1) * 64, j * 64:(j + 1) * 64], B_tm.ap[:, blk, g * 64:(g + 1) * 64], xdd_.ap[:, h, :],
                       True, True, [B_tm, xdd_], PS[pb], fresh=(h == 0))
                for j in range(2):
                    stt(hTf[d].ap[:, j, :], hTf[d].ap[:, j, :], cds_.ap[:, d, j:j + 1], PS[pb].ap[:, j * 64:(j + 1) * 64],
                        ALU.mult, ALU.add, [hTf[d], cds_, PS[pb]], [hTf[d]])

            def main_pre(blk, i):
                tk = slice(blk * 128, (blk + 1) * 128)
                pg = psA.next()
                mm(PS[pg].ap[:, 0:256], BCt.ap[:, 0, tk], BCt.ap[:, 1:3, tk], True, True, [BCt], PS[pg], fresh=True)
                small_mm(blk, [0, 1], ee[i], cds[i])
                for d in range(2):
                    tt(Gm[d].ap, PS[pg].ap[:, 0:256].rearrange("p (g s) -> p g s", g=2),
                       TRI[d].ap.unsqueeze(1).to_broadcast([128, 2, 128]), ALU.mult, [PS[pg], cstf], [Gm[d]])
                for d in range(2):
                    tt(Lm.ap, SU[d].ap.unsqueeze(1).to_broadcast([128, 4, 128]),
                       dta.ap[:, blk, d * 4:(d + 1) * 4].unsqueeze(2).to_broadcast([128, 4, 128]), ALU.mult, [cstf, dta], [Lm], eng="pool")
                    pdf = psA.next()
                    for h in range(4):
                        mm(PS[pdf].ap[:, h * 128:(h + 1) * 128], Lm.ap[:, h, :], TRI[d].ap, True, True, [Lm, cstf], PS[pdf], fresh=(h == 0))
                    act(seg.ap.rearrange("p h s -> p (h s)"), PS[pdf].ap, AF.Exp, [PS[pdf]], [seg])
                    for g in range(2):
                        tt(MT[i][d].ap[:, 2 * g:2 * g + 2, :], seg.ap[:, 2 * g:2 * g + 2, :],
                           Gm[d].ap[:, g:g + 1, :].to_broadcast([128, 2, 128]), ALU.mult, [seg, Gm[d]], [MT[i][d]])
                    tt(xdt[i][d].ap, xs_tm.ap[:, blk, :].rearrange("p (h q) -> p h q", h=4),
                       dt.ap[:, blk, d * 4:(d + 1) * 4].unsqueeze(2).to_broadcast([128, 4, 64]), ALU.mult, [xs_tm, dt], [xdt[i][d]], eng="pool")
                state_pre(blk, 0, ee[i], xdd[i])

            def main_post(blk, i):
                tk = slice(blk * 128, (blk + 1) * 128)
                tt(y3.ap.rearrange("p (h q) -> p h q", h=4), xs_tm.ap[:, blk, :].rearrange("p (h q) -> p h q", h=4),
                   dsum_b.ap[:, l, :].unsqueeze(2).to_broadcast([128, 4, 64]), ALU.mult, [xs_tm, dsum_b], [y3], eng="pool")
                py = psBk.next()
                for h in range(4):
                    for d in range(2):
                        mm(PS[py].ap[:, h * 64:(h + 1) * 64], MT[i][d].ap[:, h, :], xdt[i][d].ap[:, h, :], d == 0, d == 1,
                           [MT[i][d], xdt[i][d]], PS[py], fresh=(h == 0 and d == 0))
                po = psBk.next()
                for d in range(2):
                    hsrc_v = hTb if d == 0 else hst
                    hsrc = hTb.ap if d == 0 else hst.ap[:, blk]
                    for g in range(2):
                        mm(PS[po].ap[:, d * 256 + g * 128:d * 256 + (g + 1) * 128], BCt.ap[:, 1 + g, tk], hsrc,
                           True, True, [BCt, hsrc_v], PS[po], fresh=(d == 0 and g == 0))
                state_post(blk, 0, cds[i], xdd[i])
                cp(hTb.ap, hTf[0].ap, [hTf[0]], [hTb], eng="pool")
                for d in range(2):
                    e0 = 16 + 12 * d
                    tt(yo.ap[:, 4 * d:4 * d + 4, :], PS[po].ap[:, d * 256:(d + 1) * 256].rearrange("p (e q) -> p e q", e=4),
                       ee[i].ap[:, e0:e0 + 4].unsqueeze(2).to_broadcast([128, 4, 64]), ALU.mult, [PS[po], ee[i]], [yo])
                yof = yo.ap.rearrange("p e q -> p (e q)")
                tt(y1.ap, yof[:, 0:256], yof[:, 256:512], ALU.add, [yo], [y1])
                tt(y1.ap, y1.ap, PS[py].ap[:, 0:256], ALU.add, [y1, PS[py]], [y1])
                tt(y1.ap, y1.ap, y3.ap, ALU.add, [y1, y3], [y1])
                tt(y2.ap, y1.ap, szb.ap[:, blk, :], ALU.mult, [y1, szb], [y2])
                memset(ssq.ap[:, 0:1], 0.0, [ssq])
                act(junk.ap.rearrange("p e q -> p (e q)")[:, 0:256], y2.ap, AF.Square, [y2, ssq], [junk, ssq], accum_out=ssq.ap[:, 0:1])
                act(ssq.ap[:, 1:2], ssq.ap[:, 0:1], AF.Ln, [ssq], [ssq], scale=1.0 / 256, bias=1e-6)
                act(ssq.ap[:, 1:2], ssq.ap[:, 1:2], AF.Exp, [ssq], [ssq], scale=-0.5)
                ts(yn[i].ap, y2.ap, ssq.ap[:, 1:2], None, ALU.mult, None, [y2, ssq], [yn[i]])

            def main_tail(blk, i):
                bt = (blk * 128) % TT
                pt = psA.next()
                for j in range(2):
                    tp(PSB[pt].ap[:, j * 128:(j + 1) * 128], yn[i].ap[:, j * 128:(j + 1) * 128], identb.ap, [yn[i], identb], PS[pt], fresh=(j == 0))
                cp(smix.ap[:, :, bt:bt + 128], PSB[pt].ap[:, 0:256].rearrange("p (j t) -> p j t", j=2), [PS[pt]], [smix])
                if bt + 128 == TT:
                    t0 = blk * 128 + 128 - TT
                    xupdate(c, l, t0, TT, lambda j, oc: w_or[:, j, oc * 128:(oc + 1) * 128],
                            [smix.ap[:, 0, 0:TT], smix.ap[:, 1, 0:TT]], 16, [v_wor, smix], evac=evb)

            for si, (s0, sl) in enumerate(c.seqs):
                blks = list(range(s0 // 128, (s0 + sl) // 128))
                for d in range(2):
                    if c.ctx:
                        dma("sp", sstg.ap.rearrange("p (g n) -> p g n", g=2),
                            d_st[l, d].rearrange("(g j) p n -> (j p) g n", g=2), writes=[sstg])
                        pb = psA.next()
                        tp(PS[pb].ap[:, 0:128], sstg.ap, identf.ap, [sstg, cstf], PS[pb], fresh=True)
                        cp(hTf[d].ap.rearrange("p j q -> p (j q)"), PS[pb].ap[:, 0:128], [PS[pb]], [hTf[d]])
                    else:
                        memset(hTf[d].ap, 0.0, [hTf[d]])
                rb_ = list(reversed(blks))
                small_mm(rb_[0], [1], ee[0], cds[0])
                state_pre(rb_[0], 1, ee[0], xdd[0])
                for k, blk in enumerate(rb_):
                    i = k % 2
                    if k + 1 < len(rb_):
                        small_mm(rb_[k + 1], [1], ee[1 - i], cds[1 - i])
                        state_pre(rb_[k + 1], 1, ee[1 - i], xdd[1 - i])
                    cp(hst.ap[:, blk], hTf[1].ap, [hTf[1]], [hst], eng="pool")
                    state_post(blk, 1, cds[i], xdd[i])
                if stop == "S2":
                    return
                cp(hTb.ap, hTf[0].ap, [hTf[0]], [hTb], eng="pool")
                main_pre(blks[0], 0)
                for k, blk in enumerate(blks):
                    if k + 1 < len(blks):
                        main_pre(blks[k + 1], (k + 1) % 2)
                    main_post(blk, k % 2)
                    if k >= 1:
                        main_tail(blks[k - 1], (k - 1) % 2)
                main_tail(blks[-1], (len(blks) - 1) % 2)
                if stop == "S3":
                    return
                if c.kind == "P":
                    for d in range(2):
                        pb = psA.next()
                        tp(PS[pb].ap[:, 0:128], hTf[d].ap.rearrange("p j q -> p (j q)"), identf.ap, [hTf[d], cstf], PS[pb], fresh=True)
                        cp(sstg.ap, PS[pb].ap[:, 0:128], [PS[pb]], [sstg])
                        dma("sp", o_ssd[si, l, d].rearrange("(g j) p n -> (j p) g n", g=2),
                            sstg.ap.rearrange("p (g n) -> p g n", g=2), reads=[sstg])

        def mla_phase(c, l, w_oa, v_woa):
            T, TT = c.T, c.TT
            g_r, x_r, z_r = rg(gpad), rg(xbcp), rg(szb)
            attnT = alloc(Bump([x_r]), "attnT", [128, 4, TS], BF16)
            sp = Bump([(SC0, SBYTES), g_r, z_r])
            NKB = (PAST + TS) // 128
            KT = [alloc(sp, f"KT{i}", [128, PAST + TS], BF16) for i in range(2)]
            VB = [(alloc(sp, f"Ve{i}", [128, NKB, 128], BF16), alloc(sp, f"Vo{i}", [128, NKB, 128], BF16)) for i in range(2)]
            for i in range(2):
                memset(VB[i][0].ap[:, :, 64:128], 1.0, [VB[i][0]])
                memset(VB[i][1].ap[:, :, 0:64], 1.0, [VB[i][1]])
            QT = [alloc(sp, f"QT{i}", [128, TS], BF16) for i in range(2)]
            PT = [alloc(sp, f"PT{i}", [128, 512], BF16) for i in range(3)]
            t1 = alloc(sp, "mt1", [128, 512], F32)
            t2 = alloc(sp, "mt2", [128, 512], F32)
            rbs = Bump([(sp.take(2048),) * 2])
            _b0 = rbs.ranges[0][0]
            rbs = None
            def half_view(nm, b0, p0):
                ap = S[p0:p0 + 64, b0:b0 + 2048].bitcast(F32)
                uid[0] += 1
                n2 = f"{nm}.{uid[0]}"
                P.region(n2, "sbuf", p0, p0 + 64, b0, b0 + 2048)
                return View(n2, ap)
            _b1 = sp.take(2048)
            rb_src = {0: half_view("rbsE", _b0, 64), 1: half_view("rbsO", _b0, 0)}
            rb_dst = {0: half_view("rbdE", _b1, 0), 1: half_view("rbdO", _b1, 64)}
            ptr = Rot(PT)
            LA = 2
            psS = Rot([0, 1, 2])
            PPr = Rot([3, 7])
            accb = Rot([4, 5, 6])

            heads = []
            for si, (s0, sl) in enumerate(c.seqs):
                for h in range(8):
                    heads.append((si, s0, sl, h))

            def prep(idx):
                si, s0, sl, h = heads[idx]
                hp, hh = h // 2, h % 2
                Tk = (PAST if c.ctx else 0) + sl
                k0 = 0 if c.ctx else s0
                nkb = Tk // 128
                qtiles = [t for (s_, t) in c.tiles if s_ == si]
                kt, qt = KT[idx % 2], QT[idx % 2]
                Ve, Vo = VB[(idx // 2) % 2]
                if hh == 0:
                    vcols = wukv.ap.rearrange("p (h c) -> p h c", h=8)[:, 2 * hp:2 * hp + 2, 64:128]
                    for kb in range(nkb):
                        PP = PPr.next()
                        mm(PS[PP].ap[:, 0:128], ckvnT.ap[:, k0 + kb * 128:k0 + (kb + 1) * 128], vcols, True, True, [ckvnT, wukv], PS[PP], fresh=True)
                        cp(Ve.ap[:, kb, 0:64], PS[PP].ap[:, 0:64], [PS[PP]], [Ve])
                        cp(Vo.ap[:, kb, 64:128], PS[PP].ap[:, 64:128], [PS[PP]], [Vo])
                        yield
                for k1 in range(0, Tk, 512):
                    kn = min(512, Tk - k1)
                    PP = PPr.next()
                    mm(PS[PP].ap[0:64, 0:kn], wukv.ap[:, h * 128:h * 128 + 64], ckvnT.ap[:, k0 + k1:k0 + k1 + kn], True, True,
                       [wukv, ckvnT], PS[PP], fresh=True)
                    cp(kt.ap[0:64, k1:k1 + kn], PS[PP].ap[0:64, 0:kn], [PS[PP]], [kt])
                    yield
                cp(kt.ap[64:96, 0:Tk], krT.ap[64:96, k0:k0 + Tk], [krT], [kt], eng="pool")
                for t0 in qtiles:
                    n = TT
                    tl = t0 - s0
                    PP = PPr.next()
                    for kc in range(2):
                        mm(PS[PP].ap[0:96, 0:n], wuq.ap[:, kc, h * 96:(h + 1) * 96], qlat.ap[:, kc, t0:t0 + n], kc == 0, kc == 1,
                           [wuq, qlat], PS[PP], fresh=(kc == 0))
                    if c.rope:
                        cp(qt.ap[0:64, tl:tl + n], PS[PP].ap[0:64, 0:n], [PS[PP]], [qt])
                        tt(t1.ap[64:96, 0:n], PS[PP].ap[64:96, 0:n], ropeC.ap[64:96, t0:t0 + n], ALU.mult, [PS[PP], ropeC], [t1])
                        yield
                        PP2 = PPr.next()
                        for kc in range(2):
                            mm(PS[PP2].ap[64:96, 0:n], wuq.ap[:, kc, 768 + h * 32:768 + (h + 1) * 32], qlat.ap[:, kc, t0:t0 + n],
                               kc == 0, kc == 1, [wuq, qlat], PS[PP2], fresh=(kc == 0))
                        tt(t2.ap[64:96, 0:n], PS[PP2].ap[64:96, 0:n], ropeS.ap[64:96, t0:t0 + n], ALU.mult, [PS[PP2], ropeS], [t2])
                        tt(qt.ap[64:96, tl:tl + n], t1.ap[64:96, 0:n], t2.ap[64:96, 0:n], ALU.add, [t1, t2], [qt])
                    else:
                        cp(qt.ap[0:96, tl:tl + n], PS[PP].ap[0:96, 0:n], [PS[PP]], [qt])
                    yield

            def attn_unit(idx, t0, prev_tail, filler):
                si, s0, sl, h = heads[idx]
                hp, hh = h // 2, h % 2
                nkb = ((PAST if c.ctx else 0) + sl) // 128
                kt, qt = KT[idx % 2], QT[idx % 2]
                vv = VB[(idx // 2) % 2][hh]
                tl = t0 - s0
                n = c.TT
                po = accb.next()
                r0 = 0 if hh == 0 else 64
                d0 = 64 - r0
                pts = []
                for kb in range(nkb + LA):
                    if kb < nkb:
                        pb = psS.next()
                        mm(PS[pb].ap[:, 0:n], kt.ap[0:96, kb * 128:(kb + 1) * 128], qt.ap[0:96, tl:tl + n], True, True,
                           [kt, qt], PS[pb], fresh=True)
                        pt_ = ptr.next()
                        pts.append(pt_)
                        act(pt_.ap[:, 0:n], PS[pb].ap[:, 0:n], AF.Exp, [PS[pb]], [pt_], scale=SCALE)
                    if kb == min(LA, nkb) - 1 and prev_tail is not None:
                        prev_tail()
                        prev_tail = None
                    if kb >= LA:
                        k2 = kb - LA
                        mm(PS[po].ap[:, 0:n], vv.ap[:, k2, :], pts[k2].ap[:, 0:n], k2 == 0, k2 == nkb - 1, [vv, pts[k2]], PS[po], fresh=(k2 == 0))
                    if filler is not None:
                        next(filler, None)

                def tail():
                    rs_, rd_ = rb_src[hh], rb_dst[hh]
                    recip(rs_.ap[:, 0:n], PS[po].ap[d0:d0 + 64, 0:n], [PS[po]], [rs_])
                    dma("sp", rd_.ap[:, 0:n], rs_.ap[:, 0:n], reads=[rs_], writes=[rd_])
                    tt(attnT.ap[r0:r0 + 64, hp, t0:t0 + n], PS[po].ap[r0:r0 + 64, 0:n], rd_.ap[:, 0:n], ALU.mult, [PS[po], rd_], [attnT])
                return tail

            for _ in prep(0):
                pass
            pend = None
            for idx in range(len(heads)):
                si = heads[idx][0]
                filler = prep(idx + 1) if idx + 1 < len(heads) else None
                for t0 in [t for (s_, t) in c.tiles if s_ == si]:
                    pend = attn_unit(idx, t0, pend, filler)
                if filler is not None:
                    for _ in filler:
                        pass
            if pend is not None:
                pend()
            for (si, t0) in c.tiles:
                xupdate(c, l, t0, TT, lambda j, oc: w_oa[:, j, oc * 128:(oc + 1) * 128],
                        [attnT.ap[:, j, t0:t0 + TT] for j in range(4)], 16, [v_woa, attnT])

        def ffn_phase(c, l):
            T, TT = c.T, c.TT
            sp = Bump([(max(SC0, rg(hT)[1]), SBYTES)])
            sq = [alloc(sp, f"fsq{i}", [128, 512], BF16) for i in range(2)]
            sd = alloc(sp, "fsd", [128, 512], F32)
            rstd = alloc(sp, "frstd", [128, 512], F32)
            tmpb = [alloc(sp, f"ftmpb{i}", [128, 512], F32) for i in range(2)]
            sg = [alloc(sp, f"sg{i}", [128, 512], F32) for i in range(2)]
            actb = [alloc(sp, f"actb{i}", [128, 2, 512], BF16) for i in range(2)]
            sh2 = modv.ap[:, l, 24:32, c.cond]
            for (si, t0) in c.tiles:
                norm_tile(c, (sq, sd, rstd, tmpb), t0, TT, gsc.ap[:, 1, :], sh2, lambda kc, t0=t0: hT.ap[:, kc, t0:t0 + TT], hT)
            def load_group(g):
                c0 = g * 256
                v1 = load_piece([(lambda a: r3(a, 8)[:, :, 0:256], d_wg[l, :, c0:c0 + 256].rearrange("(k p) c -> p k c", p=128)),
                                 (lambda a: r3(a, 8)[:, :, 256:512], d_wu[l, :, c0:c0 + 256].rearrange("(k p) c -> p k c", p=128))])
                v2 = load_piece([(lambda a: r3(a, 4)[:, 0:2, :], d_wd[l, c0:c0 + 256, :].rearrange("(k p) c -> p k c", p=128))])
                return (v1, v2)

            units = [(g, t0) for g in range(NFG) for (si, t0) in c.tiles]
            wts = {0: load_group(0), 1: load_group(1)}

            psF = Rot([4, 5, 6, 7])

            def stage_a_groups(u):
                g, t0 = units[u]
                v1, v2 = wts[g]
                wgu = r3(v1.ap, 8)
                n = TT
                ab = actb[u % 2]
                outs = []
                st = {}

                def mk(j, which):
                    def f():
                        pb = psA.next()
                        c0 = (0 if which == 0 else 256) + j * 128
                        for kc in range(8):
                            mm(PS[pb].ap[:, 0:n], wgu[:, kc, c0:c0 + 128], hT.ap[:, kc, t0:t0 + n], kc == 0, kc == 7, [v1, hT], PS[pb], fresh=(kc == 0))
                        st[(j, which)] = pb
                        if which == 1:
                            pg, pu = st[(j, 0)], pb
                            act(sg[j].ap[:, 0:n], PS[pg].ap[:, 0:n], AF.Silu, [PS[pg]], [sg[j]])
                            tt(ab.ap[:, j, 0:n], sg[j].ap[:, 0:n], PS[pu].ap[:, 0:n], ALU.mult, [sg[j], PS[pu]], [ab])
                    return f
                for j in range(2):
                    outs.append(mk(j, 0))
                    outs.append(mk(j, 1))
                return outs

            def stage_b_steps(u):
                g, t0 = units[u]
                v1, v2 = wts[g]
                wdn = r3(v2.ap, 4)
                n = TT
                ab = actb[u % 2]
                outs = []

                def mk(oc):
                    def f():
                        pb = psF.next()
                        for j in range(2):
                            mm(PS[pb].ap[:, 0:n], wdn[:, j, oc * 128:(oc + 1) * 128], ab.ap[:, j, 0:n], j == 0, j == 1, [v2, ab], PS[pb], fresh=(j == 0))
                        stt(xT.ap[:, oc, t0:t0 + n], PS[pb].ap[:, 0:n], modv.ap[:, l, 40 + oc, c.cond:c.cond + 1],
                            xT.ap[:, oc, t0:t0 + n], ALU.mult, ALU.add, [PS[pb], modv, xT], [xT])
                        if oc == 7 and (u + 1 == len(units) or units[u + 1][0] != g) and g + 2 < NFG:
                            wts[g + 2] = load_group(g + 2)
                    return f
                for oc in range(8):
                    outs.append(mk(oc))
                return outs

            for u in range(len(units) + 1):
                ga = stage_a_groups(u) if u < len(units) else []
                gb = stage_b_steps(u - 1) if u >= 1 else []
                for k in range(4):
                    if ga:
                        ga[k]()
                    if gb:
                        gb[2 * k]()
                        gb[2 * k + 1]()


        def final_out(c):
            sp = Bump([(SC0, SBYTES)])
            sq = [alloc(sp, f"osq{i}", [128, 512], BF16) for i in range(2)]
            sd = alloc(sp, "osd", [128, 512], F32)
            rstd = alloc(sp, "orstd", [128, 512], F32)
            tmpb = [alloc(sp, f"otmpb{i}", [128, 512], F32) for i in range(2)]
            yf = alloc(sp, "yf", [128, 8, 128], F32)
            ytm = [alloc(sp, f"ytm{i}", [128, D], F32) for i in range(2)]
            gf = cnd.ap[:, 16:24]
            for b in range(c.nblk):
                norm_tile(c, (sq, sd, rstd, tmpb), b * 128, 128, gf, None, lambda kc: yf.ap[:, kc, :], yf)
                y = ytm[b % 2]
                for half in range(2):
                    pb = psA.next()
                    for q in range(4):
                        kc = half * 4 + q
                        tp(PS[pb].ap[:, q * 128:(q + 1) * 128], yf.ap[:, kc, :], identf.ap, [yf, cstf], PS[pb], fresh=(q == 0))
                    act(y.ap[:, half * 512:(half + 1) * 512], PS[pb].ap, AF.Copy, [PS[pb]], [y])
                dma("sp", c.oy[b * 128:(b + 1) * 128, :], y.ap, reads=[y])

        passes = []
        if do_sample:
            passes.append(make_cfg("S"))
        if do_prompt:
            passes.append(make_cfg("P"))
        for c in passes:
            load_x(c)
            for l in range(n_layers):
                layer_pass(c, l)
            final_out(c)

        names = P.finalize()
        print("nops", len(P.ops), "nsems", len(names))
        sems = {nm: es.enter_context(nc.semaphore(f"s{i}")) for i, nm in enumerate(names)}
        with nc.Block() as block:
            P.emit(block, sems)
    return nc


def _consts():
    i = np.arange(128)
    ident = np.eye(128, dtype=np.float32)
    suf = (i[:, None] > i[None, :]).astype(np.float32)
    sub = (i[:, None] < i[None, :]).astype(np.float32)
    trif = (i[:, None] <= i[None, :]).astype(np.float32)
    trib = (i[:, None] >= i[None, :]).astype(np.float32)
    cst = np.concatenate([ident, suf, sub, trif, trib], axis=1).astype(np.float32)
    t = np.arange(TS)
    row = (t // 64).astype(np.float32)
    col = (t % 64).astype(np.float32)
    nf = 8
    inv = (10000.0 ** (-np.arange(nf, dtype=np.float32) / nf)).astype(np.float32)
    ang = np.stack([row[:, None] * inv, col[:, None] * inv], axis=1)
    cos = np.cos(ang).astype(np.float32)
    sin = np.sin(ang).astype(np.float32)
    rope = np.zeros((2, 128, TS), np.float32)
    for a in range(2):
        for half in range(2):
            for f in range(nf):
                r = 64 + a * 16 + half * 8 + f
                rope[0, r] = cos[:, a, f]
                rope[1, r] = sin[:, a, f]
    return cst, rope


_CACHE = {}


def kernel(x_prompt, x_sample, c, cache_ckv, cache_krope, state_ssd, c_ctx, w_ada, b_ada,
           g_mix, w_in, g_q, w_uq, g_kv, w_ukv, ssd_conv_w, ssd_conv_b, ssd_dt_bias,
           ssd_a_log, ssd_d, ssd_norm_g, cm_conv_w, cm_conv_b, cm_ln_g, cm_ln_b, w_out,
           g_ffn, w_gate, w_up, w_down, g_final, _n_layers=DEPTH, _do_sample=True, _do_prompt=True, _stop=None):
    f = lambda a: np.ascontiguousarray(np.asarray(a, dtype=np.float32))
    key = (_n_layers, _do_sample, _do_prompt, _stop)
    if key not in _CACHE:
        _CACHE[key] = build_program(_n_layers, _do_sample, _do_prompt, _stop)
    nc = _CACHE[key]
    cst, rope = _consts()
    shared = dict(
        w_ada=f(w_ada), b_ada=f(b_ada), g_mix=f(g_mix), w_in=f(w_in), g_q=f(g_q), w_uq=f(w_uq), g_kv=f(g_kv),
        w_ukv=f(w_ukv), ssd_conv_w=f(ssd_conv_w), ssd_conv_b=f(ssd_conv_b), ssd_dt_bias=f(ssd_dt_bias).reshape(-1),
        ssd_a_log=f(ssd_a_log).reshape(-1), ssd_d=f(ssd_d).reshape(-1), ssd_norm_g=f(ssd_norm_g), cm_conv_w=f(cm_conv_w),
        cm_conv_b=f(cm_conv_b), cm_ln_g=f(cm_ln_g), cm_ln_b=f(cm_ln_b), w_out=f(w_out), g_ffn=f(g_ffn),
        w_gate=f(w_gate), w_up=f(w_up), w_down=f(w_down), g_final=f(g_final), cst=cst, rope=rope)
    x_prompt, x_sample, c, c_ctx = f(x_prompt), f(x_sample), f(c), f(c_ctx)
    cache_ckv, cache_krope, state_ssd = f(cache_ckv), f(cache_krope), f(state_ssd)
    in_maps = []
    for i in range(8):
        b = i % 4
        m = dict(shared)
        m["x_s"] = x_sample[b]
        m["x_p"] = x_prompt[2 * i:2 * i + 2].reshape(NPS * TPS, D)
        m["cond"] = np.stack([c[b], c_ctx], axis=0)
        m["cache_ckv"] = cache_ckv[b]
        m["cache_krope"] = cache_krope[b]
        m["state_ssd"] = state_ssd[b]
        in_maps.append(m)
    res = run_bass_kernel_spmd(nc, in_maps, core_ids=list(range(8)))
    r = res.results
    y_sample = np.stack([r[b]["y_s"] for b in range(4)], axis=0).astype(np.float32)
    y_prompt = np.concatenate([r[i]["y_p"].reshape(NPS, TPS, D) for i in range(8)], axis=0).astype(np.float32)
    new_ckv = np.concatenate([r[i]["o_ckv"] for i in range(8)], axis=0).astype(np.float32)
    new_kr = np.concatenate([r[i]["o_kr"] for i in range(8)], axis=0).astype(np.float32)
    new_ssd = np.concatenate([r[i]["o_ssd"] for i in range(8)], axis=0).astype(np.float32)
    return (y_prompt, y_sample, new_ckv, new_kr, new_ssd)
```

```python
import math
from contextlib import ExitStack
import numpy as np
import concourse.bass as bass
import concourse.mybir as mybir
from concourse.bass_utils import run_bass_kernel_spmd

F32 = mybir.dt.float32
BF16 = mybir.dt.bfloat16
U8 = mybir.dt.uint8
AF = mybir.ActivationFunctionType
ALU = mybir.AluOpType

ENGS = ("pe", "act", "dve", "pool", "sp")

D = 1024
DEPTH = 4
TS = 2048
TPS = 256
NPS = 2
PAST = 256
DFF = 2816
NFG = DFF // 256
IN_W = 1704
SCALE = 96 ** -0.5
NPRM = 161


class View:
    __slots__ = ("name", "ap")

    def __init__(self, name, ap):
        self.name, self.ap = name, ap


class Prog:
    def __init__(self, nc):
        self.nc = nc
        self.ops = []
        self.regions = {}
        self.overlaps = {}

    def region(self, name, space, p0, p1, b0, b1):
        assert name not in self.regions, name
        self.regions[name] = (space, p0, p1, b0, b1)
        ov = [name]
        for n, (s, q0, q1, c0, c1) in self.regions.items():
            if n == name:
                continue
            if s == space and q0 < p1 and p0 < q1 and c0 < b1 and b0 < c1:
                ov.append(n)
                self.overlaps[n].append(name)
        self.overlaps[name] = ov

    def op(self, eng, fn, reads=(), writes=(), dma=False, chan=None, fresh=()):
        rs = [r if isinstance(r, str) else r.name for r in reads]
        ws = [w if isinstance(w, str) else w.name for w in writes]
        fr = [w if isinstance(w, str) else w.name for w in fresh]
        if dma and chan is None:
            chan = ws[0] if ws else rs[0]
        self.ops.append((eng, fn, rs, ws, dma, chan, fr))

    def finalize(self):
        ops = self.ops
        n = len(ops)
        last_w = {}
        readers = {}
        deps = [None] * n
        for i, (eng, fn, rs, ws, dma, chan, fr) in enumerate(ops):
            for f in fr:
                lw = last_w.get(f)
                assert lw is None or len(readers.get(f, ())) > 0, \
                    f"PSUM collision on {f} at op {i} (prev writer {lw} unread)"
            d = set()
            for r in rs:
                for rr in self.overlaps[r]:
                    w = last_w.get(rr)
                    if w is not None:
                        d.add(w)
                    if self.regions[rr][0] == "psum":
                        for x in readers.get(rr, {}).values():
                            d.add(x)
            for w_ in ws:
                for ww in self.overlaps[w_]:
                    w = last_w.get(ww)
                    if w is not None:
                        d.add(w)
                    for x in readers.get(ww, {}).values():
                        d.add(x)
            d.discard(i)
            dd = []
            mine = set(rs) | set(ws)
            for j in d:
                ej, _, rsj, wsj, dmaj, _, _ = ops[j]
                if not dmaj and not dma and ej == eng:
                    if eng == "pe":
                        continue
                    hit = False
                    for x in wsj:
                        for y in self.overlaps[x]:
                            if y in mine:
                                hit = True
                                break
                        if hit:
                            break
                    if not hit:
                        continue
                dd.append(j)
            deps[i] = dd
            key = ("c:" + chan) if dma else eng
            for r in rs:
                readers.setdefault(r, {})[key] = i
            for w_ in ws:
                last_w[w_] = i
                readers[w_] = {}
        needed = set()
        for dd in deps:
            needed.update(dd)
        eng_cnt = {e: 0 for e in ENGS}
        chan_cnt = {}
        ticket = {}
        for i, (eng, fn, rs, ws, dma, chan, fr) in enumerate(ops):
            if dma:
                chan_cnt[chan] = chan_cnt.get(chan, 0) + 16
                ticket[i] = ("c:" + chan, chan_cnt[chan])
            elif i in needed:
                eng_cnt[eng] += 1
                ticket[i] = ("e:" + eng, eng_cnt[eng])
        self.sem_names = ["e:" + e for e in ENGS] + ["c:" + c for c in chan_cnt]
        self.final_counts = {("e:" + e): eng_cnt[e] for e in ENGS}
        self.final_counts.update({("c:" + c): v for c, v in chan_cnt.items()})
        self.deps, self.ticket = deps, ticket
        return self.sem_names

    def emit(self, block, sems):
        ops, deps, ticket = self.ops, self.deps, self.ticket
        per_eng = {e: [] for e in ENGS}
        for i, o in enumerate(ops):
            per_eng[o[0]].append(i)
        final_counts = self.final_counts

        def make(engname):
            def body(e):
                waited = {}
                for i in per_eng[engname]:
                    _, fn, rs, ws, dma, chan, _ = ops[i]
                    need = {}
                    for j in deps[i]:
                        s, v = ticket[j]
                        if need.get(s, 0) < v:
                            need[s] = v
                    for s, v in need.items():
                        if waited.get(s, 0) < v:
                            e.wait_ge(sems[s], v)
                            waited[s] = v
                    ins = fn(e)
                    if i in ticket:
                        s, v = ticket[i]
                        ins.then_inc(sems[s], 16 if dma else 1)
                if engname == "sp":
                    for s, v in final_counts.items():
                        if v > 0 and waited.get(s, 0) < v:
                            e.wait_ge(sems[s], v)
            return body

        block.tensor(make("pe"))
        block.scalar(make("act"))
        block.vector(make("dve"))
        block.gpsimd(make("pool"))
        block.sync(make("sp"))


class Bump:
    def __init__(self, ranges):
        self.ranges = [list(r) for r in ranges]

    def take(self, nb):
        nb = (nb + 31) // 32 * 32
        for r in self.ranges:
            if r[1] - r[0] >= nb:
                b0 = r[0]
                r[0] += nb
                return b0
        raise MemoryError(f"bump pool exhausted need {nb} have {self.ranges}")


def esize(dt):
    return 4 if dt == F32 else 2


def build_program(n_layers=DEPTH, do_sample=True, do_prompt=True, stop=None):
    nc = bass.Bass("TRN2", target_bir_lowering=False)
    P = Prog(nc)

    def din(name, shape):
        return nc.dram_tensor(name, list(shape), F32, kind="ExternalInput").ap()

    def dout(name, shape):
        return nc.dram_tensor(name, list(shape), F32, kind="ExternalOutput").ap()

    d_xs = din("x_s", [TS, D])
    d_xp = din("x_p", [NPS * TPS, D])
    d_cond = din("cond", [2, D])
    d_cckv = din("cache_ckv", [DEPTH, PAST, 128])
    d_ckr = din("cache_krope", [DEPTH, PAST, 32])
    d_st = din("state_ssd", [DEPTH, 2, 4, 64, 64])
    d_wada = din("w_ada", [DEPTH, D, 6 * D])
    d_bada = din("b_ada", [DEPTH, 6 * D])
    d_gmix = din("g_mix", [DEPTH, D])
    d_win = din("w_in", [DEPTH, D, IN_W])
    d_gq = din("g_q", [DEPTH, 256])
    d_wuq = din("w_uq", [DEPTH, 256, 768])
    d_gkv = din("g_kv", [DEPTH, 128])
    d_wukv = din("w_ukv", [DEPTH, 128, 1024])
    d_scw = din("ssd_conv_w", [DEPTH, 5, 512])
    d_scb = din("ssd_conv_b", [DEPTH, 512])
    d_dtb = din("ssd_dt_bias", [DEPTH * 8])
    d_alog = din("ssd_a_log", [DEPTH * 8])
    d_sd = din("ssd_d", [DEPTH * 8])
    d_sng = din("ssd_norm_g", [DEPTH, 256])
    d_ccw = din("cm_conv_w", [DEPTH, 31, 256])
    d_ccb = din("cm_conv_b", [DEPTH, 256])
    d_clg = din("cm_ln_g", [DEPTH, 256])
    d_clb = din("cm_ln_b", [DEPTH, 256])
    d_wout = din("w_out", [DEPTH, D, D])
    d_gffn = din("g_ffn", [DEPTH, D])
    d_wg = din("w_gate", [DEPTH, D, DFF])
    d_wu = din("w_up", [DEPTH, D, DFF])
    d_wd = din("w_down", [DEPTH, DFF, D])
    d_gfin = din("g_final", [D])
    d_cst = din("cst", [128, 640])
    d_rope = din("rope", [2, 128, TS])

    o_ys = dout("y_s", [TS, D])
    o_yp = dout("y_p", [NPS * TPS, D])
    o_ckv = dout("o_ckv", [NPS, DEPTH, TPS, 128])
    o_kr = dout("o_kr", [NPS, DEPTH, TPS, 32])
    o_ssd = dout("o_ssd", [NPS, DEPTH, 2, 4, 64, 64])

    es = ExitStack()
    with es:
        SBYTES = 212000
        S = es.enter_context(nc.sbuf_tensor("S", [128, SBYTES], U8))
        banks = [es.enter_context(nc.psum_tensor(f"PS{i}", [128, 512], F32)) for i in range(8)]
        for i in range(8):
            P.region(f"ps{i}", "psum", 0, 128, i * 2048, (i + 1) * 2048)
        PS = [View(f"ps{i}", banks[i][:, :]) for i in range(8)]
        PSB = [View(f"ps{i}", banks[i][:, :].bitcast(BF16)) for i in range(8)]

        uid = [0]

        def alloc(pool, name, shape, dt, p0=0):
            nel = int(np.prod(shape[1:]))
            nb = nel * esize(dt)
            b0 = pool.take(nb)
            ap = S[p0:p0 + shape[0], b0:b0 + nb].bitcast(dt)
            if len(shape) == 3:
                ap = ap.rearrange("p (a b) -> p a b", a=shape[1])
            elif len(shape) == 4:
                ap = ap.rearrange("p (a b c) -> p a b c", a=shape[1], b=shape[2])
            uid[0] += 1
            nm = f"{name}.{uid[0]}"
            P.region(nm, "sbuf", p0, p0 + shape[0], b0, b0 + ((nb + 31) // 32 * 32))
            return View(nm, ap)

        pers = Bump([(0, SBYTES)])
        xT = alloc(pers, "xT", [128, 8, TS], F32)
        RING_N = 4
        ring = [alloc(pers, f"ring{i}", [128, 4096], BF16) for i in range(RING_N)]
        wuq = alloc(pers, "wuq", [128, 2, 1024], BF16)
        wukv = alloc(pers, "wukv", [128, 1024], BF16)
        cstf = alloc(pers, "cstf", [128, 640], F32)
        identf = View(cstf.name, cstf.ap[:, 0:128])
        SU = [View(cstf.name, cstf.ap[:, 128:256]), View(cstf.name, cstf.ap[:, 256:384])]
        TRI = [View(cstf.name, cstf.ap[:, 384:512]), View(cstf.name, cstf.ap[:, 512:640])]
        identb = alloc(pers, "identb", [128, 128], BF16)
        onesb = alloc(pers, "onesb", [128, 128], BF16)
        onesf = alloc(pers, "onesf", [128, 128], F32)
        ropeC = alloc(pers, "ropeC", [128, TS], BF16)
        ropeS = alloc(pers, "ropeS", [128, TS], BF16)
        prm = alloc(pers, "prm", [128, DEPTH, NPRM], F32)
        cnd = alloc(pers, "cnd", [128, 24], F32)
        scond = alloc(pers, "scond", [128, 8, 2], BF16)
        modv = alloc(pers, "modv", [128, DEPTH, 48, 2], F32)
        gsc = alloc(pers, "gsc", [128, 2, 8], F32)
        dtb_b = alloc(pers, "dtb_b", [128, 32], F32)
        alog_b = alloc(pers, "alog_b", [128, 32], F32)
        dd_b = alloc(pers, "dd_b", [128, 32], F32)
        a_b = alloc(pers, "a_b", [128, 32], F32)
        dsum_b = alloc(pers, "dsum_b", [128, DEPTH, 4], F32)
        gkv_b = alloc(pers, "gkv_b", [128, 128], F32)
        xg = alloc(pers, "xg", [128, 8, 512], BF16)
        xg0 = xg
        pers_end = pers.ranges[0][0]
        io = Bump([(pers_end, SBYTES)])
        gpad = alloc(io, "gpad", [128, 2, TS + 32], BF16)
        xbcp = alloc(io, "xbcp", [128, 4, TS + 4], BF16)
        szb = alloc(io, "szb", [128, 16, 256], BF16)
        qlat = alloc(io, "qlat", [128, 2, TS], BF16)
        ckvnT = alloc(io, "ckvnT", [128, PAST + TS], BF16)
        krT = alloc(io, "krT", [128, PAST + TS], BF16)
        dtr = alloc(io, "dtr", [128, 16, 8], F32)
        io_end = io.ranges[0][0]
        hT = alloc(Bump([(pers_end, SBYTES)]), "hT", [128, 8, TS], BF16)
        R = P.regions
        rg = lambda v: (R[v.name][3], R[v.name][4])
        SC0 = io_end
        print("mem: pers_end", pers_end, "io_end", io_end, "scratch", SBYTES - io_end)

        def dma(eng, out, in_, reads=(), writes=(), **kw):
            P.op(eng, lambda e: e.dma_start(out=out, in_=in_, **kw), reads=reads, writes=writes, dma=True)

        def mm(out, lhsT, rhs, start, stop, reads, w, fresh=False):
            P.op("pe", lambda e: e.matmul(out, lhsT=lhsT, rhs=rhs, start=start, stop=stop),
                 reads=reads, writes=[w], fresh=[w] if fresh else ())

        def tp(out, in_, ident, reads, w, fresh=False):
            P.op("pe", lambda e: e.transpose(out, in_, ident), reads=reads, writes=[w],
                 fresh=[w] if fresh else ())

        def act(out, in_, func, reads, writes, **kw):
            P.op("act", lambda e: e.activation(out=out, in_=in_, func=func, **kw), reads=reads, writes=writes)

        def tt(out, in0, in1, op, reads, writes, eng="dve"):
            P.op(eng, lambda e: e.tensor_tensor(out=out, in0=in0, in1=in1, op=op), reads=reads, writes=writes)

        def stt(out, in0, scalar, in1, op0, op1, reads, writes):
            P.op("dve", lambda e: e.scalar_tensor_tensor(out=out, in0=in0, scalar=scalar, in1=in1, op0=op0, op1=op1),
                 reads=reads, writes=writes)

        def ts(out, in0, s1, s2, op0, op1, reads, writes, eng="dve"):
            if op1 is None:
                P.op(eng, lambda e: e.tensor_scalar(out=out, in0=in0, scalar1=s1, scalar2=None, op0=op0),
                     reads=reads, writes=writes)
            else:
                P.op(eng, lambda e: e.tensor_scalar(out=out, in0=in0, scalar1=s1, scalar2=s2, op0=op0, op1=op1),
                     reads=reads, writes=writes)

        def cp(out, in_, reads, writes, eng="dve"):
            P.op(eng, lambda e: e.tensor_copy(out, in_), reads=reads, writes=writes)

        def memset(ap, val, writes, eng="dve"):
            P.op(eng, lambda e: e.memset(ap, val), writes=writes)

        def recip(out, in_, reads, writes):
            P.op("dve", lambda e: e.reciprocal(out=out, in_=in_), reads=reads, writes=writes)

        class Rot:
            def __init__(self, items):
                self.items, self.i = items, 0

            def next(self):
                v = self.items[self.i % len(self.items)]
                self.i += 1
                return v

        psA = Rot([0, 1, 2, 3])
        psBk = Rot([4, 5])

        dma("sp", cstf.ap, d_cst, writes=[cstf])
        dma("pool", ropeC.ap, d_rope[0], writes=[ropeC])
        dma("pool", ropeS.ap, d_rope[1], writes=[ropeS])
        memset(onesb.ap, 1.0, [onesb])
        memset(onesf.ap, 1.0, [onesf])
        cp(identb.ap, identf.ap, [cstf], [identb])
        dma("sp", dtb_b.ap, d_dtb.partition_broadcast(128), writes=[dtb_b])
        dma("sp", alog_b.ap, d_alog.partition_broadcast(128), writes=[alog_b])
        dma("sp", dd_b.ap, d_sd.partition_broadcast(128), writes=[dd_b])
        act(a_b.ap, alog_b.ap, AF.Exp, [alog_b], [a_b])
        ts(a_b.ap, a_b.ap, -1.0, None, ALU.mult, None, [a_b], [a_b])
        ddv = dd_b.ap.rearrange("p (l d h) -> p l d h", l=DEPTH, d=2)
        tt(dsum_b.ap, ddv[:, :, 0, :], ddv[:, :, 1, :], ALU.add, [dd_b], [dsum_b])

        prol = Bump([(SC0, SBYTES)])
        stg = [alloc(prol, f"stg{i}", [128, 128], F32) for i in range(2)]
        O_BADA, O_GMIX, O_GFFN, O_GQ, O_GKV, O_SCW, O_SCB, O_SNG, O_CCW, O_CCB, O_CLG, O_CLB = \
            0, 48, 56, 64, 66, 67, 87, 91, 93, 155, 157, 159
        for l in range(DEPTH):
            rows = [
                (d_bada[l].rearrange("(r c) -> r c", c=128), 48),
                (d_gmix[l].rearrange("(r c) -> r c", c=128), 8),
                (d_gffn[l].rearrange("(r c) -> r c", c=128), 8),
                (d_gq[l].rearrange("(r c) -> r c", c=128), 2),
                (d_gkv[l].rearrange("(r c) -> r c", c=128), 1),
                (d_scw[l].rearrange("j (r c) -> (j r) c", c=128), 20),
                (d_scb[l].rearrange("(r c) -> r c", c=128), 4),
                (d_sng[l].rearrange("(r c) -> r c", c=128), 2),
                (d_ccw[l].rearrange("j (r c) -> (j r) c", c=128), 62),
                (d_ccb[l].rearrange("(r c) -> r c", c=128), 2),
                (d_clg[l].rearrange("(r c) -> r c", c=128), 2),
                (d_clb[l].rearrange("(r c) -> r c", c=128), 2),
            ]
            r0 = 0
            for src, nr in rows:
                done = 0
                while done < nr:
                    si = (r0 + done) // 128
                    off = (r0 + done) % 128
                    k = min(nr - done, 128 - off)
                    dma("sp", stg[si].ap[off:off + k, :], src[done:done + k, :], writes=[stg[si]])
                    done += k
                r0 += nr
            assert r0 == NPRM
            for si, (c0, ncol) in enumerate([(0, 128), (128, NPRM - 128)]):
                pb = psA.next()
                tp(PS[pb].ap[:, 0:ncol], stg[si].ap[0:ncol, :], identf.ap[0:ncol, 0:ncol], [stg[si], cstf], PS[pb], fresh=True)
                cp(prm.ap[:, l, c0:c0 + ncol], PS[pb].ap[:, 0:ncol], [PS[pb]], [prm])
        dma("sp", stg[0].ap[0:16, :], d_cond.rearrange("a (r c) -> (a r) c", c=128), writes=[stg[0]])
        dma("sp", stg[0].ap[16:24, :], d_gfin.rearrange("(r c) -> r c", c=128), writes=[stg[0]])
        pb = psA.next()
        tp(PS[pb].ap[:, 0:24], stg[0].ap[0:24, :], identf.ap[0:24, 0:24], [stg[0], cstf], PS[pb], fresh=True)
        cp(cnd.ap, PS[pb].ap[:, 0:24], [PS[pb]], [cnd])
        act(scond.ap.rearrange("p k c -> p c k"), cnd.ap[:, 0:16].rearrange("p (c k) -> p c k", c=2), AF.Silu, [cnd], [scond])

        ring_i = [0]

        def load_piece(srcs):
            v = ring[ring_i[0] % RING_N]
            ring_i[0] += 1
            for dst_fn, src in srcs:
                dma("pool", dst_fn(v.ap), src, writes=[v])
            return v

        def r3(ap, a):
            return ap.rearrange("p (a b) -> p a b", a=a)

        mods_done = set()

        def mod_layer(l, slot_fn):
            for pc in range(12):
                v = slot_fn([(lambda a: r3(a, 8), d_wada[l, :, pc * 512:(pc + 1) * 512].rearrange("(k p) c -> p k c", p=128))])
                w3 = r3(v.ap, 8)
                for q in range(4):
                    oc = pc * 4 + q
                    for kc in range(8):
                        mm(PS[7].ap[:, oc * 2:oc * 2 + 2], w3[:, kc, q * 128:(q + 1) * 128], scond.ap[:, kc, :],
                           kc == 0, kc == 7, [v, scond], PS[7], fresh=(oc == 0 and kc == 0))
                if pc == 11:
                    tt(modv.ap[:, l], PS[7].ap[:, 0:96].rearrange("p (o c) -> p o c", c=2),
                       prm.ap[:, l, O_BADA:O_BADA + 48].unsqueeze(2).to_broadcast([128, 48, 2]), ALU.add, [PS[7], prm], [modv])
                    mods_done.add(l)
                yield

        if n_layers > 0:
            for _ in mod_layer(0, load_piece):
                pass

        class Cfg:
            pass

        def make_cfg(kind):
            c = Cfg()
            c.kind = kind
            if kind == "S":
                c.T, c.TT, c.seqs, c.rope, c.ctx, c.cond = TS, 512, [(0, TS)], True, True, 0
                c.dx, c.oy = d_xs, o_ys
            else:
                c.T, c.TT, c.seqs, c.rope, c.ctx, c.cond = NPS * TPS, 256, [(0, TPS), (TPS, TPS)], False, False, 1
                c.dx, c.oy = d_xp, o_yp
            c.tiles = []
            for si, (s0, sl) in enumerate(c.seqs):
                for t in range(s0, s0 + sl, c.TT):
                    c.tiles.append((si, t))
            c.nblk = c.T // 128
            return c

        def load_x(c):
            pool = Bump([(SC0, SBYTES)])
            xs_ = [alloc(pool, f"xstg{i}", [128, D], F32) for i in range(2)]
            for b in range(c.nblk):
                st = xs_[b % 2]
                dma("sp", st.ap, c.dx[b * 128:(b + 1) * 128, :], writes=[st])
                for half in range(2):
                    pb = psA.next()
                    for q in range(4):
                        kc = half * 4 + q
                        tp(PS[pb].ap[:, q * 128:(q + 1) * 128], st.ap[:, kc * 128:(kc + 1) * 128], identf.ap,
                           [st, cstf], PS[pb], fresh=(q == 0))
                    cp(xT.ap[:, half * 4:half * 4 + 4, b * 128:(b + 1) * 128],
                       PS[pb].ap.rearrange("p (q t) -> p q t", q=4), [PS[pb]], [xT])

        def norm_tile(c, pool_views, t0, n, gsc_ap, sh_ap, out_fn, out_v, dt_out_bf=True, sq_dve=False):
            sq, sd, rstd, tmpb = pool_views
            for kc in range(8):
                q = sq[kc % 2]
                if sq_dve:
                    tt(q.ap[:, 0:n], xT.ap[:, kc, t0:t0 + n], xT.ap[:, kc, t0:t0 + n], ALU.mult, [xT], [q])
                else:
                    act(q.ap[:, 0:n], xT.ap[:, kc, t0:t0 + n], AF.Square, [xT], [q])
                mm(PS[6].ap[:, 0:n], onesb.ap, q.ap[:, 0:n], kc == 0, kc == 7, [onesb, q], PS[6], fresh=(kc == 0))
            act(sd.ap[:, 0:n], PS[6].ap[:, 0:n], AF.Ln, [PS[6]], [sd], scale=1.0 / D, bias=1e-6)
            act(rstd.ap[:, 0:n], sd.ap[:, 0:n], AF.Exp, [sd], [rstd], scale=-0.5)
            for kc in range(8):
                tb = tmpb[kc % 2]
                tt(tb.ap[:, 0:n], xT.ap[:, kc, t0:t0 + n], rstd.ap[:, 0:n], ALU.mult, [xT, rstd], [tb])
                if sh_ap is None:
                    ts(out_fn(kc), tb.ap[:, 0:n], gsc_ap[:, kc:kc + 1], None, ALU.mult, None, [tb, cnd], [out_v])
                else:
                    act(out_fn(kc), tb.ap[:, 0:n], AF.Identity, [tb, gsc, modv], [out_v],
                        scale=gsc_ap[:, kc:kc + 1], bias=sh_ap[:, kc:kc + 1])

        def xupdate(c, l, t0, n, lhs_fn, rhs_list, g_off, reads, evac=None):
            for oc in range(8):
                pb = psA.next()
                nj = len(rhs_list)
                for j in range(nj):
                    mm(PS[pb].ap[:, 0:n], lhs_fn(j, oc), rhs_list[j], j == 0, j == nj - 1, reads, PS[pb], fresh=(j == 0))
                gcol = modv.ap[:, l, g_off + oc, c.cond:c.cond + 1]
                if evac is None:
                    stt(xT.ap[:, oc, t0:t0 + n], PS[pb].ap[:, 0:n], gcol,
                        xT.ap[:, oc, t0:t0 + n], ALU.mult, ALU.add, [PS[pb], modv, xT], [xT])
                else:
                    ev = evac[oc % len(evac)]
                    act(ev.ap[:, 0:n], PS[pb].ap[:, 0:n], AF.Copy, [PS[pb], modv], [ev], scale=gcol)
                    tt(xT.ap[:, oc, t0:t0 + n], xT.ap[:, oc, t0:t0 + n], ev.ap[:, 0:n], ALU.add, [xT, ev], [xT], eng="pool")

        def layer_pass(c, l):
            T, TT = c.T, c.TT
            cd = c.cond
            for i, (og, osc) in enumerate([(O_GMIX, 8), (O_GFFN, 32)]):
                stt(gsc.ap[:, i, :], modv.ap[:, l, osc:osc + 8, cd], 1.0, prm.ap[:, l, og:og + 8], ALU.add, ALU.mult,
                    [modv, prm], [gsc])
            sh1 = modv.ap[:, l, 0:8, cd]
            sh2 = modv.ap[:, l, 24:32, cd]
            dma("sp", gkv_b.ap, d_gkv[l].partition_broadcast(128), writes=[gkv_b])
            dma("pool", wuq.ap[:, :, 0:768], d_wuq[l].rearrange("(k p) c -> p k c", p=128), writes=[wuq])
            dma("pool", wukv.ap, d_wukv[l], writes=[wukv])
            for kc in range(2):
                ts(wuq.ap[:, kc, 0:768], wuq.ap[:, kc, 0:768], prm.ap[:, l, O_GQ + kc:O_GQ + kc + 1], None, ALU.mult, None,
                   [wuq, prm], [wuq])
                src = wuq.ap[:, kc, 0:768].rearrange("p (h c) -> p h c", h=8)[:, :, 64:96].rearrange("p h (a f) -> p h a f", a=2)
                dst = wuq.ap[:, kc, 768:1024].rearrange("p (h a f) -> p h a f", h=8, a=2)
                for a in range(2):
                    ts(dst[:, :, a, 0:8], src[:, :, a, 8:16], -1.0, None, ALU.mult, None, [wuq], [wuq])
                    cp(dst[:, :, a, 8:16], src[:, :, a, 0:8], [wuq], [wuq])

            wl = d_win[l]

            def wsl(c0, c1):
                return wl[:, c0:c1].rearrange("(k p) c -> p k c", p=128)

            v_cm = load_piece([(lambda a: r3(a, 8), wsl(1192, 1704))])
            v_ssd = load_piece([(lambda a: r3(a, 8), wsl(672, 1184))])
            v_misc = load_piece([(lambda a: r3(a, 8)[:, :, 0:256], wsl(416, 672)),
                                 (lambda a: r3(a, 8)[:, :, 256:416], wsl(256, 416)),
                                 (lambda a: r3(a, 8)[:, :, 448:456], wsl(1184, 1192))])
            v_q = load_piece([(lambda a: r3(a, 8)[:, :, 0:256], wsl(0, 256))])
            w_cm, w_ssd, w_misc, w_q = r3(v_cm.ap, 8), r3(v_ssd.ap, 8), r3(v_misc.ap, 8), r3(v_q.ap, 8)
            for kc in range(8):
                src = w_misc[:, kc, 384:416].rearrange("p (a f) -> p a f", a=2)
                dst = w_misc[:, kc, 416:448].rearrange("p (a f) -> p a f", a=2)
                ts(dst[:, :, 0:8], src[:, :, 8:16], -1.0, None, ALU.mult, None, [v_misc], [v_misc])
                cp(dst[:, :, 8:16], src[:, :, 0:8], [v_misc], [v_misc])

            sp = Bump([(SC0, SBYTES)])
            xg2 = alloc(sp, "xg2", [128, 8, 512], BF16)
            xgs = [xg0, xg2]
            sq = [alloc(sp, f"sq{i}", [128, 512], BF16) for i in range(2)]
            sd = alloc(sp, "sd", [128, 512], F32)
            rstd = alloc(sp, "rstd", [128, 512], F32)
            tmpb = [alloc(sp, f"tmpb{i}", [128, 512], F32) for i in range(2)]
            sig = alloc(sp, "sig", [128, 512], F32)
            sqk = alloc(sp, "sqk", [128, 512], BF16)
            t1 = alloc(sp, "t1", [128, 512], F32)
            t2 = alloc(sp, "t2", [128, 512], F32)
            rq = alloc(sp, "rq", [128, 512], F32)
            tmo = alloc(sp, "tmo", [128, 160], F32)
            tmo2 = alloc(sp, "tmo2", [128, 128], F32)
            ssq = alloc(sp, "ssq", [128, 2], F32)
            junk = alloc(sp, "junk", [128, 256], F32)
            koff = PAST if c.ctx else 0

            for si, (s0, sl) in enumerate(c.seqs):
                g0 = s0 + si * 32
                memset(gpad.ap[:, :, g0:g0 + 16], 0.0, [gpad])
                memset(gpad.ap[:, :, g0 + 16 + sl:g0 + 32 + sl], 0.0, [gpad])
                x0 = s0 + si * 4
                memset(xbcp.ap[:, :, x0:x0 + 2], 0.0, [xbcp])
                memset(xbcp.ap[:, :, x0 + 2 + sl:x0 + 4 + sl], 0.0, [xbcp])

            if c.ctx:
                cst_ = alloc(sp, "cstg", [128, 128], F32)
                kst_ = alloc(sp, "kstg", [128, 96], F32)
                memset(kst_.ap, 0.0, [kst_])
                for b in range(2):
                    dma("sp", cst_.ap, d_cckv[l, b * 128:(b + 1) * 128, :], writes=[cst_])
                    pb = psA.next()
                    tp(PS[pb].ap[:, 0:128], cst_.ap, identf.ap, [cst_, cstf], PS[pb], fresh=True)
                    cp(ckvnT.ap[:, b * 128:(b + 1) * 128], PS[pb].ap[:, 0:128], [PS[pb]], [ckvnT])
                    dma("sp", kst_.ap[:, 64:96], d_ckr[l, b * 128:(b + 1) * 128, :], writes=[kst_])
                    pb = psA.next()
                    tp(PS[pb].ap[0:96, 0:128], kst_.ap, identf.ap, [kst_, cstf], PS[pb], fresh=True)
                    cp(krT.ap[64:96, b * 128:(b + 1) * 128], PS[pb].ap[64:96, 0:128], [PS[pb]], [krT])

            cur = [xg0]

            def fm(wap, c0, m, n, reads, out_rows=None):
                xg = cur[0]
                pb = psA.next()
                o = PS[pb].ap[0:m, 0:n] if out_rows is None else PS[pb].ap[out_rows[0]:out_rows[1], 0:n]
                for kc in range(8):
                    mm(o, wap[:, kc, c0:c0 + m], xg.ap[:, kc, 0:n], kc == 0, kc == 7, reads + [xg], PS[pb], fresh=(kc == 0))
                return pb

            def do_norm(ti):
                si_, t0_ = c.tiles[ti]
                xv = xgs[ti % 2]
                norm_tile(c, (sq, sd, rstd, tmpb), t0_, TT, gsc.ap[:, 0, :], sh1, lambda kc: xv.ap[:, kc, 0:TT], xv, sq_dve=True)

            do_norm(0)
            for ti, (si, t0) in enumerate(c.tiles):
                n = TT
                s0, sl = c.seqs[si]
                if ti + 1 < len(c.tiles):
                    do_norm(ti + 1)
                xg = xgs[ti % 2]
                cur[0] = xg
                gofs = si * 32 + 16 + t0
                for j in range(2):
                    pa = fm(w_cm, j * 128, 128, n, [v_cm])
                    pbk = fm(w_cm, 256 + j * 128, 128, n, [v_cm])
                    act(sig.ap[:, 0:n], PS[pbk].ap[:, 0:n], AF.Sigmoid, [PS[pbk]], [sig])
                    tt(gpad.ap[:, j, gofs:gofs + n], PS[pa].ap[:, 0:n], sig.ap[:, 0:n], ALU.mult, [PS[pa], sig], [gpad])
                xofs = si * 4 + 2 + t0
                for j in range(4):
                    pa = fm(w_ssd, j * 128, 128, n, [v_ssd])
                    act(xbcp.ap[:, j, xofs:xofs + n], PS[pa].ap[:, 0:n], AF.Copy, [PS[pa]], [xbcp])
                for b in range(n // 128):
                    blk = (t0 + b * 128) // 128
                    pb = psA.next()
                    for kc in range(8):
                        mm(PS[pb].ap[:, 0:256], xg.ap[:, kc, b * 128:(b + 1) * 128], w_misc[:, kc, 0:256], kc == 0, kc == 7,
                           [xg, v_misc], PS[pb], fresh=(kc == 0))
                    act(szb.ap[:, blk, :], PS[pb].ap[:, 0:256], AF.Silu, [PS[pb]], [szb])
                    pb = psA.next()
                    for kc in range(8):
                        mm(PS[pb].ap[:, 0:8], xg.ap[:, kc, b * 128:(b + 1) * 128], w_misc[:, kc, 448:456], kc == 0, kc == 7,
                           [xg, v_misc], PS[pb], fresh=(kc == 0))
                    tt(dtr.ap[:, blk, :], PS[pb].ap[:, 0:8], dtb_b.ap[:, l * 8:(l + 1) * 8], ALU.add, [PS[pb], dtb_b], [dtr])
                    if c.kind == "P":
                        pb = psA.next()
                        for kc in range(8):
                            mm(PS[pb].ap[:, 0:160], xg.ap[:, kc, b * 128:(b + 1) * 128], w_misc[:, kc, 256:416], kc == 0, kc == 7,
                               [xg, v_misc], PS[pb], fresh=(kc == 0))
                        cp(tmo.ap, PS[pb].ap[:, 0:160], [PS[pb]], [tmo])
                        tloc = t0 - s0 + b * 128
                        dma("sp", o_kr[si, l, tloc:tloc + 128, :], tmo.ap[:, 128:160], reads=[tmo])
                        memset(ssq.ap[:, 0:1], 0.0, [ssq])
                        act(junk.ap[:, 0:128], tmo.ap[:, 0:128], AF.Square, [tmo, ssq], [junk, ssq], accum_out=ssq.ap[:, 0:1])
                        act(ssq.ap[:, 1:2], ssq.ap[:, 0:1], AF.Ln, [ssq], [ssq], scale=1.0 / 128, bias=1e-6)
                        act(ssq.ap[:, 1:2], ssq.ap[:, 1:2], AF.Exp, [ssq], [ssq], scale=-0.5)
                        stt(tmo2.ap, tmo.ap[:, 0:128], ssq.ap[:, 1:2], gkv_b.ap, ALU.mult, ALU.mult, [tmo, ssq, gkv_b], [tmo2])
                        dma("sp", o_ckv[si, l, tloc:tloc + 128, :], tmo2.ap, reads=[tmo2])
                pa = fm(w_misc, 256, 128, n, [v_misc])
                act(sqk.ap[:, 0:n], PS[pa].ap[:, 0:n], AF.Square, [PS[pa]], [sqk])
                mm(PS[6].ap[:, 0:n], onesb.ap, sqk.ap[:, 0:n], True, True, [onesb, sqk], PS[6], fresh=True)
                act(sd.ap[:, 0:n], PS[6].ap[:, 0:n], AF.Ln, [PS[6]], [sd], scale=1.0 / 128, bias=1e-6)
                act(rstd.ap[:, 0:n], sd.ap[:, 0:n], AF.Exp, [sd], [rstd], scale=-0.5)
                kofs = koff + t0 if c.ctx else t0
                stt(ckvnT.ap[:, kofs:kofs + n], PS[pa].ap[:, 0:n], prm.ap[:, l, O_GKV:O_GKV + 1], rstd.ap[:, 0:n],
                    ALU.mult, ALU.mult, [PS[pa], prm, rstd], [ckvnT])
                pa = fm(w_misc, 384, 32, n, [v_misc], out_rows=(64, 96))
                if c.rope:
                    pbk = fm(w_misc, 416, 32, n, [v_misc], out_rows=(64, 96))
                    tt(t1.ap[64:96, 0:n], PS[pa].ap[64:96, 0:n], ropeC.ap[64:96, t0:t0 + n], ALU.mult, [PS[pa], ropeC], [t1])
                    tt(t2.ap[64:96, 0:n], PS[pbk].ap[64:96, 0:n], ropeS.ap[64:96, t0:t0 + n], ALU.mult, [PS[pbk], ropeS], [t2])
                    tt(krT.ap[64:96, kofs:kofs + n], t1.ap[64:96, 0:n], t2.ap[64:96, 0:n], ALU.add, [t1, t2], [krT])
                else:
                    act(krT.ap[64:96, kofs:kofs + n], PS[pa].ap[64:96, 0:n], AF.Copy, [PS[pa]], [krT])
                pq = [fm(w_q, j * 128, 128, n, [v_q]) for j in range(2)]
                for j in range(2):
                    act(sq[j].ap[:, 0:n], PS[pq[j]].ap[:, 0:n], AF.Square, [PS[pq[j]]], [sq[j]])
                    mm(PS[6].ap[:, 0:n], onesb.ap, sq[j].ap[:, 0:n], j == 0, j == 1, [onesb, sq[j]], PS[6], fresh=(j == 0))
                act(sd.ap[:, 0:n], PS[6].ap[:, 0:n], AF.Ln, [PS[6]], [sd], scale=1.0 / 256, bias=1e-6)
                act(rq.ap[:, 0:n], sd.ap[:, 0:n], AF.Exp, [sd], [rq], scale=-0.5)
                for j in range(2):
                    tt(qlat.ap[:, j, t0:t0 + n], PS[pq[j]].ap[:, 0:n], rq.ap[:, 0:n], ALU.mult, [PS[pq[j]], rq], [qlat])

            if stop == "I":
                return
            v_wor = load_piece([(lambda a: r3(a, 4), d_wout[l, 512:1024, :].rearrange("(k p) c -> p k c", p=128))])
            w_or = r3(v_wor.ap, 4)
            for j in range(2):
                ts(w_or[:, j, :], w_or[:, j, :], prm.ap[:, l, O_SNG + j:O_SNG + j + 1], None, ALU.mult, None, [v_wor, prm], [v_wor])

            sp = Bump([(SC0, SBYTES), rg(xg0)])
            dg = alloc(sp, "dg", [128, 2, 31, 128], BF16)
            cvf = [alloc(sp, f"cvf{j}", [128, 512], F32) for j in range(2)]
            sqf = [alloc(sp, f"sqf{j}", [128, 512], F32) for j in range(2)]
            mean = alloc(sp, "mean", [128, 512], F32)
            var = alloc(sp, "var", [128, 512], F32)
            rr = alloc(sp, "rr", [128, 512], F32)
            uu = alloc(sp, "uu", [128, 512], F32)
            cmix = alloc(sp, "cmix", [128, 2, 512], BF16)
            for j in range(2):
                for tap in range(31):
                    col = O_CCW + tap * 2 + j
                    ts(dg.ap[:, j, tap, :], identb.ap, prm.ap[:, l, col:col + 1], None, ALU.mult, None, [identb, prm], [dg])
            for (si, t0) in c.tiles:
                n = TT
                gofs = si * 32 + 1 + t0
                for j in range(2):
                    pb = psBk.next()
                    for tap in range(31):
                        mm(PS[pb].ap[:, 0:n], dg.ap[:, j, tap, :], gpad.ap[:, j, gofs + tap:gofs + tap + n], tap == 0, tap == 30,
                           [dg, gpad], PS[pb], fresh=(tap == 0))
                    bcol = prm.ap[:, l, O_CCB + j:O_CCB + j + 1]
                    act(cvf[j].ap[:, 0:n], PS[pb].ap[:, 0:n], AF.Identity, [PS[pb], prm], [cvf[j]], bias=bcol)
                    act(sqf[j].ap[:, 0:n], PS[pb].ap[:, 0:n], AF.Square, [PS[pb], prm], [sqf[j]], bias=bcol)
                for j in range(2):
                    mm(PS[6].ap[:, 0:n], onesf.ap, cvf[j].ap[:, 0:n], j == 0, j == 1, [onesf, cvf[j]], PS[6], fresh=(j == 0))
                for j in range(2):
                    mm(PS[7].ap[:, 0:n], onesf.ap, sqf[j].ap[:, 0:n], j == 0, j == 1, [onesf, sqf[j]], PS[7], fresh=(j == 0))
                ts(mean.ap[:, 0:n], PS[6].ap[:, 0:n], 1.0 / 256, None, ALU.mult, None, [PS[6]], [mean])
                tt(var.ap[:, 0:n], mean.ap[:, 0:n], mean.ap[:, 0:n], ALU.mult, [mean], [var])
                stt(var.ap[:, 0:n], PS[7].ap[:, 0:n], 1.0 / 256, var.ap[:, 0:n], ALU.mult, ALU.subtract, [PS[7], var], [var])
                act(var.ap[:, 0:n], var.ap[:, 0:n], AF.Ln, [var], [var], bias=1e-5)
                act(rr.ap[:, 0:n], var.ap[:, 0:n], AF.Exp, [var], [rr], scale=-0.5)
                for j in range(2):
                    tt(uu.ap[:, 0:n], cvf[j].ap[:, 0:n], mean.ap[:, 0:n], ALU.subtract, [cvf[j], mean], [uu])
                    tt(uu.ap[:, 0:n], uu.ap[:, 0:n], rr.ap[:, 0:n], ALU.mult, [uu, rr], [uu])
                    act(cmix.ap[:, j, 0:n], uu.ap[:, 0:n], AF.Silu, [uu, prm], [cmix],
                        scale=prm.ap[:, l, O_CLG + j:O_CLG + j + 1], bias=prm.ap[:, l, O_CLB + j:O_CLB + j + 1])
                xupdate(c, l, t0, n, lambda j, oc: w_or[:, 2 + j, oc * 128:(oc + 1) * 128],
                        [cmix.ap[:, 0, 0:n], cmix.ap[:, 1, 0:n]], 16, [v_wor, cmix])

            if stop == "C":
                return
            ssd_phase(c, l, w_or, v_wor)
            if stop in ("S", "S1", "S2", "S3"):
                return
            v_woa = load_piece([(lambda a: r3(a, 4), d_wout[l, 0:512, :].rearrange("(k p) c -> p k c", p=128))])
            mla_phase(c, l, r3(v_woa.ap, 4), v_woa)
            if stop == "M":
                return
            ffn_phase(c, l)

        def ssd_phase(c, l, w_or, v_wor):
            T, TT = c.T, c.TT
            nblk = c.nblk
            g_r, x_r, z_r = rg(gpad), rg(xbcp), rg(szb)
            sp = Bump([(SC0, SBYTES), g_r])
            dg5 = alloc(sp, "dg5", [128, 4, 5, 128], BF16)
            xsT = alloc(sp, "xsT", [128, 2, 512], BF16)
            BCt = alloc(sp, "BCt", [128, 3, TS], BF16)
            xs_tm = alloc(sp, "xs_tm", [128, 16, 256], BF16)
            B_tm = alloc(sp, "B_tm", [128, 16, 128], BF16)
            dt = alloc(sp, "dt", [128, 16, 8], F32)
            dta = alloc(sp, "dta", [128, 16, 8], F32)
            hTf = [alloc(sp, f"hTf{d}", [128, 2, 64], F32) for d in range(2)]
            hTb = alloc(sp, "hTb", [128, 2, 64], BF16)
            sstg = alloc(sp, "sstg", [128, 128], F32)
            sp2 = Bump([tuple(r) for r in sp.ranges] + [x_r, rg(xg)])
            nb8 = nblk * 8
            dtf = dtr.ap.rearrange("p b e -> p (b e)")[:, 0:nb8]
            act(dt.ap.rearrange("p b e -> p (b e)")[:, 0:nb8], dtf, AF.Exp, [dtr], [dt])
            act(dt.ap.rearrange("p b e -> p (b e)")[:, 0:nb8], dt.ap.rearrange("p b e -> p (b e)")[:, 0:nb8], AF.Ln, [dt], [dt], bias=1.0)
            tt(dta.ap[:, 0:nblk, :], dt.ap[:, 0:nblk, :], a_b.ap[:, l * 8:(l + 1) * 8].unsqueeze(1).to_broadcast([128, nblk, 8]),
               ALU.mult, [dt, a_b], [dta])
            memset(BCt.ap[64:128, 1, 0:T], 0.0, [BCt])
            memset(BCt.ap[0:64, 2, 0:T], 0.0, [BCt])
            for ch in range(4):
                for tap in range(5):
                    col = O_SCW + tap * 4 + ch
                    ts(dg5.ap[:, ch, tap, :], identb.ap, prm.ap[:, l, col:col + 1], None, ALU.mult, None, [identb, prm], [dg5])
            for (si, t0) in c.tiles:
                n = TT
                xofs = si * 4 + t0
                for ch in range(4):
                    pb = psBk.next()
                    for tap in range(5):
                        mm(PS[pb].ap[:, 0:n], dg5.ap[:, ch, tap, :], xbcp.ap[:, ch, xofs + tap:xofs + tap + n], tap == 0, tap == 4,
                           [dg5, xbcp], PS[pb], fresh=(tap == 0))
                    bias_ = prm.ap[:, l, O_SCB + ch:O_SCB + ch + 1]
                    if ch < 2:
                        act(xsT.ap[:, ch, 0:n], PS[pb].ap[:, 0:n], AF.Silu, [PS[pb], prm], [xsT], bias=bias_)
                    elif ch == 2:
                        act(BCt.ap[:, 0, t0:t0 + n], PS[pb].ap[:, 0:n], AF.Silu, [PS[pb], prm], [BCt], bias=bias_)
                    else:
                        act(BCt.ap[0:64, 1, t0:t0 + n], PS[pb].ap[0:64, 0:n], AF.Silu, [PS[pb], prm], [BCt], bias=bias_[0:64])
                        act(BCt.ap[64:128, 2, t0:t0 + n], PS[pb].ap[64:128, 0:n], AF.Silu, [PS[pb], prm], [BCt], bias=bias_[64:128])
                for b in range(n // 128):
                    blk = (t0 + b * 128) // 128
                    pb = psA.next()
                    for j in range(2):
                        tp(PSB[pb].ap[:, j * 128:(j + 1) * 128], xsT.ap[:, j, b * 128:(b + 1) * 128], identb.ap, [xsT, identb], PS[pb], fresh=(j == 0))
                    tp(PSB[pb].ap[:, 256:384], BCt.ap[:, 0, t0 + b * 128:t0 + (b + 1) * 128], identb.ap, [BCt, identb], PS[pb])
                    cp(xs_tm.ap[:, blk, :], PSB[pb].ap[:, 0:256], [PS[pb]], [xs_tm])
                    cp(B_tm.ap[:, blk, :], PSB[pb].ap[:, 256:384], [PS[pb]], [B_tm])
            if stop == "S1":
                return
            hst = alloc(sp2, "hst", [128, 16, 2, 64], BF16)
            Gm = [alloc(sp2, f"Gm{d}", [128, 2, 128], F32) for d in range(2)]
            Lm = alloc(sp2, "Lm", [128, 4, 128], F32)
            seg = alloc(sp2, "seg", [128, 4, 128], F32)
            MT = [[alloc(sp2, f"MT{i}{d}", [128, 4, 128], BF16) for d in range(2)] for i in range(2)]
            xdt = [[alloc(sp2, f"xdt{i}{d}", [128, 4, 64], BF16) for d in range(2)] for i in range(2)]
            ee = [alloc(sp2, f"ee{i}", [128, 32], F32) for i in range(2)]
            cds = [alloc(sp2, f"cds{i}", [128, 2, 2], F32) for i in range(2)]
            xdd = [alloc(sp2, f"xdd{i}", [128, 4, 64], BF16) for i in range(2)]
            wv = alloc(sp2, "wv", [128, 4], F32)
            yo = alloc(sp2, "yo", [128, 8, 64], F32)
            y1 = alloc(sp2, "y1", [128, 256], F32)
            y2 = alloc(sp2, "y2", [128, 256], F32)
            y3 = alloc(sp2, "y3", [128, 256], F32)
            yn = [alloc(sp2, f"yn{i}", [128, 256], BF16) for i in range(2)]
            ssq = alloc(sp2, "ssq2", [128, 2], F32)
            smix = alloc(sp2, "smix", [128, 2, 512], BF16)
            junk = yo
            evb = [View(Lm.name, Lm.ap.rearrange("p h s -> p (h s)")), View(seg.name, seg.ap.rearrange("p h s -> p (h s)"))]

            def small_mm(blk, dirs, ee_, cds_):
                if len(dirs) == 2:
                    A = dta.ap[:, blk, 0:8]
                    for k, (lt, lv) in enumerate([(SU[0].ap, cstf), (SU[1].ap, cstf), (TRI[0].ap, cstf), (TRI[1].ap, cstf), (onesf.ap, onesf)]):
                        mm(PS[6].ap[:, k * 8:k * 8 + 8], lt, A, True, True, [lv, dta], PS[6], fresh=(k == 0))
                    act(ee_.ap[:, 0:32], PS[6].ap[:, 0:32], AF.Exp, [PS[6]], [ee_])
                else:
                    A = dta.ap[:, blk, 4:8]
                    mm(PS[6].ap[:, 12:16], SU[1].ap, A, True, True, [cstf, dta], PS[6], fresh=True)
                    mm(PS[6].ap[:, 36:40], onesf.ap, A, True, True, [onesf, dta], PS[6])
                    act(ee_.ap[:, 12:16], PS[6].ap[:, 12:16], AF.Exp, [PS[6]], [ee_])
                for d in dirs:
                    act(cds_.ap[0:64, d, :], PS[6].ap[0:64, 32 + d * 4:34 + d * 4], AF.Exp, [PS[6]], [cds_])
                    act(cds_.ap[64:128, d, :], PS[6].ap[64:128, 34 + d * 4:36 + d * 4], AF.Exp, [PS[6]], [cds_])

            DEC = [slice(0, 4), slice(12, 16)]

            def state_pre(blk, d, ee_, xdd_):
                tt(wv.ap, dt.ap[:, blk, d * 4:(d + 1) * 4], ee_.ap[:, DEC[d]], ALU.mult, [dt, ee_], [wv])
                tt(xdd_.ap, xs_tm.ap[:, blk, :].rearrange("p (h q) -> p h q", h=4), wv.ap.unsqueeze(2).to_broadcast([128, 4, 64]),
                   ALU.mult, [xs_tm, wv], [xdd_])

            def state_post(blk, d, cds_, xdd_):
                pb = 7
                for h in range(4):
                    g, j = h // 2, h % 2
                    mm(PS[pb].ap[g * 64:(g + 1) * 64, j * 64:(j + 1) * 64], B_tm.ap[:, blk, g * 64:(g + 1) * 64], xdd_.ap[:, h, :],
                       True, True, [B_tm, xdd_], PS[pb], fresh=(h == 0))
                for j in range(2):
                    stt(hTf[d].ap[:, j, :], hTf[d].ap[:, j, :], cds_.ap[:, d, j:j + 1], PS[pb].ap[:, j * 64:(j + 1) * 64],
                        ALU.mult, ALU.add, [hTf[d], cds_, PS[pb]], [hTf[d]])

            def main_pre(blk, i):
                tk = slice(blk * 128, (blk + 1) * 128)
                pg = psA.next()
                mm(PS[pg].ap[:, 0:256], BCt.ap[:, 0, tk], BCt.ap[:, 1:3, tk], True, True, [BCt], PS[pg], fresh=True)
                small_mm(blk, [0, 1], ee[i], cds[i])
                for d in range(2):
                    tt(Gm[d].ap, PS[pg].ap[:, 0:256].rearrange("p (g s) -> p g s", g=2),
                       TRI[d].ap.unsqueeze(1).to_broadcast([128, 2, 128]), ALU.mult, [PS[pg], cstf], [Gm[d]])
                for d in range(2):
                    tt(Lm.ap, SU[d].ap.unsqueeze(1).to_broadcast([128, 4, 128]),
                       dta.ap[:, blk, d * 4:(d + 1) * 4].unsqueeze(2).to_broadcast([128, 4, 128]), ALU.mult, [cstf, dta], [Lm], eng="pool")
                    pdf = psA.next()
                    for h in range(4):
                        mm(PS[pdf].ap[:, h * 128:(h + 1) * 128], Lm.ap[:, h, :], TRI[d].ap, True, True, [Lm, cstf], PS[pdf], fresh=(h == 0))
                    act(seg.ap.rearrange("p h s -> p (h s)"), PS[pdf].ap, AF.Exp, [PS[pdf]], [seg])
                    for g in range(2):
                        tt(MT[i][d].ap[:, 2 * g:2 * g + 2, :], seg.ap[:, 2 * g:2 * g + 2, :],
                           Gm[d].ap[:, g:g + 1, :].to_broadcast([128, 2, 128]), ALU.mult, [seg, Gm[d]], [MT[i][d]])
                    tt(xdt[i][d].ap, xs_tm.ap[:, blk, :].rearrange("p (h q) -> p h q", h=4),
                       dt.ap[:, blk, d * 4:(d + 1) * 4].unsqueeze(2).to_broadcast([128, 4, 64]), ALU.mult, [xs_tm, dt], [xdt[i][d]], eng="pool")
                state_pre(blk, 0, ee[i], xdd[i])

            def main_post(blk, i):
                tk = slice(blk * 128, (blk + 1) * 128)
                tt(y3.ap.rearrange("p (h q) -> p h q", h=4), xs_tm.ap[:, blk, :].rearrange("p (h q) -> p h q", h=4),
                   dsum_b.ap[:, l, :].unsqueeze(2).to_broadcast([128, 4, 64]), ALU.mult, [xs_tm, dsum_b], [y3], eng="pool")
                py = psBk.next()
                for h in range(4):
                    for d in range(2):
                        mm(PS[py].ap[:, h * 64:(h + 1) * 64], MT[i][d].ap[:, h, :], xdt[i][d].ap[:, h, :], d == 0, d == 1,
                           [MT[i][d], xdt[i][d]], PS[py], fresh=(h == 0 and d == 0))
                po = psBk.next()
                for d in range(2):
                    hsrc_v = hTb if d == 0 else hst
                    hsrc = hTb.ap if d == 0 else hst.ap[:, blk]
                    for g in range(2):
                        mm(PS[po].ap[:, d * 256 + g * 128:d * 256 + (g + 1) * 128], BCt.ap[:, 1 + g, tk], hsrc,
                           True, True, [BCt, hsrc_v], PS[po], fresh=(d == 0 and g == 0))
                state_post(blk, 0, cds[i], xdd[i])
                cp(hTb.ap, hTf[0].ap, [hTf[0]], [hTb], eng="pool")
                for d in range(2):
                    e0 = 16 + 12 * d
                    tt(yo.ap[:, 4 * d:4 * d + 4, :], PS[po].ap[:, d * 256:(d + 1) * 256].rearrange("p (e q) -> p e q", e=4),
                       ee[i].ap[:, e0:e0 + 4].unsqueeze(2).to_broadcast([128, 4, 64]), ALU.mult, [PS[po], ee[i]], [yo])
                yof = yo.ap.rearrange("p e q -> p (e q)")
                tt(y1.ap, yof[:, 0:256], yof[:, 256:512], ALU.add, [yo], [y1])
                tt(y1.ap, y1.ap, PS[py].ap[:, 0:256], ALU.add, [y1, PS[py]], [y1])
                tt(y1.ap, y1.ap, y3.ap, ALU.add, [y1, y3], [y1])
                tt(y2.ap, y1.ap, szb.ap[:, blk, :], ALU.mult, [y1, szb], [y2])
                memset(ssq.ap[:, 0:1], 0.0, [ssq])
                act(junk.ap.rearrange("p e q -> p (e q)")[:, 0:256], y2.ap, AF.Square, [y2, ssq], [junk, ssq], accum_out=ssq.ap[:, 0:1])
                act(ssq.ap[:, 1:2], ssq.ap[:, 0:1], AF.Ln, [ssq], [ssq], scale=1.0 / 256, bias=1e-6)
                act(ssq.ap[:, 1:2], ssq.ap[:, 1:2], AF.Exp, [ssq], [ssq], scale=-0.5)
                ts(yn[i].ap, y2.ap, ssq.ap[:, 1:2], None, ALU.mult, None, [y2, ssq], [yn[i]])

            def main_tail(blk, i):
                bt = (blk * 128) % TT
                pt = psA.next()
                for j in range(2):
                    tp(PSB[pt].ap[:, j * 128:(j + 1) * 128], yn[i].ap[:, j * 128:(j + 1) * 128], identb.ap, [yn[i], identb], PS[pt], fresh=(j == 0))
                cp(smix.ap[:, :, bt:bt + 128], PSB[pt].ap[:, 0:256].rearrange("p (j t) -> p j t", j=2), [PS[pt]], [smix])
                if bt + 128 == TT:
                    t0 = blk * 128 + 128 - TT
                    xupdate(c, l, t0, TT, lambda j, oc: w_or[:, j, oc * 128:(oc + 1) * 128],
                            [smix.ap[:, 0, 0:TT], smix.ap[:, 1, 0:TT]], 16, [v_wor, smix], evac=evb)

            for si, (s0, sl) in enumerate(c.seqs):
                blks = list(range(s0 // 128, (s0 + sl) // 128))
                for d in range(2):
                    if c.ctx:
                        dma("sp", sstg.ap.rearrange("p (g n) -> p g n", g=2),
                            d_st[l, d].rearrange("(g j) p n -> (j p) g n", g=2), writes=[sstg])
                        pb = psA.next()
                        tp(PS[pb].ap[:, 0:128], sstg.ap, identf.ap, [sstg, cstf], PS[pb], fresh=True)
                        cp(hTf[d].ap.rearrange("p j q -> p (j q)"), PS[pb].ap[:, 0:128], [PS[pb]], [hTf[d]])
                    else:
                        memset(hTf[d].ap, 0.0, [hTf[d]])
                rb_ = list(reversed(blks))
                small_mm(rb_[0], [1], ee[0], cds[0])
                state_pre(rb_[0], 1, ee[0], xdd[0])
                for k, blk in enumerate(rb_):
                    i = k % 2
                    if k + 1 < len(rb_):
                        small_mm(rb_[k + 1], [1], ee[1 - i], cds[1 - i])
                        state_pre(rb_[k + 1], 1, ee[1 - i], xdd[1 - i])
                    cp(hst.ap[:, blk], hTf[1].ap, [hTf[1]], [hst], eng="pool")
                    state_post(blk, 1, cds[i], xdd[i])
                if stop == "S2":
                    return
                cp(hTb.ap, hTf[0].ap, [hTf[0]], [hTb], eng="pool")
                main_pre(blks[0], 0)
                for k, blk in enumerate(blks):
                    if k + 1 < len(blks):
                        main_pre(blks[k + 1], (k + 1) % 2)
                    main_post(blk, k % 2)
                    if k >= 1:
                        main_tail(blks[k - 1], (k - 1) % 2)
                main_tail(blks[-1], (len(blks) - 1) % 2)
                if stop == "S3":
                    return
                if c.kind == "P":
                    for d in range(2):
                        pb = psA.next()
                        tp(PS[pb].ap[:, 0:128], hTf[d].ap.rearrange("p j q -> p (j q)"), identf.ap, [hTf[d], cstf], PS[pb], fresh=True)
                        cp(sstg.ap, PS[pb].ap[:, 0:128], [PS[pb]], [sstg])
                        dma("sp", o_ssd[si, l, d].rearrange("(g j) p n -> (j p) g n", g=2),
                            sstg.ap.rearrange("p (g n) -> p g n", g=2), reads=[sstg])

        def mla_phase(c, l, w_oa, v_woa):
            T, TT = c.T, c.TT
            g_r, x_r, z_r = rg(gpad), rg(xbcp), rg(szb)
            attnT = alloc(Bump([x_r]), "attnT", [128, 4, TS], BF16)
            sp = Bump([(SC0, SBYTES), g_r, z_r])
            NKB = (PAST + TS) // 128
            KT = [alloc(sp, f"KT{i}", [128, PAST + TS], BF16) for i in range(2)]
            VB = [(alloc(sp, f"Ve{i}", [128, NKB, 128], BF16), alloc(sp, f"Vo{i}", [128, NKB, 128], BF16)) for i in range(2)]
            for i in range(2):
                memset(VB[i][0].ap[:, :, 64:128], 1.0, [VB[i][0]])
                memset(VB[i][1].ap[:, :, 0:64], 1.0, [VB[i][1]])
            QT = [alloc(sp, f"QT{i}", [128, TS], BF16) for i in range(2)]
            PT = [alloc(sp, f"PT{i}", [128, 512], BF16) for i in range(3)]
            t1 = alloc(sp, "mt1", [128, 512], F32)
            t2 = alloc(sp, "mt2", [128, 512], F32)
            rbs = Bump([(sp.take(2048),) * 2])
            _b0 = rbs.ranges[0][0]
            rbs = None
            def half_view(nm, b0, p0):
                ap = S[p0:p0 + 64, b0:b0 + 2048].bitcast(F32)
                uid[0] += 1
                n2 = f"{nm}.{uid[0]}"
                P.region(n2, "sbuf", p0, p0 + 64, b0, b0 + 2048)
                return View(n2, ap)
            _b1 = sp.take(2048)
            rb_src = {0: half_view("rbsE", _b0, 64), 1: half_view("rbsO", _b0, 0)}
            rb_dst = {0: half_view("rbdE", _b1, 0), 1: half_view("rbdO", _b1, 64)}
            ptr = Rot(PT)
            LA = 2
            psS = Rot([0, 1, 2])
            PPr = Rot([3, 7])
            accb = Rot([4, 5, 6])

            heads = []
            for si, (s0, sl) in enumerate(c.seqs):
                for h in range(8):
                    heads.append((si, s0, sl, h))

            def prep(idx):
                si, s0, sl, h = heads[idx]
                hp, hh = h // 2, h % 2
                Tk = (PAST if c.ctx else 0) + sl
                k0 = 0 if c.ctx else s0
                nkb = Tk // 128
                qtiles = [t for (s_, t) in c.tiles if s_ == si]
                kt, qt = KT[idx % 2], QT[idx % 2]
                Ve, Vo = VB[(idx // 2) % 2]
                if hh == 0:
                    vcols = wukv.ap.rearrange("p (h c) -> p h c", h=8)[:, 2 * hp:2 * hp + 2, 64:128]
                    for kb in range(nkb):
                        PP = PPr.next()
                        mm(PS[PP].ap[:, 0:128], ckvnT.ap[:, k0 + kb * 128:k0 + (kb + 1) * 128], vcols, True, True, [ckvnT, wukv], PS[PP], fresh=True)
                        cp(Ve.ap[:, kb, 0:64], PS[PP].ap[:, 0:64], [PS[PP]], [Ve])
                        cp(Vo.ap[:, kb, 64:128], PS[PP].ap[:, 64:128], [PS[PP]], [Vo])
                        yield
                for k1 in range(0, Tk, 512):
                    kn = min(512, Tk - k1)
                    PP = PPr.next()
                    mm(PS[PP].ap[0:64, 0:kn], wukv.ap[:, h * 128:h * 128 + 64], ckvnT.ap[:, k0 + k1:k0 + k1 + kn], True, True,
                       [wukv, ckvnT], PS[PP], fresh=True)
                    cp(kt.ap[0:64, k1:k1 + kn], PS[PP].ap[0:64, 0:kn], [PS[PP]], [kt])
                    yield
                cp(kt.ap[64:96, 0:Tk], krT.ap[64:96, k0:k0 + Tk], [krT], [kt], eng="pool")
                for t0 in qtiles:
                    n = TT
                    tl = t0 - s0
                    PP = PPr.next()
                    for kc in range(2):
                        mm(PS[PP].ap[0:96, 0:n], wuq.ap[:, kc, h * 96:(h + 1) * 96], qlat.ap[:, kc, t0:t0 + n], kc == 0, kc == 1,
                           [wuq, qlat], PS[PP], fresh=(kc == 0))
                    if c.rope:
                        cp(qt.ap[0:64, tl:tl + n], PS[PP].ap[0:64, 0:n], [PS[PP]], [qt])
                        tt(t1.ap[64:96, 0:n], PS[PP].ap[64:96, 0:n], ropeC.ap[64:96, t0:t0 + n], ALU.mult, [PS[PP], ropeC], [t1])
                        yield
                        PP2 = PPr.next()
                        for kc in range(2):
                            mm(PS[PP2].ap[64:96, 0:n], wuq.ap[:, kc, 768 + h * 32:768 + (h + 1) * 32], qlat.ap[:, kc, t0:t0 + n],
                               kc == 0, kc == 1, [wuq, qlat], PS[PP2], fresh=(kc == 0))
                        tt(t2.ap[64:96, 0:n], PS[PP2].ap[64:96, 0:n], ropeS.ap[64:96, t0:t0 + n], ALU.mult, [PS[PP2], ropeS], [t2])
                        tt(qt.ap[64:96, tl:tl + n], t1.ap[64:96, 0:n], t2.ap[64:96, 0:n], ALU.add, [t1, t2], [qt])
                    else:
                        cp(qt.ap[0:96, tl:tl + n], PS[PP].ap[0:96, 0:n], [PS[PP]], [qt])
                    yield

            def attn_unit(idx, t0, prev_tail, filler):
                si, s0, sl, h = heads[idx]
                hp, hh = h // 2, h % 2
                nkb = ((PAST if c.ctx else 0) + sl) // 128
                kt, qt = KT[idx % 2], QT[idx % 2]
                vv = VB[(idx // 2) % 2][hh]
                tl = t0 - s0
                n = c.TT
                po = accb.next()
                r0 = 0 if hh == 0 else 64
                d0 = 64 - r0
                pts = []
                for kb in range(nkb + LA):
                    if kb < nkb:
                        pb = psS.next()
                        mm(PS[pb].ap[:, 0:n], kt.ap[0:96, kb * 128:(kb + 1) * 128], qt.ap[0:96, tl:tl + n], True, True,
                           [kt, qt], PS[pb], fresh=True)
                        pt_ = ptr.next()
                        pts.append(pt_)
                        act(pt_.ap[:, 0:n], PS[pb].ap[:, 0:n], AF.Exp, [PS[pb]], [pt_], scale=SCALE)
                    if kb == min(LA, nkb) - 1 and prev_tail is not None:
                        prev_tail()
                        prev_tail = None
                    if kb >= LA:
                        k2 = kb - LA
                        mm(PS[po].ap[:, 0:n], vv.ap[:, k2, :], pts[k2].ap[:, 0:n], k2 == 0, k2 == nkb - 1, [vv, pts[k2]], PS[po], fresh=(k2 == 0))
                    if filler is not None:
                        next(filler, None)

                def tail():
                    rs_, rd_ = rb_src[hh], rb_dst[hh]
                    recip(rs_.ap[:, 0:n], PS[po].ap[d0:d0 + 64, 0:n], [PS[po]], [rs_])
                    dma("sp", rd_.ap[:, 0:n], rs_.ap[:, 0:n], reads=[rs_], writes=[rd_])
                    tt(attnT.ap[r0:r0 + 64, hp, t0:t0 + n], PS[po].ap[r0:r0 + 64, 0:n], rd_.ap[:, 0:n], ALU.mult, [PS[po], rd_], [attnT])
                return tail

            for _ in prep(0):
                pass
            pend = None
            for idx in range(len(heads)):
                si = heads[idx][0]
                filler = prep(idx + 1) if idx + 1 < len(heads) else None
                for t0 in [t for (s_, t) in c.tiles if s_ == si]:
                    pend = attn_unit(idx, t0, pend, filler)
                if filler is not None:
                    for _ in filler:
                        pass
            if pend is not None:
                pend()
            for (si, t0) in c.tiles:
                xupdate(c, l, t0, TT, lambda j, oc: w_oa[:, j, oc * 128:(oc + 1) * 128],
                        [attnT.ap[:, j, t0:t0 + TT] for j in range(4)], 16, [v_woa, attnT])

        def ffn_phase(c, l):
            T, TT = c.T, c.TT
            sp = Bump([(max(SC0, rg(hT)[1]), SBYTES)])
            sq = [alloc(sp, f"fsq{i}", [128, 512], BF16) for i in range(2)]
            sd = alloc(sp, "fsd", [128, 512], F32)
            rstd = alloc(sp, "frstd", [128, 512], F32)
            tmpb = [alloc(sp, f"ftmpb{i}", [128, 512], F32) for i in range(2)]
            sg = [alloc(sp, f"sg{i}", [128, 512], F32) for i in range(2)]
            actb = [alloc(sp, f"actb{i}", [128, 2, 512], BF16) for i in range(2)]
            sh2 = modv.ap[:, l, 24:32, c.cond]
            for (si, t0) in c.tiles:
                norm_tile(c, (sq, sd, rstd, tmpb), t0, TT, gsc.ap[:, 1, :], sh2, lambda kc, t0=t0: hT.ap[:, kc, t0:t0 + TT], hT)
            def load_group(g):
                c0 = g * 256
                v1 = load_piece([(lambda a: r3(a, 8)[:, :, 0:256], d_wg[l, :, c0:c0 + 256].rearrange("(k p) c -> p k c", p=128)),
                                 (lambda a: r3(a, 8)[:, :, 256:512], d_wu[l, :, c0:c0 + 256].rearrange("(k p) c -> p k c", p=128))])
                v2 = load_piece([(lambda a: r3(a, 4)[:, 0:2, :], d_wd[l, c0:c0 + 256, :].rearrange("(k p) c -> p k c", p=128))])
                return (v1, v2)

            units = [(g, t0) for g in range(NFG) for (si, t0) in c.tiles]
            wts = {0: load_group(0), 1: load_group(1)}

            psF = Rot([4, 5, 6])

            def stage_a_groups(u):
                g, t0 = units[u]
                v1, v2 = wts[g]
                wgu = r3(v1.ap, 8)
                n = TT
                ab = actb[u % 2]
                outs = []
                st = {}

                def mk(j, which):
                    def f():
                        pb = psA.next()
                        c0 = (0 if which == 0 else 256) + j * 128
                        for kc in range(8):
                            mm(PS[pb].ap[:, 0:n], wgu[:, kc, c0:c0 + 128], hT.ap[:, kc, t0:t0 + n], kc == 0, kc == 7, [v1, hT], PS[pb], fresh=(kc == 0))
                        st[(j, which)] = pb
                        if which == 1:
                            pg, pu = st[(j, 0)], pb
                            act(sg[j].ap[:, 0:n], PS[pg].ap[:, 0:n], AF.Silu, [PS[pg]], [sg[j]])
                            tt(ab.ap[:, j, 0:n], sg[j].ap[:, 0:n], PS[pu].ap[:, 0:n], ALU.mult, [sg[j], PS[pu]], [ab])
                    return f
                for j in range(2):
                    outs.append(mk(j, 0))
                    outs.append(mk(j, 1))
                return outs

            def stage_b_steps(u):
                g, t0 = units[u]
                v1, v2 = wts[g]
                wdn = r3(v2.ap, 4)
                n = TT
                ab = actb[u % 2]
                outs = []

                def mk(oc):
                    def f():
                        pb = psF.next()
                        for j in range(2):
                            mm(PS[pb].ap[:, 0:n], wdn[:, j, oc * 128:(oc + 1) * 128], ab.ap[:, j, 0:n], j == 0, j == 1, [v2, ab], PS[pb], fresh=(j == 0))
                        stt(xT.ap[:, oc, t0:t0 + n], PS[pb].ap[:, 0:n], modv.ap[:, l, 40 + oc, c.cond:c.cond + 1],
                            xT.ap[:, oc, t0:t0 + n], ALU.mult, ALU.add, [PS[pb], modv, xT], [xT])
                        if oc == 7 and (u + 1 == len(units) or units[u + 1][0] != g) and g + 2 < NFG:
                            wts[g + 2] = load_group(g + 2)
                    return f
                for oc in range(8):
                    outs.append(mk(oc))
                return outs

            modgen = None
            if l + 1 < n_layers and (l + 1) not in mods_done:
                mslot = alloc(sp, "mslot", [128, 4096], BF16)

                def slot_fn(srcs):
                    for dst_fn, src in srcs:
                        dma("pool", dst_fn(mslot.ap), src, writes=[mslot])
                    return mslot
                modgen = mod_layer(l + 1, slot_fn)
            every = max(1, len(units) // 12)
            for u in range(len(units) + 1):
                ga = stage_a_groups(u) if u < len(units) else []
                gb = stage_b_steps(u - 1) if u >= 1 else []
                for k in range(4):
                    if ga:
                        ga[k]()
                    if gb:
                        gb[2 * k]()
                        gb[2 * k + 1]()
                if modgen is not None and u % every == every - 1:
                    next(modgen, None)
            if modgen is not None:
                for _ in modgen:
                    pass


        def final_out(c):
            sp = Bump([(SC0, SBYTES)])
            sq = [alloc(sp, f"osq{i}", [128, 512], BF16) for i in range(2)]
            sd = alloc(sp, "osd", [128, 512], F32)
            rstd = alloc(sp, "orstd", [128, 512], F32)
            tmpb = [alloc(sp, f"otmpb{i}", [128, 512], F32) for i in range(2)]
            yf = alloc(sp, "yf", [128, 8, 128], F32)
            ytm = [alloc(sp, f"ytm{i}", [128, D], F32) for i in range(2)]
            gf = cnd.ap[:, 16:24]
            for b in range(c.nblk):
                norm_tile(c, (sq, sd, rstd, tmpb), b * 128, 128, gf, None, lambda kc: yf.ap[:, kc, :], yf)
                y = ytm[b % 2]
                for half in range(2):
                    pb = psA.next()
                    for q in range(4):
                        kc = half * 4 + q
                        tp(PS[pb].ap[:, q * 128:(q + 1) * 128], yf.ap[:, kc, :], identf.ap, [yf, cstf], PS[pb], fresh=(q == 0))
                    act(y.ap[:, half * 512:(half + 1) * 512], PS[pb].ap, AF.Copy, [PS[pb]], [y])
                dma("sp", c.oy[b * 128:(b + 1) * 128, :], y.ap, reads=[y])

        passes = []
        if do_sample:
            passes.append(make_cfg("S"))
        if do_prompt:
            passes.append(make_cfg("P"))
        for c in passes:
            load_x(c)
            for l in range(n_layers):
                layer_pass(c, l)
            final_out(c)

        names = P.finalize()
        print("nops", len(P.ops), "nsems", len(names))
        sems = {nm: es.enter_context(nc.semaphore(f"s{i}")) for i, nm in enumerate(names)}
        with nc.Block() as block:
            P.emit(block, sems)
    return nc


def _consts():
    i = np.arange(128)
    ident = np.eye(128, dtype=np.float32)
    suf = (i[:, None] > i[None, :]).astype(np.float32)
    sub = (i[:, None] < i[None, :]).astype(np.float32)
    trif = (i[:, None] <= i[None, :]).astype(np.float32)
    trib = (i[:, None] >= i[None, :]).astype(np.float32)
    cst = np.concatenate([ident, suf, sub, trif, trib], axis=1).astype(np.float32)
    t = np.arange(TS)
    row = (t // 64).astype(np.float32)
    col = (t % 64).astype(np.float32)
    nf = 8
    inv = (10000.0 ** (-np.arange(nf, dtype=np.float32) / nf)).astype(np.float32)
    ang = np.stack([row[:, None] * inv, col[:, None] * inv], axis=1)
    cos = np.cos(ang).astype(np.float32)
    sin = np.sin(ang).astype(np.float32)
    rope = np.zeros((2, 128, TS), np.float32)
    for a in range(2):
        for half in range(2):
            for f in range(nf):
                r = 64 + a * 16 + half * 8 + f
                rope[0, r] = cos[:, a, f]
                rope[1, r] = sin[:, a, f]
    return cst, rope


_CACHE = {}


def kernel(x_prompt, x_sample, c, cache_ckv, cache_krope, state_ssd, c_ctx, w_ada, b_ada,
           g_mix, w_in, g_q, w_uq, g_kv, w_ukv, ssd_conv_w, ssd_conv_b, ssd_dt_bias,
           ssd_a_log, ssd_d, ssd_norm_g, cm_conv_w, cm_conv_b, cm_ln_g, cm_ln_b, w_out,
           g_ffn, w_gate, w_up, w_down, g_final, _n_layers=DEPTH, _do_sample=True, _do_prompt=True, _stop=None):
    f = lambda a: np.ascontiguousarray(np.asarray(a, dtype=np.float32))
    key = (_n_layers, _do_sample, _do_prompt, _stop)
    if key not in _CACHE:
        _CACHE[key] = build_program(_n_layers, _do_sample, _do_prompt, _stop)
    nc = _CACHE[key]
    cst, rope = _consts()
    shared = dict(
        w_ada=f(w_ada), b_ada=f(b_ada), g_mix=f(g_mix), w_in=f(w_in), g_q=f(g_q), w_uq=f(w_uq), g_kv=f(g_kv),
        w_ukv=f(w_ukv), ssd_conv_w=f(ssd_conv_w), ssd_conv_b=f(ssd_conv_b), ssd_dt_bias=f(ssd_dt_bias).reshape(-1),
        ssd_a_log=f(ssd_a_log).reshape(-1), ssd_d=f(ssd_d).reshape(-1), ssd_norm_g=f(ssd_norm_g), cm_conv_w=f(cm_conv_w),
        cm_conv_b=f(cm_conv_b), cm_ln_g=f(cm_ln_g), cm_ln_b=f(cm_ln_b), w_out=f(w_out), g_ffn=f(g_ffn),
        w_gate=f(w_gate), w_up=f(w_up), w_down=f(w_down), g_final=f(g_final), cst=cst, rope=rope)
    x_prompt, x_sample, c, c_ctx = f(x_prompt), f(x_sample), f(c), f(c_ctx)
    cache_ckv, cache_krope, state_ssd = f(cache_ckv), f(cache_krope), f(state_ssd)
    in_maps = []
    for i in range(8):
        b = i % 4
        m = dict(shared)
        m["x_s"] = x_sample[b]
        m["x_p"] = x_prompt[2 * i:2 * i + 2].reshape(NPS * TPS, D)
        m["cond"] = np.stack([c[b], c_ctx], axis=0)
        m["cache_ckv"] = cache_ckv[b]
        m["cache_krope"] = cache_krope[b]
        m["state_ssd"] = state_ssd[b]
        in_maps.append(m)
    res = run_bass_kernel_spmd(nc, in_maps, core_ids=list(range(8)))
    r = res.results
    y_sample = np.stack([r[b]["y_s"] for b in range(4)], axis=0).astype(np.float32)
    y_prompt = np.concatenate([r[i]["y_p"].reshape(NPS, TPS, D) for i in range(8)], axis=0).astype(np.float32)
    new_ckv = np.concatenate([r[i]["o_ckv"] for i in range(8)], axis=0).astype(np.float32)
    new_kr = np.concatenate([r[i]["o_kr"] for i in range(8)], axis=0).astype(np.float32)
    new_ssd = np.concatenate([r[i]["o_ssd"] for i in range(8)], axis=0).astype(np.float32)
    return (y_prompt, y_sample, new_ckv, new_kr, new_ssd)
```

```python
import math
from contextlib import ExitStack
import numpy as np
import concourse.bass as bass
import concourse.mybir as mybir
from concourse.bass_utils import run_bass_kernel_spmd

F32 = mybir.dt.float32
BF16 = mybir.dt.bfloat16
U8 = mybir.dt.uint8
AF = mybir.ActivationFunctionType
ALU = mybir.AluOpType

ENGS = ("pe", "act", "dve", "pool", "sp")

D = 1024
DEPTH = 4
TS = 2048
TPS = 256
NPS = 2
PAST = 256
DFF = 2816
NFG = DFF // 256
IN_W = 1704
SCALE = 96 ** -0.5
NPRM = 161


class View:
    __slots__ = ("name", "ap")

    def __init__(self, name, ap):
        self.name, self.ap = name, ap


class Prog:
    def __init__(self, nc):
        self.nc = nc
        self.ops = []
        self.regions = {}
        self.overlaps = {}

    def region(self, name, space, p0, p1, b0, b1):
        assert name not in self.regions, name
        self.regions[name] = (space, p0, p1, b0, b1)
        ov = [name]
        for n, (s, q0, q1, c0, c1) in self.regions.items():
            if n == name:
                continue
            if s == space and q0 < p1 and p0 < q1 and c0 < b1 and b0 < c1:
                ov.append(n)
                self.overlaps[n].append(name)
        self.overlaps[name] = ov

    def op(self, eng, fn, reads=(), writes=(), dma=False, chan=None, fresh=()):
        rs = [r if isinstance(r, str) else r.name for r in reads]
        ws = [w if isinstance(w, str) else w.name for w in writes]
        fr = [w if isinstance(w, str) else w.name for w in fresh]
        if dma and chan is None:
            chan = ws[0] if ws else rs[0]
        self.ops.append((eng, fn, rs, ws, dma, chan, fr))

    def finalize(self):
        ops = self.ops
        n = len(ops)
        last_w = {}
        readers = {}
        deps = [None] * n
        for i, (eng, fn, rs, ws, dma, chan, fr) in enumerate(ops):
            for f in fr:
                lw = last_w.get(f)
                assert lw is None or len(readers.get(f, ())) > 0, \
                    f"PSUM collision on {f} at op {i} (prev writer {lw} unread)"
            d = set()
            for r in rs:
                for rr in self.overlaps[r]:
                    w = last_w.get(rr)
                    if w is not None:
                        d.add(w)
                    if self.regions[rr][0] == "psum":
                        for x in readers.get(rr, {}).values():
                            d.add(x)
            for w_ in ws:
                for ww in self.overlaps[w_]:
                    w = last_w.get(ww)
                    if w is not None:
                        d.add(w)
                    for x in readers.get(ww, {}).values():
                        d.add(x)
            d.discard(i)
            dd = []
            mine = set(rs) | set(ws)
            for j in d:
                ej, _, rsj, wsj, dmaj, _, _ = ops[j]
                if not dmaj and not dma and ej == eng:
                    if eng == "pe":
                        continue
                    hit = False
                    for x in wsj:
                        for y in self.overlaps[x]:
                            if y in mine:
                                hit = True
                                break
                        if hit:
                            break
                    if not hit:
                        continue
                dd.append(j)
            deps[i] = dd
            key = ("c:" + chan) if dma else eng
            for r in rs:
                readers.setdefault(r, {})[key] = i
            for w_ in ws:
                last_w[w_] = i
                readers[w_] = {}
        needed = set()
        for dd in deps:
            needed.update(dd)
        eng_cnt = {e: 0 for e in ENGS}
        chan_cnt = {}
        ticket = {}
        for i, (eng, fn, rs, ws, dma, chan, fr) in enumerate(ops):
            if dma:
                chan_cnt[chan] = chan_cnt.get(chan, 0) + 16
                ticket[i] = ("c:" + chan, chan_cnt[chan])
            elif i in needed:
                eng_cnt[eng] += 1
                ticket[i] = ("e:" + eng, eng_cnt[eng])
        self.sem_names = ["e:" + e for e in ENGS] + ["c:" + c for c in chan_cnt]
        self.final_counts = {("e:" + e): eng_cnt[e] for e in ENGS}
        self.final_counts.update({("c:" + c): v for c, v in chan_cnt.items()})
        self.deps, self.ticket = deps, ticket
        return self.sem_names

    def emit(self, block, sems):
        ops, deps, ticket = self.ops, self.deps, self.ticket
        per_eng = {e: [] for e in ENGS}
        for i, o in enumerate(ops):
            per_eng[o[0]].append(i)
        final_counts = self.final_counts

        def make(engname):
            def body(e):
                waited = {}
                for i in per_eng[engname]:
                    _, fn, rs, ws, dma, chan, _ = ops[i]
                    need = {}
                    for j in deps[i]:
                        s, v = ticket[j]
                        if need.get(s, 0) < v:
                            need[s] = v
                    for s, v in need.items():
                        if waited.get(s, 0) < v:
                            e.wait_ge(sems[s], v)
                            waited[s] = v
                    ins = fn(e)
                    if i in ticket:
                        s, v = ticket[i]
                        ins.then_inc(sems[s], 16 if dma else 1)
                if engname == "sp":
                    for s, v in final_counts.items():
                        if v > 0 and waited.get(s, 0) < v:
                            e.wait_ge(sems[s], v)
            return body

        block.tensor(make("pe"))
        block.scalar(make("act"))
        block.vector(make("dve"))
        block.gpsimd(make("pool"))
        block.sync(make("sp"))


class Bump:
    def __init__(self, ranges):
        self.ranges = [list(r) for r in ranges]

    def take(self, nb):
        nb = (nb + 31) // 32 * 32
        for r in self.ranges:
            if r[1] - r[0] >= nb:
                b0 = r[0]
                r[0] += nb
                return b0
        raise MemoryError(f"bump pool exhausted need {nb} have {self.ranges}")


def esize(dt):
    return 4 if dt == F32 else 2


def build_program(n_layers=DEPTH, do_sample=True, do_prompt=True, stop=None):
    nc = bass.Bass("TRN2", target_bir_lowering=False)
    P = Prog(nc)

    def din(name, shape):
        return nc.dram_tensor(name, list(shape), F32, kind="ExternalInput").ap()

    def dout(name, shape):
        return nc.dram_tensor(name, list(shape), F32, kind="ExternalOutput").ap()

    d_xs = din("x_s", [TS, D])
    d_xp = din("x_p", [NPS * TPS, D])
    d_cond = din("cond", [2, D])
    d_cckv = din("cache_ckv", [DEPTH, PAST, 128])
    d_ckr = din("cache_krope", [DEPTH, PAST, 32])
    d_st = din("state_ssd", [DEPTH, 2, 4, 64, 64])
    d_wada = din("w_ada", [DEPTH, D, 6 * D])
    d_bada = din("b_ada", [DEPTH, 6 * D])
    d_gmix = din("g_mix", [DEPTH, D])
    d_win = din("w_in", [DEPTH, D, IN_W])
    d_gq = din("g_q", [DEPTH, 256])
    d_wuq = din("w_uq", [DEPTH, 256, 768])
    d_gkv = din("g_kv", [DEPTH, 128])
    d_wukv = din("w_ukv", [DEPTH, 128, 1024])
    d_scw = din("ssd_conv_w", [DEPTH, 5, 512])
    d_scb = din("ssd_conv_b", [DEPTH, 512])
    d_dtb = din("ssd_dt_bias", [DEPTH * 8])
    d_alog = din("ssd_a_log", [DEPTH * 8])
    d_sd = din("ssd_d", [DEPTH * 8])
    d_sng = din("ssd_norm_g", [DEPTH, 256])
    d_ccw = din("cm_conv_w", [DEPTH, 31, 256])
    d_ccb = din("cm_conv_b", [DEPTH, 256])
    d_clg = din("cm_ln_g", [DEPTH, 256])
    d_clb = din("cm_ln_b", [DEPTH, 256])
    d_wout = din("w_out", [DEPTH, D, D])
    d_gffn = din("g_ffn", [DEPTH, D])
    d_wg = din("w_gate", [DEPTH, D, DFF])
    d_wu = din("w_up", [DEPTH, D, DFF])
    d_wd = din("w_down", [DEPTH, DFF, D])
    d_gfin = din("g_final", [D])
    d_cst = din("cst", [128, 640])
    d_rope = din("rope", [2, 128, TS])

    o_ys = dout("y_s", [TS, D])
    o_yp = dout("y_p", [NPS * TPS, D])
    o_ckv = dout("o_ckv", [NPS, DEPTH, TPS, 128])
    o_kr = dout("o_kr", [NPS, DEPTH, TPS, 32])
    o_ssd = dout("o_ssd", [NPS, DEPTH, 2, 4, 64, 64])

    es = ExitStack()
    with es:
        SBYTES = 212000
        S = es.enter_context(nc.sbuf_tensor("S", [128, SBYTES], U8))
        banks = [es.enter_context(nc.psum_tensor(f"PS{i}", [128, 512], F32)) for i in range(8)]
        for i in range(8):
            P.region(f"ps{i}", "psum", 0, 128, i * 2048, (i + 1) * 2048)
        PS = [View(f"ps{i}", banks[i][:, :]) for i in range(8)]
        PSB = [View(f"ps{i}", banks[i][:, :].bitcast(BF16)) for i in range(8)]

        uid = [0]

        def alloc(pool, name, shape, dt, p0=0):
            nel = int(np.prod(shape[1:]))
            nb = nel * esize(dt)
            b0 = pool.take(nb)
            ap = S[p0:p0 + shape[0], b0:b0 + nb].bitcast(dt)
            if len(shape) == 3:
                ap = ap.rearrange("p (a b) -> p a b", a=shape[1])
            elif len(shape) == 4:
                ap = ap.rearrange("p (a b c) -> p a b c", a=shape[1], b=shape[2])
            uid[0] += 1
            nm = f"{name}.{uid[0]}"
            P.region(nm, "sbuf", p0, p0 + shape[0], b0, b0 + ((nb + 31) // 32 * 32))
            return View(nm, ap)

        pers = Bump([(0, SBYTES)])
        xT = alloc(pers, "xT", [128, 8, TS], F32)
        RING_N = 4
        ring = [alloc(pers, f"ring{i}", [128, 4096], BF16) for i in range(RING_N)]
        wuq = alloc(pers, "wuq", [128, 2, 1024], BF16)
        wukv = alloc(pers, "wukv", [128, 1024], BF16)
        cstf = alloc(pers, "cstf", [128, 640], F32)
        identf = View(cstf.name, cstf.ap[:, 0:128])
        SU = [View(cstf.name, cstf.ap[:, 128:256]), View(cstf.name, cstf.ap[:, 256:384])]
        TRI = [View(cstf.name, cstf.ap[:, 384:512]), View(cstf.name, cstf.ap[:, 512:640])]
        identb = alloc(pers, "identb", [128, 128], BF16)
        onesb = alloc(pers, "onesb", [128, 128], BF16)
        onesf = alloc(pers, "onesf", [128, 128], F32)
        ropeC = alloc(pers, "ropeC", [128, TS], BF16)
        ropeS = alloc(pers, "ropeS", [128, TS], BF16)
        prm = alloc(pers, "prm", [128, DEPTH, NPRM], F32)
        cnd = alloc(pers, "cnd", [128, 24], F32)
        scond = alloc(pers, "scond", [128, 8, 2], BF16)
        modv = alloc(pers, "modv", [128, DEPTH, 48, 2], F32)
        gsc = alloc(pers, "gsc", [128, 2, 8], F32)
        dtb_b = alloc(pers, "dtb_b", [128, 32], F32)
        alog_b = alloc(pers, "alog_b", [128, 32], F32)
        dd_b = alloc(pers, "dd_b", [128, 32], F32)
        a_b = alloc(pers, "a_b", [128, 32], F32)
        dsum_b = alloc(pers, "dsum_b", [128, DEPTH, 4], F32)
        gkv_b = alloc(pers, "gkv_b", [128, 128], F32)
        xg = alloc(pers, "xg", [128, 8, 512], BF16)
        xg0 = xg
        pers_end = pers.ranges[0][0]
        io = Bump([(pers_end, SBYTES)])
        gpad = alloc(io, "gpad", [128, 2, TS + 32], BF16)
        xbcp = alloc(io, "xbcp", [128, 4, TS + 4], BF16)
        szb = alloc(io, "szb", [128, 16, 256], BF16)
        qlat = alloc(io, "qlat", [128, 2, TS], BF16)
        ckvnT = alloc(io, "ckvnT", [128, PAST + TS], BF16)
        krT = alloc(io, "krT", [128, PAST + TS], BF16)
        dtr = alloc(io, "dtr", [128, 16, 8], F32)
        io_end = io.ranges[0][0]
        hT = alloc(Bump([(pers_end, SBYTES)]), "hT", [128, 8, TS], BF16)
        R = P.regions
        rg = lambda v: (R[v.name][3], R[v.name][4])
        SC0 = io_end
        print("mem: pers_end", pers_end, "io_end", io_end, "scratch", SBYTES - io_end)

        def dma(eng, out, in_, reads=(), writes=(), **kw):
            P.op(eng, lambda e: e.dma_start(out=out, in_=in_, **kw), reads=reads, writes=writes, dma=True)

        def mm(out, lhsT, rhs, start, stop, reads, w, fresh=False):
            P.op("pe", lambda e: e.matmul(out, lhsT=lhsT, rhs=rhs, start=start, stop=stop),
                 reads=reads, writes=[w], fresh=[w] if fresh else ())

        def tp(out, in_, ident, reads, w, fresh=False):
            P.op("pe", lambda e: e.transpose(out, in_, ident), reads=reads, writes=[w],
                 fresh=[w] if fresh else ())

        def act(out, in_, func, reads, writes, **kw):
            P.op("act", lambda e: e.activation(out=out, in_=in_, func=func, **kw), reads=reads, writes=writes)

        def tt(out, in0, in1, op, reads, writes, eng="dve"):
            P.op(eng, lambda e: e.tensor_tensor(out=out, in0=in0, in1=in1, op=op), reads=reads, writes=writes)

        def stt(out, in0, scalar, in1, op0, op1, reads, writes):
            P.op("dve", lambda e: e.scalar_tensor_tensor(out=out, in0=in0, scalar=scalar, in1=in1, op0=op0, op1=op1),
                 reads=reads, writes=writes)

        def ts(out, in0, s1, s2, op0, op1, reads, writes, eng="dve"):
            if op1 is None:
                P.op(eng, lambda e: e.tensor_scalar(out=out, in0=in0, scalar1=s1, scalar2=None, op0=op0),
                     reads=reads, writes=writes)
            else:
                P.op(eng, lambda e: e.tensor_scalar(out=out, in0=in0, scalar1=s1, scalar2=s2, op0=op0, op1=op1),
                     reads=reads, writes=writes)

        def cp(out, in_, reads, writes, eng="dve"):
            P.op(eng, lambda e: e.tensor_copy(out, in_), reads=reads, writes=writes)

        def memset(ap, val, writes, eng="dve"):
            P.op(eng, lambda e: e.memset(ap, val), writes=writes)

        def recip(out, in_, reads, writes):
            P.op("dve", lambda e: e.reciprocal(out=out, in_=in_), reads=reads, writes=writes)

        class Rot:
            def __init__(self, items):
                self.items, self.i = items, 0

            def next(self):
                v = self.items[self.i % len(self.items)]
                self.i += 1
                return v

        psA = Rot([0, 1, 2, 3])
        psBk = Rot([4, 5])

        dma("sp", cstf.ap, d_cst, writes=[cstf])
        dma("pool", ropeC.ap, d_rope[0], writes=[ropeC])
        dma("pool", ropeS.ap, d_rope[1], writes=[ropeS])
        memset(onesb.ap, 1.0, [onesb])
        memset(onesf.ap, 1.0, [onesf])
        cp(identb.ap, identf.ap, [cstf], [identb])
        dma("sp", dtb_b.ap, d_dtb.partition_broadcast(128), writes=[dtb_b])
        dma("sp", alog_b.ap, d_alog.partition_broadcast(128), writes=[alog_b])
        dma("sp", dd_b.ap, d_sd.partition_broadcast(128), writes=[dd_b])
        act(a_b.ap, alog_b.ap, AF.Exp, [alog_b], [a_b])
        ts(a_b.ap, a_b.ap, -1.0, None, ALU.mult, None, [a_b], [a_b])
        ddv = dd_b.ap.rearrange("p (l d h) -> p l d h", l=DEPTH, d=2)
        tt(dsum_b.ap, ddv[:, :, 0, :], ddv[:, :, 1, :], ALU.add, [dd_b], [dsum_b])

        prol = Bump([(SC0, SBYTES)])
        stg = [alloc(prol, f"stg{i}", [128, 128], F32) for i in range(2)]
        O_BADA, O_GMIX, O_GFFN, O_GQ, O_GKV, O_SCW, O_SCB, O_SNG, O_CCW, O_CCB, O_CLG, O_CLB = \
            0, 48, 56, 64, 66, 67, 87, 91, 93, 155, 157, 159
        for l in range(DEPTH):
            rows = [
                (d_bada[l].rearrange("(r c) -> r c", c=128), 48),
                (d_gmix[l].rearrange("(r c) -> r c", c=128), 8),
                (d_gffn[l].rearrange("(r c) -> r c", c=128), 8),
                (d_gq[l].rearrange("(r c) -> r c", c=128), 2),
                (d_gkv[l].rearrange("(r c) -> r c", c=128), 1),
                (d_scw[l].rearrange("j (r c) -> (j r) c", c=128), 20),
                (d_scb[l].rearrange("(r c) -> r c", c=128), 4),
                (d_sng[l].rearrange("(r c) -> r c", c=128), 2),
                (d_ccw[l].rearrange("j (r c) -> (j r) c", c=128), 62),
                (d_ccb[l].rearrange("(r c) -> r c", c=128), 2),
                (d_clg[l].rearrange("(r c) -> r c", c=128), 2),
                (d_clb[l].rearrange("(r c) -> r c", c=128), 2),
            ]
            r0 = 0
            for src, nr in rows:
                done = 0
                while done < nr:
                    si = (r0 + done) // 128
                    off = (r0 + done) % 128
                    k = min(nr - done, 128 - off)
                    dma("sp", stg[si].ap[off:off + k, :], src[done:done + k, :], writes=[stg[si]])
                    done += k
                r0 += nr
            assert r0 == NPRM
            for si, (c0, ncol) in enumerate([(0, 128), (128, NPRM - 128)]):
                pb = psA.next()
                tp(PS[pb].ap[:, 0:ncol], stg[si].ap[0:ncol, :], identf.ap[0:ncol, 0:ncol], [stg[si], cstf], PS[pb], fresh=True)
                cp(prm.ap[:, l, c0:c0 + ncol], PS[pb].ap[:, 0:ncol], [PS[pb]], [prm])
        dma("sp", stg[0].ap[0:16, :], d_cond.rearrange("a (r c) -> (a r) c", c=128), writes=[stg[0]])
        dma("sp", stg[0].ap[16:24, :], d_gfin.rearrange("(r c) -> r c", c=128), writes=[stg[0]])
        pb = psA.next()
        tp(PS[pb].ap[:, 0:24], stg[0].ap[0:24, :], identf.ap[0:24, 0:24], [stg[0], cstf], PS[pb], fresh=True)
        cp(cnd.ap, PS[pb].ap[:, 0:24], [PS[pb]], [cnd])
        act(scond.ap.rearrange("p k c -> p c k"), cnd.ap[:, 0:16].rearrange("p (c k) -> p c k", c=2), AF.Silu, [cnd], [scond])

        ring_i = [0]

        def load_piece(srcs):
            v = ring[ring_i[0] % RING_N]
            ring_i[0] += 1
            for dst_fn, src in srcs:
                dma("pool", dst_fn(v.ap), src, writes=[v])
            return v

        def r3(ap, a):
            return ap.rearrange("p (a b) -> p a b", a=a)

        mods_done = set()

        def mod_layer(l, slots):
            NP = 24

            def issue(pc):
                v = slots[pc % 2]
                dma("pool", r3(v.ap, 8), d_wada[l, :, pc * 256:(pc + 1) * 256].rearrange("(k p) c -> p k c", p=128), writes=[v])
            issue(0)
            for pc in range(NP):
                if pc + 1 < NP:
                    issue(pc + 1)
                v = slots[pc % 2]
                w3 = r3(v.ap, 8)
                for q in range(2):
                    oc = pc * 2 + q
                    for kc in range(8):
                        mm(PS[7].ap[:, oc * 2:oc * 2 + 2], w3[:, kc, q * 128:(q + 1) * 128], scond.ap[:, kc, :],
                           kc == 0, kc == 7, [v, scond], PS[7], fresh=(oc == 0 and kc == 0))
                if pc == NP - 1:
                    tt(modv.ap[:, l], PS[7].ap[:, 0:96].rearrange("p (o c) -> p o c", c=2),
                       prm.ap[:, l, O_BADA:O_BADA + 48].unsqueeze(2).to_broadcast([128, 48, 2]), ALU.add, [PS[7], prm], [modv])
                    mods_done.add(l)
                yield

        if n_layers > 0:
            pslots = [alloc(prol, f"pmslot{i}", [128, 2048], BF16) for i in range(4)]
            for _ in mod_layer(0, pslots[0:2]):
                pass

        class Cfg:
            pass

        def make_cfg(kind):
            c = Cfg()
            c.kind = kind
            if kind == "S":
                c.T, c.TT, c.seqs, c.rope, c.ctx, c.cond = TS, 512, [(0, TS)], True, True, 0
                c.dx, c.oy = d_xs, o_ys
            else:
                c.T, c.TT, c.seqs, c.rope, c.ctx, c.cond = NPS * TPS, 256, [(0, TPS), (TPS, TPS)], False, False, 1
                c.dx, c.oy = d_xp, o_yp
            c.tiles = []
            for si, (s0, sl) in enumerate(c.seqs):
                for t in range(s0, s0 + sl, c.TT):
                    c.tiles.append((si, t))
            c.nblk = c.T // 128
            return c

        def load_x(c):
            pool = Bump([(SC0, SBYTES)])
            xs_ = [alloc(pool, f"xstg{i}", [128, D], F32) for i in range(2)]
            for b in range(c.nblk):
                st = xs_[b % 2]
                dma("sp", st.ap, c.dx[b * 128:(b + 1) * 128, :], writes=[st])
                for half in range(2):
                    pb = psA.next()
                    for q in range(4):
                        kc = half * 4 + q
                        tp(PS[pb].ap[:, q * 128:(q + 1) * 128], st.ap[:, kc * 128:(kc + 1) * 128], identf.ap,
                           [st, cstf], PS[pb], fresh=(q == 0))
                    cp(xT.ap[:, half * 4:half * 4 + 4, b * 128:(b + 1) * 128],
                       PS[pb].ap.rearrange("p (q t) -> p q t", q=4), [PS[pb]], [xT])

        def norm_tile(c, pool_views, t0, n, gsc_ap, sh_ap, out_fn, out_v, dt_out_bf=True, sq_dve=False):
            sq, sd, rstd, tmpb = pool_views
            for kc in range(8):
                q = sq[kc % 2]
                if sq_dve:
                    tt(q.ap[:, 0:n], xT.ap[:, kc, t0:t0 + n], xT.ap[:, kc, t0:t0 + n], ALU.mult, [xT], [q])
                else:
                    act(q.ap[:, 0:n], xT.ap[:, kc, t0:t0 + n], AF.Square, [xT], [q])
                mm(PS[6].ap[:, 0:n], onesb.ap, q.ap[:, 0:n], kc == 0, kc == 7, [onesb, q], PS[6], fresh=(kc == 0))
            act(sd.ap[:, 0:n], PS[6].ap[:, 0:n], AF.Ln, [PS[6]], [sd], scale=1.0 / D, bias=1e-6)
            act(rstd.ap[:, 0:n], sd.ap[:, 0:n], AF.Exp, [sd], [rstd], scale=-0.5)
            for kc in range(8):
                tb = tmpb[kc % 2]
                tt(tb.ap[:, 0:n], xT.ap[:, kc, t0:t0 + n], rstd.ap[:, 0:n], ALU.mult, [xT, rstd], [tb])
                if sh_ap is None:
                    ts(out_fn(kc), tb.ap[:, 0:n], gsc_ap[:, kc:kc + 1], None, ALU.mult, None, [tb, cnd], [out_v])
                else:
                    act(out_fn(kc), tb.ap[:, 0:n], AF.Identity, [tb, gsc, modv], [out_v],
                        scale=gsc_ap[:, kc:kc + 1], bias=sh_ap[:, kc:kc + 1])

        def xupdate(c, l, t0, n, lhs_fn, rhs_list, g_off, reads, evac=None):
            for oc in range(8):
                pb = psA.next()
                nj = len(rhs_list)
                for j in range(nj):
                    mm(PS[pb].ap[:, 0:n], lhs_fn(j, oc), rhs_list[j], j == 0, j == nj - 1, reads, PS[pb], fresh=(j == 0))
                gcol = modv.ap[:, l, g_off + oc, c.cond:c.cond + 1]
                if evac is None:
                    stt(xT.ap[:, oc, t0:t0 + n], PS[pb].ap[:, 0:n], gcol,
                        xT.ap[:, oc, t0:t0 + n], ALU.mult, ALU.add, [PS[pb], modv, xT], [xT])
                else:
                    ev = evac[oc % len(evac)]
                    act(ev.ap[:, 0:n], PS[pb].ap[:, 0:n], AF.Copy, [PS[pb], modv], [ev], scale=gcol)
                    tt(xT.ap[:, oc, t0:t0 + n], xT.ap[:, oc, t0:t0 + n], ev.ap[:, 0:n], ALU.add, [xT, ev], [xT], eng="pool")

        def layer_pass(c, l):
            T, TT = c.T, c.TT
            cd = c.cond
            for i, (og, osc) in enumerate([(O_GMIX, 8), (O_GFFN, 32)]):
                stt(gsc.ap[:, i, :], modv.ap[:, l, osc:osc + 8, cd], 1.0, prm.ap[:, l, og:og + 8], ALU.add, ALU.mult,
                    [modv, prm], [gsc])
            sh1 = modv.ap[:, l, 0:8, cd]
            sh2 = modv.ap[:, l, 24:32, cd]
            dma("sp", gkv_b.ap, d_gkv[l].partition_broadcast(128), writes=[gkv_b])
            dma("pool", wuq.ap[:, :, 0:768], d_wuq[l].rearrange("(k p) c -> p k c", p=128), writes=[wuq])
            dma("pool", wukv.ap, d_wukv[l], writes=[wukv])
            for kc in range(2):
                ts(wuq.ap[:, kc, 0:768], wuq.ap[:, kc, 0:768], prm.ap[:, l, O_GQ + kc:O_GQ + kc + 1], None, ALU.mult, None,
                   [wuq, prm], [wuq])
                src = wuq.ap[:, kc, 0:768].rearrange("p (h c) -> p h c", h=8)[:, :, 64:96].rearrange("p h (a f) -> p h a f", a=2)
                dst = wuq.ap[:, kc, 768:1024].rearrange("p (h a f) -> p h a f", h=8, a=2)
                for a in range(2):
                    ts(dst[:, :, a, 0:8], src[:, :, a, 8:16], -1.0, None, ALU.mult, None, [wuq], [wuq])
                    cp(dst[:, :, a, 8:16], src[:, :, a, 0:8], [wuq], [wuq])

            wl = d_win[l]

            def wsl(c0, c1):
                return wl[:, c0:c1].rearrange("(k p) c -> p k c", p=128)

            v_cm = load_piece([(lambda a: r3(a, 8), wsl(1192, 1704))])
            v_ssd = load_piece([(lambda a: r3(a, 8), wsl(672, 1184))])
            v_misc = load_piece([(lambda a: r3(a, 8)[:, :, 0:256], wsl(416, 672)),
                                 (lambda a: r3(a, 8)[:, :, 256:416], wsl(256, 416)),
                                 (lambda a: r3(a, 8)[:, :, 448:456], wsl(1184, 1192))])
            v_q = load_piece([(lambda a: r3(a, 8)[:, :, 0:256], wsl(0, 256))])
            w_cm, w_ssd, w_misc, w_q = r3(v_cm.ap, 8), r3(v_ssd.ap, 8), r3(v_misc.ap, 8), r3(v_q.ap, 8)
            for kc in range(8):
                src = w_misc[:, kc, 384:416].rearrange("p (a f) -> p a f", a=2)
                dst = w_misc[:, kc, 416:448].rearrange("p (a f) -> p a f", a=2)
                ts(dst[:, :, 0:8], src[:, :, 8:16], -1.0, None, ALU.mult, None, [v_misc], [v_misc])
                cp(dst[:, :, 8:16], src[:, :, 0:8], [v_misc], [v_misc])

            sp = Bump([(SC0, SBYTES)])
            xg2 = alloc(sp, "xg2", [128, 8, 512], BF16)
            xgs = [xg0, xg2]
            sq = [alloc(sp, f"sq{i}", [128, 512], BF16) for i in range(2)]
            sd = alloc(sp, "sd", [128, 512], F32)
            rstd = alloc(sp, "rstd", [128, 512], F32)
            tmpb = [alloc(sp, f"tmpb{i}", [128, 512], F32) for i in range(2)]
            sig = alloc(sp, "sig", [128, 512], F32)
            sqk = alloc(sp, "sqk", [128, 512], BF16)
            t1 = alloc(sp, "t1", [128, 512], F32)
            t2 = alloc(sp, "t2", [128, 512], F32)
            rq = alloc(sp, "rq", [128, 512], F32)
            tmo = alloc(sp, "tmo", [128, 160], F32)
            tmo2 = alloc(sp, "tmo2", [128, 128], F32)
            ssq = alloc(sp, "ssq", [128, 2], F32)
            junk = alloc(sp, "junk", [128, 256], F32)
            koff = PAST if c.ctx else 0

            for si, (s0, sl) in enumerate(c.seqs):
                g0 = s0 + si * 32
                memset(gpad.ap[:, :, g0:g0 + 16], 0.0, [gpad])
                memset(gpad.ap[:, :, g0 + 16 + sl:g0 + 32 + sl], 0.0, [gpad])
                x0 = s0 + si * 4
                memset(xbcp.ap[:, :, x0:x0 + 2], 0.0, [xbcp])
                memset(xbcp.ap[:, :, x0 + 2 + sl:x0 + 4 + sl], 0.0, [xbcp])

            if c.ctx:
                cst_ = alloc(sp, "cstg", [128, 128], F32)
                kst_ = alloc(sp, "kstg", [128, 96], F32)
                memset(kst_.ap, 0.0, [kst_])
                for b in range(2):
                    dma("sp", cst_.ap, d_cckv[l, b * 128:(b + 1) * 128, :], writes=[cst_])
                    pb = psA.next()
                    tp(PS[pb].ap[:, 0:128], cst_.ap, identf.ap, [cst_, cstf], PS[pb], fresh=True)
                    cp(ckvnT.ap[:, b * 128:(b + 1) * 128], PS[pb].ap[:, 0:128], [PS[pb]], [ckvnT])
                    dma("sp", kst_.ap[:, 64:96], d_ckr[l, b * 128:(b + 1) * 128, :], writes=[kst_])
                    pb = psA.next()
                    tp(PS[pb].ap[0:96, 0:128], kst_.ap, identf.ap, [kst_, cstf], PS[pb], fresh=True)
                    cp(krT.ap[64:96, b * 128:(b + 1) * 128], PS[pb].ap[64:96, 0:128], [PS[pb]], [krT])

            cur = [xg0]

            def fm(wap, c0, m, n, reads, out_rows=None):
                xg = cur[0]
                pb = psA.next()
                o = PS[pb].ap[0:m, 0:n] if out_rows is None else PS[pb].ap[out_rows[0]:out_rows[1], 0:n]
                for kc in range(8):
                    mm(o, wap[:, kc, c0:c0 + m], xg.ap[:, kc, 0:n], kc == 0, kc == 7, reads + [xg], PS[pb], fresh=(kc == 0))
                return pb

            def do_norm(ti):
                si_, t0_ = c.tiles[ti]
                xv = xgs[ti % 2]
                norm_tile(c, (sq, sd, rstd, tmpb), t0_, TT, gsc.ap[:, 0, :], sh1, lambda kc: xv.ap[:, kc, 0:TT], xv, sq_dve=True)

            do_norm(0)
            for ti, (si, t0) in enumerate(c.tiles):
                n = TT
                s0, sl = c.seqs[si]
                if ti + 1 < len(c.tiles):
                    do_norm(ti + 1)
                xg = xgs[ti % 2]
                cur[0] = xg
                gofs = si * 32 + 16 + t0
                for j in range(2):
                    pa = fm(w_cm, j * 128, 128, n, [v_cm])
                    pbk = fm(w_cm, 256 + j * 128, 128, n, [v_cm])
                    act(sig.ap[:, 0:n], PS[pbk].ap[:, 0:n], AF.Sigmoid, [PS[pbk]], [sig])
                    tt(gpad.ap[:, j, gofs:gofs + n], PS[pa].ap[:, 0:n], sig.ap[:, 0:n], ALU.mult, [PS[pa], sig], [gpad])
                xofs = si * 4 + 2 + t0
                for j in range(4):
                    pa = fm(w_ssd, j * 128, 128, n, [v_ssd])
                    act(xbcp.ap[:, j, xofs:xofs + n], PS[pa].ap[:, 0:n], AF.Copy, [PS[pa]], [xbcp])
                for b in range(n // 128):
                    blk = (t0 + b * 128) // 128
                    pb = psA.next()
                    for kc in range(8):
                        mm(PS[pb].ap[:, 0:256], xg.ap[:, kc, b * 128:(b + 1) * 128], w_misc[:, kc, 0:256], kc == 0, kc == 7,
                           [xg, v_misc], PS[pb], fresh=(kc == 0))
                    act(szb.ap[:, blk, :], PS[pb].ap[:, 0:256], AF.Silu, [PS[pb]], [szb])
                    pb = psA.next()
                    for kc in range(8):
                        mm(PS[pb].ap[:, 0:8], xg.ap[:, kc, b * 128:(b + 1) * 128], w_misc[:, kc, 448:456], kc == 0, kc == 7,
                           [xg, v_misc], PS[pb], fresh=(kc == 0))
                    tt(dtr.ap[:, blk, :], PS[pb].ap[:, 0:8], dtb_b.ap[:, l * 8:(l + 1) * 8], ALU.add, [PS[pb], dtb_b], [dtr])
                    if c.kind == "P":
                        pb = psA.next()
                        for kc in range(8):
                            mm(PS[pb].ap[:, 0:160], xg.ap[:, kc, b * 128:(b + 1) * 128], w_misc[:, kc, 256:416], kc == 0, kc == 7,
                               [xg, v_misc], PS[pb], fresh=(kc == 0))
                        cp(tmo.ap, PS[pb].ap[:, 0:160], [PS[pb]], [tmo])
                        tloc = t0 - s0 + b * 128
                        dma("sp", o_kr[si, l, tloc:tloc + 128, :], tmo.ap[:, 128:160], reads=[tmo])
                        memset(ssq.ap[:, 0:1], 0.0, [ssq])
                        act(junk.ap[:, 0:128], tmo.ap[:, 0:128], AF.Square, [tmo, ssq], [junk, ssq], accum_out=ssq.ap[:, 0:1])
                        act(ssq.ap[:, 1:2], ssq.ap[:, 0:1], AF.Ln, [ssq], [ssq], scale=1.0 / 128, bias=1e-6)
                        act(ssq.ap[:, 1:2], ssq.ap[:, 1:2], AF.Exp, [ssq], [ssq], scale=-0.5)
                        stt(tmo2.ap, tmo.ap[:, 0:128], ssq.ap[:, 1:2], gkv_b.ap, ALU.mult, ALU.mult, [tmo, ssq, gkv_b], [tmo2])
                        dma("sp", o_ckv[si, l, tloc:tloc + 128, :], tmo2.ap, reads=[tmo2])
                pa = fm(w_misc, 256, 128, n, [v_misc])
                act(sqk.ap[:, 0:n], PS[pa].ap[:, 0:n], AF.Square, [PS[pa]], [sqk])
                mm(PS[6].ap[:, 0:n], onesb.ap, sqk.ap[:, 0:n], True, True, [onesb, sqk], PS[6], fresh=True)
                act(sd.ap[:, 0:n], PS[6].ap[:, 0:n], AF.Ln, [PS[6]], [sd], scale=1.0 / 128, bias=1e-6)
                act(rstd.ap[:, 0:n], sd.ap[:, 0:n], AF.Exp, [sd], [rstd], scale=-0.5)
                kofs = koff + t0 if c.ctx else t0
                stt(ckvnT.ap[:, kofs:kofs + n], PS[pa].ap[:, 0:n], prm.ap[:, l, O_GKV:O_GKV + 1], rstd.ap[:, 0:n],
                    ALU.mult, ALU.mult, [PS[pa], prm, rstd], [ckvnT])
                pa = fm(w_misc, 384, 32, n, [v_misc], out_rows=(64, 96))
                if c.rope:
                    pbk = fm(w_misc, 416, 32, n, [v_misc], out_rows=(64, 96))
                    tt(t1.ap[64:96, 0:n], PS[pa].ap[64:96, 0:n], ropeC.ap[64:96, t0:t0 + n], ALU.mult, [PS[pa], ropeC], [t1])
                    tt(t2.ap[64:96, 0:n], PS[pbk].ap[64:96, 0:n], ropeS.ap[64:96, t0:t0 + n], ALU.mult, [PS[pbk], ropeS], [t2])
                    tt(krT.ap[64:96, kofs:kofs + n], t1.ap[64:96, 0:n], t2.ap[64:96, 0:n], ALU.add, [t1, t2], [krT])
                else:
                    act(krT.ap[64:96, kofs:kofs + n], PS[pa].ap[64:96, 0:n], AF.Copy, [PS[pa]], [krT])
                pq = [fm(w_q, j * 128, 128, n, [v_q]) for j in range(2)]
                for j in range(2):
                    act(sq[j].ap[:, 0:n], PS[pq[j]].ap[:, 0:n], AF.Square, [PS[pq[j]]], [sq[j]])
                    mm(PS[6].ap[:, 0:n], onesb.ap, sq[j].ap[:, 0:n], j == 0, j == 1, [onesb, sq[j]], PS[6], fresh=(j == 0))
                act(sd.ap[:, 0:n], PS[6].ap[:, 0:n], AF.Ln, [PS[6]], [sd], scale=1.0 / 256, bias=1e-6)
                act(rq.ap[:, 0:n], sd.ap[:, 0:n], AF.Exp, [sd], [rq], scale=-0.5)
                for j in range(2):
                    tt(qlat.ap[:, j, t0:t0 + n], PS[pq[j]].ap[:, 0:n], rq.ap[:, 0:n], ALU.mult, [PS[pq[j]], rq], [qlat])

            if stop == "I":
                return
            v_wor = load_piece([(lambda a: r3(a, 4), d_wout[l, 512:1024, :].rearrange("(k p) c -> p k c", p=128))])
            w_or = r3(v_wor.ap, 4)
            for j in range(2):
                ts(w_or[:, j, :], w_or[:, j, :], prm.ap[:, l, O_SNG + j:O_SNG + j + 1], None, ALU.mult, None, [v_wor, prm], [v_wor])

            sp = Bump([(SC0, SBYTES), rg(xg0)])
            dg = alloc(sp, "dg", [128, 2, 31, 128], BF16)
            cvf = [alloc(sp, f"cvf{j}", [128, 512], F32) for j in range(2)]
            sqf = [alloc(sp, f"sqf{j}", [128, 512], F32) for j in range(2)]
            mean = alloc(sp, "mean", [128, 512], F32)
            var = alloc(sp, "var", [128, 512], F32)
            rr = alloc(sp, "rr", [128, 512], F32)
            uu = alloc(sp, "uu", [128, 512], F32)
            cmix = alloc(sp, "cmix", [128, 2, 512], BF16)
            for j in range(2):
                for tap in range(31):
                    col = O_CCW + tap * 2 + j
                    ts(dg.ap[:, j, tap, :], identb.ap, prm.ap[:, l, col:col + 1], None, ALU.mult, None, [identb, prm], [dg])
            for (si, t0) in c.tiles:
                n = TT
                gofs = si * 32 + 1 + t0
                for j in range(2):
                    pb = psBk.next()
                    for tap in range(31):
                        mm(PS[pb].ap[:, 0:n], dg.ap[:, j, tap, :], gpad.ap[:, j, gofs + tap:gofs + tap + n], tap == 0, tap == 30,
                           [dg, gpad], PS[pb], fresh=(tap == 0))
                    bcol = prm.ap[:, l, O_CCB + j:O_CCB + j + 1]
                    act(cvf[j].ap[:, 0:n], PS[pb].ap[:, 0:n], AF.Identity, [PS[pb], prm], [cvf[j]], bias=bcol)
                    act(sqf[j].ap[:, 0:n], PS[pb].ap[:, 0:n], AF.Square, [PS[pb], prm], [sqf[j]], bias=bcol)
                for j in range(2):
                    mm(PS[6].ap[:, 0:n], onesf.ap, cvf[j].ap[:, 0:n], j == 0, j == 1, [onesf, cvf[j]], PS[6], fresh=(j == 0))
                for j in range(2):
                    mm(PS[7].ap[:, 0:n], onesf.ap, sqf[j].ap[:, 0:n], j == 0, j == 1, [onesf, sqf[j]], PS[7], fresh=(j == 0))
                ts(mean.ap[:, 0:n], PS[6].ap[:, 0:n], 1.0 / 256, None, ALU.mult, None, [PS[6]], [mean])
                tt(var.ap[:, 0:n], mean.ap[:, 0:n], mean.ap[:, 0:n], ALU.mult, [mean], [var])
                stt(var.ap[:, 0:n], PS[7].ap[:, 0:n], 1.0 / 256, var.ap[:, 0:n], ALU.mult, ALU.subtract, [PS[7], var], [var])
                act(var.ap[:, 0:n], var.ap[:, 0:n], AF.Ln, [var], [var], bias=1e-5)
                act(rr.ap[:, 0:n], var.ap[:, 0:n], AF.Exp, [var], [rr], scale=-0.5)
                for j in range(2):
                    tt(uu.ap[:, 0:n], cvf[j].ap[:, 0:n], mean.ap[:, 0:n], ALU.subtract, [cvf[j], mean], [uu])
                    tt(uu.ap[:, 0:n], uu.ap[:, 0:n], rr.ap[:, 0:n], ALU.mult, [uu, rr], [uu])
                    act(cmix.ap[:, j, 0:n], uu.ap[:, 0:n], AF.Silu, [uu, prm], [cmix],
                        scale=prm.ap[:, l, O_CLG + j:O_CLG + j + 1], bias=prm.ap[:, l, O_CLB + j:O_CLB + j + 1])
                xupdate(c, l, t0, n, lambda j, oc: w_or[:, 2 + j, oc * 128:(oc + 1) * 128],
                        [cmix.ap[:, 0, 0:n], cmix.ap[:, 1, 0:n]], 16, [v_wor, cmix])

            if stop == "C":
                return
            ssd_phase(c, l, w_or, v_wor)
            if stop in ("S", "S1", "S2", "S3"):
                return
            v_woa = load_piece([(lambda a: r3(a, 4), d_wout[l, 0:512, :].rearrange("(k p) c -> p k c", p=128))])
            mla_phase(c, l, r3(v_woa.ap, 4), v_woa)
            if stop == "M":
                return
            ffn_phase(c, l)

        def ssd_phase(c, l, w_or, v_wor):
            T, TT = c.T, c.TT
            nblk = c.nblk
            g_r, x_r, z_r = rg(gpad), rg(xbcp), rg(szb)
            sp = Bump([(SC0, SBYTES), g_r])
            dg5 = alloc(sp, "dg5", [128, 4, 5, 128], BF16)
            xsT = alloc(sp, "xsT", [128, 2, 512], BF16)
            BCt = alloc(sp, "BCt", [128, 3, TS], BF16)
            xs_tm = alloc(sp, "xs_tm", [128, 16, 256], BF16)
            B_tm = alloc(sp, "B_tm", [128, 16, 128], BF16)
            dt = alloc(sp, "dt", [128, 16, 8], F32)
            dta = alloc(sp, "dta", [128, 16, 8], F32)
            hTf = [alloc(sp, f"hTf{d}", [128, 2, 64], F32) for d in range(2)]
            hTb = alloc(sp, "hTb", [128, 2, 64], BF16)
            sstg = alloc(sp, "sstg", [128, 128], F32)
            sp2 = Bump([tuple(r) for r in sp.ranges] + [x_r, rg(xg)])
            nb8 = nblk * 8
            dtf = dtr.ap.rearrange("p b e -> p (b e)")[:, 0:nb8]
            act(dt.ap.rearrange("p b e -> p (b e)")[:, 0:nb8], dtf, AF.Exp, [dtr], [dt])
            act(dt.ap.rearrange("p b e -> p (b e)")[:, 0:nb8], dt.ap.rearrange("p b e -> p (b e)")[:, 0:nb8], AF.Ln, [dt], [dt], bias=1.0)
            tt(dta.ap[:, 0:nblk, :], dt.ap[:, 0:nblk, :], a_b.ap[:, l * 8:(l + 1) * 8].unsqueeze(1).to_broadcast([128, nblk, 8]),
               ALU.mult, [dt, a_b], [dta])
            memset(BCt.ap[64:128, 1, 0:T], 0.0, [BCt])
            memset(BCt.ap[0:64, 2, 0:T], 0.0, [BCt])
            for ch in range(4):
                for tap in range(5):
                    col = O_SCW + tap * 4 + ch
                    ts(dg5.ap[:, ch, tap, :], identb.ap, prm.ap[:, l, col:col + 1], None, ALU.mult, None, [identb, prm], [dg5])
            for (si, t0) in c.tiles:
                n = TT
                xofs = si * 4 + t0
                for ch in range(4):
                    pb = psBk.next()
                    for tap in range(5):
                        mm(PS[pb].ap[:, 0:n], dg5.ap[:, ch, tap, :], xbcp.ap[:, ch, xofs + tap:xofs + tap + n], tap == 0, tap == 4,
                           [dg5, xbcp], PS[pb], fresh=(tap == 0))
                    bias_ = prm.ap[:, l, O_SCB + ch:O_SCB + ch + 1]
                    if ch < 2:
                        act(xsT.ap[:, ch, 0:n], PS[pb].ap[:, 0:n], AF.Silu, [PS[pb], prm], [xsT], bias=bias_)
                    elif ch == 2:
                        act(BCt.ap[:, 0, t0:t0 + n], PS[pb].ap[:, 0:n], AF.Silu, [PS[pb], prm], [BCt], bias=bias_)
                    else:
                        act(BCt.ap[0:64, 1, t0:t0 + n], PS[pb].ap[0:64, 0:n], AF.Silu, [PS[pb], prm], [BCt], bias=bias_[0:64])
                        act(BCt.ap[64:128, 2, t0:t0 + n], PS[pb].ap[64:128, 0:n], AF.Silu, [PS[pb], prm], [BCt], bias=bias_[64:128])
                for b in range(n // 128):
                    blk = (t0 + b * 128) // 128
                    pb = psA.next()
                    for j in range(2):
                        tp(PSB[pb].ap[:, j * 128:(j + 1) * 128], xsT.ap[:, j, b * 128:(b + 1) * 128], identb.ap, [xsT, identb], PS[pb], fresh=(j == 0))
                    tp(PSB[pb].ap[:, 256:384], BCt.ap[:, 0, t0 + b * 128:t0 + (b + 1) * 128], identb.ap, [BCt, identb], PS[pb])
                    cp(xs_tm.ap[:, blk, :], PSB[pb].ap[:, 0:256], [PS[pb]], [xs_tm])
                    cp(B_tm.ap[:, blk, :], PSB[pb].ap[:, 256:384], [PS[pb]], [B_tm])
            if stop == "S1":
                return
            hst = alloc(sp2, "hst", [128, 16, 2, 64], BF16)
            Gm = [alloc(sp2, f"Gm{d}", [128, 2, 128], F32) for d in range(2)]
            Lm = alloc(sp2, "Lm", [128, 4, 128], F32)
            seg = alloc(sp2, "seg", [128, 4, 128], F32)
            MT = [[alloc(sp2, f"MT{i}{d}", [128, 4, 128], BF16) for d in range(2)] for i in range(2)]
            xdt = [[alloc(sp2, f"xdt{i}{d}", [128, 4, 64], BF16) for d in range(2)] for i in range(2)]
            ee = [alloc(sp2, f"ee{i}", [128, 32], F32) for i in range(2)]
            cds = [alloc(sp2, f"cds{i}", [128, 2, 2], F32) for i in range(2)]
            xdd = [alloc(sp2, f"xdd{i}", [128, 4, 64], BF16) for i in range(2)]
            wv = alloc(sp2, "wv", [128, 4], F32)
            yo = alloc(sp2, "yo", [128, 8, 64], F32)
            y1 = alloc(sp2, "y1", [128, 256], F32)
            y2 = alloc(sp2, "y2", [128, 256], F32)
            y3 = alloc(sp2, "y3", [128, 256], F32)
            yn = [alloc(sp2, f"yn{i}", [128, 256], BF16) for i in range(2)]
            ssq = alloc(sp2, "ssq2", [128, 2], F32)
            smix = alloc(sp2, "smix", [128, 2, 512], BF16)
            junk = yo
            evb = [View(Lm.name, Lm.ap.rearrange("p h s -> p (h s)")), View(seg.name, seg.ap.rearrange("p h s -> p (h s)"))]

            def small_mm(blk, dirs, ee_, cds_):
                if len(dirs) == 2:
                    A = dta.ap[:, blk, 0:8]
                    for k, (lt, lv) in enumerate([(SU[0].ap, cstf), (SU[1].ap, cstf), (TRI[0].ap, cstf), (TRI[1].ap, cstf), (onesf.ap, onesf)]):
                        mm(PS[6].ap[:, k * 8:k * 8 + 8], lt, A, True, True, [lv, dta], PS[6], fresh=(k == 0))
                    act(ee_.ap[:, 0:32], PS[6].ap[:, 0:32], AF.Exp, [PS[6]], [ee_])
                else:
                    A = dta.ap[:, blk, 4:8]
                    mm(PS[6].ap[:, 12:16], SU[1].ap, A, True, True, [cstf, dta], PS[6], fresh=True)
                    mm(PS[6].ap[:, 36:40], onesf.ap, A, True, True, [onesf, dta], PS[6])
                    act(ee_.ap[:, 12:16], PS[6].ap[:, 12:16], AF.Exp, [PS[6]], [ee_])
                for d in dirs:
                    act(cds_.ap[0:64, d, :], PS[6].ap[0:64, 32 + d * 4:34 + d * 4], AF.Exp, [PS[6]], [cds_])
                    act(cds_.ap[64:128, d, :], PS[6].ap[64:128, 34 + d * 4:36 + d * 4], AF.Exp, [PS[6]], [cds_])

            DEC = [slice(0, 4), slice(12, 16)]

            def state_pre(blk, d, ee_, xdd_):
                tt(wv.ap, dt.ap[:, blk, d * 4:(d + 1) * 4], ee_.ap[:, DEC[d]], ALU.mult, [dt, ee_], [wv])
                tt(xdd_.ap, xs_tm.ap[:, blk, :].rearrange("p (h q) -> p h q", h=4), wv.ap.unsqueeze(2).to_broadcast([128, 4, 64]),
                   ALU.mult, [xs_tm, wv], [xdd_])

            def state_post(blk, d, cds_, xdd_):
                pb = 7
                for h in range(4):
                    g, j = h // 2, h % 2
                    mm(PS[pb].ap[g * 64:(g + 1) * 64, j * 64:(j + 1) * 64], B_tm.ap[:, blk, g * 64:(g + 1) * 64], xdd_.ap[:, h, :],
                       True, True, [B_tm, xdd_], PS[pb], fresh=(h == 0))
                for j in range(2):
                    stt(hTf[d].ap[:, j, :], hTf[d].ap[:, j, :], cds_.ap[:, d, j:j + 1], PS[pb].ap[:, j * 64:(j + 1) * 64],
                        ALU.mult, ALU.add, [hTf[d], cds_, PS[pb]], [hTf[d]])

            def main_pre(blk, i):
                tk = slice(blk * 128, (blk + 1) * 128)
                pg = psA.next()
                mm(PS[pg].ap[:, 0:256], BCt.ap[:, 0, tk], BCt.ap[:, 1:3, tk], True, True, [BCt], PS[pg], fresh=True)
                small_mm(blk, [0, 1], ee[i], cds[i])
                for d in range(2):
                    tt(Gm[d].ap, PS[pg].ap[:, 0:256].rearrange("p (g s) -> p g s", g=2),
                       TRI[d].ap.unsqueeze(1).to_broadcast([128, 2, 128]), ALU.mult, [PS[pg], cstf], [Gm[d]])
                for d in range(2):
                    tt(Lm.ap, SU[d].ap.unsqueeze(1).to_broadcast([128, 4, 128]),
                       dta.ap[:, blk, d * 4:(d + 1) * 4].unsqueeze(2).to_broadcast([128, 4, 128]), ALU.mult, [cstf, dta], [Lm], eng="pool")
                    pdf = psA.next()
                    for h in range(4):
                        mm(PS[pdf].ap[:, h * 128:(h + 1) * 128], Lm.ap[:, h, :], TRI[d].ap, True, True, [Lm, cstf], PS[pdf], fresh=(h == 0))
                    act(seg.ap.rearrange("p h s -> p (h s)"), PS[pdf].ap, AF.Exp, [PS[pdf]], [seg])
                    for g in range(2):
                        tt(MT[i][d].ap[:, 2 * g:2 * g + 2, :], seg.ap[:, 2 * g:2 * g + 2, :],
                           Gm[d].ap[:, g:g + 1, :].to_broadcast([128, 2, 128]), ALU.mult, [seg, Gm[d]], [MT[i][d]])
                    tt(xdt[i][d].ap, xs_tm.ap[:, blk, :].rearrange("p (h q) -> p h q", h=4),
                       dt.ap[:, blk, d * 4:(d + 1) * 4].unsqueeze(2).to_broadcast([128, 4, 64]), ALU.mult, [xs_tm, dt], [xdt[i][d]], eng="pool")
                state_pre(blk, 0, ee[i], xdd[i])

            def main_post(blk, i):
                tk = slice(blk * 128, (blk + 1) * 128)
                tt(y3.ap.rearrange("p (h q) -> p h q", h=4), xs_tm.ap[:, blk, :].rearrange("p (h q) -> p h q", h=4),
                   dsum_b.ap[:, l, :].unsqueeze(2).to_broadcast([128, 4, 64]), ALU.mult, [xs_tm, dsum_b], [y3], eng="pool")
                py = psBk.next()
                for h in range(4):
                    for d in range(2):
                        mm(PS[py].ap[:, h * 64:(h + 1) * 64], MT[i][d].ap[:, h, :], xdt[i][d].ap[:, h, :], d == 0, d == 1,
                           [MT[i][d], xdt[i][d]], PS[py], fresh=(h == 0 and d == 0))
                po = psBk.next()
                for d in range(2):
                    hsrc_v = hTb if d == 0 else hst
                    hsrc = hTb.ap if d == 0 else hst.ap[:, blk]
                    for g in range(2):
                        mm(PS[po].ap[:, d * 256 + g * 128:d * 256 + (g + 1) * 128], BCt.ap[:, 1 + g, tk], hsrc,
                           True, True, [BCt, hsrc_v], PS[po], fresh=(d == 0 and g == 0))
                state_post(blk, 0, cds[i], xdd[i])
                cp(hTb.ap, hTf[0].ap, [hTf[0]], [hTb], eng="pool")
                for d in range(2):
                    e0 = 16 + 12 * d
                    tt(yo.ap[:, 4 * d:4 * d + 4, :], PS[po].ap[:, d * 256:(d + 1) * 256].rearrange("p (e q) -> p e q", e=4),
                       ee[i].ap[:, e0:e0 + 4].unsqueeze(2).to_broadcast([128, 4, 64]), ALU.mult, [PS[po], ee[i]], [yo])
                yof = yo.ap.rearrange("p e q -> p (e q)")
                tt(y1.ap, yof[:, 0:256], yof[:, 256:512], ALU.add, [yo], [y1])
                tt(y1.ap, y1.ap, PS[py].ap[:, 0:256], ALU.add, [y1, PS[py]], [y1])
                tt(y1.ap, y1.ap, y3.ap, ALU.add, [y1, y3], [y1])
                tt(y2.ap, y1.ap, szb.ap[:, blk, :], ALU.mult, [y1, szb], [y2])
                memset(ssq.ap[:, 0:1], 0.0, [ssq])
                act(junk.ap.rearrange("p e q -> p (e q)")[:, 0:256], y2.ap, AF.Square, [y2, ssq], [junk, ssq], accum_out=ssq.ap[:, 0:1])
                act(ssq.ap[:, 1:2], ssq.ap[:, 0:1], AF.Ln, [ssq], [ssq], scale=1.0 / 256, bias=1e-6)
                act(ssq.ap[:, 1:2], ssq.ap[:, 1:2], AF.Exp, [ssq], [ssq], scale=-0.5)
                ts(yn[i].ap, y2.ap, ssq.ap[:, 1:2], None, ALU.mult, None, [y2, ssq], [yn[i]])

            def main_tail(blk, i):
                bt = (blk * 128) % TT
                pt = psA.next()
                for j in range(2):
                    tp(PSB[pt].ap[:, j * 128:(j + 1) * 128], yn[i].ap[:, j * 128:(j + 1) * 128], identb.ap, [yn[i], identb], PS[pt], fresh=(j == 0))
                cp(smix.ap[:, :, bt:bt + 128], PSB[pt].ap[:, 0:256].rearrange("p (j t) -> p j t", j=2), [PS[pt]], [smix])
                if bt + 128 == TT:
                    t0 = blk * 128 + 128 - TT
                    xupdate(c, l, t0, TT, lambda j, oc: w_or[:, j, oc * 128:(oc + 1) * 128],
                            [smix.ap[:, 0, 0:TT], smix.ap[:, 1, 0:TT]], 16, [v_wor, smix], evac=evb)

            for si, (s0, sl) in enumerate(c.seqs):
                blks = list(range(s0 // 128, (s0 + sl) // 128))
                for d in range(2):
                    if c.ctx:
                        dma("sp", sstg.ap.rearrange("p (g n) -> p g n", g=2),
                            d_st[l, d].rearrange("(g j) p n -> (j p) g n", g=2), writes=[sstg])
                        pb = psA.next()
                        tp(PS[pb].ap[:, 0:128], sstg.ap, identf.ap, [sstg, cstf], PS[pb], fresh=True)
                        cp(hTf[d].ap.rearrange("p j q -> p (j q)"), PS[pb].ap[:, 0:128], [PS[pb]], [hTf[d]])
                    else:
                        memset(hTf[d].ap, 0.0, [hTf[d]])
                rb_ = list(reversed(blks))
                small_mm(rb_[0], [1], ee[0], cds[0])
                state_pre(rb_[0], 1, ee[0], xdd[0])
                for k, blk in enumerate(rb_):
                    i = k % 2
                    if k + 1 < len(rb_):
                        small_mm(rb_[k + 1], [1], ee[1 - i], cds[1 - i])
                        state_pre(rb_[k + 1], 1, ee[1 - i], xdd[1 - i])
                    cp(hst.ap[:, blk], hTf[1].ap, [hTf[1]], [hst], eng="pool")
                    state_post(blk, 1, cds[i], xdd[i])
                if stop == "S2":
                    return
                cp(hTb.ap, hTf[0].ap, [hTf[0]], [hTb], eng="pool")
                main_pre(blks[0], 0)
                for k, blk in enumerate(blks):
                    if k + 1 < len(blks):
                        main_pre(blks[k + 1], (k + 1) % 2)
                    main_post(blk, k % 2)
                    if k >= 1:
                        main_tail(blks[k - 1], (k - 1) % 2)
                main_tail(blks[-1], (len(blks) - 1) % 2)
                if stop == "S3":
                    return
                if c.kind == "P":
                    for d in range(2):
                        pb = psA.next()
                        tp(PS[pb].ap[:, 0:128], hTf[d].ap.rearrange("p j q -> p (j q)"), identf.ap, [hTf[d], cstf], PS[pb], fresh=True)
                        cp(sstg.ap, PS[pb].ap[:, 0:128], [PS[pb]], [sstg])
                        dma("sp", o_ssd[si, l, d].rearrange("(g j) p n -> (j p) g n", g=2),
                            sstg.ap.rearrange("p (g n) -> p g n", g=2), reads=[sstg])

        def mla_phase(c, l, w_oa, v_woa):
            T, TT = c.T, c.TT
            g_r, x_r, z_r = rg(gpad), rg(xbcp), rg(szb)
            attnT = alloc(Bump([x_r]), "attnT", [128, 4, TS], BF16)
            sp = Bump([(SC0, SBYTES), g_r, z_r])
            NKB = (PAST + TS) // 128
            KT = [alloc(sp, f"KT{i}", [128, PAST + TS], BF16) for i in range(2)]
            VB = [(alloc(sp, f"Ve{i}", [128, NKB, 128], BF16), alloc(sp, f"Vo{i}", [128, NKB, 128], BF16)) for i in range(2)]
            for i in range(2):
                memset(VB[i][0].ap[:, :, 64:128], 1.0, [VB[i][0]])
                memset(VB[i][1].ap[:, :, 0:64], 1.0, [VB[i][1]])
            QT = [alloc(sp, f"QT{i}", [128, TS], BF16) for i in range(2)]
            PT = [alloc(sp, f"PT{i}", [128, 512], BF16) for i in range(3)]
            t1 = alloc(sp, "mt1", [128, 512], F32)
            t2 = alloc(sp, "mt2", [128, 512], F32)
            rbs = Bump([(sp.take(2048),) * 2])
            _b0 = rbs.ranges[0][0]
            rbs = None
            def half_view(nm, b0, p0):
                ap = S[p0:p0 + 64, b0:b0 + 2048].bitcast(F32)
                uid[0] += 1
                n2 = f"{nm}.{uid[0]}"
                P.region(n2, "sbuf", p0, p0 + 64, b0, b0 + 2048)
                return View(n2, ap)
            _b1 = sp.take(2048)
            rb_src = {0: half_view("rbsE", _b0, 64), 1: half_view("rbsO", _b0, 0)}
            rb_dst = {0: half_view("rbdE", _b1, 0), 1: half_view("rbdO", _b1, 64)}
            ptr = Rot(PT)
            LA = 2
            psS = Rot([0, 1, 2])
            PPr = Rot([3, 7])
            accb = Rot([4, 5, 6])

            heads = []
            for si, (s0, sl) in enumerate(c.seqs):
                for h in range(8):
                    heads.append((si, s0, sl, h))

            def prep(idx):
                si, s0, sl, h = heads[idx]
                hp, hh = h // 2, h % 2
                Tk = (PAST if c.ctx else 0) + sl
                k0 = 0 if c.ctx else s0
                nkb = Tk // 128
                qtiles = [t for (s_, t) in c.tiles if s_ == si]
                kt, qt = KT[idx % 2], QT[idx % 2]
                Ve, Vo = VB[(idx // 2) % 2]
                if hh == 0:
                    vcols = wukv.ap.rearrange("p (h c) -> p h c", h=8)[:, 2 * hp:2 * hp + 2, 64:128]
                    for kb in range(nkb):
                        PP = PPr.next()
                        mm(PS[PP].ap[:, 0:128], ckvnT.ap[:, k0 + kb * 128:k0 + (kb + 1) * 128], vcols, True, True, [ckvnT, wukv], PS[PP], fresh=True)
                        cp(Ve.ap[:, kb, 0:64], PS[PP].ap[:, 0:64], [PS[PP]], [Ve])
                        cp(Vo.ap[:, kb, 64:128], PS[PP].ap[:, 64:128], [PS[PP]], [Vo])
                        yield
                for k1 in range(0, Tk, 512):
                    kn = min(512, Tk - k1)
                    PP = PPr.next()
                    mm(PS[PP].ap[0:64, 0:kn], wukv.ap[:, h * 128:h * 128 + 64], ckvnT.ap[:, k0 + k1:k0 + k1 + kn], True, True,
                       [wukv, ckvnT], PS[PP], fresh=True)
                    cp(kt.ap[0:64, k1:k1 + kn], PS[PP].ap[0:64, 0:kn], [PS[PP]], [kt])
                    yield
                cp(kt.ap[64:96, 0:Tk], krT.ap[64:96, k0:k0 + Tk], [krT], [kt], eng="pool")
                for t0 in qtiles:
                    n = TT
                    tl = t0 - s0
                    PP = PPr.next()
                    for kc in range(2):
                        mm(PS[PP].ap[0:96, 0:n], wuq.ap[:, kc, h * 96:(h + 1) * 96], qlat.ap[:, kc, t0:t0 + n], kc == 0, kc == 1,
                           [wuq, qlat], PS[PP], fresh=(kc == 0))
                    if c.rope:
                        cp(qt.ap[0:64, tl:tl + n], PS[PP].ap[0:64, 0:n], [PS[PP]], [qt])
                        tt(t1.ap[64:96, 0:n], PS[PP].ap[64:96, 0:n], ropeC.ap[64:96, t0:t0 + n], ALU.mult, [PS[PP], ropeC], [t1])
                        yield
                        PP2 = PPr.next()
                        for kc in range(2):
                            mm(PS[PP2].ap[64:96, 0:n], wuq.ap[:, kc, 768 + h * 32:768 + (h + 1) * 32], qlat.ap[:, kc, t0:t0 + n],
                               kc == 0, kc == 1, [wuq, qlat], PS[PP2], fresh=(kc == 0))
                        tt(t2.ap[64:96, 0:n], PS[PP2].ap[64:96, 0:n], ropeS.ap[64:96, t0:t0 + n], ALU.mult, [PS[PP2], ropeS], [t2])
                        tt(qt.ap[64:96, tl:tl + n], t1.ap[64:96, 0:n], t2.ap[64:96, 0:n], ALU.add, [t1, t2], [qt])
                    else:
                        cp(qt.ap[0:96, tl:tl + n], PS[PP].ap[0:96, 0:n], [PS[PP]], [qt])
                    yield

            def attn_unit(idx, t0, prev_tail, filler):
                si, s0, sl, h = heads[idx]
                hp, hh = h // 2, h % 2
                nkb = ((PAST if c.ctx else 0) + sl) // 128
                kt, qt = KT[idx % 2], QT[idx % 2]
                vv = VB[(idx // 2) % 2][hh]
                tl = t0 - s0
                n = c.TT
                po = accb.next()
                r0 = 0 if hh == 0 else 64
                d0 = 64 - r0
                pts = []
                for kb in range(nkb + LA):
                    if kb < nkb:
                        pb = psS.next()
                        mm(PS[pb].ap[:, 0:n], kt.ap[0:96, kb * 128:(kb + 1) * 128], qt.ap[0:96, tl:tl + n], True, True,
                           [kt, qt], PS[pb], fresh=True)
                        pt_ = ptr.next()
                        pts.append(pt_)
                        act(pt_.ap[:, 0:n], PS[pb].ap[:, 0:n], AF.Exp, [PS[pb]], [pt_], scale=SCALE)
                    if kb == min(LA, nkb) - 1 and prev_tail is not None:
                        prev_tail()
                        prev_tail = None
                    if kb >= LA:
                        k2 = kb - LA
                        mm(PS[po].ap[:, 0:n], vv.ap[:, k2, :], pts[k2].ap[:, 0:n], k2 == 0, k2 == nkb - 1, [vv, pts[k2]], PS[po], fresh=(k2 == 0))
                    if filler is not None:
                        next(filler, None)

                def tail():
                    rs_, rd_ = rb_src[hh], rb_dst[hh]
                    recip(rs_.ap[:, 0:n], PS[po].ap[d0:d0 + 64, 0:n], [PS[po]], [rs_])
                    dma("sp", rd_.ap[:, 0:n], rs_.ap[:, 0:n], reads=[rs_], writes=[rd_])
                    tt(attnT.ap[r0:r0 + 64, hp, t0:t0 + n], PS[po].ap[r0:r0 + 64, 0:n], rd_.ap[:, 0:n], ALU.mult, [PS[po], rd_], [attnT])
                return tail

            for _ in prep(0):
                pass
            pend = None
            for idx in range(len(heads)):
                si = heads[idx][0]
                filler = prep(idx + 1) if idx + 1 < len(heads) else None
                for t0 in [t for (s_, t) in c.tiles if s_ == si]:
                    pend = attn_unit(idx, t0, pend, filler)
                if filler is not None:
                    for _ in filler:
                        pass
            if pend is not None:
                pend()
            for (si, t0) in c.tiles:
                xupdate(c, l, t0, TT, lambda j, oc: w_oa[:, j, oc * 128:(oc + 1) * 128],
                        [attnT.ap[:, j, t0:t0 + TT] for j in range(4)], 16, [v_woa, attnT])

        def ffn_phase(c, l):
            T, TT = c.T, c.TT
            sp = Bump([(max(SC0, rg(hT)[1]), SBYTES)])
            sq = [alloc(sp, f"fsq{i}", [128, 512], BF16) for i in range(2)]
            sd = alloc(sp, "fsd", [128, 512], F32)
            rstd = alloc(sp, "frstd", [128, 512], F32)
            tmpb = [alloc(sp, f"ftmpb{i}", [128, 512], F32) for i in range(2)]
            sg = [alloc(sp, f"sg{i}", [128, 512], F32) for i in range(2)]
            actb = [alloc(sp, f"actb{i}", [128, 2, 512], BF16) for i in range(2)]
            sh2 = modv.ap[:, l, 24:32, c.cond]
            for (si, t0) in c.tiles:
                norm_tile(c, (sq, sd, rstd, tmpb), t0, TT, gsc.ap[:, 1, :], sh2, lambda kc, t0=t0: hT.ap[:, kc, t0:t0 + TT], hT)
            def load_group(g):
                c0 = g * 256
                v1 = load_piece([(lambda a: r3(a, 8)[:, :, 0:256], d_wg[l, :, c0:c0 + 256].rearrange("(k p) c -> p k c", p=128)),
                                 (lambda a: r3(a, 8)[:, :, 256:512], d_wu[l, :, c0:c0 + 256].rearrange("(k p) c -> p k c", p=128))])
                v2 = load_piece([(lambda a: r3(a, 4)[:, 0:2, :], d_wd[l, c0:c0 + 256, :].rearrange("(k p) c -> p k c", p=128))])
                return (v1, v2)

            units = [(g, t0) for g in range(NFG) for (si, t0) in c.tiles]
            wts = {0: load_group(0), 1: load_group(1)}

            psF = Rot([4, 5, 6])

            def stage_a_groups(u):
                g, t0 = units[u]
                v1, v2 = wts[g]
                wgu = r3(v1.ap, 8)
                n = TT
                ab = actb[u % 2]
                outs = []
                st = {}

                def mk(j, which):
                    def f():
                        pb = psA.next()
                        c0 = (0 if which == 0 else 256) + j * 128
                        for kc in range(8):
                            mm(PS[pb].ap[:, 0:n], wgu[:, kc, c0:c0 + 128], hT.ap[:, kc, t0:t0 + n], kc == 0, kc == 7, [v1, hT], PS[pb], fresh=(kc == 0))
                        st[(j, which)] = pb
                        if which == 1:
                            pg, pu = st[(j, 0)], pb
                            act(sg[j].ap[:, 0:n], PS[pg].ap[:, 0:n], AF.Silu, [PS[pg]], [sg[j]])
                            tt(ab.ap[:, j, 0:n], sg[j].ap[:, 0:n], PS[pu].ap[:, 0:n], ALU.mult, [sg[j], PS[pu]], [ab])
                    return f
                for j in range(2):
                    outs.append(mk(j, 0))
                    outs.append(mk(j, 1))
                return outs

            def stage_b_steps(u):
                g, t0 = units[u]
                v1, v2 = wts[g]
                wdn = r3(v2.ap, 4)
                n = TT
                ab = actb[u % 2]
                outs = []

                def mk(oc):
                    def f():
                        pb = psF.next()
                        for j in range(2):
                            mm(PS[pb].ap[:, 0:n], wdn[:, j, oc * 128:(oc + 1) * 128], ab.ap[:, j, 0:n], j == 0, j == 1, [v2, ab], PS[pb], fresh=(j == 0))
                        stt(xT.ap[:, oc, t0:t0 + n], PS[pb].ap[:, 0:n], modv.ap[:, l, 40 + oc, c.cond:c.cond + 1],
                            xT.ap[:, oc, t0:t0 + n], ALU.mult, ALU.add, [PS[pb], modv, xT], [xT])
                        if oc == 7 and (u + 1 == len(units) or units[u + 1][0] != g) and g + 2 < NFG:
                            wts[g + 2] = load_group(g + 2)
                    return f
                for oc in range(8):
                    outs.append(mk(oc))
                return outs

            modgen = None
            if l + 1 < n_layers and (l + 1) not in mods_done:
                mslots = [alloc(sp, f"mslot{i}", [128, 2048], BF16) for i in range(2)]
                modgen = mod_layer(l + 1, mslots)
            every = max(1, len(units) // 24)
            for u in range(len(units) + 1):
                ga = stage_a_groups(u) if u < len(units) else []
                gb = stage_b_steps(u - 1) if u >= 1 else []
                for k in range(4):
                    if ga:
                        ga[k]()
                    if gb:
                        gb[2 * k]()
                        gb[2 * k + 1]()
                if modgen is not None and u % every == every - 1:
                    next(modgen, None)
            if modgen is not None:
                for _ in modgen:
                    pass


        def final_out(c):
            sp = Bump([(SC0, SBYTES)])
            sq = [alloc(sp, f"osq{i}", [128, 512], BF16) for i in range(2)]
            sd = alloc(sp, "osd", [128, 512], F32)
            rstd = alloc(sp, "orstd", [128, 512], F32)
            tmpb = [alloc(sp, f"otmpb{i}", [128, 512], F32) for i in range(2)]
            yf = alloc(sp, "yf", [128, 8, 128], F32)
            ytm = [alloc(sp, f"ytm{i}", [128, D], F32) for i in range(2)]
            gf = cnd.ap[:, 16:24]
            for b in range(c.nblk):
                norm_tile(c, (sq, sd, rstd, tmpb), b * 128, 128, gf, None, lambda kc: yf.ap[:, kc, :], yf)
                y = ytm[b % 2]
                for half in range(2):
                    pb = psA.next()
                    for q in range(4):
                        kc = half * 4 + q
                        tp(PS[pb].ap[:, q * 128:(q + 1) * 128], yf.ap[:, kc, :], identf.ap, [yf, cstf], PS[pb], fresh=(q == 0))
                    act(y.ap[:, half * 512:(half + 1) * 512], PS[pb].ap, AF.Copy, [PS[pb]], [y])
                dma("sp", c.oy[b * 128:(b + 1) * 128, :], y.ap, reads=[y])

        passes = []
        if do_sample:
            passes.append(make_cfg("S"))
        if do_prompt:
            passes.append(make_cfg("P"))
        for c in passes:
            load_x(c)
            for l in range(n_layers):
                layer_pass(c, l)
            final_out(c)

        names = P.finalize()
        print("nops", len(P.ops), "nsems", len(names))
        sems = {nm: es.enter_context(nc.semaphore(f"s{i}")) for i, nm in enumerate(names)}
        with nc.Block() as block:
            P.emit(block, sems)
    return nc


def _consts():
    i = np.arange(128)
    ident = np.eye(128, dtype=np.float32)
    suf = (i[:, None] > i[None, :]).astype(np.float32)
    sub = (i[:, None] < i[None, :]).astype(np.float32)
    trif = (i[:, None] <= i[None, :]).astype(np.float32)
    trib = (i[:, None] >= i[None, :]).astype(np.float32)
    cst = np.concatenate([ident, suf, sub, trif, trib], axis=1).astype(np.float32)
    t = np.arange(TS)
    row = (t // 64).astype(np.float32)
    col = (t % 64).astype(np.float32)
    nf = 8
    inv = (10000.0 ** (-np.arange(nf, dtype=np.float32) / nf)).astype(np.float32)
    ang = np.stack([row[:, None] * inv, col[:, None] * inv], axis=1)
    cos = np.cos(ang).astype(np.float32)
    sin = np.sin(ang).astype(np.float32)
    rope = np.zeros((2, 128, TS), np.float32)
    for a in range(2):
        for half in range(2):
            for f in range(nf):
                r = 64 + a * 16 + half * 8 + f
                rope[0, r] = cos[:, a, f]
                rope[1, r] = sin[:, a, f]
    return cst, rope


_CACHE = {}


def kernel(x_prompt, x_sample, c, cache_ckv, cache_krope, state_ssd, c_ctx, w_ada, b_ada,
           g_mix, w_in, g_q, w_uq, g_kv, w_ukv, ssd_conv_w, ssd_conv_b, ssd_dt_bias,
           ssd_a_log, ssd_d, ssd_norm_g, cm_conv_w, cm_conv_b, cm_ln_g, cm_ln_b, w_out,
           g_ffn, w_gate, w_up, w_down, g_final, _n_layers=DEPTH, _do_sample=True, _do_prompt=True, _stop=None):
    f = lambda a: np.ascontiguousarray(np.asarray(a, dtype=np.float32))
    key = (_n_layers, _do_sample, _do_prompt, _stop)
    if key not in _CACHE:
        _CACHE[key] = build_program(_n_layers, _do_sample, _do_prompt, _stop)
    nc = _CACHE[key]
    cst, rope = _consts()
    shared = dict(
        w_ada=f(w_ada), b_ada=f(b_ada), g_mix=f(g_mix), w_in=f(w_in), g_q=f(g_q), w_uq=f(w_uq), g_kv=f(g_kv),
        w_ukv=f(w_ukv), ssd_conv_w=f(ssd_conv_w), ssd_conv_b=f(ssd_conv_b), ssd_dt_bias=f(ssd_dt_bias).reshape(-1),
        ssd_a_log=f(ssd_a_log).reshape(-1), ssd_d=f(ssd_d).reshape(-1), ssd_norm_g=f(ssd_norm_g), cm_conv_w=f(cm_conv_w),
        cm_conv_b=f(cm_conv_b), cm_ln_g=f(cm_ln_g), cm_ln_b=f(cm_ln_b), w_out=f(w_out), g_ffn=f(g_ffn),
        w_gate=f(w_gate), w_up=f(w_up), w_down=f(w_down), g_final=f(g_final), cst=cst, rope=rope)
    x_prompt, x_sample, c, c_ctx = f(x_prompt), f(x_sample), f(c), f(c_ctx)
    cache_ckv, cache_krope, state_ssd = f(cache_ckv), f(cache_krope), f(state_ssd)
    in_maps = []
    for i in range(8):
        b = i % 4
        m = dict(shared)
        m["x_s"] = x_sample[b]
        m["x_p"] = x_prompt[2 * i:2 * i + 2].reshape(NPS * TPS, D)
        m["cond"] = np.stack([c[b], c_ctx], axis=0)
        m["cache_ckv"] = cache_ckv[b]
        m["cache_krope"] = cache_krope[b]
        m["state_ssd"] = state_ssd[b]
        in_maps.append(m)
    res = run_bass_kernel_spmd(nc, in_maps, core_ids=list(range(8)))
    r = res.results
    y_sample = np.stack([r[b]["y_s"] for b in range(4)], axis=0).astype(np.float32)
    y_prompt = np.concatenate([r[i]["y_p"].reshape(NPS, TPS, D) for i in range(8)], axis=0).astype(np.float32)
    new_ckv = np.concatenate([r[i]["o_ckv"] for i in range(8)], axis=0).astype(np.float32)
    new_kr = np.concatenate([r[i]["o_kr"] for i in range(8)], axis=0).astype(np.float32)
    new_ssd = np.concatenate([r[i]["o_ssd"] for i in range(8)], axis=0).astype(np.float32)
    return (y_prompt, y_sample, new_ckv, new_kr, new_ssd)
```
